# Optimizing a Trainium2 kernel written in Bass

```python
import math
import jax, jax.numpy as jnp
from jax import lax
import numpy as np

D_MODEL = 1024
BATCH = 8
SEQ = 2048
DEPTH = 2

DIFF_HEADS = 4
DIFF_QK_DIM = D_MODEL // 16
DIFF_V_DIM = 2 * DIFF_QK_DIM
DIFF_WIDTH = DIFF_HEADS * DIFF_V_DIM
DIFF_QK_COLS = DIFF_HEADS * 2 * DIFF_QK_DIM
Q_BLOCK = 128
ROPE_THETA = 10000.0
FOURIER_GROUPS = 4
FOURIER_GROUP_DIM = D_MODEL // 16
FOURIER_WIDTH = FOURIER_GROUPS * FOURIER_GROUP_DIM
GLA_HEADS = 4
GLA_V_DIM = D_MODEL // 16
GLA_K_DIM = GLA_V_DIM // 2
GLA_WIDTH = GLA_HEADS * GLA_V_DIM
GLA_K_COLS = GLA_HEADS * GLA_K_DIM
GLA_GATE_RANK = 16
GLA_GATE_TAU = 16.0
GLA_CHUNK = 64
IN_WIDTH = 3 * DIFF_QK_COLS + FOURIER_WIDTH + 2 * GLA_K_COLS + 2 * GLA_WIDTH + 2 * GLA_GATE_RANK
MIX_WIDTH = DIFF_WIDTH + FOURIER_WIDTH + GLA_WIDTH
FFN_HIDDEN = ((math.ceil(8 * D_MODEL / 3) + 255) // 256) * 256
DEEPNORM_ALPHA = (2 * DEPTH) ** 0.25
DEEPNORM_BETA = (8 * DEPTH) ** -0.25
LN_EPS = 1e-5

kernel_name = "hybrid_diffattn_fnet_gla_deepnorm_encoder"


def layer_norm(x, g, b):
    xf = x.astype(jnp.float32)
    mu = jnp.mean(xf, axis=-1, keepdims=True)
    var = jnp.mean(jnp.square(xf - mu), axis=-1, keepdims=True)
    y = (xf - mu) * lax.rsqrt(var + LN_EPS)
    return (y * g.astype(jnp.float32) + b.astype(jnp.float32)).astype(x.dtype)


def rms_norm(x, g):
    xf = x.astype(jnp.float32)
    y = xf * lax.rsqrt(jnp.mean(jnp.square(xf), axis=-1, keepdims=True) + LN_EPS)
    return (y * g.astype(jnp.float32)).astype(x.dtype)


def rope_tables(seq_len, dim):
    pos = jnp.arange(seq_len, dtype=jnp.float32)
    inv_freq = ROPE_THETA ** (-jnp.arange(0, dim, 2, dtype=jnp.float32) / dim)
    ang = pos[:, None] * inv_freq[None, :]
    return jnp.cos(ang), jnp.sin(ang)


def apply_rope(t, cos, sin):
    tf = t.astype(jnp.float32)
    half = tf.shape[-1] // 2
    t1, t2 = tf[..., :half], tf[..., half:]
    c = cos[None, :, None, None, :]
    s = sin[None, :, None, None, :]
    return jnp.concatenate([t1 * c - t2 * s, t2 * c + t1 * s], axis=-1).astype(t.dtype)


def split_in_proj(h):
    sizes = (DIFF_QK_COLS, DIFF_QK_COLS, DIFF_WIDTH, FOURIER_WIDTH,
             GLA_K_COLS, GLA_K_COLS, GLA_WIDTH, GLA_WIDTH, 2 * GLA_GATE_RANK)
    idx = np.cumsum(sizes)[:-1].tolist()
    return jnp.split(h, idx, axis=-1)


def diff_attention(q, k, v, lam_params, lam_init, g, cos, sin):
    B, S, _ = q.shape
    qh = apply_rope(q.reshape(B, S, DIFF_HEADS, 2, DIFF_QK_DIM), cos, sin)
    kh = apply_rope(k.reshape(B, S, DIFF_HEADS, 2, DIFF_QK_DIM), cos, sin)
    vh = v.reshape(B, S, DIFF_HEADS, DIFF_V_DIM)
    lp = lam_params.astype(jnp.float32)
    lam = jnp.exp(jnp.sum(lp[0] * lp[1])) - jnp.exp(jnp.sum(lp[2] * lp[3])) + lam_init
    scale = DIFF_QK_DIM ** -0.5
    nb = S // Q_BLOCK
    qb = jnp.moveaxis(qh.reshape(B, nb, Q_BLOCK, DIFF_HEADS, 2, DIFF_QK_DIM), 1, 0)

    def block(qi):
        s = jnp.einsum('bqhmd,bkhmd->bhmqk', qi, kh,
                       preferred_element_type=jnp.float32) * scale
        p = jax.nn.softmax(s, axis=-1)
        a = p[:, :, 0] - lam * p[:, :, 1]
        return jnp.einsum('bhqk,bkhe->bqhe', a.astype(vh.dtype), vh)

    o = lax.map(block, qb)
    o = jnp.moveaxis(o, 0, 1).reshape(B, S, DIFF_HEADS, DIFF_V_DIM)
    o = rms_norm(o, g) * (1.0 - lam_init)
    return o.reshape(B, S, DIFF_WIDTH)


def fourier_mix(u, w):
    B, S, _ = u.shape
    uf = u.astype(jnp.float32).reshape(B, S, FOURIER_GROUPS, FOURIER_GROUP_DIM)
    z = jnp.fft.fft2(uf, axes=(1, 3), norm='ortho').real
    y = jnp.einsum('bsgc,gce->bsge', z.astype(u.dtype), w)
    return y.reshape(B, S, FOURIER_WIDTH)


def gla_causal_chunked(q, k, v, g):
    B, H, S, dk = q.shape
    dv = v.shape[-1]
    C = GLA_CHUNK
    N = S // C
    q = q.reshape(B, H, N, C, dk)
    k = k.reshape(B, H, N, C, dk)
    v = v.reshape(B, H, N, C, dv)
    b = lax.cumsum(g.reshape(B, H, N, C, dk), axis=3)
    q_t = q * jnp.exp(b)
    k_t = k * jnp.exp(-b)
    mask = jnp.tril(jnp.ones((C, C), jnp.float32))
    a = jnp.einsum('bhncd,bhned->bhnce', q_t, k_t) * mask
    o_intra = jnp.einsum('bhnce,bhnev->bhncv', a, v)
    b_last = b[:, :, :, -1:, :]
    chunk_kv = jnp.einsum('bhncd,bhncv->bhndv', k * jnp.exp(b_last - b), v)
    chunk_decay = jnp.exp(b_last[:, :, :, 0, :])

    def step(state, inp):
        dec, kv = inp
        return dec[..., None] * state + kv, state

    init = jnp.zeros((B, H, dk, dv), jnp.float32)
    _, s_prev = lax.scan(step, init, (jnp.moveaxis(chunk_decay, 2, 0),
                                      jnp.moveaxis(chunk_kv, 2, 0)))
    s_prev = jnp.moveaxis(s_prev, 0, 2)
    o_inter = jnp.einsum('bhncd,bhndv->bhncv', q_t, s_prev)
    return (o_intra + o_inter).reshape(B, H, S, dv)


def gla_bidirectional(q, k, v, r, z, w2, b2, g):
    B, S, _ = q.shape

    def heads(t, d):
        return t.astype(jnp.float32).reshape(B, S, GLA_HEADS, d).transpose(0, 2, 1, 3)

    qh = heads(q, GLA_K_DIM) * GLA_K_DIM ** -0.5
    kh = heads(k, GLA_K_DIM)
    vh = heads(v, GLA_V_DIM)
    zz = z.astype(jnp.float32).reshape(B, S, 2, GLA_GATE_RANK)
    logit = jnp.einsum('bsdr,drk->dbsk', zz, w2.astype(jnp.float32)) \
        + b2.astype(jnp.float32)[:, None, None, :]
    log_a = jax.nn.log_sigmoid(logit) / GLA_GATE_TAU
    ga_f = heads(log_a[0], GLA_K_DIM)
    ga_b = heads(log_a[1], GLA_K_DIM)
    flip = lambda t: jnp.flip(t, axis=2)
    o_f = gla_causal_chunked(qh, kh, vh, ga_f)
    o_b = flip(gla_causal_chunked(flip(qh), flip(kh), flip(vh), flip(ga_b)))
    o = (o_f + o_b).transpose(0, 2, 1, 3)
    gate = jax.nn.silu(r.astype(jnp.float32)).reshape(B, S, GLA_HEADS, GLA_V_DIM)
    o = rms_norm(o, g) * gate
    return o.reshape(B, S, GLA_WIDTH).astype(q.dtype)


def hybrid_mixer(x, w_in, lam_params, lam_init, diff_g, fourier_w, gla_w2, gla_b2,
                 gla_g, w_out, cos, sin):
    h = jnp.einsum('bsd,de->bse', x, w_in)
    dq, dk, dv, fu, gq, gk, gv, gr, gz = split_in_proj(h)
    o_diff = diff_attention(dq, dk, dv, lam_params, lam_init, diff_g, cos, sin)
    o_four = fourier_mix(fu, fourier_w)
    o_gla = gla_bidirectional(gq, gk, gv, gr, gz, gla_w2, gla_b2, gla_g)
    o = jnp.concatenate([o_diff, o_four.astype(o_diff.dtype), o_gla.astype(o_diff.dtype)], axis=-1)
    return jnp.einsum('bse,ed->bsd', o, w_out)


def swiglu_ffn(x, w_gate, w_up, w_down):
    hg = jnp.einsum('bsd,df->bsf', x, w_gate)
    hu = jnp.einsum('bsd,df->bsf', x, w_up)
    return jnp.einsum('bsf,fd->bsd', jax.nn.silu(hg) * hu, w_down)


def setup_inputs(seed: int = 0) -> dict:
    key = jax.random.key(seed)
    ks = jax.random.split(key, 16)
    f32 = jnp.float32
    nrm = lambda k, shape: jax.random.normal(k, shape, f32)
    x = nrm(ks[0], (BATCH, SEQ, D_MODEL))
    col_scale = jnp.concatenate([
        jnp.ones((2 * DIFF_QK_COLS,), f32),
        jnp.full((DIFF_WIDTH,), DEEPNORM_BETA, f32),
        jnp.ones((FOURIER_WIDTH + 2 * GLA_K_COLS,), f32),
        jnp.full((GLA_WIDTH,), DEEPNORM_BETA, f32),
        jnp.ones((GLA_WIDTH + 2 * GLA_GATE_RANK,), f32)])
    w_in = nrm(ks[1], (DEPTH, D_MODEL, IN_WIDTH)) * (D_MODEL ** -0.5) * col_scale
    diff_lambda = nrm(ks[2], (DEPTH, 4, DIFF_QK_DIM)) * 0.1
    diff_norm_g = 1.0 + 0.02 * nrm(ks[3], (DEPTH, DIFF_V_DIM))
    fourier_w = nrm(ks[4], (DEPTH, FOURIER_GROUPS, FOURIER_GROUP_DIM, FOURIER_GROUP_DIM)) \
        * (FOURIER_GROUP_DIM ** -0.5)
    gla_gate_w2 = nrm(ks[5], (DEPTH, 2, GLA_GATE_RANK, GLA_K_COLS)) * (GLA_GATE_RANK ** -0.5)
    gla_gate_b2 = 0.1 * nrm(ks[6], (DEPTH, 2, GLA_K_COLS))
    gla_norm_g = 1.0 + 0.02 * nrm(ks[7], (DEPTH, GLA_V_DIM))
    w_out = nrm(ks[8], (DEPTH, MIX_WIDTH, D_MODEL)) * (MIX_WIDTH ** -0.5) * DEEPNORM_BETA
    ln1_g = 1.0 + 0.02 * nrm(ks[9], (DEPTH, D_MODEL))
    ln1_b = 0.02 * nrm(ks[10], (DEPTH, D_MODEL))
    ffn_w_gate = nrm(ks[11], (DEPTH, D_MODEL, FFN_HIDDEN)) * (D_MODEL ** -0.5)
    ffn_w_up = nrm(ks[12], (DEPTH, D_MODEL, FFN_HIDDEN)) * (D_MODEL ** -0.5)
    ffn_w_down = nrm(ks[13], (DEPTH, FFN_HIDDEN, D_MODEL)) * (FFN_HIDDEN ** -0.5) * DEEPNORM_BETA
    ln2_g = 1.0 + 0.02 * nrm(ks[14], (DEPTH, D_MODEL))
    ln2_b = 0.02 * nrm(ks[15], (DEPTH, D_MODEL))
    return {"x": x, "w_in": w_in, "diff_lambda": diff_lambda, "diff_norm_g": diff_norm_g,
            "fourier_w": fourier_w, "gla_gate_w2": gla_gate_w2, "gla_gate_b2": gla_gate_b2,
            "gla_norm_g": gla_norm_g, "w_out": w_out, "ln1_g": ln1_g, "ln1_b": ln1_b,
            "ffn_w_gate": ffn_w_gate, "ffn_w_up": ffn_w_up, "ffn_w_down": ffn_w_down,
            "ln2_g": ln2_g, "ln2_b": ln2_b}


def reference(x, w_in, diff_lambda, diff_norm_g, fourier_w, gla_gate_w2, gla_gate_b2,
              gla_norm_g, w_out, ln1_g, ln1_b, ffn_w_gate, ffn_w_up, ffn_w_down,
              ln2_g, ln2_b):
    cos, sin = rope_tables(x.shape[1], DIFF_QK_DIM)
    for l in range(DEPTH):
        lam_init = 0.8 - 0.6 * math.exp(-0.3 * l)
        m = hybrid_mixer(x, w_in[l], diff_lambda[l], lam_init, diff_norm_g[l], fourier_w[l],
                         gla_gate_w2[l], gla_gate_b2[l], gla_norm_g[l], w_out[l], cos, sin)
        x = layer_norm(DEEPNORM_ALPHA * x + m.astype(x.dtype), ln1_g[l], ln1_b[l])
        f = swiglu_ffn(x, ffn_w_gate[l], ffn_w_up[l], ffn_w_down[l])
        x = layer_norm(DEEPNORM_ALPHA * x + f.astype(x.dtype), ln2_g[l], ln2_b[l])
    return x
```

```python
import numpy as np
import concourse.bass as bass
import concourse.mybir as mybir

F32 = mybir.dt.float32
BF16 = mybir.dt.bfloat16
ALU = mybir.AluOpType
AF = mybir.ActivationFunctionType
AX = mybir.AxisListType

_DT_SIZE = {F32: 4, BF16: 2, mybir.dt.float32r: 4, mybir.dt.int32: 4,
            mybir.dt.uint32: 4, mybir.dt.float16: 2, mybir.dt.uint16: 2,
            mybir.dt.int16: 2, mybir.dt.uint8: 1, mybir.dt.int8: 1}

ENGS = ("pe", "act", "dve", "pool", "sp")
N_DMA_SEMS = 24
TINY_BYTES = 256
SAME_ENGINE_SYNC = False


def ap_box(ap):
    t = ap.tensor
    name = t.name
    esz = _DT_SIZE[ap.dtype]
    dims = list(ap.ap)
    off = ap.offset
    space = str(ap.space)
    if "DRAM" in space.upper() or "HBM" in space.upper():
        lo = off
        hi = off
        for (st, n) in dims:
            if st >= 0:
                hi += st * (n - 1)
            else:
                lo += st * (n - 1)
        return (name, 0, 1, lo * esz, (hi + 1) * esz)
    pstep, pcnt = dims[0]
    if pstep == 0:
        pstep = 1 << 40
    p0 = off // pstep if pstep < (1 << 40) else 0
    f = off - p0 * pstep if pstep < (1 << 40) else off
    lo = f
    hi = f
    for (st, n) in dims[1:]:
        if st >= 0:
            hi += st * (n - 1)
        else:
            lo += st * (n - 1)
    if "PSUM" in space.upper():
        b0 = (lo * esz) // 2048
        b1 = ((hi + 1) * esz - 1) // 2048
        return (name, 0, 128, b0 * 2048, (b1 + 1) * 2048, True)
    return (name, p0, p0 + pcnt, lo * esz, (hi + 1) * esz)


def _overlap(a, b):
    return a[1] < b[2] and b[1] < a[2] and a[3] < b[4] and b[3] < a[4]


def _contains(a, b):
    return a[1] <= b[1] and a[2] >= b[2] and a[3] <= b[3] and a[4] >= b[4]


class Prog:
    def __init__(self, nc):
        self.nc = nc
        self.ins = []
        self.recs = {}
        self.dma_rr = {e: 0 for e in ENGS}
        self.trace = {}

    def add(self, eng, fn, reads=(), writes=(), dma=False):
        idx = len(self.ins)
        rb = list(dict.fromkeys(ap_box(a) for a in reads))
        wb = list(dict.fromkeys(ap_box(a) for a in writes))
        deps = set()
        tiny_deps = set()
        for b in rb:
            psum = len(b) > 5
            tiny = (not psum) and (b[4] - b[3]) <= TINY_BYTES
            for rec in self.recs.get(b[0], ()):
                if (rec[1] or (psum and rec[3] != eng)) and _overlap(rec[0], b):
                    deps.add(rec[2])
                    if tiny and rec[1]:
                        tiny_deps.add(rec[2])
        for b in wb:
            for rec in self.recs.get(b[0], ()):
                if _overlap(rec[0], b):
                    deps.add(rec[2])
        for b in wb:
            lst = self.recs.setdefault(b[0], [])
            lst[:] = [r for r in lst if not _contains(b, r[0])]
            lst.append([b, True, idx, eng])
        for b in rb:
            lst = self.recs.setdefault(b[0], [])
            found = False
            for r in lst:
                if (not r[1]) and r[3] == eng and r[0] == b and not dma \
                        and not self.ins[r[2]]["dma"]:
                    r[2] = idx
                    found = True
                    break
            if not found:
                lst.append([b, False, idx, eng])
        real = set()
        for d in deps:
            p = self.ins[d]
            if p["eng"] == "pe" and eng == "pe" and not p["dma"] and not dma:
                continue
            if (not SAME_ENGINE_SYNC) and p["eng"] == eng and eng != "pool" and not p["dma"] and not dma \
                    and d not in tiny_deps:
                continue
            real.add(d)
        self.ins.append(dict(eng=eng, fn=fn, deps=real, dma=dma, needed=False,
                             sem=None, val=None))
        return idx

    def emit(self, final_wait_eng="sp"):
        nc = self.nc
        ins = self.ins
        for r in ins:
            for d in r["deps"]:
                ins[d]["needed"] = True
        last_dmas = [i for i, r in enumerate(ins) if r["dma"]]
        import contextlib
        with contextlib.ExitStack() as st:
            esem = {e: st.enter_context(nc.semaphore("s_" + e)) for e in ENGS}
            dsem = {e: [st.enter_context(nc.semaphore("d_%s_%d" % (e, i)))
                        for i in range(N_DMA_SEMS)] for e in ("sp", "act", "pool")}
            cnt = {e: 0 for e in ENGS}
            dcnt = {e: [0] * N_DMA_SEMS for e in dsem}
            drr = {e: 0 for e in dsem}
            prev_use = {}
            for i, r in enumerate(ins):
                e = r["eng"]
                if r["dma"]:
                    s = drr[e]
                    drr[e] = (s + 1) % N_DMA_SEMS
                    if dcnt[e][s] > 0:
                        r["prev"] = (dsem[e][s], dcnt[e][s], ("d", e, s))
                    else:
                        r["prev"] = None
                    dcnt[e][s] += 16
                    r["sem"] = dsem[e][s]
                    r["val"] = dcnt[e][s]
                    r["semkey"] = ("d", e, s)
                elif r["needed"]:
                    cnt[e] += 1
                    r["sem"] = esem[e]
                    r["val"] = cnt[e]
                    r["semkey"] = ("e", e)
            block = st.enter_context(nc.Block())
            per_eng = {e: [i for i, r in enumerate(ins) if r["eng"] == e] for e in ENGS}
            final = {}
            for i in last_dmas:
                r = ins[i]
                final[r["semkey"]] = (r["sem"], max(r["val"], final.get(r["semkey"], (None, 0))[1]))

            def body(ename):
                def run(engobj):
                    waited = {}
                    for i in per_eng[ename]:
                        r = ins[i]
                        need = {}
                        for d in r["deps"]:
                            p = ins[d]
                            k = p["semkey"]
                            if need.get(k, (None, 0))[1] < p["val"]:
                                need[k] = (p["sem"], p["val"])
                        if r["dma"] and r["prev"] is not None:
                            s, v, k = r["prev"]
                            if need.get(k, (None, 0))[1] < v:
                                need[k] = (s, v)
                        for k, (s, v) in need.items():
                            if waited.get(k, 0) < v:
                                engobj.wait_ge(s, v)
                                waited[k] = v
                                self.trace.setdefault(ename, []).append(("w", k, v, i))
                        h = r["fn"](engobj)
                        if r["dma"]:
                            h.then_inc(r["sem"], 16)
                            self.trace.setdefault(ename, []).append(("i", r["semkey"], 16, i))
                        elif r["needed"]:
                            h.then_inc(r["sem"], 1)
                            self.trace.setdefault(ename, []).append(("i", r["semkey"], 1, i))
                    if ename == final_wait_eng:
                        for k, (s, v) in final.items():
                            if waited.get(k, 0) < v:
                                engobj.wait_ge(s, v)
                return run

            block.tensor(body("pe"))
            block.scalar(body("act"))
            block.vector(body("dve"))
            block.gpsimd(body("pool"))
            block.sync(body("sp"))

    def dma(self, eng, out, in_, **kw):
        return self.add(eng, lambda e: e.dma_start(out=out, in_=in_, **kw),
                        reads=[in_], writes=[out], dma=True)

    def matmul(self, out, lhsT, rhs, start=True, stop=True, **kw):
        return self.add("pe", lambda e: e.matmul(out, lhsT, rhs, start=start, stop=stop, **kw),
                        reads=[lhsT, rhs], writes=[out])

    def transpose(self, out, in_, ident):
        return self.add("pe", lambda e: e.transpose(out, in_, ident),
                        reads=[in_, ident], writes=[out])

    def act(self, out, in_, func, bias=None, scale=1.0, accum_out=None, eng="act"):
        reads = [in_]
        writes = [out]
        kw = {}
        if bias is not None:
            kw["bias"] = bias
            if not isinstance(bias, (int, float)):
                reads.append(bias)
        if not isinstance(scale, (int, float)):
            reads.append(scale)
        if accum_out is not None:
            kw["accum_out"] = accum_out
            writes.append(accum_out)
        return self.add(eng, lambda e: e.activation(out=out, in_=in_, func=func, scale=scale, **kw),
                        reads=reads, writes=writes)

    def tt(self, out, in0, in1, op, eng="dve"):
        return self.add(eng, lambda e: e.tensor_tensor(out=out, in0=in0, in1=in1, op=op),
                        reads=[in0, in1], writes=[out])

    def ts(self, out, in0, s1, op0, s2=None, op1=None, eng="dve", accum_out=None):
        reads = [in0]
        if not isinstance(s1, (int, float)):
            reads.append(s1)
        if s2 is not None and not isinstance(s2, (int, float)):
            reads.append(s2)
        kw = {}
        writes = [out]
        if op1 is not None:
            kw["op1"] = op1
        if accum_out is not None:
            kw["accum_out"] = accum_out
            writes.append(accum_out)
        return self.add(eng, lambda e: e.tensor_scalar(out=out, in0=in0, scalar1=s1, scalar2=s2,
                                                       op0=op0, **kw),
                        reads=reads, writes=writes)

    def stt(self, out, in0, scalar, in1, op0, op1, eng="dve"):
        reads = [in0, in1]
        if not isinstance(scalar, (int, float)):
            reads.append(scalar)
        return self.add(eng, lambda e: e.scalar_tensor_tensor(out=out, in0=in0, scalar=scalar,
                                                              in1=in1, op0=op0, op1=op1),
                        reads=reads, writes=[out])

    def copy(self, out, in_, eng="dve"):
        if eng == "act":
            return self.add(eng, lambda e: e.activation(out=out, in_=in_, func=AF.Identity),
                            reads=[in_], writes=[out])
        return self.add(eng, lambda e: e.tensor_copy(out=out, in_=in_), reads=[in_], writes=[out])

    def memset(self, ap, val, eng="dve"):
        return self.add(eng, lambda e: e.memset(ap, val), reads=[], writes=[ap])

    def reduce(self, out, in_, op, axis=AX.X, eng="dve", **kw):
        return self.add(eng, lambda e: e.tensor_reduce(out=out, in_=in_, op=op, axis=axis, **kw),
                        reads=[in_], writes=[out])

    def recip(self, out, in_, eng="dve"):
        return self.add(eng, lambda e: e.reciprocal(out=out, in_=in_), reads=[in_], writes=[out])


def simulate_trace(trace):
    pos = {e: 0 for e in trace}
    sem = {}
    progress = True
    while progress:
        progress = False
        for e, ops in trace.items():
            while pos[e] < len(ops):
                kind, k, v, i = ops[pos[e]]
                if kind == "w":
                    if sem.get(k, 0) >= v:
                        pos[e] += 1
                        progress = True
                    else:
                        break
                else:
                    sem[k] = sem.get(k, 0) + v
                    pos[e] += 1
                    progress = True
    stuck = {e: ops[pos[e]] for e, ops in trace.items() if pos[e] < len(ops)}
    return stuck, sem

import contextlib
import os
_SK = set(os.environ.get('DBGSKIP', '').split(','))
import math
import ml_dtypes
from concourse.bass_utils import run_bass_kernel_spmd

S = 2048
D = 1024
L = 2
INW = 2592
FF = 2816
NT = 16
NFC = FF // 128
ALPHA = float((2 * L) ** 0.25)
EPS = 1e-5
C_DQ, C_DK, C_DV, C_FU, C_GQ, C_GK, C_GV, C_GR, C_GZ = 0, 512, 1024, 1536, 1792, 1920, 2048, 2304, 2560

CF_ID, CF_COS, CF_SIN, CF_HM, CF_CC, CF_SC, CF_ONES, CF_NEGH, CF_N = 0, 128, 640, 1152, 1156, 1284, 1412, 1540, 1548
CB_ID, CB_TRIF, CB_TRIB, CB_BD, CB_ONE, CB_N = 0, 128, 256, 384, 640, 768


def host_consts():
    p = np.arange(128)
    cf = np.zeros((128, CF_N), np.float32)
    cf[:, CF_ID:CF_ID + 128] = np.eye(128, dtype=np.float32)
    inv_freq = (10000.0 ** (-np.arange(0, 64, 2, dtype=np.float32) / 64)).astype(np.float32)
    pos = (np.arange(NT)[None, :] * 128 + p[:, None]).astype(np.float32)
    ang = pos[:, :, None] * inv_freq[None, None, :]
    cf[:, CF_COS:CF_COS + 512] = np.cos(ang).reshape(128, 512)
    cf[:, CF_SIN:CF_SIN + 512] = np.sin(ang).reshape(128, 512)
    cf[:, CF_HM:CF_HM + 4] = (p[:, None] // 32 == np.arange(4)[None, :]).astype(np.float32)
    c = np.arange(64)
    a = 2 * np.pi * np.outer(c, c) / 64.0
    cc = np.cos(a) / 8.0
    sc = np.sin(a) / 8.0
    z = np.zeros((64, 64))
    cf[:, CF_CC:CF_CC + 128] = np.block([[cc, z], [z, cc]])
    cf[:, CF_SC:CF_SC + 128] = np.block([[sc, z], [z, sc]])
    cf[:, CF_ONES:CF_ONES + 128] = 1.0
    cf[:, CF_NEGH:CF_NEGH + 8] = -0.5
    cb = np.zeros((128, CB_N), np.float32)
    cb[:, CB_ID:CB_ID + 128] = np.eye(128)
    cb[:, CB_TRIF:CB_TRIF + 128] = (p[:, None] <= p[None, :])
    cb[:, CB_TRIB:CB_TRIB + 128] = (p[:, None] >= p[None, :])
    cb[:, CB_BD:CB_BD + 256] = (p[:, None] // 32 == np.arange(256)[None, :] // 64)
    cb[:, CB_ONE:CB_ONE + 128] = 1.0
    s = np.arange(S, dtype=np.float64)
    sk = np.outer(s, s) % S
    ang = 2 * np.pi * sk / S
    dc = (np.cos(ang) / math.sqrt(S)).astype(np.float32).astype(ml_dtypes.bfloat16)
    ds = (-np.sin(ang) / math.sqrt(S)).astype(np.float32).astype(ml_dtypes.bfloat16)
    return cf, cb.astype(ml_dtypes.bfloat16), dc, ds


class Arena:
    def __init__(self, t, nbytes):
        self.t = t
        self.n = nbytes
        self.top = 0

    def mark(self):
        return self.top

    def release(self, m):
        self.top = m

    def alloc(self, shape, dt):
        n = 1
        for s_ in shape:
            n *= s_
        b = n * _DT_SIZE[dt]
        off = self.top
        self.top += (b + 63) // 64 * 64
        assert self.top <= self.n, ("arena overflow", self.top, self.n)
        ap = self.t[:, off // 2:(off + b) // 2]
        if dt != BF16:
            ap = ap.bitcast(dt)
        if len(shape) == 2:
            ap = ap.rearrange("p (a b) -> p a b", a=shape[0])
        elif len(shape) == 3:
            ap = ap.rearrange("p (a b c) -> p a b c", a=shape[0], b=shape[1])
        return ap


class Rot:
    def __init__(self, items):
        self.items = items
        self.i = 0

    def next(self):
        r = self.items[self.i % len(self.items)]
        self.i += 1
        return r


_CACHE = {}


class _Stop(Exception):
    pass


def build(n_layers=L, dbg=None, upto=None):
    nc = bass.Bass("TRN2", target_bir_lowering=False)
    dram = lambda name, shape, dt, kind="ExternalInput": nc.dram_tensor(name, shape, dt, kind=kind).ap()
    x_d = dram("x", [S, D], F32)
    w_in_d = dram("w_in", [L, D, INW], F32)
    lam_d = dram("diff_lambda", [L, 256], F32)
    dng_d = dram("diff_norm_g", [L, 128], F32)
    fw_d = dram("fourier_w", [L, 4, 64, 64], F32)
    w2_d = dram("gla_gate_w2", [L, 2, 16, 128], F32)
    b2_d = dram("gla_gate_b2", [L, 2, 128], F32)
    gng_d = dram("gla_norm_g", [L, 64], F32)
    wout_d = dram("w_out", [L, D, D], F32)
    ln_d = {k: dram(k, [L, D], F32) for k in ("ln1_g", "ln1_b", "ln2_g", "ln2_b")}
    wg_d = dram("ffn_w_gate", [L, D, FF], F32)
    wu_d = dram("ffn_w_up", [L, D, FF], F32)
    wd_d = dram("ffn_w_down", [L, FF, D], F32)
    cf_d = dram("c_f32", [128, CF_N], F32)
    cb_d = dram("c_bf", [128, CB_N], BF16)
    dftc_d = dram("dft_c", [S, S], BF16)
    dfts_d = dram("dft_s", [S, S], BF16)
    y_d = dram("y", [S, D], F32, kind="ExternalOutput")
    xs_d = dram("xs_scr", [S, D], F32, kind="Internal")
    dbg_outs = {}

    ARENA = 207 * 1024
    XTB = NT * D * 4
    with contextlib.ExitStack() as st:
        arena_t = st.enter_context(nc.sbuf_tensor("arena", [128, ARENA // 2], BF16))
        ps = st.enter_context(nc.psum_tensor("ps", [128, 4096], F32))
        P = Prog(nc)
        A = Arena(arena_t, ARENA)
        x_tok = arena_t[:, (ARENA - XTB) // 2:ARENA // 2].bitcast(F32).rearrange("p (t d) -> p t d", t=NT)

        def bank(b, n=512, off=0):
            return ps[:, b * 512 + off:b * 512 + off + n]

        def bankbf(b):
            return ps[:, b * 512:(b + 1) * 512].bitcast(BF16)

        def dump(name, ap, shape, dt=F32):
            if dbg is None or name not in dbg:
                return
            d_ = nc.dram_tensor("dbg_" + name, shape, dt, kind="ExternalOutput").ap()
            dbg_outs[name] = d_
            P.dma("sp", d_, ap)

        cf = A.alloc([CF_N], F32)
        cb = A.alloc([CB_N], BF16)
        xT = A.alloc([8, S], BF16)
        lnbuf = A.alloc([2, D], F32)
        small = A.alloc([640], F32)
        P.dma("sp", cf, cf_d)
        P.dma("sp", cb, cb_d)
        ident_f = cf[:, CF_ID:CF_ID + 128]
        ident_b = cb[:, CB_ID:CB_ID + 128]
        cos_t = cf[:, CF_COS:CF_COS + 512].rearrange("p (t i) -> p t i", t=NT)
        sin_t = cf[:, CF_SIN:CF_SIN + 512].rearrange("p (t i) -> p t i", t=NT)
        hmask4 = cf[:, CF_HM:CF_HM + 4]
        ccbd = cf[:, CF_CC:CF_CC + 128]
        scbd = cf[:, CF_SC:CF_SC + 128]
        ones_f = cf[:, CF_ONES:CF_ONES + 128]
        negh8 = cf[:, CF_NEGH:CF_NEGH + 8]
        tri = [cb[:, CB_TRIF:CB_TRIF + 128], cb[:, CB_TRIB:CB_TRIB + 128]]
        bdmask = cb[:, CB_BD:CB_BD + 256]
        lp_bc = small[:, 0:256]
        gdiff = small[:, 256:384]
        ggla = small[:, 384:448]
        negb2 = small[:, 448:450]
        neglam = small[:, 450:451]
        negM = small[:, 451:452]
        sc_tmp = small[:, 452:500]
        nrm = small[:, 500:504]
        epsc = small[:, 504:505]
        base_mark = A.mark()

        win = lambda l: w_in_d[l].rearrange("(k p) n -> p k n", p=128)

        def rstd_from(out, ss, n, width):
            tmp = sc_tmp[:, 40:40 + width]
            P.ts(tmp, ss, 1.0 / n, ALU.mult, EPS, ALU.add)
            P.tt(out, tmp, negh8[:, 0:width], ALU.pow, eng="pool")

        xb_rot = None

        def make_xT(x_tile_f32, t, xb):
            P.copy(xb, x_tile_f32, eng="act")
            pst = bankbf(7)
            for k in range(8):
                P.transpose(pst[:, k * 128:(k + 1) * 128], xb[:, k * 128:(k + 1) * 128], ident_b)
            P.copy(xT[:, :, t * 128:(t + 1) * 128], pst.rearrange("p (k n) -> p k n", k=8), eng="dve")

        def layer_norm_tile(psum2, xres, gb, out_tok, ytmp):
            P.stt(ytmp, xres, ALPHA, psum2, ALU.mult, ALU.add)
            stats = sc_tmp[:, 0:12]
            mv = sc_tmp[:, 12:14]
            for c_ in range(2):
                P.add("dve", lambda e, c_=c_: e.bn_stats(out=stats[:, c_ * 6:(c_ + 1) * 6],
                                                          in_=ytmp[:, c_ * 512:(c_ + 1) * 512]),
                      reads=[ytmp[:, c_ * 512:(c_ + 1) * 512]], writes=[stats[:, c_ * 6:(c_ + 1) * 6]])
            P.add("dve", lambda e: e.bn_aggr(out=mv, in_=stats), reads=[stats], writes=[mv])
            rs = sc_tmp[:, 14:15]
            rstd_from(rs, mv[:, 1:2], 1.0, 1)
            P.stt(ytmp, ytmp, mv[:, 0:1], gb[:, 0, :], ALU.subtract, ALU.mult)
            P.stt(out_tok, ytmp, rs, gb[:, 1, :], ALU.mult, ALU.add)

        def ck(name):
            if upto == name:
                raise _Stop()

        try:
          for l in range(n_layers):
              lam_init = 0.8 - 0.6 * math.exp(-0.3 * l)
              A.release(base_mark)
              P.dma("sp", lp_bc, lam_d[l].partition_broadcast(128))
              P.dma("sp", gdiff, dng_d[l].partition_broadcast(128))
              P.dma("sp", ggla, gng_d[l].partition_broadcast(128))
              for d_ in range(2):
                  P.dma("sp", negb2[:, d_:d_ + 1], b2_d[l, d_].rearrange("(p o) -> p o", o=1))
              P.ts(negb2, negb2, -1.0, ALU.mult)
              P.ts(gdiff, gdiff, 1.0 - lam_init, ALU.mult)
              P.memset(epsc, EPS)
              pr = sc_tmp[:, 16:18]
              prod = A.alloc([128], F32)
              lp4 = lp_bc.rearrange("p (a d) -> p a d", a=4)
              for i_ in range(2):
                  P.tt(prod[:, 0:64], lp4[:, 2 * i_, :], lp4[:, 2 * i_ + 1, :], ALU.mult)
                  P.reduce(pr[:, i_:i_ + 1], prod[:, 0:64], ALU.add)
              P.act(pr, pr, AF.Exp)
              P.tt(neglam, pr[:, 1:2], pr[:, 0:1], ALU.subtract)
              P.ts(neglam, neglam, -lam_init, ALU.add)
              A.release(base_mark)
              ck('params')

              if l == 0:
                  m_ = A.mark()
                  xin = [A.alloc([D], F32) for _ in range(3)]
                  xbs = [A.alloc([D], BF16) for _ in range(2)]
                  for t in range(NT):
                      xi = xin[t % 3]
                      P.dma("sp", xi, x_d[t * 128:(t + 1) * 128, :])
                      make_xT(xi, t, xbs[t % 2])
                  A.release(m_)
              ck('xT')
              xres_d = x_d if l == 0 else xs_d

              ocat = A.alloc([8, S], BF16)
              Wout = A.alloc([8, D], BF16)
              mixer_mark = A.mark()

              QKT = A.alloc([2, S], BF16)
              Vaug = A.alloc([NT, 132], BF16)
              Wh = [A.alloc([8, 384], BF16) for _ in range(2)]
              PT = [A.alloc([1024], BF16) for _ in range(3)]
              O1n = A.alloc([8, 128], F32)
              O2t = A.alloc([8, 128], F32)
              otoks = [A.alloc([8, 128], BF16) for _ in range(2)]
              qkr = [A.alloc([256], BF16) for _ in range(3)]
              tmpAs = [A.alloc([128], F32) for _ in range(2)]
              tmpBs = [A.alloc([128], F32) for _ in range(2)]
              sq = A.alloc([256], F32)
              D2 = A.alloc([256], F32)
              red4 = sc_tmp[:, 20:24]
              gm = sc_tmp[:, 24:26]
              nrm2 = sc_tmp[:, 26:28]
              rz = sc_tmp[:, 28:36]
              ss8 = small[:, 512:520]
              rstd8 = small[:, 520:528]
              P.memset(Vaug[:, :, 128:129], 1.0)
              pso = [ps[:, (4 + qi // 3) * 512 + (qi % 3) * 160:(4 + qi // 3) * 512 + (qi % 3) * 160 + 129]
                     for qi in range(8)]
              grp = [(0, 3), (3, 3), (6, 2)]

              def pso_grp(gi, c0, c1):
                  q0, n_ = grp[gi]
                  base_ = (4 + gi) * 512
                  return ps[:, base_:base_ + n_ * 160].rearrange("p (a b) -> p a b", b=160)[:, :, c0:c1]

              def load_wh(h_):
                  for j_, c0 in enumerate((C_DQ, C_DK, C_DV)):
                      P.dma("pool", Wh[h_ % 2][:, :, j_ * 128:(j_ + 1) * 128],
                            win(l)[:, :, c0 + h_ * 128:c0 + (h_ + 1) * 128])

              load_wh(0)
              fin_pending = []
              for h in range(4):
                  wh = Wh[h % 2]
                  P.memset(nrm, 0.0)
                  tr_pending = []
                  for t in range(NT):
                      pb = bank(t % 4, 256)
                      pv_ = bank(4 + t % 2, 128)
                      for k in range(8):
                          P.matmul(pb, xT[:, k, t * 128:(t + 1) * 128], wh[:, k, 0:256], start=(k == 0), stop=(k == 7))
                      for k in range(8):
                          P.matmul(pv_, xT[:, k, t * 128:(t + 1) * 128], wh[:, k, 256:384], start=(k == 0), stop=(k == 7))
                      qk4 = pb.rearrange("p (g h d) -> p g h d", g=4, h=2)
                      t1 = qk4[:, :, 0, :]
                      t2 = qk4[:, :, 1, :]
                      cbt = cos_t[:, t:t + 1, :].broadcast_to([128, 4, 32])
                      sbt = sin_t[:, t:t + 1, :].broadcast_to([128, 4, 32])
                      q_ = qkr[t % 3]
                      q4 = q_.rearrange("p (g h d) -> p g h d", g=4, h=2)
                      ta = tmpAs[t % 2].rearrange("p (g d) -> p g d", g=4)
                      tb_ = tmpBs[t % 2].rearrange("p (g d) -> p g d", g=4)
                      P.tt(ta, t1, cbt, ALU.mult)
                      P.tt(tb_, t2, sbt, ALU.mult)
                      P.tt(q4[:, :, 0, :], ta, tb_, ALU.subtract)
                      P.tt(ta, t2, cbt, ALU.mult)
                      P.tt(tb_, t1, sbt, ALU.mult)
                      P.tt(q4[:, :, 1, :], ta, tb_, ALU.add)
                      P.copy(Vaug[:, t, 0:128], pv_, eng="act")
                      P.tt(sq, q_, q_, ALU.mult, eng="pool")
                      P.reduce(red4, sq.rearrange("p (g d) -> p g d", g=4), ALU.add)
                      P.tt(nrm, nrm, red4, ALU.max)
                      def tr_(t=t, q_=q_):
                          pst = bankbf(7 if t % 2 else 6)
                          P.transpose(pst[:, 0:128], q_[:, 0:128], ident_b)
                          P.transpose(pst[:, 128:256], q_[:, 128:256], ident_b)
                          P.copy(QKT[:, :, t * 128:(t + 1) * 128], pst[:, 0:256].rearrange("p (a n) -> p a n", a=2),
                                 eng="act")
                      tr_pending.append(tr_)
                      if len(tr_pending) > 1:
                          tr_pending.pop(0)()
                  while tr_pending:
                      tr_pending.pop(0)()
                  ck('h0proj')
                  if h + 1 < 4:
                      load_wh(h + 1)
                  P.reduce(nrm2, nrm.rearrange("p (a b) -> p a b", a=2), ALU.max)
                  P.ts(D2[:, 0:128], ident_f, nrm2[:, 0:1], ALU.mult)
                  P.ts(D2[:, 128:256], ident_f, nrm2[:, 1:2], ALU.mult)
                  pbm = bank(6, 256)
                  P.matmul(pbm, ones_f, D2, start=True, stop=True)
                  P.reduce(gm, pbm.rearrange("p (a b) -> p a b", a=2), ALU.max)
                  P.tt(negM, gm[:, 0:1], gm[:, 1:2], ALU.add)
                  P.ts(negM, negM, -0.5 * 0.125, ALU.mult)
                  if l == 0 and h == 0:
                      dump("xT", xT, [128, 8, S], BF16)
                      dump("QKT", QKT, [128, 2, S], BF16)
                      dump("negM", negM, [128, 1])
                      ck('qkt')
                  for qh in range(2):
                      otok = otoks[qh]
                      steps = [(m, kt) for m in range(2) for kt in range(NT)]

                      def emit_scores(i):
                          m, kt = steps[i]
                          sb_i = (i % 2) * 2
                          for j_ in range(2):
                              P.matmul(ps[:, (sb_i + j_) * 512:(sb_i + j_ + 1) * 512],
                                       QKT[m * 64:(m + 1) * 64, 1, kt * 128:(kt + 1) * 128],
                                       QKT[m * 64:(m + 1) * 64, 0, qh * 1024 + j_ * 512:qh * 1024 + (j_ + 1) * 512],
                                       start=True, stop=True)

                      emit_scores(0)
                      for i, (m, kt) in enumerate(steps):
                          if i + 1 < len(steps):
                              emit_scores(i + 1)
                          if i == 8 and fin_pending:
                              fin_pending.pop(0)()
                          sb_i = (i % 2) * 2
                          pss = ps[:, sb_i * 512:(sb_i + 2) * 512]
                          pt = PT[i % 3]
                          P.act(pt, pss, AF.Exp, bias=negM, scale=0.125)
                          for qi in range(8):
                              P.matmul(pso[qi], pt[:, qi * 128:(qi + 1) * 128], Vaug[:, kt, 0:129],
                                       start=(kt == 0 and qi % 3 == 0), stop=(kt == NT - 1), skip_group_check=True)
                          if kt != NT - 1:
                              continue
                          if m == 0:
                              for gi, (q0, n_) in enumerate(grp):
                                  P.recip(rz[:, q0:q0 + n_], pso_grp(gi, 128, 129).rearrange("p a b -> p (a b)"))
                                  P.tt(O1n[:, q0:q0 + n_, :], pso_grp(gi, 0, 128),
                                       rz[:, q0:q0 + n_].unsqueeze(2).broadcast_to([128, n_, 128]), ALU.mult)
                          else:
                              for gi, (q0, n_) in enumerate(grp):
                                  P.recip(rz[:, q0:q0 + n_], pso_grp(gi, 128, 129).rearrange("p a b -> p (a b)"))
                              P.ts(rz, rz, neglam, ALU.mult)
                              for gi, (q0, n_) in enumerate(grp):
                                  P.tt(O2t[:, q0:q0 + n_, :], pso_grp(gi, 0, 128),
                                       rz[:, q0:q0 + n_].unsqueeze(2).broadcast_to([128, n_, 128]), ALU.mult)
                              P.tt(O1n, O1n, O2t, ALU.add)
                              P.tt(O2t, O1n, O1n, ALU.mult, eng="pool")
                              P.reduce(ss8, O2t, ALU.add)
                              rstd_from(rstd8, ss8, 128.0, 8)
                              P.tt(O1n, O1n, rstd8.unsqueeze(2).broadcast_to([128, 8, 128]), ALU.mult)
                              P.tt(otok, O1n, gdiff.unsqueeze(1).broadcast_to([128, 8, 128]), ALU.mult, eng="pool")
                              def fin_(h=h, qh=qh, otok=otok):
                                  pst = bankbf(7)
                                  for qi in range(8):
                                      P.transpose(pst[:, qi * 128:(qi + 1) * 128], otok[:, qi, :], ident_b)
                                  P.copy(ocat[:, h, qh * 1024:(qh + 1) * 1024], pst, eng="dve")
                              fin_()
              while fin_pending:
                  fin_pending.pop(0)()
              A.release(mixer_mark)
              if l == 0:
                  dump("odiff", ocat[:, 0:4, :], [128, 4, S], BF16)
              ck('att')

              for k in range(8):
                  P.dma("pool", Wout[:, k, :], wout_d[l, k * 128:(k + 1) * 128, :])
              P.dma("sp", lnbuf[:, 0, :], ln_d["ln1_g"][l].partition_broadcast(128))
              P.dma("sp", lnbuf[:, 1, :], ln_d["ln1_b"][l].partition_broadcast(128))
              Wf = A.alloc([8, 256], BF16)
              uT = A.alloc([2, S], BF16)
              wbd = A.alloc([2, 128], F32)
              Wcs = A.alloc([2, 256], BF16)
              ucs = A.alloc([NT, 512], BF16)
              NDB = 4
              dbuf = [A.alloc([8, 512], BF16) for _ in range(NDB)]
              P.dma("pool", Wf, win(l)[:, :, C_FU:C_FU + 256])
              P.memset(wbd, 0.0)
              for g_ in range(4):
                  c_, gl = g_ // 2, g_ % 2
                  P.dma("sp", wbd[gl * 64:(gl + 1) * 64, c_, gl * 64:(gl + 1) * 64], fw_d[l, g_])
              for c_ in range(2):
                  pb = bank(4, 256)
                  P.matmul(pb[:, 0:128], ccbd, wbd[:, c_, :], start=True, stop=True)
                  P.matmul(pb[:, 128:256], scbd, wbd[:, c_, :], start=True, stop=True)
                  P.copy(Wcs[:, c_, :], pb, eng="dve")
              for tb in range(4):
                  for c_ in range(2):
                      pb = bank((tb * 2 + c_) % 4)
                      for k in range(8):
                          P.matmul(pb, Wf[:, k, c_ * 128:(c_ + 1) * 128], xT[:, k, tb * 512:(tb + 1) * 512],
                                   start=(k == 0), stop=(k == 7))
                      P.copy(uT[:, c_, tb * 512:(tb + 1) * 512], pb, eng=("act" if c_ else "dve"))
              for t in range(NT):
                  pb = bank(4 + t % 2)
                  for c_ in range(2):
                      P.matmul(pb[:, c_ * 256:(c_ + 1) * 256], uT[:, c_, t * 128:(t + 1) * 128], Wcs[:, c_, :],
                               start=True, stop=True)
                  P.copy(ucs[:, t, :], pb, eng=("act" if t % 2 else "dve"))
              dft_v = [dftc_d.rearrange("(t p) k -> p t k", p=128), dfts_d.rearrange("(t p) k -> p t k", p=128)]
              di = 0
              for kb in range(4):
                  pbs = [bank(0 + (kb % 2) * 2), bank(1 + (kb % 2) * 2)]
                  first = True
                  for which in range(2):
                      for tg in range(2):
                          db = dbuf[di % NDB]
                          di += 1
                          P.dma("sp", db, dft_v[which][:, tg * 8:(tg + 1) * 8, kb * 512:(kb + 1) * 512])
                          for tt_ in range(8):
                              t = tg * 8 + tt_
                              last = (which == 1 and t == NT - 1)
                              for c_ in range(2):
                                  P.matmul(pbs[c_], ucs[:, t, c_ * 256 + which * 128:c_ * 256 + (which + 1) * 128],
                                           db[:, tt_, :], start=first, stop=last)
                              first = False
                  for c_ in range(2):
                      P.copy(ocat[:, 4 + c_, kb * 512:(kb + 1) * 512], pbs[c_], eng=("act" if c_ else "dve"))
              A.release(mixer_mark)
              if l == 0:
                  dump("ofour", ocat[:, 4:6, :], [128, 2, S], BF16)
              ck('four')

              gqk = A.alloc([2, S], F32)
              gzT = A.alloc([S], BF16)
              gv = A.alloc([NT, 256], BF16)
              gate = A.alloc([NT, 256], BF16)
              w2pad = A.alloc([2, 128], BF16)
              qt = [A.alloc([S], BF16) for _ in range(2)]
              kt_ = [A.alloc([S], BF16) for _ in range(2)]
              Sbf = [A.alloc([NT, 256], BF16) for _ in range(2)]
              gla_mark = A.mark()
              Wgf = A.alloc([8, 288], BF16)
              Wgt = A.alloc([8, 512], BF16)
              P.dma("pool", Wgf[:, :, 0:256], win(l)[:, :, C_GQ:C_GQ + 256])
              P.dma("pool", Wgf[:, :, 256:288], win(l)[:, :, C_GZ:C_GZ + 32])
              P.dma("pool", Wgt, win(l)[:, :, C_GV:C_GV + 512])
              P.memset(w2pad, 0.0)
              for d_ in range(2):
                  P.dma("pool", w2pad[d_ * 16:(d_ + 1) * 16, d_, :], w2_d[l, d_])
              for tb in range(4):
                  for c_ in range(2):
                      pb = bank((tb * 3 + c_) % 4)
                      for k in range(8):
                          P.matmul(pb, Wgf[:, k, c_ * 128:(c_ + 1) * 128], xT[:, k, tb * 512:(tb + 1) * 512],
                                   start=(k == 0), stop=(k == 7))
                      P.copy(gqk[:, c_, tb * 512:(tb + 1) * 512], pb, eng=("act" if c_ else "dve"))
                  pb = bank((tb * 3 + 2) % 4)
                  for k in range(8):
                      P.matmul(pb[0:32, :], Wgf[:, k, 256:288], xT[:, k, tb * 512:(tb + 1) * 512],
                               start=(k == 0), stop=(k == 7))
                  P.copy(gzT[0:32, tb * 512:(tb + 1) * 512], pb[0:32, :], eng="dve")
              for t in range(NT):
                  pb = bank(4 + t % 2)
                  for k in range(8):
                      P.matmul(pb, xT[:, k, t * 128:(t + 1) * 128], Wgt[:, k, :], start=(k == 0), stop=(k == 7))
                  P.copy(gv[:, t, :], pb[:, 0:256], eng="act")
                  P.act(gate[:, t, :], pb[:, 256:512], AF.Silu)
              A.release(gla_mark)
              Bc = A.alloc([S], F32)
              Ec = A.alloc([S], F32)
              kdec_tok = A.alloc([NT, 128], BF16)
              kdT = [A.alloc([128], BF16) for _ in range(2)]
              Srot = [A.alloc([256], F32) for _ in range(2)]
              for d_ in range(2):
                  for tb in range(4):
                      pb = bank(tb % 4)
                      P.matmul(pb, w2pad[0:32, d_, :], gzT[0:32, tb * 512:(tb + 1) * 512], start=True, stop=True)
                      P.act(Ec[:, tb * 512:(tb + 1) * 512], pb, AF.Exp, bias=negb2[:, d_:d_ + 1], scale=-1.0)
                  P.act(Ec, Ec, AF.Ln, bias=1.0)
                  for n in range(NT):
                      o_ = Bc[:, n * 128:(n + 1) * 128]
                      i_ = Ec[:, n * 128:(n + 1) * 128]
                      if d_ == 1:
                          o_ = o_[:, ::-1]
                          i_ = i_[:, ::-1]
                      P.add("dve", lambda e, o_=o_, i_=i_: e.tensor_tensor_scan(
                          out=o_, data0=ones_f, data1=i_, initial=0.0, op0=ALU.mult, op1=ALU.add),
                          reads=[ones_f, Ec[:, n * 128:(n + 1) * 128]], writes=[Bc[:, n * 128:(n + 1) * 128]])
                  P.act(Ec, Bc, AF.Exp, scale=-1.0 / 16.0)
                  P.act(Bc, Bc, AF.Exp, scale=1.0 / 16.0)
                  P.stt(qt[d_], gqk[:, 0, :], 32.0 ** -0.5, Ec, ALU.mult, ALU.mult)
                  P.tt(kt_[d_], gqk[:, 1, :], Bc, ALU.mult)
                  elast = [Ec[:, n * 128 + (127 if d_ == 0 else 0):n * 128 + (127 if d_ == 0 else 0) + 1]
                           for n in range(NT)]
                  pst = bankbf(7)
                  for n in range(NT):
                      kd = kdT[n % 2]
                      P.stt(kd, gqk[:, 1, n * 128:(n + 1) * 128], elast[n], Bc[:, n * 128:(n + 1) * 128],
                            ALU.mult, ALU.mult)
                      P.transpose(pst[:, (n % 8) * 128:(n % 8 + 1) * 128], kd, ident_b)
                      if n % 8 == 7:
                          P.copy(kdec_tok[:, n - 7:n + 1, :], pst.rearrange("p (a b) -> p a b", a=8), eng="act")
                  order = list(range(NT)) if d_ == 0 else list(range(NT - 1, -1, -1))
                  prev = Srot[0]
                  P.memset(prev, 0.0)
                  P.memset(Sbf[d_][:, order[0], :], 0.0)
                  for i_, n in enumerate(order[:-1]):
                      pb = bank(4 + i_ % 2, 256)
                      P.matmul(pb, kdec_tok[:, n, :], gv[:, n, :], start=True, stop=True)
                      cur = Srot[(i_ + 1) % 2]
                      P.stt(cur, prev, elast[n], pb, ALU.mult, ALU.add)
                      P.tt(Sbf[d_][:, order[i_ + 1], :], cur, bdmask, ALU.mult, eng="pool")
                      prev = cur
              A.release(gla_mark)
              Qbd = [A.alloc([4, 128], BF16) for _ in range(3)]
              Asb = [A.alloc([4, 128], BF16) for _ in range(3)]
              ogf = [A.alloc([256], F32) for _ in range(2)]
              ogb = [A.alloc([256], BF16) for _ in range(2)]
              sq2 = A.alloc([256], F32)
              ss4 = small[:, 528:532]
              rs4 = small[:, 532:536]
              ci = 0
              for n in range(NT):
                  po = bank(4 + n % 2, 256)
                  P.matmul(po, qt[0][:, n * 128:(n + 1) * 128], Sbf[0][:, n, :], start=True, stop=False)
                  P.matmul(po, qt[1][:, n * 128:(n + 1) * 128], Sbf[1][:, n, :], start=False, stop=False)
                  for d_ in range(2):
                      qb = Qbd[ci % 3]
                      asb = Asb[ci % 3]
                      pa = bank(ci % 4)
                      ci += 1
                      P.tt(qb, qt[d_][:, n * 128:(n + 1) * 128].unsqueeze(1).broadcast_to([128, 4, 128]),
                           hmask4.unsqueeze(2).broadcast_to([128, 4, 128]), ALU.mult, eng="pool")
                      P.matmul(pa, kt_[d_][:, n * 128:(n + 1) * 128], qb.rearrange("p a b -> p (a b)"),
                               start=True, stop=True)
                      P.tt(asb, pa.rearrange("p (a b) -> p a b", a=4),
                           tri[d_].unsqueeze(1).broadcast_to([128, 4, 128]), ALU.mult)
                      for h in range(4):
                          P.matmul(po[:, h * 64:(h + 1) * 64], asb[:, h, :], gv[:, n, h * 64:(h + 1) * 64],
                                   start=False, stop=(d_ == 1 and h == 3))
                  of = ogf[n % 2]
                  P.copy(of, po, eng="act")
                  P.tt(sq2, of, of, ALU.mult, eng="pool")
                  P.reduce(ss4, sq2.rearrange("p (h v) -> p h v", h=4), ALU.add)
                  rstd_from(rs4, ss4, 64.0, 4)
                  of3 = of.rearrange("p (h v) -> p h v", h=4)
                  P.tt(of3, of3, rs4.unsqueeze(2).broadcast_to([128, 4, 64]), ALU.mult)
                  P.tt(of3, of3, ggla.unsqueeze(1).broadcast_to([128, 4, 64]), ALU.mult)
                  ob = ogb[n % 2]
                  P.tt(ob, of, gate[:, n, :], ALU.mult)
                  pst = bankbf(7)
                  for c_ in range(2):
                      P.transpose(pst[:, c_ * 128:(c_ + 1) * 128], ob[:, c_ * 128:(c_ + 1) * 128], ident_b)
                  P.copy(ocat[:, 6:8, n * 128:(n + 1) * 128], pst[:, 0:256].rearrange("p (a b) -> p a b", a=2),
                         eng="dve")
              A.release(mixer_mark)
              if l == 0:
                  dump("ogla", ocat[:, 6:8, :], [128, 2, S], BF16)
              ck('gla')

              A.n = ARENA - XTB
              assert A.top <= A.n
              post_mark = A.mark()
              ytmp = [A.alloc([D], F32) for _ in range(2)]
              xbs = [A.alloc([D], BF16) for _ in range(2)]
              for t in range(NT):
                  P.dma("sp", x_tok[:, t, :], xres_d[t * 128:(t + 1) * 128, :])
              for t in range(NT):
                  b0 = (t % 2) * 2
                  p2 = ps[:, b0 * 512:(b0 + 2) * 512]
                  for hf in range(2):
                      for e_ in range(8):
                          P.matmul(p2[:, hf * 512:(hf + 1) * 512], ocat[:, e_, t * 128:(t + 1) * 128],
                                   Wout[:, e_, hf * 512:(hf + 1) * 512], start=(e_ == 0), stop=(e_ == 7))
                  layer_norm_tile(p2, x_tok[:, t, :], lnbuf, x_tok[:, t, :], ytmp[t % 2])
                  make_xT(x_tok[:, t, :], t, xbs[t % 2])
              A.release(base_mark)
              if l == 0:
                  dump("x1", x_tok, [128, NT, D])
              ck('ln1')

              Wd = A.alloc([NFC, D], BF16)
              hT = A.alloc([NFC, 512], BF16)
              NWB = 3
              wgu = [A.alloc([2, 8, 128], BF16) for _ in range(NWB)]
              sg = [A.alloc([512], BF16) for _ in range(2)]
              ytmp = [A.alloc([D], F32) for _ in range(1)]
              xbs = [A.alloc([D], BF16) for _ in range(2)]
              wdv = wd_d[l].rearrange("(f p) n -> p f n", p=128)
              wgv = wg_d[l].rearrange("(k p) n -> p k n", p=128)
              wuv = wu_d[l].rearrange("(k p) n -> p k n", p=128)
              P.dma("sp", lnbuf[:, 0, :], ln_d["ln2_g"][l].partition_broadcast(128))
              P.dma("sp", lnbuf[:, 1, :], ln_d["ln2_b"][l].partition_broadcast(128))
              wi = 0
              for tb in range(4):
                  for f in range(NFC):
                      wb = wgu[wi % NWB]
                      wi += 1
                      P.dma("pool", wb[:, 0, :, :], wgv[:, :, f * 128:(f + 1) * 128])
                      P.dma("pool", wb[:, 1, :, :], wuv[:, :, f * 128:(f + 1) * 128])
                      if tb == 0:
                          P.dma("pool", Wd[:, f, :], wdv[:, f, :])
                      pg = bank((f % 2) * 2)
                      pu = bank((f % 2) * 2 + 1)
                      for k in range(8):
                          P.matmul(pg, wb[:, 0, k, :], xT[:, k, tb * 512:(tb + 1) * 512], start=(k == 0), stop=(k == 7))
                      for k in range(8):
                          P.matmul(pu, wb[:, 1, k, :], xT[:, k, tb * 512:(tb + 1) * 512], start=(k == 0), stop=(k == 7))
                      s_ = sg[f % 2]
                      P.act(s_, pg, AF.Silu)
                      P.tt(hT[:, f, :], pu, s_, ALU.mult)
                  for ti in range(4):
                      t = tb * 4 + ti
                      p2 = ps[:, 4 * 512:6 * 512]
                      for hf in range(2):
                          for f in range(NFC):
                              P.matmul(p2[:, hf * 512:(hf + 1) * 512], hT[:, f, ti * 128:(ti + 1) * 128],
                                       Wd[:, f, hf * 512:(hf + 1) * 512], start=(f == 0), stop=(f == NFC - 1))
                      layer_norm_tile(p2, x_tok[:, t, :], lnbuf, x_tok[:, t, :], ytmp[0])
                      if l == n_layers - 1:
                          P.dma("sp", y_d[t * 128:(t + 1) * 128, :], x_tok[:, t, :])
                      else:
                          P.dma("sp", xs_d[t * 128:(t + 1) * 128, :], x_tok[:, t, :])
                          make_xT(x_tok[:, t, :], t, xbs[t % 2])
              A.release(base_mark)
              A.n = ARENA

        except _Stop:
            pass
        P.emit()
    _CACHE['P'] = P
    return nc, dbg_outs


def kernel(**inputs):
    if "c" not in _CACHE:
        _CACHE["c"] = host_consts()
    cf, cb, dc, ds = _CACHE["c"]
    nc, _ = build()
    x = np.ascontiguousarray(inputs["x"], dtype=np.float32)
    common = {
        "w_in": np.ascontiguousarray(inputs["w_in"], dtype=np.float32),
        "diff_lambda": np.ascontiguousarray(inputs["diff_lambda"], dtype=np.float32).reshape(L, 256),
        "diff_norm_g": np.ascontiguousarray(inputs["diff_norm_g"], dtype=np.float32),
        "fourier_w": np.ascontiguousarray(inputs["fourier_w"], dtype=np.float32),
        "gla_gate_w2": np.ascontiguousarray(inputs["gla_gate_w2"], dtype=np.float32),
        "gla_gate_b2": np.ascontiguousarray(inputs["gla_gate_b2"], dtype=np.float32),
        "gla_norm_g": np.ascontiguousarray(inputs["gla_norm_g"], dtype=np.float32),
        "w_out": np.ascontiguousarray(inputs["w_out"], dtype=np.float32),
        "ln1_g": np.ascontiguousarray(inputs["ln1_g"], dtype=np.float32),
        "ln1_b": np.ascontiguousarray(inputs["ln1_b"], dtype=np.float32),
        "ln2_g": np.ascontiguousarray(inputs["ln2_g"], dtype=np.float32),
        "ln2_b": np.ascontiguousarray(inputs["ln2_b"], dtype=np.float32),
        "ffn_w_gate": np.ascontiguousarray(inputs["ffn_w_gate"], dtype=np.float32),
        "ffn_w_up": np.ascontiguousarray(inputs["ffn_w_up"], dtype=np.float32),
        "ffn_w_down": np.ascontiguousarray(inputs["ffn_w_down"], dtype=np.float32),
        "c_f32": cf, "c_bf": cb, "dft_c": dc, "dft_s": ds,
    }
    in_maps = [dict(common, x=x[b]) for b in range(8)]
    res = run_bass_kernel_spmd(nc, in_maps, core_ids=list(range(8)))
    return np.stack([np.asarray(r["y"], dtype=np.float32) for r in res.results], axis=0)
```

```python
import numpy as np
import concourse.bass as bass
import concourse.mybir as mybir

F32 = mybir.dt.float32
BF16 = mybir.dt.bfloat16
ALU = mybir.AluOpType
AF = mybir.ActivationFunctionType
AX = mybir.AxisListType

_DT_SIZE = {F32: 4, BF16: 2, mybir.dt.float32r: 4, mybir.dt.int32: 4,
            mybir.dt.uint32: 4, mybir.dt.float16: 2, mybir.dt.uint16: 2,
            mybir.dt.int16: 2, mybir.dt.uint8: 1, mybir.dt.int8: 1}

ENGS = ("pe", "act", "dve", "pool", "sp")
N_DMA_SEMS = 24
TINY_BYTES = 256
SAME_ENGINE_SYNC = False


def ap_box(ap):
    t = ap.tensor
    name = t.name
    esz = _DT_SIZE[ap.dtype]
    dims = list(ap.ap)
    off = ap.offset
    space = str(ap.space)
    if "DRAM" in space.upper() or "HBM" in space.upper():
        lo = off
        hi = off
        for (st, n) in dims:
            if st >= 0:
                hi += st * (n - 1)
            else:
                lo += st * (n - 1)
        return (name, 0, 1, lo * esz, (hi + 1) * esz)
    pstep, pcnt = dims[0]
    if pstep == 0:
        pstep = 1 << 40
    p0 = off // pstep if pstep < (1 << 40) else 0
    f = off - p0 * pstep if pstep < (1 << 40) else off
    lo = f
    hi = f
    for (st, n) in dims[1:]:
        if st >= 0:
            hi += st * (n - 1)
        else:
            lo += st * (n - 1)
    if "PSUM" in space.upper():
        b0 = (lo * esz) // 2048
        b1 = ((hi + 1) * esz - 1) // 2048
        return (name, 0, 128, b0 * 2048, (b1 + 1) * 2048, True)
    return (name, p0, p0 + pcnt, lo * esz, (hi + 1) * esz)


def _overlap(a, b):
    return a[1] < b[2] and b[1] < a[2] and a[3] < b[4] and b[3] < a[4]


def _contains(a, b):
    return a[1] <= b[1] and a[2] >= b[2] and a[3] <= b[3] and a[4] >= b[4]


class Prog:
    def __init__(self, nc):
        self.nc = nc
        self.ins = []
        self.recs = {}
        self.dma_rr = {e: 0 for e in ENGS}
        self.trace = {}

    def add(self, eng, fn, reads=(), writes=(), dma=False):
        idx = len(self.ins)
        rb = list(dict.fromkeys(ap_box(a) for a in reads))
        wb = list(dict.fromkeys(ap_box(a) for a in writes))
        deps = set()
        tiny_deps = set()
        for b in rb:
            psum = len(b) > 5
            tiny = (not psum) and (b[4] - b[3]) <= TINY_BYTES
            for rec in self.recs.get(b[0], ()):
                if (rec[1] or (psum and rec[3] != eng)) and _overlap(rec[0], b):
                    deps.add(rec[2])
                    if tiny and rec[1]:
                        tiny_deps.add(rec[2])
        for b in wb:
            for rec in self.recs.get(b[0], ()):
                if _overlap(rec[0], b):
                    deps.add(rec[2])
        for b in wb:
            lst = self.recs.setdefault(b[0], [])
            lst[:] = [r for r in lst if not _contains(b, r[0])]
            lst.append([b, True, idx, eng])
        for b in rb:
            lst = self.recs.setdefault(b[0], [])
            found = False
            for r in lst:
                if (not r[1]) and r[3] == eng and r[0] == b and not dma \
                        and not self.ins[r[2]]["dma"]:
                    r[2] = idx
                    found = True
                    break
            if not found:
                lst.append([b, False, idx, eng])
        real = set()
        for d in deps:
            p = self.ins[d]
            if p["eng"] == "pe" and eng == "pe" and not p["dma"] and not dma:
                continue
            if (not SAME_ENGINE_SYNC) and p["eng"] == eng and eng != "pool" and not p["dma"] and not dma \
                    and d not in tiny_deps:
                continue
            real.add(d)
        self.ins.append(dict(eng=eng, fn=fn, deps=real, dma=dma, needed=False,
                             sem=None, val=None))
        return idx

    def emit(self, final_wait_eng="sp"):
        nc = self.nc
        ins = self.ins
        for r in ins:
            for d in r["deps"]:
                ins[d]["needed"] = True
        last_dmas = [i for i, r in enumerate(ins) if r["dma"]]
        import contextlib
        with contextlib.ExitStack() as st:
            esem = {e: st.enter_context(nc.semaphore("s_" + e)) for e in ENGS}
            dsem = {e: [st.enter_context(nc.semaphore("d_%s_%d" % (e, i)))
                        for i in range(N_DMA_SEMS)] for e in ("sp", "act", "pool")}
            cnt = {e: 0 for e in ENGS}
            dcnt = {e: [0] * N_DMA_SEMS for e in dsem}
            drr = {e: 0 for e in dsem}
            prev_use = {}
            for i, r in enumerate(ins):
                e = r["eng"]
                if r["dma"]:
                    s = drr[e]
                    drr[e] = (s + 1) % N_DMA_SEMS
                    if dcnt[e][s] > 0:
                        r["prev"] = (dsem[e][s], dcnt[e][s], ("d", e, s))
                    else:
                        r["prev"] = None
                    dcnt[e][s] += 16
                    r["sem"] = dsem[e][s]
                    r["val"] = dcnt[e][s]
                    r["semkey"] = ("d", e, s)
                elif r["needed"]:
                    cnt[e] += 1
                    r["sem"] = esem[e]
                    r["val"] = cnt[e]
                    r["semkey"] = ("e", e)
            block = st.enter_context(nc.Block())
            per_eng = {e: [i for i, r in enumerate(ins) if r["eng"] == e] for e in ENGS}
            final = {}
            for i in last_dmas:
                r = ins[i]
                final[r["semkey"]] = (r["sem"], max(r["val"], final.get(r["semkey"], (None, 0))[1]))

            def body(ename):
                def run(engobj):
                    waited = {}
                    for i in per_eng[ename]:
                        r = ins[i]
                        need = {}
                        for d in r["deps"]:
                            p = ins[d]
                            k = p["semkey"]
                            if need.get(k, (None, 0))[1] < p["val"]:
                                need[k] = (p["sem"], p["val"])
                        if r["dma"] and r["prev"] is not None:
                            s, v, k = r["prev"]
                            if need.get(k, (None, 0))[1] < v:
                                need[k] = (s, v)
                        for k, (s, v) in need.items():
                            if waited.get(k, 0) < v:
                                engobj.wait_ge(s, v)
                                waited[k] = v
                                self.trace.setdefault(ename, []).append(("w", k, v, i))
                        h = r["fn"](engobj)
                        if r["dma"]:
                            h.then_inc(r["sem"], 16)
                            self.trace.setdefault(ename, []).append(("i", r["semkey"], 16, i))
                        elif r["needed"]:
                            h.then_inc(r["sem"], 1)
                            self.trace.setdefault(ename, []).append(("i", r["semkey"], 1, i))
                    if ename == final_wait_eng:
                        for k, (s, v) in final.items():
                            if waited.get(k, 0) < v:
                                engobj.wait_ge(s, v)
                return run

            block.tensor(body("pe"))
            block.scalar(body("act"))
            block.vector(body("dve"))
            block.gpsimd(body("pool"))
            block.sync(body("sp"))

    def dma(self, eng, out, in_, **kw):
        return self.add(eng, lambda e: e.dma_start(out=out, in_=in_, **kw),
                        reads=[in_], writes=[out], dma=True)

    def matmul(self, out, lhsT, rhs, start=True, stop=True, **kw):
        return self.add("pe", lambda e: e.matmul(out, lhsT, rhs, start=start, stop=stop, **kw),
                        reads=[lhsT, rhs], writes=[out])

    def transpose(self, out, in_, ident):
        return self.add("pe", lambda e: e.transpose(out, in_, ident),
                        reads=[in_, ident], writes=[out])

    def act(self, out, in_, func, bias=None, scale=1.0, accum_out=None, eng="act"):
        reads = [in_]
        writes = [out]
        kw = {}
        if bias is not None:
            kw["bias"] = bias
            if not isinstance(bias, (int, float)):
                reads.append(bias)
        if not isinstance(scale, (int, float)):
            reads.append(scale)
        if accum_out is not None:
            kw["accum_out"] = accum_out
            writes.append(accum_out)
        return self.add(eng, lambda e: e.activation(out=out, in_=in_, func=func, scale=scale, **kw),
                        reads=reads, writes=writes)

    def tt(self, out, in0, in1, op, eng="dve"):
        return self.add(eng, lambda e: e.tensor_tensor(out=out, in0=in0, in1=in1, op=op),
                        reads=[in0, in1], writes=[out])

    def ts(self, out, in0, s1, op0, s2=None, op1=None, eng="dve", accum_out=None):
        reads = [in0]
        if not isinstance(s1, (int, float)):
            reads.append(s1)
        if s2 is not None and not isinstance(s2, (int, float)):
            reads.append(s2)
        kw = {}
        writes = [out]
        if op1 is not None:
            kw["op1"] = op1
        if accum_out is not None:
            kw["accum_out"] = accum_out
            writes.append(accum_out)
        return self.add(eng, lambda e: e.tensor_scalar(out=out, in0=in0, scalar1=s1, scalar2=s2,
                                                       op0=op0, **kw),
                        reads=reads, writes=writes)

    def stt(self, out, in0, scalar, in1, op0, op1, eng="dve"):
        reads = [in0, in1]
        if not isinstance(scalar, (int, float)):
            reads.append(scalar)
        return self.add(eng, lambda e: e.scalar_tensor_tensor(out=out, in0=in0, scalar=scalar,
                                                              in1=in1, op0=op0, op1=op1),
                        reads=reads, writes=[out])

    def copy(self, out, in_, eng="dve"):
        if eng == "act":
            return self.add(eng, lambda e: e.activation(out=out, in_=in_, func=AF.Identity),
                            reads=[in_], writes=[out])
        return self.add(eng, lambda e: e.tensor_copy(out=out, in_=in_), reads=[in_], writes=[out])

    def memset(self, ap, val, eng="dve"):
        return self.add(eng, lambda e: e.memset(ap, val), reads=[], writes=[ap])

    def reduce(self, out, in_, op, axis=AX.X, eng="dve", **kw):
        return self.add(eng, lambda e: e.tensor_reduce(out=out, in_=in_, op=op, axis=axis, **kw),
                        reads=[in_], writes=[out])

    def recip(self, out, in_, eng="dve"):
        return self.add(eng, lambda e: e.reciprocal(out=out, in_=in_), reads=[in_], writes=[out])


def simulate_trace(trace):
    pos = {e: 0 for e in trace}
    sem = {}
    progress = True
    while progress:
        progress = False
        for e, ops in trace.items():
            while pos[e] < len(ops):
                kind, k, v, i = ops[pos[e]]
                if kind == "w":
                    if sem.get(k, 0) >= v:
                        pos[e] += 1
                        progress = True
                    else:
                        break
                else:
                    sem[k] = sem.get(k, 0) + v
                    pos[e] += 1
                    progress = True
    stuck = {e: ops[pos[e]] for e, ops in trace.items() if pos[e] < len(ops)}
    return stuck, sem

import contextlib
import os
_SK = set(os.environ.get('DBGSKIP', '').split(','))
import math
import ml_dtypes
from concourse.bass_utils import run_bass_kernel_spmd

S = 2048
D = 1024
L = 2
INW = 2592
FF = 2816
NT = 16
NFC = FF // 128
ALPHA = float((2 * L) ** 0.25)
EPS = 1e-5
C_DQ, C_DK, C_DV, C_FU, C_GQ, C_GK, C_GV, C_GR, C_GZ = 0, 512, 1024, 1536, 1792, 1920, 2048, 2304, 2560

CF_ID, CF_COS, CF_SIN, CF_HM, CF_CC, CF_SC, CF_ONES, CF_NEGH, CF_N = 0, 128, 640, 1152, 1156, 1284, 1412, 1540, 1548
CB_ID, CB_TRIF, CB_TRIB, CB_BD, CB_ONE, CB_N = 0, 128, 256, 384, 640, 768


def host_consts():
    p = np.arange(128)
    cf = np.zeros((128, CF_N), np.float32)
    cf[:, CF_ID:CF_ID + 128] = np.eye(128, dtype=np.float32)
    inv_freq = (10000.0 ** (-np.arange(0, 64, 2, dtype=np.float32) / 64)).astype(np.float32)
    pos = (np.arange(NT)[None, :] * 128 + p[:, None]).astype(np.float32)
    ang = pos[:, :, None] * inv_freq[None, None, :]
    cf[:, CF_COS:CF_COS + 512] = np.cos(ang).reshape(128, 512)
    cf[:, CF_SIN:CF_SIN + 512] = np.sin(ang).reshape(128, 512)
    cf[:, CF_HM:CF_HM + 4] = (p[:, None] // 32 == np.arange(4)[None, :]).astype(np.float32)
    c = np.arange(64)
    a = 2 * np.pi * np.outer(c, c) / 64.0
    cc = np.cos(a) / 8.0
    sc = np.sin(a) / 8.0
    z = np.zeros((64, 64))
    cf[:, CF_CC:CF_CC + 128] = np.block([[cc, z], [z, cc]])
    cf[:, CF_SC:CF_SC + 128] = np.block([[sc, z], [z, sc]])
    cf[:, CF_ONES:CF_ONES + 128] = 1.0
    cf[:, CF_NEGH:CF_NEGH + 8] = -0.5
    cb = np.zeros((128, CB_N), np.float32)
    cb[:, CB_ID:CB_ID + 128] = np.eye(128)
    cb[:, CB_TRIF:CB_TRIF + 128] = (p[:, None] <= p[None, :])
    cb[:, CB_TRIB:CB_TRIB + 128] = (p[:, None] >= p[None, :])
    cb[:, CB_BD:CB_BD + 256] = (p[:, None] // 32 == np.arange(256)[None, :] // 64)
    cb[:, CB_ONE:CB_ONE + 128] = 1.0
    s = np.arange(S, dtype=np.float64)
    sk = np.outer(s, s) % S
    ang = 2 * np.pi * sk / S
    dc = (np.cos(ang) / math.sqrt(S)).astype(np.float32).astype(ml_dtypes.bfloat16)
    ds = (-np.sin(ang) / math.sqrt(S)).astype(np.float32).astype(ml_dtypes.bfloat16)
    return cf, cb.astype(ml_dtypes.bfloat16), dc, ds


class Arena:
    def __init__(self, t, nbytes):
        self.t = t
        self.n = nbytes
        self.top = 0

    def mark(self):
        return self.top

    def release(self, m):
        self.top = m

    def alloc(self, shape, dt):
        n = 1
        for s_ in shape:
            n *= s_
        b = n * _DT_SIZE[dt]
        off = self.top
        self.top += (b + 63) // 64 * 64
        assert self.top <= self.n, ("arena overflow", self.top, self.n)
        ap = self.t[:, off // 2:(off + b) // 2]
        if dt != BF16:
            ap = ap.bitcast(dt)
        if len(shape) == 2:
            ap = ap.rearrange("p (a b) -> p a b", a=shape[0])
        elif len(shape) == 3:
            ap = ap.rearrange("p (a b c) -> p a b c", a=shape[0], b=shape[1])
        return ap


class Rot:
    def __init__(self, items):
        self.items = items
        self.i = 0

    def next(self):
        r = self.items[self.i % len(self.items)]
        self.i += 1
        return r


_CACHE = {}


class _Stop(Exception):
    pass


def build(n_layers=L, dbg=None, upto=None):
    nc = bass.Bass("TRN2", target_bir_lowering=False)
    dram = lambda name, shape, dt, kind="ExternalInput": nc.dram_tensor(name, shape, dt, kind=kind).ap()
    x_d = dram("x", [S, D], F32)
    w_in_d = dram("w_in", [L, D, INW], F32)
    lam_d = dram("diff_lambda", [L, 256], F32)
    dng_d = dram("diff_norm_g", [L, 128], F32)
    fw_d = dram("fourier_w", [L, 4, 64, 64], F32)
    w2_d = dram("gla_gate_w2", [L, 2, 16, 128], F32)
    b2_d = dram("gla_gate_b2", [L, 2, 128], F32)
    gng_d = dram("gla_norm_g", [L, 64], F32)
    wout_d = dram("w_out", [L, D, D], F32)
    ln_d = {k: dram(k, [L, D], F32) for k in ("ln1_g", "ln1_b", "ln2_g", "ln2_b")}
    wg_d = dram("ffn_w_gate", [L, D, FF], F32)
    wu_d = dram("ffn_w_up", [L, D, FF], F32)
    wd_d = dram("ffn_w_down", [L, FF, D], F32)
    cf_d = dram("c_f32", [128, CF_N], F32)
    cb_d = dram("c_bf", [128, CB_N], BF16)
    dftc_d = dram("dft_c", [S, S], BF16)
    dfts_d = dram("dft_s", [S, S], BF16)
    y_d = dram("y", [S, D], F32, kind="ExternalOutput")
    xs_d = dram("xs_scr", [S, D], F32, kind="Internal")
    dbg_outs = {}

    ARENA = 207 * 1024
    XTB = NT * D * 4
    with contextlib.ExitStack() as st:
        arena_t = st.enter_context(nc.sbuf_tensor("arena", [128, ARENA // 2], BF16))
        ps = st.enter_context(nc.psum_tensor("ps", [128, 4096], F32))
        P = Prog(nc)
        A = Arena(arena_t, ARENA)
        x_tok = arena_t[:, (ARENA - XTB) // 2:ARENA // 2].bitcast(F32).rearrange("p (t d) -> p t d", t=NT)

        def bank(b, n=512, off=0):
            return ps[:, b * 512 + off:b * 512 + off + n]

        def bankbf(b):
            return ps[:, b * 512:(b + 1) * 512].bitcast(BF16)

        def dump(name, ap, shape, dt=F32):
            if dbg is None or name not in dbg:
                return
            d_ = nc.dram_tensor("dbg_" + name, shape, dt, kind="ExternalOutput").ap()
            dbg_outs[name] = d_
            P.dma("sp", d_, ap)

        cf = A.alloc([CF_N], F32)
        cb = A.alloc([CB_N], BF16)
        xT = A.alloc([8, S], BF16)
        lnbuf = A.alloc([2, D], F32)
        small = A.alloc([640], F32)
        P.dma("sp", cf, cf_d)
        P.dma("sp", cb, cb_d)
        ident_f = cf[:, CF_ID:CF_ID + 128]
        ident_b = cb[:, CB_ID:CB_ID + 128]
        cos_t = cf[:, CF_COS:CF_COS + 512].rearrange("p (t i) -> p t i", t=NT)
        sin_t = cf[:, CF_SIN:CF_SIN + 512].rearrange("p (t i) -> p t i", t=NT)
        hmask4 = cf[:, CF_HM:CF_HM + 4]
        ccbd = cf[:, CF_CC:CF_CC + 128]
        scbd = cf[:, CF_SC:CF_SC + 128]
        ones_f = cf[:, CF_ONES:CF_ONES + 128]
        negh8 = cf[:, CF_NEGH:CF_NEGH + 8]
        tri = [cb[:, CB_TRIF:CB_TRIF + 128], cb[:, CB_TRIB:CB_TRIB + 128]]
        bdmask = cb[:, CB_BD:CB_BD + 256]
        lp_bc = small[:, 0:256]
        gdiff = small[:, 256:384]
        ggla = small[:, 384:448]
        negb2 = small[:, 448:450]
        neglam = small[:, 450:451]
        negM = small[:, 451:452]
        sc_tmp = small[:, 452:500]
        nrm = small[:, 500:504]
        epsc = small[:, 504:505]
        base_mark = A.mark()

        win = lambda l: w_in_d[l].rearrange("(k p) n -> p k n", p=128)

        def rstd_from(out, ss, n, width):
            tmp = sc_tmp[:, 40:40 + width]
            P.ts(tmp, ss, 1.0 / n, ALU.mult, EPS, ALU.add)
            P.tt(out, tmp, negh8[:, 0:width], ALU.pow, eng="pool")

        xb_rot = None

        def make_xT(x_tile_f32, t, xb):
            P.copy(xb, x_tile_f32, eng="act")
            pst = bankbf(7)
            for k in range(8):
                P.transpose(pst[:, k * 128:(k + 1) * 128], xb[:, k * 128:(k + 1) * 128], ident_b)
            P.copy(xT[:, :, t * 128:(t + 1) * 128], pst.rearrange("p (k n) -> p k n", k=8), eng="dve")

        def layer_norm_tile(psum2, xres, gb, out_tok, ytmp):
            P.stt(ytmp, xres, ALPHA, psum2, ALU.mult, ALU.add)
            stats = sc_tmp[:, 0:12]
            mv = sc_tmp[:, 12:14]
            for c_ in range(2):
                P.add("dve", lambda e, c_=c_: e.bn_stats(out=stats[:, c_ * 6:(c_ + 1) * 6],
                                                          in_=ytmp[:, c_ * 512:(c_ + 1) * 512]),
                      reads=[ytmp[:, c_ * 512:(c_ + 1) * 512]], writes=[stats[:, c_ * 6:(c_ + 1) * 6]])
            P.add("dve", lambda e: e.bn_aggr(out=mv, in_=stats), reads=[stats], writes=[mv])
            rs = sc_tmp[:, 14:15]
            rstd_from(rs, mv[:, 1:2], 1.0, 1)
            P.stt(ytmp, ytmp, mv[:, 0:1], gb[:, 0, :], ALU.subtract, ALU.mult)
            P.stt(out_tok, ytmp, rs, gb[:, 1, :], ALU.mult, ALU.add)

        def ck(name):
            if upto == name:
                raise _Stop()

        try:
          for l in range(n_layers):
              lam_init = 0.8 - 0.6 * math.exp(-0.3 * l)
              A.release(base_mark)
              P.dma("sp", lp_bc, lam_d[l].partition_broadcast(128))
              P.dma("sp", gdiff, dng_d[l].partition_broadcast(128))
              P.dma("sp", ggla, gng_d[l].partition_broadcast(128))
              for d_ in range(2):
                  P.dma("sp", negb2[:, d_:d_ + 1], b2_d[l, d_].rearrange("(p o) -> p o", o=1))
              P.ts(negb2, negb2, -1.0, ALU.mult)
              P.ts(gdiff, gdiff, 1.0 - lam_init, ALU.mult)
              P.memset(epsc, EPS)
              pr = sc_tmp[:, 16:18]
              prod = A.alloc([128], F32)
              lp4 = lp_bc.rearrange("p (a d) -> p a d", a=4)
              for i_ in range(2):
                  P.tt(prod[:, 0:64], lp4[:, 2 * i_, :], lp4[:, 2 * i_ + 1, :], ALU.mult)
                  P.reduce(pr[:, i_:i_ + 1], prod[:, 0:64], ALU.add)
              P.act(pr, pr, AF.Exp)
              P.tt(neglam, pr[:, 1:2], pr[:, 0:1], ALU.subtract)
              P.ts(neglam, neglam, -lam_init, ALU.add)
              A.release(base_mark)
              ck('params')

              if l == 0:
                  m_ = A.mark()
                  xin = [A.alloc([D], F32) for _ in range(3)]
                  xbs = [A.alloc([D], BF16) for _ in range(2)]
                  for t in range(NT):
                      xi = xin[t % 3]
                      P.dma("sp", xi, x_d[t * 128:(t + 1) * 128, :])
                      make_xT(xi, t, xbs[t % 2])
                  A.release(m_)
              ck('xT')
              xres_d = x_d if l == 0 else xs_d

              ocat = A.alloc([8, S], BF16)
              Wout = A.alloc([8, D], BF16)
              mixer_mark = A.mark()

              QTz = A.alloc([2, S], BF16)
              KT = A.alloc([S], BF16)
              Vaug = A.alloc([NT, 132], BF16)
              Wh = [A.alloc([8, 384], BF16) for _ in range(2)]
              PT = [A.alloc([1024], BF16) for _ in range(3)]
              O1n = A.alloc([8, 128], F32)
              O2t = A.alloc([8, 128], F32)
              otoks = [A.alloc([8, 128], BF16) for _ in range(2)]
              qkr = [A.alloc([256], BF16) for _ in range(3)]
              tmpAs = [A.alloc([128], F32) for _ in range(2)]
              tmpBs = [A.alloc([128], F32) for _ in range(2)]
              sq = A.alloc([256], F32)
              D2 = A.alloc([256], F32)
              sqs = [sq, A.alloc([256], F32)]
              red_all = A.alloc([NT, 4], F32)
              red4 = sc_tmp[:, 20:24]
              gm = sc_tmp[:, 24:26]
              nrm2 = sc_tmp[:, 26:28]
              rz = sc_tmp[:, 28:36]
              ss8 = small[:, 512:520]
              rstd8 = small[:, 520:528]
              P.memset(Vaug[:, :, 128:129], 1.0)
              P.memset(QTz[64:128, 0, :], 0.0)
              P.memset(QTz[0:64, 1, :], 0.0, eng="pool")
              pso = [ps[:, (4 + qi // 3) * 512 + (qi % 3) * 160:(4 + qi // 3) * 512 + (qi % 3) * 160 + 129]
                     for qi in range(8)]
              grp = [(0, 3), (3, 3), (6, 2)]

              def pso_grp(gi, c0, c1):
                  q0, n_ = grp[gi]
                  base_ = (4 + gi) * 512
                  return ps[:, base_:base_ + n_ * 160].rearrange("p (a b) -> p a b", b=160)[:, :, c0:c1]

              def load_wh(h_):
                  for j_, c0 in enumerate((C_DQ, C_DK, C_DV)):
                      P.dma("pool", Wh[h_ % 2][:, :, j_ * 128:(j_ + 1) * 128],
                            win(l)[:, :, c0 + h_ * 128:c0 + (h_ + 1) * 128])

              load_wh(0)
              fin_pending = []
              for h in range(4):
                  wh = Wh[h % 2]
                  P.memset(nrm, 0.0)
                  tr_pending = []
                  for t in range(NT):
                      pb = bank(t % 4, 256)
                      pv_ = bank(4 + t % 2, 128)
                      for k in range(8):
                          P.matmul(pb, xT[:, k, t * 128:(t + 1) * 128], wh[:, k, 0:256], start=(k == 0), stop=(k == 7))
                      for k in range(8):
                          P.matmul(pv_, xT[:, k, t * 128:(t + 1) * 128], wh[:, k, 256:384], start=(k == 0), stop=(k == 7))
                      qk4 = pb.rearrange("p (g h d) -> p g h d", g=4, h=2)
                      t1 = qk4[:, :, 0, :]
                      t2 = qk4[:, :, 1, :]
                      cbt = cos_t[:, t:t + 1, :].broadcast_to([128, 4, 32])
                      sbt = sin_t[:, t:t + 1, :].broadcast_to([128, 4, 32])
                      q_ = qkr[t % 3]
                      q4 = q_.rearrange("p (g h d) -> p g h d", g=4, h=2)
                      ta = tmpAs[t % 2].rearrange("p (g d) -> p g d", g=4)
                      tb_ = tmpBs[t % 2].rearrange("p (g d) -> p g d", g=4)
                      P.tt(ta, t1, cbt, ALU.mult)
                      P.tt(tb_, t2, sbt, ALU.mult)
                      P.tt(q4[:, :, 0, :], ta, tb_, ALU.subtract)
                      P.tt(ta, t2, cbt, ALU.mult)
                      P.tt(tb_, t1, sbt, ALU.mult)
                      P.tt(q4[:, :, 1, :], ta, tb_, ALU.add)
                      P.copy(Vaug[:, t, 0:128], pv_, eng="act")
                      P.tt(sqs[t % 2], q_, q_, ALU.mult, eng="pool")
                      def tr_(t=t, q_=q_, sq_=sqs[t % 2]):
                          P.reduce(red_all[:, t, :], sq_.rearrange("p (g d) -> p g d", g=4), ALU.add)
                          pst = bankbf(7 if t % 2 else 6)
                          P.transpose(pst[:, 0:128], q_[:, 0:128], ident_b)
                          P.transpose(pst[:, 128:256], q_[:, 128:256], ident_b)
                          P.copy(QTz[0:64, 0, t * 128:(t + 1) * 128], pst[0:64, 0:128], eng="act")
                          P.copy(QTz[64:128, 1, t * 128:(t + 1) * 128], pst[64:128, 0:128], eng="act")
                          P.copy(KT[:, t * 128:(t + 1) * 128], pst[:, 128:256], eng="act")
                      tr_pending.append(tr_)
                      if len(tr_pending) > 1:
                          tr_pending.pop(0)()
                  while tr_pending:
                      tr_pending.pop(0)()
                  P.reduce(nrm, red_all.rearrange("p t g -> p g t"), ALU.max)
                  ck('h0proj')
                  if h + 1 < 4:
                      load_wh(h + 1)
                  P.reduce(nrm2, nrm.rearrange("p (a b) -> p a b", a=2), ALU.max)
                  P.ts(D2[:, 0:128], ident_f, nrm2[:, 0:1], ALU.mult)
                  P.ts(D2[:, 128:256], ident_f, nrm2[:, 1:2], ALU.mult)
                  pbm = bank(6, 256)
                  P.matmul(pbm, ones_f, D2, start=True, stop=True)
                  P.reduce(gm, pbm.rearrange("p (a b) -> p a b", a=2), ALU.max)
                  P.tt(negM, gm[:, 0:1], gm[:, 1:2], ALU.add)
                  P.ts(negM, negM, -0.5 * 0.125, ALU.mult)
                  if l == 0 and h == 0:
                      dump("xT", xT, [128, 8, S], BF16)
                      dump("negM", negM, [128, 1])
                      ck('qkt')
                  for qh in range(2):
                      otok = otoks[qh]
                      steps = [(m, kt) for m in range(2) for kt in range(NT)]

                      def emit_scores(i):
                          m, kt = steps[i]
                          sb_i = (i % 2) * 2
                          for j_ in range(2):
                              P.matmul(ps[:, (sb_i + j_) * 512:(sb_i + j_ + 1) * 512],
                                       KT[:, kt * 128:(kt + 1) * 128],
                                       QTz[:, m, qh * 1024 + j_ * 512:qh * 1024 + (j_ + 1) * 512],
                                       start=True, stop=True)

                      emit_scores(0)
                      for i, (m, kt) in enumerate(steps):
                          if i + 1 < len(steps):
                              emit_scores(i + 1)
                          if i == 8 and fin_pending:
                              fin_pending.pop(0)()
                          sb_i = (i % 2) * 2
                          pss = ps[:, sb_i * 512:(sb_i + 2) * 512]
                          pt = PT[i % 3]
                          P.act(pt, pss, AF.Exp, bias=negM, scale=0.125)
                          for qi in range(8):
                              P.matmul(pso[qi], pt[:, qi * 128:(qi + 1) * 128], Vaug[:, kt, 0:129],
                                       start=(kt == 0 and qi % 3 == 0), stop=(kt == NT - 1), skip_group_check=True)
                          if kt != NT - 1:
                              continue
                          if m == 0:
                              for gi, (q0, n_) in enumerate(grp):
                                  P.recip(rz[:, q0:q0 + n_], pso_grp(gi, 128, 129).rearrange("p a b -> p (a b)"))
                                  P.tt(O1n[:, q0:q0 + n_, :], pso_grp(gi, 0, 128),
                                       rz[:, q0:q0 + n_].unsqueeze(2).broadcast_to([128, n_, 128]), ALU.mult)
                          else:
                              for gi, (q0, n_) in enumerate(grp):
                                  P.recip(rz[:, q0:q0 + n_], pso_grp(gi, 128, 129).rearrange("p a b -> p (a b)"))
                              P.ts(rz, rz, neglam, ALU.mult)
                              for gi, (q0, n_) in enumerate(grp):
                                  P.tt(O2t[:, q0:q0 + n_, :], pso_grp(gi, 0, 128),
                                       rz[:, q0:q0 + n_].unsqueeze(2).broadcast_to([128, n_, 128]), ALU.mult)
                              P.tt(O1n, O1n, O2t, ALU.add)
                              P.tt(O2t, O1n, O1n, ALU.mult, eng="pool")
                              P.reduce(ss8, O2t, ALU.add)
                              rstd_from(rstd8, ss8, 128.0, 8)
                              P.tt(O1n, O1n, rstd8.unsqueeze(2).broadcast_to([128, 8, 128]), ALU.mult)
                              P.tt(otok, O1n, gdiff.unsqueeze(1).broadcast_to([128, 8, 128]), ALU.mult, eng="pool")
                              for qi in range(8):
                                  dst_ = ocat[:, h, qh * 1024 + qi * 128:qh * 1024 + (qi + 1) * 128]
                                  P.add("sp", lambda e, dst_=dst_, src_=otok[:, qi, :]: e.dma_start(
                                      out=dst_, in_=src_, transpose=True),
                                      reads=[otok[:, qi, :]], writes=[dst_], dma=True)
              while fin_pending:
                  fin_pending.pop(0)()
              A.release(mixer_mark)
              if l == 0:
                  dump("odiff", ocat[:, 0:4, :], [128, 4, S], BF16)
              ck('att')

              for k in range(8):
                  P.dma("pool", Wout[:, k, :], wout_d[l, k * 128:(k + 1) * 128, :])
              P.dma("sp", lnbuf[:, 0, :], ln_d["ln1_g"][l].partition_broadcast(128))
              P.dma("sp", lnbuf[:, 1, :], ln_d["ln1_b"][l].partition_broadcast(128))
              Wf = A.alloc([8, 256], BF16)
              uT = A.alloc([2, S], BF16)
              wbd = A.alloc([2, 128], F32)
              Wcs = A.alloc([2, 256], BF16)
              ucs = A.alloc([NT, 512], BF16)
              NDB = 4
              dbuf = [A.alloc([8, 512], BF16) for _ in range(NDB)]
              P.dma("pool", Wf, win(l)[:, :, C_FU:C_FU + 256])
              P.memset(wbd, 0.0)
              for g_ in range(4):
                  c_, gl = g_ // 2, g_ % 2
                  P.dma("sp", wbd[gl * 64:(gl + 1) * 64, c_, gl * 64:(gl + 1) * 64], fw_d[l, g_])
              for c_ in range(2):
                  pb = bank(4, 256)
                  P.matmul(pb[:, 0:128], ccbd, wbd[:, c_, :], start=True, stop=True)
                  P.matmul(pb[:, 128:256], scbd, wbd[:, c_, :], start=True, stop=True)
                  P.copy(Wcs[:, c_, :], pb, eng="dve")
              for tb in range(4):
                  for c_ in range(2):
                      pb = bank((tb * 2 + c_) % 4)
                      for k in range(8):
                          P.matmul(pb, Wf[:, k, c_ * 128:(c_ + 1) * 128], xT[:, k, tb * 512:(tb + 1) * 512],
                                   start=(k == 0), stop=(k == 7))
                      P.copy(uT[:, c_, tb * 512:(tb + 1) * 512], pb, eng=("act" if c_ else "dve"))
              for t in range(NT):
                  pb = bank(4 + t % 2)
                  for c_ in range(2):
                      P.matmul(pb[:, c_ * 256:(c_ + 1) * 256], uT[:, c_, t * 128:(t + 1) * 128], Wcs[:, c_, :],
                               start=True, stop=True)
                  P.copy(ucs[:, t, :], pb, eng=("act" if t % 2 else "dve"))
              dft_v = [dftc_d.rearrange("(t p) k -> p t k", p=128), dfts_d.rearrange("(t p) k -> p t k", p=128)]
              di = 0
              for kb in range(4):
                  pbs = [bank(0 + (kb % 2) * 2), bank(1 + (kb % 2) * 2)]
                  first = True
                  for which in range(2):
                      for tg in range(2):
                          db = dbuf[di % NDB]
                          di += 1
                          P.dma("sp", db, dft_v[which][:, tg * 8:(tg + 1) * 8, kb * 512:(kb + 1) * 512])
                          for tt_ in range(8):
                              t = tg * 8 + tt_
                              last = (which == 1 and t == NT - 1)
                              for c_ in range(2):
                                  P.matmul(pbs[c_], ucs[:, t, c_ * 256 + which * 128:c_ * 256 + (which + 1) * 128],
                                           db[:, tt_, :], start=first, stop=last)
                              first = False
                  for c_ in range(2):
                      P.copy(ocat[:, 4 + c_, kb * 512:(kb + 1) * 512], pbs[c_], eng=("act" if c_ else "dve"))
              A.release(mixer_mark)
              if l == 0:
                  dump("ofour", ocat[:, 4:6, :], [128, 2, S], BF16)
              ck('four')

              gqk = A.alloc([2, S], F32)
              gzT = A.alloc([S], BF16)
              gv = A.alloc([NT, 256], BF16)
              gate = A.alloc([NT, 256], BF16)
              w2pad = A.alloc([2, 128], BF16)
              qt = [A.alloc([S], BF16) for _ in range(2)]
              kt_ = [A.alloc([S], BF16) for _ in range(2)]
              Sbf = [A.alloc([NT, 256], BF16) for _ in range(2)]
              gla_mark = A.mark()
              Wgf = A.alloc([8, 288], BF16)
              Wgt = A.alloc([8, 512], BF16)
              P.dma("pool", Wgf[:, :, 0:256], win(l)[:, :, C_GQ:C_GQ + 256])
              P.dma("pool", Wgf[:, :, 256:288], win(l)[:, :, C_GZ:C_GZ + 32])
              P.dma("pool", Wgt, win(l)[:, :, C_GV:C_GV + 512])
              P.memset(w2pad, 0.0)
              for d_ in range(2):
                  P.dma("pool", w2pad[d_ * 16:(d_ + 1) * 16, d_, :], w2_d[l, d_])
              for tb in range(4):
                  for c_ in range(2):
                      pb = bank((tb * 3 + c_) % 4)
                      for k in range(8):
                          P.matmul(pb, Wgf[:, k, c_ * 128:(c_ + 1) * 128], xT[:, k, tb * 512:(tb + 1) * 512],
                                   start=(k == 0), stop=(k == 7))
                      P.copy(gqk[:, c_, tb * 512:(tb + 1) * 512], pb, eng=("act" if c_ else "dve"))
                  pb = bank((tb * 3 + 2) % 4)
                  for k in range(8):
                      P.matmul(pb[0:32, :], Wgf[:, k, 256:288], xT[:, k, tb * 512:(tb + 1) * 512],
                               start=(k == 0), stop=(k == 7))
                  P.copy(gzT[0:32, tb * 512:(tb + 1) * 512], pb[0:32, :], eng="dve")
              for t in range(NT):
                  pb = bank(4 + t % 2)
                  for k in range(8):
                      P.matmul(pb, xT[:, k, t * 128:(t + 1) * 128], Wgt[:, k, :], start=(k == 0), stop=(k == 7))
                  P.copy(gv[:, t, :], pb[:, 0:256], eng="act")
                  P.act(gate[:, t, :], pb[:, 256:512], AF.Silu)
              A.release(gla_mark)
              Bc = A.alloc([S], F32)
              Ec = A.alloc([S], F32)
              kdec_tok = A.alloc([NT, 128], BF16)
              kdT = [A.alloc([128], BF16) for _ in range(2)]
              Srot = [A.alloc([256], F32) for _ in range(2)]
              for d_ in range(2):
                  for tb in range(4):
                      pb = bank(tb % 4)
                      P.matmul(pb, w2pad[0:32, d_, :], gzT[0:32, tb * 512:(tb + 1) * 512], start=True, stop=True)
                      P.act(Ec[:, tb * 512:(tb + 1) * 512], pb, AF.Exp, bias=negb2[:, d_:d_ + 1], scale=-1.0)
                  P.act(Ec, Ec, AF.Ln, bias=1.0)
                  for n in range(NT):
                      o_ = Bc[:, n * 128:(n + 1) * 128]
                      i_ = Ec[:, n * 128:(n + 1) * 128]
                      if d_ == 1:
                          o_ = o_[:, ::-1]
                          i_ = i_[:, ::-1]
                      P.add("dve", lambda e, o_=o_, i_=i_: e.tensor_tensor_scan(
                          out=o_, data0=ones_f, data1=i_, initial=0.0, op0=ALU.mult, op1=ALU.add),
                          reads=[ones_f, Ec[:, n * 128:(n + 1) * 128]], writes=[Bc[:, n * 128:(n + 1) * 128]])
                  P.act(Ec, Bc, AF.Exp, scale=-1.0 / 16.0)
                  P.act(Bc, Bc, AF.Exp, scale=1.0 / 16.0)
                  P.stt(qt[d_], gqk[:, 0, :], 32.0 ** -0.5, Ec, ALU.mult, ALU.mult)
                  P.tt(kt_[d_], gqk[:, 1, :], Bc, ALU.mult)
                  elast = [Ec[:, n * 128 + (127 if d_ == 0 else 0):n * 128 + (127 if d_ == 0 else 0) + 1]
                           for n in range(NT)]
                  pst = bankbf(7)
                  for n in range(NT):
                      kd = kdT[n % 2]
                      P.stt(kd, gqk[:, 1, n * 128:(n + 1) * 128], elast[n], Bc[:, n * 128:(n + 1) * 128],
                            ALU.mult, ALU.mult)
                      P.transpose(pst[:, (n % 8) * 128:(n % 8 + 1) * 128], kd, ident_b)
                      if n % 8 == 7:
                          P.copy(kdec_tok[:, n - 7:n + 1, :], pst.rearrange("p (a b) -> p a b", a=8), eng="act")
                  order = list(range(NT)) if d_ == 0 else list(range(NT - 1, -1, -1))
                  prev = Srot[0]
                  P.memset(prev, 0.0)
                  P.memset(Sbf[d_][:, order[0], :], 0.0)
                  for i_, n in enumerate(order[:-1]):
                      pb = bank(4 + i_ % 2, 256)
                      P.matmul(pb, kdec_tok[:, n, :], gv[:, n, :], start=True, stop=True)
                      cur = Srot[(i_ + 1) % 2]
                      P.stt(cur, prev, elast[n], pb, ALU.mult, ALU.add)
                      P.tt(Sbf[d_][:, order[i_ + 1], :], cur, bdmask, ALU.mult, eng="pool")
                      prev = cur
              A.release(gla_mark)
              Qbd = [A.alloc([4, 128], BF16) for _ in range(3)]
              Asb = [A.alloc([4, 128], BF16) for _ in range(3)]
              ogf = [A.alloc([256], F32) for _ in range(2)]
              ogb = [A.alloc([256], BF16) for _ in range(2)]
              sq2 = A.alloc([256], F32)
              ss4 = small[:, 528:532]
              rs4 = small[:, 532:536]
              ci = 0
              for n in range(NT):
                  po = bank(4 + n % 2, 256)
                  P.matmul(po, qt[0][:, n * 128:(n + 1) * 128], Sbf[0][:, n, :], start=True, stop=False)
                  P.matmul(po, qt[1][:, n * 128:(n + 1) * 128], Sbf[1][:, n, :], start=False, stop=False)
                  for d_ in range(2):
                      qb = Qbd[ci % 3]
                      asb = Asb[ci % 3]
                      pa = bank(ci % 4)
                      ci += 1
                      P.tt(qb, qt[d_][:, n * 128:(n + 1) * 128].unsqueeze(1).broadcast_to([128, 4, 128]),
                           hmask4.unsqueeze(2).broadcast_to([128, 4, 128]), ALU.mult, eng="pool")
                      P.matmul(pa, kt_[d_][:, n * 128:(n + 1) * 128], qb.rearrange("p a b -> p (a b)"),
                               start=True, stop=True)
                      P.tt(asb, pa.rearrange("p (a b) -> p a b", a=4),
                           tri[d_].unsqueeze(1).broadcast_to([128, 4, 128]), ALU.mult)
                      for h in range(4):
                          P.matmul(po[:, h * 64:(h + 1) * 64], asb[:, h, :], gv[:, n, h * 64:(h + 1) * 64],
                                   start=False, stop=(d_ == 1 and h == 3))
                  of = ogf[n % 2]
                  P.copy(of, po, eng="act")
                  P.tt(sq2, of, of, ALU.mult, eng="pool")
                  P.reduce(ss4, sq2.rearrange("p (h v) -> p h v", h=4), ALU.add)
                  rstd_from(rs4, ss4, 64.0, 4)
                  of3 = of.rearrange("p (h v) -> p h v", h=4)
                  P.tt(of3, of3, rs4.unsqueeze(2).broadcast_to([128, 4, 64]), ALU.mult)
                  P.tt(of3, of3, ggla.unsqueeze(1).broadcast_to([128, 4, 64]), ALU.mult)
                  ob = ogb[n % 2]
                  P.tt(ob, of, gate[:, n, :], ALU.mult)
                  pst = bankbf(7)
                  for c_ in range(2):
                      P.transpose(pst[:, c_ * 128:(c_ + 1) * 128], ob[:, c_ * 128:(c_ + 1) * 128], ident_b)
                  P.copy(ocat[:, 6:8, n * 128:(n + 1) * 128], pst[:, 0:256].rearrange("p (a b) -> p a b", a=2),
                         eng="dve")
              A.release(mixer_mark)
              if l == 0:
                  dump("ogla", ocat[:, 6:8, :], [128, 2, S], BF16)
              ck('gla')

              A.n = ARENA - XTB
              assert A.top <= A.n
              post_mark = A.mark()
              ytmp = [A.alloc([D], F32) for _ in range(2)]
              xbs = [A.alloc([D], BF16) for _ in range(2)]
              for t in range(NT):
                  P.dma("sp", x_tok[:, t, :], xres_d[t * 128:(t + 1) * 128, :])
              for t in range(NT):
                  b0 = (t % 2) * 2
                  p2 = ps[:, b0 * 512:(b0 + 2) * 512]
                  for hf in range(2):
                      for e_ in range(8):
                          P.matmul(p2[:, hf * 512:(hf + 1) * 512], ocat[:, e_, t * 128:(t + 1) * 128],
                                   Wout[:, e_, hf * 512:(hf + 1) * 512], start=(e_ == 0), stop=(e_ == 7))
                  layer_norm_tile(p2, x_tok[:, t, :], lnbuf, x_tok[:, t, :], ytmp[t % 2])
                  make_xT(x_tok[:, t, :], t, xbs[t % 2])
              A.release(base_mark)
              if l == 0:
                  dump("x1", x_tok, [128, NT, D])
              ck('ln1')

              Wd = A.alloc([NFC, D], BF16)
              hT = A.alloc([NFC, 512], BF16)
              NWB = 3
              wgu = [A.alloc([2, 8, 128], BF16) for _ in range(NWB)]
              sg = [A.alloc([512], BF16) for _ in range(2)]
              ytmp = [A.alloc([D], F32) for _ in range(1)]
              xbs = [A.alloc([D], BF16) for _ in range(2)]
              wdv = wd_d[l].rearrange("(f p) n -> p f n", p=128)
              wgv = wg_d[l].rearrange("(k p) n -> p k n", p=128)
              wuv = wu_d[l].rearrange("(k p) n -> p k n", p=128)
              P.dma("sp", lnbuf[:, 0, :], ln_d["ln2_g"][l].partition_broadcast(128))
              P.dma("sp", lnbuf[:, 1, :], ln_d["ln2_b"][l].partition_broadcast(128))
              wi = 0
              for tb in range(4):
                  for f in range(NFC):
                      wb = wgu[wi % NWB]
                      wi += 1
                      P.dma("pool", wb[:, 0, :, :], wgv[:, :, f * 128:(f + 1) * 128])
                      P.dma("pool", wb[:, 1, :, :], wuv[:, :, f * 128:(f + 1) * 128])
                      if tb == 0:
                          P.dma("pool", Wd[:, f, :], wdv[:, f, :])
                      pg = bank((f % 2) * 2)
                      pu = bank((f % 2) * 2 + 1)
                      for k in range(8):
                          P.matmul(pg, wb[:, 0, k, :], xT[:, k, tb * 512:(tb + 1) * 512], start=(k == 0), stop=(k == 7))
                      for k in range(8):
                          P.matmul(pu, wb[:, 1, k, :], xT[:, k, tb * 512:(tb + 1) * 512], start=(k == 0), stop=(k == 7))
                      s_ = sg[f % 2]
                      P.act(s_, pg, AF.Silu)
                      P.tt(hT[:, f, :], pu, s_, ALU.mult)
                  for ti in range(4):
                      t = tb * 4 + ti
                      p2 = ps[:, 4 * 512:6 * 512]
                      for hf in range(2):
                          for f in range(NFC):
                              P.matmul(p2[:, hf * 512:(hf + 1) * 512], hT[:, f, ti * 128:(ti + 1) * 128],
                                       Wd[:, f, hf * 512:(hf + 1) * 512], start=(f == 0), stop=(f == NFC - 1))
                      layer_norm_tile(p2, x_tok[:, t, :], lnbuf, x_tok[:, t, :], ytmp[0])
                      if l == n_layers - 1:
                          P.dma("sp", y_d[t * 128:(t + 1) * 128, :], x_tok[:, t, :])
                      else:
                          P.dma("sp", xs_d[t * 128:(t + 1) * 128, :], x_tok[:, t, :])
                          make_xT(x_tok[:, t, :], t, xbs[t % 2])
              A.release(base_mark)
              A.n = ARENA

        except _Stop:
            pass
        P.emit()
    _CACHE['P'] = P
    return nc, dbg_outs


def kernel(**inputs):
    if "c" not in _CACHE:
        _CACHE["c"] = host_consts()
    cf, cb, dc, ds = _CACHE["c"]
    nc, _ = build()
    x = np.ascontiguousarray(inputs["x"], dtype=np.float32)
    common = {
        "w_in": np.ascontiguousarray(inputs["w_in"], dtype=np.float32),
        "diff_lambda": np.ascontiguousarray(inputs["diff_lambda"], dtype=np.float32).reshape(L, 256),
        "diff_norm_g": np.ascontiguousarray(inputs["diff_norm_g"], dtype=np.float32),
        "fourier_w": np.ascontiguousarray(inputs["fourier_w"], dtype=np.float32),
        "gla_gate_w2": np.ascontiguousarray(inputs["gla_gate_w2"], dtype=np.float32),
        "gla_gate_b2": np.ascontiguousarray(inputs["gla_gate_b2"], dtype=np.float32),
        "gla_norm_g": np.ascontiguousarray(inputs["gla_norm_g"], dtype=np.float32),
        "w_out": np.ascontiguousarray(inputs["w_out"], dtype=np.float32),
        "ln1_g": np.ascontiguousarray(inputs["ln1_g"], dtype=np.float32),
        "ln1_b": np.ascontiguousarray(inputs["ln1_b"], dtype=np.float32),
        "ln2_g": np.ascontiguousarray(inputs["ln2_g"], dtype=np.float32),
        "ln2_b": np.ascontiguousarray(inputs["ln2_b"], dtype=np.float32),
        "ffn_w_gate": np.ascontiguousarray(inputs["ffn_w_gate"], dtype=np.float32),
        "ffn_w_up": np.ascontiguousarray(inputs["ffn_w_up"], dtype=np.float32),
        "ffn_w_down": np.ascontiguousarray(inputs["ffn_w_down"], dtype=np.float32),
        "c_f32": cf, "c_bf": cb, "dft_c": dc, "dft_s": ds,
    }
    in_maps = [dict(common, x=x[b]) for b in range(8)]
    res = run_bass_kernel_spmd(nc, in_maps, core_ids=list(range(8)))
    return np.stack([np.asarray(r["y"], dtype=np.float32) for r in res.results], axis=0)
```

```python
import numpy as np
import concourse.bass as bass
import concourse.mybir as mybir

F32 = mybir.dt.float32
BF16 = mybir.dt.bfloat16
ALU = mybir.AluOpType
AF = mybir.ActivationFunctionType
AX = mybir.AxisListType

_DT_SIZE = {F32: 4, BF16: 2, mybir.dt.float32r: 4, mybir.dt.int32: 4,
            mybir.dt.uint32: 4, mybir.dt.float16: 2, mybir.dt.uint16: 2,
            mybir.dt.int16: 2, mybir.dt.uint8: 1, mybir.dt.int8: 1}

ENGS = ("pe", "act", "dve", "pool", "sp")
N_DMA_SEMS = 24
TINY_BYTES = 256
SAME_ENGINE_SYNC = False


def ap_box(ap):
    t = ap.tensor
    name = t.name
    esz = _DT_SIZE[ap.dtype]
    dims = list(ap.ap)
    off = ap.offset
    space = str(ap.space)
    if "DRAM" in space.upper() or "HBM" in space.upper():
        lo = off
        hi = off
        for (st, n) in dims:
            if st >= 0:
                hi += st * (n - 1)
            else:
                lo += st * (n - 1)
        return (name, 0, 1, lo * esz, (hi + 1) * esz)
    pstep, pcnt = dims[0]
    if pstep == 0:
        pstep = 1 << 40
    p0 = off // pstep if pstep < (1 << 40) else 0
    f = off - p0 * pstep if pstep < (1 << 40) else off
    lo = f
    hi = f
    for (st, n) in dims[1:]:
        if st >= 0:
            hi += st * (n - 1)
        else:
            lo += st * (n - 1)
    if "PSUM" in space.upper():
        b0 = (lo * esz) // 2048
        b1 = ((hi + 1) * esz - 1) // 2048
        return (name, 0, 128, b0 * 2048, (b1 + 1) * 2048, True)
    return (name, p0, p0 + pcnt, lo * esz, (hi + 1) * esz)


def _overlap(a, b):
    return a[1] < b[2] and b[1] < a[2] and a[3] < b[4] and b[3] < a[4]


def _contains(a, b):
    return a[1] <= b[1] and a[2] >= b[2] and a[3] <= b[3] and a[4] >= b[4]


class Prog:
    def __init__(self, nc):
        self.nc = nc
        self.ins = []
        self.recs = {}
        self.dma_rr = {e: 0 for e in ENGS}
        self.trace = {}

    def add(self, eng, fn, reads=(), writes=(), dma=False):
        idx = len(self.ins)
        rb = list(dict.fromkeys(ap_box(a) for a in reads))
        wb = list(dict.fromkeys(ap_box(a) for a in writes))
        deps = set()
        tiny_deps = set()
        for b in rb:
            psum = len(b) > 5
            tiny = (not psum) and (b[4] - b[3]) <= TINY_BYTES
            for rec in self.recs.get(b[0], ()):
                if (rec[1] or (psum and rec[3] != eng)) and _overlap(rec[0], b):
                    deps.add(rec[2])
                    if tiny and rec[1]:
                        tiny_deps.add(rec[2])
        for b in wb:
            for rec in self.recs.get(b[0], ()):
                if _overlap(rec[0], b):
                    deps.add(rec[2])
        for b in wb:
            lst = self.recs.setdefault(b[0], [])
            lst[:] = [r for r in lst if not _contains(b, r[0])]
            lst.append([b, True, idx, eng])
        for b in rb:
            lst = self.recs.setdefault(b[0], [])
            found = False
            for r in lst:
                if (not r[1]) and r[3] == eng and r[0] == b and not dma \
                        and not self.ins[r[2]]["dma"]:
                    r[2] = idx
                    found = True
                    break
            if not found:
                lst.append([b, False, idx, eng])
        real = set()
        for d in deps:
            p = self.ins[d]
            if p["eng"] == "pe" and eng == "pe" and not p["dma"] and not dma:
                continue
            if (not SAME_ENGINE_SYNC) and p["eng"] == eng and eng != "pool" and not p["dma"] and not dma \
                    and d not in tiny_deps:
                continue
            real.add(d)
        self.ins.append(dict(eng=eng, fn=fn, deps=real, dma=dma, needed=False,
                             sem=None, val=None))
        return idx

    def emit(self, final_wait_eng="sp"):
        nc = self.nc
        ins = self.ins
        for r in ins:
            for d in r["deps"]:
                ins[d]["needed"] = True
        last_dmas = [i for i, r in enumerate(ins) if r["dma"]]
        import contextlib
        with contextlib.ExitStack() as st:
            esem = {e: st.enter_context(nc.semaphore("s_" + e)) for e in ENGS}
            dsem = {e: [st.enter_context(nc.semaphore("d_%s_%d" % (e, i)))
                        for i in range(N_DMA_SEMS)] for e in ("sp", "act", "pool")}
            cnt = {e: 0 for e in ENGS}
            dcnt = {e: [0] * N_DMA_SEMS for e in dsem}
            drr = {e: 0 for e in dsem}
            prev_use = {}
            for i, r in enumerate(ins):
                e = r["eng"]
                if r["dma"]:
                    s = drr[e]
                    drr[e] = (s + 1) % N_DMA_SEMS
                    if dcnt[e][s] > 0:
                        r["prev"] = (dsem[e][s], dcnt[e][s], ("d", e, s))
                    else:
                        r["prev"] = None
                    dcnt[e][s] += 16
                    r["sem"] = dsem[e][s]
                    r["val"] = dcnt[e][s]
                    r["semkey"] = ("d", e, s)
                elif r["needed"]:
                    cnt[e] += 1
                    r["sem"] = esem[e]
                    r["val"] = cnt[e]
                    r["semkey"] = ("e", e)
            block = st.enter_context(nc.Block())
            per_eng = {e: [i for i, r in enumerate(ins) if r["eng"] == e] for e in ENGS}
            final = {}
            for i in last_dmas:
                r = ins[i]
                final[r["semkey"]] = (r["sem"], max(r["val"], final.get(r["semkey"], (None, 0))[1]))

            def body(ename):
                def run(engobj):
                    waited = {}
                    for i in per_eng[ename]:
                        r = ins[i]
                        need = {}
                        for d in r["deps"]:
                            p = ins[d]
                            k = p["semkey"]
                            if need.get(k, (None, 0))[1] < p["val"]:
                                need[k] = (p["sem"], p["val"])
                        if r["dma"] and r["prev"] is not None:
                            s, v, k = r["prev"]
                            if need.get(k, (None, 0))[1] < v:
                                need[k] = (s, v)
                        for k, (s, v) in need.items():
                            if waited.get(k, 0) < v:
                                engobj.wait_ge(s, v)
                                waited[k] = v
                                self.trace.setdefault(ename, []).append(("w", k, v, i))
                        h = r["fn"](engobj)
                        if r["dma"]:
                            h.then_inc(r["sem"], 16)
                            self.trace.setdefault(ename, []).append(("i", r["semkey"], 16, i))
                        elif r["needed"]:
                            h.then_inc(r["sem"], 1)
                            self.trace.setdefault(ename, []).append(("i", r["semkey"], 1, i))
                    if ename == final_wait_eng:
                        for k, (s, v) in final.items():
                            if waited.get(k, 0) < v:
                                engobj.wait_ge(s, v)
                return run

            block.tensor(body("pe"))
            block.scalar(body("act"))
            block.vector(body("dve"))
            block.gpsimd(body("pool"))
            block.sync(body("sp"))

    def dma(self, eng, out, in_, **kw):
        return self.add(eng, lambda e: e.dma_start(out=out, in_=in_, **kw),
                        reads=[in_], writes=[out], dma=True)

    def matmul(self, out, lhsT, rhs, start=True, stop=True, **kw):
        return self.add("pe", lambda e: e.matmul(out, lhsT, rhs, start=start, stop=stop, **kw),
                        reads=[lhsT, rhs], writes=[out])

    def transpose(self, out, in_, ident):
        return self.add("pe", lambda e: e.transpose(out, in_, ident),
                        reads=[in_, ident], writes=[out])

    def act(self, out, in_, func, bias=None, scale=1.0, accum_out=None, eng="act"):
        reads = [in_]
        writes = [out]
        kw = {}
        if bias is not None:
            kw["bias"] = bias
            if not isinstance(bias, (int, float)):
                reads.append(bias)
        if not isinstance(scale, (int, float)):
            reads.append(scale)
        if accum_out is not None:
            kw["accum_out"] = accum_out
            writes.append(accum_out)
        return self.add(eng, lambda e: e.activation(out=out, in_=in_, func=func, scale=scale, **kw),
                        reads=reads, writes=writes)

    def tt(self, out, in0, in1, op, eng="dve"):
        return self.add(eng, lambda e: e.tensor_tensor(out=out, in0=in0, in1=in1, op=op),
                        reads=[in0, in1], writes=[out])

    def ts(self, out, in0, s1, op0, s2=None, op1=None, eng="dve", accum_out=None):
        reads = [in0]
        if not isinstance(s1, (int, float)):
            reads.append(s1)
        if s2 is not None and not isinstance(s2, (int, float)):
            reads.append(s2)
        kw = {}
        writes = [out]
        if op1 is not None:
            kw["op1"] = op1
        if accum_out is not None:
            kw["accum_out"] = accum_out
            writes.append(accum_out)
        return self.add(eng, lambda e: e.tensor_scalar(out=out, in0=in0, scalar1=s1, scalar2=s2,
                                                       op0=op0, **kw),
                        reads=reads, writes=writes)

    def stt(self, out, in0, scalar, in1, op0, op1, eng="dve"):
        reads = [in0, in1]
        if not isinstance(scalar, (int, float)):
            reads.append(scalar)
        return self.add(eng, lambda e: e.scalar_tensor_tensor(out=out, in0=in0, scalar=scalar,
                                                              in1=in1, op0=op0, op1=op1),
                        reads=reads, writes=[out])

    def copy(self, out, in_, eng="dve"):
        if eng == "act":
            return self.add(eng, lambda e: e.activation(out=out, in_=in_, func=AF.Identity),
                            reads=[in_], writes=[out])
        return self.add(eng, lambda e: e.tensor_copy(out=out, in_=in_), reads=[in_], writes=[out])

    def memset(self, ap, val, eng="dve"):
        return self.add(eng, lambda e: e.memset(ap, val), reads=[], writes=[ap])

    def reduce(self, out, in_, op, axis=AX.X, eng="dve", **kw):
        return self.add(eng, lambda e: e.tensor_reduce(out=out, in_=in_, op=op, axis=axis, **kw),
                        reads=[in_], writes=[out])

    def recip(self, out, in_, eng="dve"):
        return self.add(eng, lambda e: e.reciprocal(out=out, in_=in_), reads=[in_], writes=[out])


def simulate_trace(trace):
    pos = {e: 0 for e in trace}
    sem = {}
    progress = True
    while progress:
        progress = False
        for e, ops in trace.items():
            while pos[e] < len(ops):
                kind, k, v, i = ops[pos[e]]
                if kind == "w":
                    if sem.get(k, 0) >= v:
                        pos[e] += 1
                        progress = True
                    else:
                        break
                else:
                    sem[k] = sem.get(k, 0) + v
                    pos[e] += 1
                    progress = True
    stuck = {e: ops[pos[e]] for e, ops in trace.items() if pos[e] < len(ops)}
    return stuck, sem

import contextlib
import os
_SK = set(os.environ.get('DBGSKIP', '').split(','))
import math
import ml_dtypes
from concourse.bass_utils import run_bass_kernel_spmd

S = 2048
D = 1024
L = 2
INW = 2592
FF = 2816
NT = 16
NFC = FF // 128
ALPHA = float((2 * L) ** 0.25)
EPS = 1e-5
C_DQ, C_DK, C_DV, C_FU, C_GQ, C_GK, C_GV, C_GR, C_GZ = 0, 512, 1024, 1536, 1792, 1920, 2048, 2304, 2560

CF_ID, CF_COS, CF_SIN, CF_HM, CF_CC, CF_SC, CF_ONES, CF_NEGH, CF_N = 0, 128, 640, 1152, 1156, 1284, 1412, 1540, 1548
CB_ID, CB_TRIF, CB_TRIB, CB_BD, CB_ONE, CB_N = 0, 128, 256, 384, 640, 768


def host_consts():
    p = np.arange(128)
    cf = np.zeros((128, CF_N), np.float32)
    cf[:, CF_ID:CF_ID + 128] = np.eye(128, dtype=np.float32)
    inv_freq = (10000.0 ** (-np.arange(0, 64, 2, dtype=np.float32) / 64)).astype(np.float32)
    pos = (np.arange(NT)[None, :] * 128 + p[:, None]).astype(np.float32)
    ang = pos[:, :, None] * inv_freq[None, None, :]
    cf[:, CF_COS:CF_COS + 512] = np.cos(ang).reshape(128, 512)
    cf[:, CF_SIN:CF_SIN + 512] = np.sin(ang).reshape(128, 512)
    cf[:, CF_HM:CF_HM + 4] = (p[:, None] // 32 == np.arange(4)[None, :]).astype(np.float32)
    c = np.arange(64)
    a = 2 * np.pi * np.outer(c, c) / 64.0
    cc = np.cos(a) / 8.0
    sc = np.sin(a) / 8.0
    z = np.zeros((64, 64))
    cf[:, CF_CC:CF_CC + 128] = np.block([[cc, z], [z, cc]])
    cf[:, CF_SC:CF_SC + 128] = np.block([[sc, z], [z, sc]])
    cf[:, CF_ONES:CF_ONES + 128] = 1.0
    cf[:, CF_NEGH:CF_NEGH + 8] = -0.5
    cb = np.zeros((128, CB_N), np.float32)
    cb[:, CB_ID:CB_ID + 128] = np.eye(128)
    cb[:, CB_TRIF:CB_TRIF + 128] = (p[:, None] <= p[None, :])
    cb[:, CB_TRIB:CB_TRIB + 128] = (p[:, None] >= p[None, :])
    cb[:, CB_BD:CB_BD + 256] = (p[:, None] // 32 == np.arange(256)[None, :] // 64)
    cb[:, CB_ONE:CB_ONE + 128] = 1.0
    s = np.arange(S, dtype=np.float64)
    sk = np.outer(s, s) % S
    ang = 2 * np.pi * sk / S
    dc = (np.cos(ang) / math.sqrt(S)).astype(np.float32).astype(ml_dtypes.bfloat16)
    ds = (-np.sin(ang) / math.sqrt(S)).astype(np.float32).astype(ml_dtypes.bfloat16)
    return cf, cb.astype(ml_dtypes.bfloat16), dc, ds


class Arena:
    def __init__(self, t, nbytes):
        self.t = t
        self.n = nbytes
        self.top = 0

    def mark(self):
        return self.top

    def release(self, m):
        self.top = m

    def alloc(self, shape, dt):
        n = 1
        for s_ in shape:
            n *= s_
        b = n * _DT_SIZE[dt]
        off = self.top
        self.top += (b + 63) // 64 * 64
        assert self.top <= self.n, ("arena overflow", self.top, self.n)
        ap = self.t[:, off // 2:(off + b) // 2]
        if dt != BF16:
            ap = ap.bitcast(dt)
        if len(shape) == 2:
            ap = ap.rearrange("p (a b) -> p a b", a=shape[0])
        elif len(shape) == 3:
            ap = ap.rearrange("p (a b c) -> p a b c", a=shape[0], b=shape[1])
        return ap


class Rot:
    def __init__(self, items):
        self.items = items
        self.i = 0

    def next(self):
        r = self.items[self.i % len(self.items)]
        self.i += 1
        return r


_CACHE = {}


class _Stop(Exception):
    pass


def build(n_layers=L, dbg=None, upto=None):
    nc = bass.Bass("TRN2", target_bir_lowering=False)
    dram = lambda name, shape, dt, kind="ExternalInput": nc.dram_tensor(name, shape, dt, kind=kind).ap()
    x_d = dram("x", [S, D], F32)
    w_in_d = dram("w_in", [L, D, INW], F32)
    lam_d = dram("diff_lambda", [L, 256], F32)
    dng_d = dram("diff_norm_g", [L, 128], F32)
    fw_d = dram("fourier_w", [L, 4, 64, 64], F32)
    w2_d = dram("gla_gate_w2", [L, 2, 16, 128], F32)
    b2_d = dram("gla_gate_b2", [L, 2, 128], F32)
    gng_d = dram("gla_norm_g", [L, 64], F32)
    wout_d = dram("w_out", [L, D, D], F32)
    ln_d = {k: dram(k, [L, D], F32) for k in ("ln1_g", "ln1_b", "ln2_g", "ln2_b")}
    wg_d = dram("ffn_w_gate", [L, D, FF], F32)
    wu_d = dram("ffn_w_up", [L, D, FF], F32)
    wd_d = dram("ffn_w_down", [L, FF, D], F32)
    cf_d = dram("c_f32", [128, CF_N], F32)
    cb_d = dram("c_bf", [128, CB_N], BF16)
    dftc_d = dram("dft_c", [S, S], BF16)
    dfts_d = dram("dft_s", [S, S], BF16)
    y_d = dram("y", [S, D], F32, kind="ExternalOutput")
    xs_d = dram("xs_scr", [S, D], F32, kind="Internal")
    dbg_outs = {}

    ARENA = 207 * 1024
    XTB = NT * D * 4
    with contextlib.ExitStack() as st:
        arena_t = st.enter_context(nc.sbuf_tensor("arena", [128, ARENA // 2], BF16))
        ps = st.enter_context(nc.psum_tensor("ps", [128, 4096], F32))
        P = Prog(nc)
        A = Arena(arena_t, ARENA)
        x_tok = arena_t[:, (ARENA - XTB) // 2:ARENA // 2].bitcast(F32).rearrange("p (t d) -> p t d", t=NT)

        def bank(b, n=512, off=0):
            return ps[:, b * 512 + off:b * 512 + off + n]

        def bankbf(b):
            return ps[:, b * 512:(b + 1) * 512].bitcast(BF16)

        def dump(name, ap, shape, dt=F32):
            if dbg is None or name not in dbg:
                return
            d_ = nc.dram_tensor("dbg_" + name, shape, dt, kind="ExternalOutput").ap()
            dbg_outs[name] = d_
            P.dma("sp", d_, ap)

        cf = A.alloc([CF_N], F32)
        cb = A.alloc([CB_N], BF16)
        xT = A.alloc([8, S], BF16)
        lnbuf = A.alloc([2, D], F32)
        small = A.alloc([640], F32)
        P.dma("sp", cf, cf_d)
        P.dma("sp", cb, cb_d)
        ident_f = cf[:, CF_ID:CF_ID + 128]
        ident_b = cb[:, CB_ID:CB_ID + 128]
        cos_t = cf[:, CF_COS:CF_COS + 512].rearrange("p (t i) -> p t i", t=NT)
        sin_t = cf[:, CF_SIN:CF_SIN + 512].rearrange("p (t i) -> p t i", t=NT)
        hmask4 = cf[:, CF_HM:CF_HM + 4]
        ccbd = cf[:, CF_CC:CF_CC + 128]
        scbd = cf[:, CF_SC:CF_SC + 128]
        ones_f = cf[:, CF_ONES:CF_ONES + 128]
        negh8 = cf[:, CF_NEGH:CF_NEGH + 8]
        tri = [cb[:, CB_TRIF:CB_TRIF + 128], cb[:, CB_TRIB:CB_TRIB + 128]]
        bdmask = cb[:, CB_BD:CB_BD + 256]
        lp_bc = small[:, 0:256]
        gdiff = small[:, 256:384]
        ggla = small[:, 384:448]
        negb2 = small[:, 448:450]
        neglam = small[:, 450:451]
        negM = small[:, 451:452]
        sc_tmp = small[:, 452:500]
        nrm = small[:, 500:504]
        epsc = small[:, 504:505]
        base_mark = A.mark()

        win = lambda l: w_in_d[l].rearrange("(k p) n -> p k n", p=128)

        def rstd_from(out, ss, n, width):
            tmp = sc_tmp[:, 40:40 + width]
            P.ts(tmp, ss, 1.0 / n, ALU.mult, EPS, ALU.add)
            P.tt(out, tmp, negh8[:, 0:width], ALU.pow, eng="pool")

        xb_rot = None

        def make_xT(x_tile_f32, t, xb):
            P.copy(xb, x_tile_f32, eng="act")
            pst = bankbf(7)
            for k in range(8):
                P.transpose(pst[:, k * 128:(k + 1) * 128], xb[:, k * 128:(k + 1) * 128], ident_b)
            P.copy(xT[:, :, t * 128:(t + 1) * 128], pst.rearrange("p (k n) -> p k n", k=8), eng="act")

        def ln_a(psum2, xres, gb, ytmp, par, alpha=ALPHA):
            sct = sc_tmp[:, 0:16] if par == 0 else small[:, 540:556]
            P.stt(ytmp, xres, alpha, psum2, ALU.mult, ALU.add)
            stats = sct[:, 0:12]
            mv = sct[:, 12:14]
            for c_ in range(2):
                P.add("dve", lambda e, c_=c_: e.bn_stats(out=stats[:, c_ * 6:(c_ + 1) * 6],
                                                          in_=ytmp[:, c_ * 512:(c_ + 1) * 512]),
                      reads=[ytmp[:, c_ * 512:(c_ + 1) * 512]], writes=[stats[:, c_ * 6:(c_ + 1) * 6]])
            P.add("dve", lambda e: e.bn_aggr(out=mv, in_=stats), reads=[stats], writes=[mv])
            rs = sct[:, 14:15]
            nmr = sct[:, 15:16]
            P.act(rs, mv[:, 1:2], AF.Ln, bias=epsc)
            P.act(rs, rs, AF.Exp, scale=-0.5)
            P.stt(nmr, mv[:, 0:1], -1.0, rs, ALU.mult, ALU.mult)
            P.act(ytmp, ytmp, AF.Identity, bias=nmr, scale=rs)
            P.tt(ytmp, ytmp, gb[:, 0, :], ALU.mult, eng="pool")

        def ln_b(gb, out_tok, ytmp):
            P.tt(out_tok, ytmp, gb[:, 1, :], ALU.add)

        def ck(name):
            if upto == name:
                raise _Stop()

        try:
          for l in range(n_layers):
              lam_init = 0.8 - 0.6 * math.exp(-0.3 * l)
              A.release(base_mark)
              P.dma("sp", lp_bc, lam_d[l].partition_broadcast(128))
              P.dma("sp", gdiff, dng_d[l].partition_broadcast(128))
              P.dma("sp", ggla, gng_d[l].partition_broadcast(128))
              for d_ in range(2):
                  P.dma("sp", negb2[:, d_:d_ + 1], b2_d[l, d_].rearrange("(p o) -> p o", o=1))
              P.ts(negb2, negb2, -1.0, ALU.mult)
              P.ts(gdiff, gdiff, 1.0 - lam_init, ALU.mult)
              P.memset(epsc, EPS)
              pr = sc_tmp[:, 16:18]
              prod = A.alloc([128], F32)
              lp4 = lp_bc.rearrange("p (a d) -> p a d", a=4)
              for i_ in range(2):
                  P.tt(prod[:, 0:64], lp4[:, 2 * i_, :], lp4[:, 2 * i_ + 1, :], ALU.mult)
                  P.reduce(pr[:, i_:i_ + 1], prod[:, 0:64], ALU.add)
              P.act(pr, pr, AF.Exp)
              P.tt(neglam, pr[:, 1:2], pr[:, 0:1], ALU.subtract)
              P.ts(neglam, neglam, -lam_init, ALU.add)
              A.release(base_mark)
              ck('params')

              if l == 0:
                  m_ = A.mark()
                  xin = [A.alloc([D], F32) for _ in range(3)]
                  xbs = [A.alloc([D], BF16) for _ in range(2)]
                  for t in range(NT):
                      xi = xin[t % 3]
                      P.dma("sp", xi, x_d[t * 128:(t + 1) * 128, :])
                      make_xT(xi, t, xbs[t % 2])
                  A.release(m_)
              ck('xT')
              xres_d = x_d if l == 0 else xs_d

              ocat = A.alloc([8, S], BF16)
              Wout = A.alloc([8, D], BF16)
              mixer_mark = A.mark()

              QTz = A.alloc([2, S], BF16)
              KT = A.alloc([S], BF16)
              Vaug = A.alloc([NT, 132], BF16)
              Wh = [A.alloc([8, 384], BF16) for _ in range(2)]
              PT = [A.alloc([1024], BF16) for _ in range(3)]
              O1n = A.alloc([8, 128], F32)
              O2t = A.alloc([8, 128], F32)
              otoks = [A.alloc([8, 128], BF16) for _ in range(2)]
              qkr = [A.alloc([256], BF16) for _ in range(3)]
              tmpAs = [A.alloc([128], F32) for _ in range(2)]
              tmpBs = [A.alloc([128], F32) for _ in range(2)]
              sq = A.alloc([256], F32)
              D2 = A.alloc([256], F32)
              sqs = [sq, A.alloc([256], F32)]
              red_all = A.alloc([NT, 4], F32)
              red4 = sc_tmp[:, 20:24]
              gm = sc_tmp[:, 24:26]
              nrm2 = sc_tmp[:, 26:28]
              rz = sc_tmp[:, 28:36]
              ss8 = small[:, 512:520]
              rstd8 = small[:, 520:528]
              P.memset(Vaug[:, :, 128:129], 1.0)
              P.memset(QTz[64:128, 0, :], 0.0)
              P.memset(QTz[0:64, 1, :], 0.0, eng="pool")
              pso = [ps[:, (4 + qi // 3) * 512 + (qi % 3) * 160:(4 + qi // 3) * 512 + (qi % 3) * 160 + 129]
                     for qi in range(8)]
              grp = [(0, 3), (3, 3), (6, 2)]

              def pso_grp(gi, c0, c1):
                  q0, n_ = grp[gi]
                  base_ = (4 + gi) * 512
                  return ps[:, base_:base_ + n_ * 160].rearrange("p (a b) -> p a b", b=160)[:, :, c0:c1]

              def load_wh(h_):
                  for j_, c0 in enumerate((C_DQ, C_DK, C_DV)):
                      P.dma("pool", Wh[h_ % 2][:, :, j_ * 128:(j_ + 1) * 128],
                            win(l)[:, :, c0 + h_ * 128:c0 + (h_ + 1) * 128])

              load_wh(0)
              fin_pending = []
              for h in range(4):
                  wh = Wh[h % 2]
                  P.memset(nrm, 0.0)
                  tr_pending = []
                  for t in range(NT):
                      pb = bank(t % 4, 256)
                      pv_ = bank(4 + t % 2, 128)
                      for k in range(8):
                          P.matmul(pb, xT[:, k, t * 128:(t + 1) * 128], wh[:, k, 0:256], start=(k == 0), stop=(k == 7))
                      for k in range(8):
                          P.matmul(pv_, xT[:, k, t * 128:(t + 1) * 128], wh[:, k, 256:384], start=(k == 0), stop=(k == 7))
                      qk4 = pb.rearrange("p (g h d) -> p g h d", g=4, h=2)
                      t1 = qk4[:, :, 0, :]
                      t2 = qk4[:, :, 1, :]
                      cbt = cos_t[:, t:t + 1, :].broadcast_to([128, 4, 32])
                      sbt = sin_t[:, t:t + 1, :].broadcast_to([128, 4, 32])
                      q_ = qkr[t % 3]
                      q4 = q_.rearrange("p (g h d) -> p g h d", g=4, h=2)
                      ta = tmpAs[t % 2].rearrange("p (g d) -> p g d", g=4)
                      tb_ = tmpBs[t % 2].rearrange("p (g d) -> p g d", g=4)
                      P.tt(ta, t1, cbt, ALU.mult)
                      P.tt(tb_, t2, sbt, ALU.mult)
                      P.tt(q4[:, :, 0, :], ta, tb_, ALU.subtract)
                      P.tt(ta, t2, cbt, ALU.mult)
                      P.tt(tb_, t1, sbt, ALU.mult)
                      P.tt(q4[:, :, 1, :], ta, tb_, ALU.add)
                      P.copy(Vaug[:, t, 0:128], pv_, eng="act")
                      P.tt(sqs[t % 2], q_, q_, ALU.mult, eng="pool")
                      def tr_(t=t, q_=q_, sq_=sqs[t % 2]):
                          P.reduce(red_all[:, t, :], sq_.rearrange("p (g d) -> p g d", g=4), ALU.add)
                          pst = bankbf(7 if t % 2 else 6)
                          P.transpose(pst[:, 0:128], q_[:, 0:128], ident_b)
                          P.transpose(pst[:, 128:256], q_[:, 128:256], ident_b)
                          P.copy(QTz[0:64, 0, t * 128:(t + 1) * 128], pst[0:64, 0:128], eng="act")
                          P.copy(QTz[64:128, 1, t * 128:(t + 1) * 128], pst[64:128, 0:128], eng="act")
                          P.copy(KT[:, t * 128:(t + 1) * 128], pst[:, 128:256], eng="act")
                      tr_pending.append(tr_)
                      if len(tr_pending) > 1:
                          tr_pending.pop(0)()
                  while tr_pending:
                      tr_pending.pop(0)()
                  P.reduce(nrm, red_all.rearrange("p t g -> p g t"), ALU.max)
                  ck('h0proj')
                  if h + 1 < 4:
                      load_wh(h + 1)
                  P.reduce(nrm2, nrm.rearrange("p (a b) -> p a b", a=2), ALU.max)
                  P.ts(D2[:, 0:128], ident_f, nrm2[:, 0:1], ALU.mult)
                  P.ts(D2[:, 128:256], ident_f, nrm2[:, 1:2], ALU.mult)
                  pbm = bank(6, 256)
                  P.matmul(pbm, ones_f, D2, start=True, stop=True)
                  P.reduce(gm, pbm.rearrange("p (a b) -> p a b", a=2), ALU.max)
                  P.tt(negM, gm[:, 0:1], gm[:, 1:2], ALU.add)
                  P.ts(negM, negM, -0.5 * 0.125, ALU.mult)
                  if l == 0 and h == 0:
                      dump("xT", xT, [128, 8, S], BF16)
                      dump("negM", negM, [128, 1])
                      ck('qkt')
                  for qh in range(2):
                      otok = otoks[qh]
                      steps = [(m, kt) for m in range(2) for kt in range(NT)]

                      def emit_scores(i):
                          m, kt = steps[i]
                          sb_i = (i % 2) * 2
                          for j_ in range(2):
                              P.matmul(ps[:, (sb_i + j_) * 512:(sb_i + j_ + 1) * 512],
                                       KT[:, kt * 128:(kt + 1) * 128],
                                       QTz[:, m, qh * 1024 + j_ * 512:qh * 1024 + (j_ + 1) * 512],
                                       start=True, stop=True)

                      emit_scores(0)
                      for i, (m, kt) in enumerate(steps):
                          if i + 1 < len(steps):
                              emit_scores(i + 1)
                          if i == 8 and fin_pending:
                              fin_pending.pop(0)()
                          sb_i = (i % 2) * 2
                          pss = ps[:, sb_i * 512:(sb_i + 2) * 512]
                          pt = PT[i % 3]
                          P.act(pt, pss, AF.Exp, bias=negM, scale=0.125)
                          for qi in range(8):
                              P.matmul(pso[qi], pt[:, qi * 128:(qi + 1) * 128], Vaug[:, kt, 0:129],
                                       start=(kt == 0 and qi % 3 == 0), stop=(kt == NT - 1), skip_group_check=True)
                          if kt != NT - 1:
                              continue
                          if m == 0:
                              for gi, (q0, n_) in enumerate(grp):
                                  P.recip(rz[:, q0:q0 + n_], pso_grp(gi, 128, 129).rearrange("p a b -> p (a b)"))
                                  P.tt(O1n[:, q0:q0 + n_, :], pso_grp(gi, 0, 128),
                                       rz[:, q0:q0 + n_].unsqueeze(2).broadcast_to([128, n_, 128]), ALU.mult)
                          else:
                              for gi, (q0, n_) in enumerate(grp):
                                  P.recip(rz[:, q0:q0 + n_], pso_grp(gi, 128, 129).rearrange("p a b -> p (a b)"))
                              P.ts(rz, rz, neglam, ALU.mult)
                              for gi, (q0, n_) in enumerate(grp):
                                  P.tt(O2t[:, q0:q0 + n_, :], pso_grp(gi, 0, 128),
                                       rz[:, q0:q0 + n_].unsqueeze(2).broadcast_to([128, n_, 128]), ALU.mult)
                              P.tt(O1n, O1n, O2t, ALU.add)
                              P.tt(O2t, O1n, O1n, ALU.mult, eng="pool")
                              P.reduce(ss8, O2t, ALU.add)
                              rstd_from(rstd8, ss8, 128.0, 8)
                              P.tt(O1n, O1n, rstd8.unsqueeze(2).broadcast_to([128, 8, 128]), ALU.mult)
                              P.tt(otok, O1n, gdiff.unsqueeze(1).broadcast_to([128, 8, 128]), ALU.mult, eng="pool")
                              for qi in range(8):
                                  dst_ = ocat[:, h, qh * 1024 + qi * 128:qh * 1024 + (qi + 1) * 128]
                                  P.add("sp", lambda e, dst_=dst_, src_=otok[:, qi, :]: e.dma_start(
                                      out=dst_, in_=src_, transpose=True),
                                      reads=[otok[:, qi, :]], writes=[dst_], dma=True)
              while fin_pending:
                  fin_pending.pop(0)()
              A.release(mixer_mark)
              if l == 0:
                  dump("odiff", ocat[:, 0:4, :], [128, 4, S], BF16)
              ck('att')

              for k in range(8):
                  P.dma("pool", Wout[:, k, :], wout_d[l, k * 128:(k + 1) * 128, :])
              P.dma("sp", lnbuf[:, 0, :], ln_d["ln1_g"][l].partition_broadcast(128))
              P.dma("sp", lnbuf[:, 1, :], ln_d["ln1_b"][l].partition_broadcast(128))
              Wf = A.alloc([8, 256], BF16)
              uT = A.alloc([2, S], BF16)
              wbd = A.alloc([2, 128], F32)
              Wcs = A.alloc([2, 256], BF16)
              ucs = A.alloc([NT, 512], BF16)
              NDB = 4
              dbuf = [A.alloc([8, 512], BF16) for _ in range(NDB)]
              P.dma("pool", Wf, win(l)[:, :, C_FU:C_FU + 256])
              P.memset(wbd, 0.0)
              for g_ in range(4):
                  c_, gl = g_ // 2, g_ % 2
                  P.dma("sp", wbd[gl * 64:(gl + 1) * 64, c_, gl * 64:(gl + 1) * 64], fw_d[l, g_])
              for c_ in range(2):
                  pb = bank(4, 256)
                  P.matmul(pb[:, 0:128], ccbd, wbd[:, c_, :], start=True, stop=True)
                  P.matmul(pb[:, 128:256], scbd, wbd[:, c_, :], start=True, stop=True)
                  P.copy(Wcs[:, c_, :], pb, eng="dve")
              for tb in range(4):
                  for c_ in range(2):
                      pb = bank((tb * 2 + c_) % 4)
                      for k in range(8):
                          P.matmul(pb, Wf[:, k, c_ * 128:(c_ + 1) * 128], xT[:, k, tb * 512:(tb + 1) * 512],
                                   start=(k == 0), stop=(k == 7))
                      P.copy(uT[:, c_, tb * 512:(tb + 1) * 512], pb, eng=("act" if c_ else "dve"))
              for t in range(NT):
                  pb = bank(4 + t % 2)
                  for c_ in range(2):
                      P.matmul(pb[:, c_ * 256:(c_ + 1) * 256], uT[:, c_, t * 128:(t + 1) * 128], Wcs[:, c_, :],
                               start=True, stop=True)
                  P.copy(ucs[:, t, :], pb, eng=("act" if t % 2 else "dve"))
              dft_v = [dftc_d.rearrange("(t p) k -> p t k", p=128), dfts_d.rearrange("(t p) k -> p t k", p=128)]
              di = 0
              for kb in range(4):
                  pbs = [bank(0 + (kb % 2) * 2), bank(1 + (kb % 2) * 2)]
                  first = True
                  for which in range(2):
                      for tg in range(2):
                          db = dbuf[di % NDB]
                          di += 1
                          P.dma("sp", db, dft_v[which][:, tg * 8:(tg + 1) * 8, kb * 512:(kb + 1) * 512])
                          for tt_ in range(8):
                              t = tg * 8 + tt_
                              last = (which == 1 and t == NT - 1)
                              for c_ in range(2):
                                  P.matmul(pbs[c_], ucs[:, t, c_ * 256 + which * 128:c_ * 256 + (which + 1) * 128],
                                           db[:, tt_, :], start=first, stop=last)
                              first = False
                  for c_ in range(2):
                      P.copy(ocat[:, 4 + c_, kb * 512:(kb + 1) * 512], pbs[c_], eng=("act" if c_ else "dve"))
              A.release(mixer_mark)
              if l == 0:
                  dump("ofour", ocat[:, 4:6, :], [128, 2, S], BF16)
              ck('four')

              gqk = A.alloc([2, S], F32)
              gzT = A.alloc([S], BF16)
              gv = A.alloc([NT, 256], BF16)
              gate = A.alloc([NT, 256], BF16)
              w2pad = A.alloc([2, 128], BF16)
              qt = [A.alloc([S], BF16) for _ in range(2)]
              kt_ = [A.alloc([S], BF16) for _ in range(2)]
              Sbf = [A.alloc([NT, 256], BF16) for _ in range(2)]
              gla_mark = A.mark()
              Wgf = A.alloc([8, 288], BF16)
              Wgt = A.alloc([8, 512], BF16)
              P.dma("pool", Wgf[:, :, 0:256], win(l)[:, :, C_GQ:C_GQ + 256])
              P.dma("pool", Wgf[:, :, 256:288], win(l)[:, :, C_GZ:C_GZ + 32])
              P.dma("pool", Wgt, win(l)[:, :, C_GV:C_GV + 512])
              P.memset(w2pad, 0.0)
              for d_ in range(2):
                  P.dma("pool", w2pad[d_ * 16:(d_ + 1) * 16, d_, :], w2_d[l, d_])
              for tb in range(4):
                  for c_ in range(2):
                      pb = bank((tb * 3 + c_) % 4)
                      for k in range(8):
                          P.matmul(pb, Wgf[:, k, c_ * 128:(c_ + 1) * 128], xT[:, k, tb * 512:(tb + 1) * 512],
                                   start=(k == 0), stop=(k == 7))
                      P.copy(gqk[:, c_, tb * 512:(tb + 1) * 512], pb, eng=("act" if c_ else "dve"))
                  pb = bank((tb * 3 + 2) % 4)
                  for k in range(8):
                      P.matmul(pb[0:32, :], Wgf[:, k, 256:288], xT[:, k, tb * 512:(tb + 1) * 512],
                               start=(k == 0), stop=(k == 7))
                  P.copy(gzT[0:32, tb * 512:(tb + 1) * 512], pb[0:32, :], eng="dve")
              for t in range(NT):
                  pb = bank(4 + t % 2)
                  for k in range(8):
                      P.matmul(pb, xT[:, k, t * 128:(t + 1) * 128], Wgt[:, k, :], start=(k == 0), stop=(k == 7))
                  P.copy(gv[:, t, :], pb[:, 0:256], eng="act")
                  P.act(gate[:, t, :], pb[:, 256:512], AF.Silu)
              A.release(gla_mark)
              Bc = A.alloc([S], F32)
              Ec = A.alloc([S], F32)
              kdec_tok = A.alloc([NT, 128], BF16)
              kdT = [A.alloc([128], BF16) for _ in range(2)]
              Srot = [A.alloc([256], F32) for _ in range(2)]
              for d_ in range(2):
                  for tb in range(4):
                      pb = bank(tb % 4)
                      P.matmul(pb, w2pad[0:32, d_, :], gzT[0:32, tb * 512:(tb + 1) * 512], start=True, stop=True)
                      P.act(Ec[:, tb * 512:(tb + 1) * 512], pb, AF.Exp, bias=negb2[:, d_:d_ + 1], scale=-1.0)
                  P.act(Ec, Ec, AF.Ln, bias=1.0)
                  for n in range(NT):
                      o_ = Bc[:, n * 128:(n + 1) * 128]
                      i_ = Ec[:, n * 128:(n + 1) * 128]
                      if d_ == 1:
                          o_ = o_[:, ::-1]
                          i_ = i_[:, ::-1]
                      P.add("dve", lambda e, o_=o_, i_=i_: e.tensor_tensor_scan(
                          out=o_, data0=ones_f, data1=i_, initial=0.0, op0=ALU.mult, op1=ALU.add),
                          reads=[ones_f, Ec[:, n * 128:(n + 1) * 128]], writes=[Bc[:, n * 128:(n + 1) * 128]])
                  P.act(Ec, Bc, AF.Exp, scale=-1.0 / 16.0)
                  P.act(Bc, Bc, AF.Exp, scale=1.0 / 16.0)
                  P.stt(qt[d_], gqk[:, 0, :], 32.0 ** -0.5, Ec, ALU.mult, ALU.mult)
                  P.tt(kt_[d_], gqk[:, 1, :], Bc, ALU.mult)
                  elast = [Ec[:, n * 128 + (127 if d_ == 0 else 0):n * 128 + (127 if d_ == 0 else 0) + 1]
                           for n in range(NT)]
                  pst = bankbf(7)
                  for n in range(NT):
                      kd = kdT[n % 2]
                      P.stt(kd, gqk[:, 1, n * 128:(n + 1) * 128], elast[n], Bc[:, n * 128:(n + 1) * 128],
                            ALU.mult, ALU.mult)
                      P.transpose(pst[:, (n % 8) * 128:(n % 8 + 1) * 128], kd, ident_b)
                      if n % 8 == 7:
                          P.copy(kdec_tok[:, n - 7:n + 1, :], pst.rearrange("p (a b) -> p a b", a=8), eng="act")
                  order = list(range(NT)) if d_ == 0 else list(range(NT - 1, -1, -1))
                  prev = Srot[0]
                  P.memset(prev, 0.0)
                  P.memset(Sbf[d_][:, order[0], :], 0.0)
                  for i_, n in enumerate(order[:-1]):
                      pb = bank(4 + i_ % 2, 256)
                      P.matmul(pb, kdec_tok[:, n, :], gv[:, n, :], start=True, stop=True)
                      cur = Srot[(i_ + 1) % 2]
                      P.stt(cur, prev, elast[n], pb, ALU.mult, ALU.add)
                      P.tt(Sbf[d_][:, order[i_ + 1], :], cur, bdmask, ALU.mult, eng="pool")
                      prev = cur
              A.release(gla_mark)
              Qbd = [A.alloc([4, 128], BF16) for _ in range(3)]
              Asb = [A.alloc([4, 128], BF16) for _ in range(3)]
              ogf = [A.alloc([256], F32) for _ in range(2)]
              ogb = [A.alloc([256], BF16) for _ in range(2)]
              sq2 = A.alloc([256], F32)
              ss4 = small[:, 528:532]
              rs4 = small[:, 532:536]
              ci = 0
              for n in range(NT):
                  po = bank(4 + n % 2, 256)
                  P.matmul(po, qt[0][:, n * 128:(n + 1) * 128], Sbf[0][:, n, :], start=True, stop=False)
                  P.matmul(po, qt[1][:, n * 128:(n + 1) * 128], Sbf[1][:, n, :], start=False, stop=False)
                  for d_ in range(2):
                      qb = Qbd[ci % 3]
                      asb = Asb[ci % 3]
                      pa = bank(ci % 4)
                      ci += 1
                      P.tt(qb, qt[d_][:, n * 128:(n + 1) * 128].unsqueeze(1).broadcast_to([128, 4, 128]),
                           hmask4.unsqueeze(2).broadcast_to([128, 4, 128]), ALU.mult, eng="pool")
                      P.matmul(pa, kt_[d_][:, n * 128:(n + 1) * 128], qb.rearrange("p a b -> p (a b)"),
                               start=True, stop=True)
                      P.tt(asb, pa.rearrange("p (a b) -> p a b", a=4),
                           tri[d_].unsqueeze(1).broadcast_to([128, 4, 128]), ALU.mult)
                      for h in range(4):
                          P.matmul(po[:, h * 64:(h + 1) * 64], asb[:, h, :], gv[:, n, h * 64:(h + 1) * 64],
                                   start=False, stop=(d_ == 1 and h == 3))
                  of = ogf[n % 2]
                  P.copy(of, po, eng="act")
                  P.tt(sq2, of, of, ALU.mult, eng="pool")
                  P.reduce(ss4, sq2.rearrange("p (h v) -> p h v", h=4), ALU.add)
                  rstd_from(rs4, ss4, 64.0, 4)
                  of3 = of.rearrange("p (h v) -> p h v", h=4)
                  P.tt(of3, of3, rs4.unsqueeze(2).broadcast_to([128, 4, 64]), ALU.mult)
                  P.tt(of3, of3, ggla.unsqueeze(1).broadcast_to([128, 4, 64]), ALU.mult)
                  ob = ogb[n % 2]
                  P.tt(ob, of, gate[:, n, :], ALU.mult)
                  pst = bankbf(7)
                  for c_ in range(2):
                      P.transpose(pst[:, c_ * 128:(c_ + 1) * 128], ob[:, c_ * 128:(c_ + 1) * 128], ident_b)
                  P.copy(ocat[:, 6:8, n * 128:(n + 1) * 128], pst[:, 0:256].rearrange("p (a b) -> p a b", a=2),
                         eng="dve")
              A.release(mixer_mark)
              if l == 0:
                  dump("ogla", ocat[:, 6:8, :], [128, 2, S], BF16)
              ck('gla')

              A.n = ARENA - XTB
              assert A.top <= A.n
              post_mark = A.mark()
              ytmp = [A.alloc([D], F32) for _ in range(2)]
              xbs = [A.alloc([D], BF16) for _ in range(2)]
              for t in range(NT):
                  P.dma("sp", x_tok[:, t, :], xres_d[t * 128:(t + 1) * 128, :])
              def mm1_(t):
                  b0 = (t % 3) * 2
                  p2 = ps[:, b0 * 512:(b0 + 2) * 512]
                  for hf in range(2):
                      for e_ in range(8):
                          P.matmul(p2[:, hf * 512:(hf + 1) * 512], ocat[:, e_, t * 128:(t + 1) * 128],
                                   Wout[:, e_, hf * 512:(hf + 1) * 512], start=(e_ == 0), stop=(e_ == 7))

              def a1_(t):
                  b0 = (t % 3) * 2
                  ln_a(ps[:, b0 * 512:(b0 + 2) * 512], x_tok[:, t, :], lnbuf, ytmp[t % 2], t % 2)

              def b1_(t):
                  ln_b(lnbuf, x_tok[:, t, :], ytmp[t % 2])
                  make_xT(x_tok[:, t, :], t, xbs[t % 2])

              for s_ in range(NT + 2):
                  if s_ < NT:
                      mm1_(s_)
                  if 0 <= s_ - 1 < NT:
                      a1_(s_ - 1)
                  if 0 <= s_ - 2 < NT:
                      b1_(s_ - 2)
              A.release(base_mark)
              if l == 0:
                  dump("x1", x_tok, [128, NT, D])
              ck('ln1')

              GF = NFC // 2
              Wdg = A.alloc([GF, D], BF16)
              hT = A.alloc([GF, S], BF16)
              NWB = 3
              wgu = [A.alloc([2, 8, 128], BF16) for _ in range(NWB)]
              sg = [A.alloc([512], BF16) for _ in range(2)]
              ytmp = [A.alloc([D], F32) for _ in range(1)]
              xbs = [A.alloc([D], BF16) for _ in range(2)]
              wdv = wd_d[l].rearrange("(f p) n -> p f n", p=128)
              wgv = wg_d[l].rearrange("(k p) n -> p k n", p=128)
              wuv = wu_d[l].rearrange("(k p) n -> p k n", p=128)
              P.dma("sp", lnbuf[:, 0, :], ln_d["ln2_g"][l].partition_broadcast(128))
              P.dma("sp", lnbuf[:, 1, :], ln_d["ln2_b"][l].partition_broadcast(128))
              wi = 0
              ui = 0
              for g_ in range(2):
                  for fi in range(GF):
                      f = g_ * GF + fi
                      wb = wgu[wi % NWB]
                      wi += 1
                      P.dma("pool", wb[:, 0, :, :], wgv[:, :, f * 128:(f + 1) * 128])
                      P.dma("pool", wb[:, 1, :, :], wuv[:, :, f * 128:(f + 1) * 128])
                      P.dma("pool", Wdg[:, fi, :], wdv[:, f, :])
                      for tb in range(4):
                          pg = bank((ui % 2) * 2)
                          pu = bank((ui % 2) * 2 + 1)
                          ui += 1
                          for k in range(8):
                              P.matmul(pg, wb[:, 0, k, :], xT[:, k, tb * 512:(tb + 1) * 512],
                                       start=(k == 0), stop=(k == 7))
                          for k in range(8):
                              P.matmul(pu, wb[:, 1, k, :], xT[:, k, tb * 512:(tb + 1) * 512],
                                       start=(k == 0), stop=(k == 7))
                          s_ = sg[ui % 2]
                          P.act(s_, pg, AF.Silu)
                          P.tt(hT[:, fi, tb * 512:(tb + 1) * 512], pu, s_, ALU.mult)
                  ytmps = [ytmp[0], wgu[0].rearrange("p a b c -> p (a b c)").bitcast(F32)]

                  def mm2_(t):
                      b0 = (t % 3) * 2
                      p2 = ps[:, b0 * 512:(b0 + 2) * 512]
                      for hf in range(2):
                          for fi in range(GF):
                              P.matmul(p2[:, hf * 512:(hf + 1) * 512], hT[:, fi, t * 128:(t + 1) * 128],
                                       Wdg[:, fi, hf * 512:(hf + 1) * 512], start=(fi == 0), stop=(fi == GF - 1))

                  def a2_(t):
                      b0 = (t % 3) * 2
                      p2 = ps[:, b0 * 512:(b0 + 2) * 512]
                      if g_ == 0:
                          P.stt(x_tok[:, t, :], x_tok[:, t, :], ALPHA, p2, ALU.mult, ALU.add)
                      else:
                          ln_a(p2, x_tok[:, t, :], lnbuf, ytmps[t % 2], t % 2, alpha=1.0)

                  def b2_(t):
                      if g_ == 0:
                          return
                      ln_b(lnbuf, x_tok[:, t, :], ytmps[t % 2])
                      if l == n_layers - 1:
                          P.dma("sp", y_d[t * 128:(t + 1) * 128, :], x_tok[:, t, :])
                      else:
                          P.dma("sp", xs_d[t * 128:(t + 1) * 128, :], x_tok[:, t, :])
                          make_xT(x_tok[:, t, :], t, xbs[t % 2])

                  for s_ in range(NT + 2):
                      if s_ < NT:
                          mm2_(s_)
                      if 0 <= s_ - 1 < NT:
                          a2_(s_ - 1)
                      if 0 <= s_ - 2 < NT:
                          b2_(s_ - 2)
              A.release(base_mark)
              A.n = ARENA

        except _Stop:
            pass
        P.emit()
    _CACHE['P'] = P
    return nc, dbg_outs


def kernel(**inputs):
    if "c" not in _CACHE:
        _CACHE["c"] = host_consts()
    cf, cb, dc, ds = _CACHE["c"]
    nc, _ = build()
    x = np.ascontiguousarray(inputs["x"], dtype=np.float32)
    common = {
        "w_in": np.ascontiguousarray(inputs["w_in"], dtype=np.float32),
        "diff_lambda": np.ascontiguousarray(inputs["diff_lambda"], dtype=np.float32).reshape(L, 256),
        "diff_norm_g": np.ascontiguousarray(inputs["diff_norm_g"], dtype=np.float32),
        "fourier_w": np.ascontiguousarray(inputs["fourier_w"], dtype=np.float32),
        "gla_gate_w2": np.ascontiguousarray(inputs["gla_gate_w2"], dtype=np.float32),
        "gla_gate_b2": np.ascontiguousarray(inputs["gla_gate_b2"], dtype=np.float32),
        "gla_norm_g": np.ascontiguousarray(inputs["gla_norm_g"], dtype=np.float32),
        "w_out": np.ascontiguousarray(inputs["w_out"], dtype=np.float32),
        "ln1_g": np.ascontiguousarray(inputs["ln1_g"], dtype=np.float32),
        "ln1_b": np.ascontiguousarray(inputs["ln1_b"], dtype=np.float32),
        "ln2_g": np.ascontiguousarray(inputs["ln2_g"], dtype=np.float32),
        "ln2_b": np.ascontiguousarray(inputs["ln2_b"], dtype=np.float32),
        "ffn_w_gate": np.ascontiguousarray(inputs["ffn_w_gate"], dtype=np.float32),
        "ffn_w_up": np.ascontiguousarray(inputs["ffn_w_up"], dtype=np.float32),
        "ffn_w_down": np.ascontiguousarray(inputs["ffn_w_down"], dtype=np.float32),
        "c_f32": cf, "c_bf": cb, "dft_c": dc, "dft_s": ds,
    }
    in_maps = [dict(common, x=x[b]) for b in range(8)]
    res = run_bass_kernel_spmd(nc, in_maps, core_ids=list(range(8)))
    return np.stack([np.asarray(r["y"], dtype=np.float32) for r in res.results], axis=0)
```

```python
import numpy as np
import concourse.bass as bass
import concourse.mybir as mybir

F32 = mybir.dt.float32
BF16 = mybir.dt.bfloat16
ALU = mybir.AluOpType
AF = mybir.ActivationFunctionType
AX = mybir.AxisListType

_DT_SIZE = {F32: 4, BF16: 2, mybir.dt.float32r: 4, mybir.dt.int32: 4,
            mybir.dt.uint32: 4, mybir.dt.float16: 2, mybir.dt.uint16: 2,
            mybir.dt.int16: 2, mybir.dt.uint8: 1, mybir.dt.int8: 1}

ENGS = ("pe", "act", "dve", "pool", "sp")
N_DMA_SEMS = 24
TINY_BYTES = 256
SAME_ENGINE_SYNC = False


def ap_box(ap):
    t = ap.tensor
    name = t.name
    esz = _DT_SIZE[ap.dtype]
    dims = list(ap.ap)
    off = ap.offset
    space = str(ap.space)
    if "DRAM" in space.upper() or "HBM" in space.upper():
        lo = off
        hi = off
        for (st, n) in dims:
            if st >= 0:
                hi += st * (n - 1)
            else:
                lo += st * (n - 1)
        return (name, 0, 1, lo * esz, (hi + 1) * esz)
    pstep, pcnt = dims[0]
    if pstep == 0:
        pstep = 1 << 40
    p0 = off // pstep if pstep < (1 << 40) else 0
    f = off - p0 * pstep if pstep < (1 << 40) else off
    lo = f
    hi = f
    for (st, n) in dims[1:]:
        if st >= 0:
            hi += st * (n - 1)
        else:
            lo += st * (n - 1)
    if "PSUM" in space.upper():
        b0 = (lo * esz) // 2048
        b1 = ((hi + 1) * esz - 1) // 2048
        return (name, 0, 128, b0 * 2048, (b1 + 1) * 2048, True)
    return (name, p0, p0 + pcnt, lo * esz, (hi + 1) * esz)


def _overlap(a, b):
    return a[1] < b[2] and b[1] < a[2] and a[3] < b[4] and b[3] < a[4]


def _contains(a, b):
    return a[1] <= b[1] and a[2] >= b[2] and a[3] <= b[3] and a[4] >= b[4]


class Prog:
    def __init__(self, nc):
        self.nc = nc
        self.ins = []
        self.recs = {}
        self.dma_rr = {e: 0 for e in ENGS}
        self.trace = {}

    def add(self, eng, fn, reads=(), writes=(), dma=False):
        idx = len(self.ins)
        rb = list(dict.fromkeys(ap_box(a) for a in reads))
        wb = list(dict.fromkeys(ap_box(a) for a in writes))
        deps = set()
        tiny_deps = set()
        for b in rb:
            psum = len(b) > 5
            tiny = (not psum) and (b[4] - b[3]) <= TINY_BYTES
            for rec in self.recs.get(b[0], ()):
                if (rec[1] or (psum and rec[3] != eng)) and _overlap(rec[0], b):
                    deps.add(rec[2])
                    if tiny and rec[1]:
                        tiny_deps.add(rec[2])
        for b in wb:
            for rec in self.recs.get(b[0], ()):
                if _overlap(rec[0], b):
                    deps.add(rec[2])
        for b in wb:
            lst = self.recs.setdefault(b[0], [])
            lst[:] = [r for r in lst if not _contains(b, r[0])]
            lst.append([b, True, idx, eng])
        for b in rb:
            lst = self.recs.setdefault(b[0], [])
            found = False
            for r in lst:
                if (not r[1]) and r[3] == eng and r[0] == b and not dma \
                        and not self.ins[r[2]]["dma"]:
                    r[2] = idx
                    found = True
                    break
            if not found:
                lst.append([b, False, idx, eng])
        real = set()
        for d in deps:
            p = self.ins[d]
            if p["eng"] == "pe" and eng == "pe" and not p["dma"] and not dma:
                continue
            if (not SAME_ENGINE_SYNC) and p["eng"] == eng and eng != "pool" and not p["dma"] and not dma \
                    and d not in tiny_deps:
                continue
            real.add(d)
        self.ins.append(dict(eng=eng, fn=fn, deps=real, dma=dma, needed=False,
                             sem=None, val=None))
        return idx

    def emit(self, final_wait_eng="sp"):
        nc = self.nc
        ins = self.ins
        for r in ins:
            for d in r["deps"]:
                ins[d]["needed"] = True
        last_dmas = [i for i, r in enumerate(ins) if r["dma"]]
        import contextlib
        with contextlib.ExitStack() as st:
            esem = {e: st.enter_context(nc.semaphore("s_" + e)) for e in ENGS}
            dsem = {e: [st.enter_context(nc.semaphore("d_%s_%d" % (e, i)))
                        for i in range(N_DMA_SEMS)] for e in ("sp", "act", "pool")}
            cnt = {e: 0 for e in ENGS}
            dcnt = {e: [0] * N_DMA_SEMS for e in dsem}
            drr = {e: 0 for e in dsem}
            prev_use = {}
            for i, r in enumerate(ins):
                e = r["eng"]
                if r["dma"]:
                    s = drr[e]
                    drr[e] = (s + 1) % N_DMA_SEMS
                    if dcnt[e][s] > 0:
                        r["prev"] = (dsem[e][s], dcnt[e][s], ("d", e, s))
                    else:
                        r["prev"] = None
                    dcnt[e][s] += 16
                    r["sem"] = dsem[e][s]
                    r["val"] = dcnt[e][s]
                    r["semkey"] = ("d", e, s)
                elif r["needed"]:
                    cnt[e] += 1
                    r["sem"] = esem[e]
                    r["val"] = cnt[e]
                    r["semkey"] = ("e", e)
            block = st.enter_context(nc.Block())
            per_eng = {e: [i for i, r in enumerate(ins) if r["eng"] == e] for e in ENGS}
            final = {}
            for i in last_dmas:
                r = ins[i]
                final[r["semkey"]] = (r["sem"], max(r["val"], final.get(r["semkey"], (None, 0))[1]))

            def body(ename):
                def run(engobj):
                    waited = {}
                    for i in per_eng[ename]:
                        r = ins[i]
                        need = {}
                        for d in r["deps"]:
                            p = ins[d]
                            k = p["semkey"]
                            if need.get(k, (None, 0))[1] < p["val"]:
                                need[k] = (p["sem"], p["val"])
                        if r["dma"] and r["prev"] is not None:
                            s, v, k = r["prev"]
                            if need.get(k, (None, 0))[1] < v:
                                need[k] = (s, v)
                        for k, (s, v) in need.items():
                            if waited.get(k, 0) < v:
                                engobj.wait_ge(s, v)
                                waited[k] = v
                                self.trace.setdefault(ename, []).append(("w", k, v, i))
                        h = r["fn"](engobj)
                        if r["dma"]:
                            h.then_inc(r["sem"], 16)
                            self.trace.setdefault(ename, []).append(("i", r["semkey"], 16, i))
                        elif r["needed"]:
                            h.then_inc(r["sem"], 1)
                            self.trace.setdefault(ename, []).append(("i", r["semkey"], 1, i))
                    if ename == final_wait_eng:
                        for k, (s, v) in final.items():
                            if waited.get(k, 0) < v:
                                engobj.wait_ge(s, v)
                return run

            block.tensor(body("pe"))
            block.scalar(body("act"))
            block.vector(body("dve"))
            block.gpsimd(body("pool"))
            block.sync(body("sp"))

    def dma(self, eng, out, in_, **kw):
        return self.add(eng, lambda e: e.dma_start(out=out, in_=in_, **kw),
                        reads=[in_], writes=[out], dma=True)

    def matmul(self, out, lhsT, rhs, start=True, stop=True, **kw):
        return self.add("pe", lambda e: e.matmul(out, lhsT, rhs, start=start, stop=stop, **kw),
                        reads=[lhsT, rhs], writes=[out])

    def transpose(self, out, in_, ident):
        return self.add("pe", lambda e: e.transpose(out, in_, ident),
                        reads=[in_, ident], writes=[out])

    def act(self, out, in_, func, bias=None, scale=1.0, accum_out=None, eng="act"):
        reads = [in_]
        writes = [out]
        kw = {}
        if bias is not None:
            kw["bias"] = bias
            if not isinstance(bias, (int, float)):
                reads.append(bias)
        if not isinstance(scale, (int, float)):
            reads.append(scale)
        if accum_out is not None:
            kw["accum_out"] = accum_out
            writes.append(accum_out)
        return self.add(eng, lambda e: e.activation(out=out, in_=in_, func=func, scale=scale, **kw),
                        reads=reads, writes=writes)

    def tt(self, out, in0, in1, op, eng="dve"):
        return self.add(eng, lambda e: e.tensor_tensor(out=out, in0=in0, in1=in1, op=op),
                        reads=[in0, in1], writes=[out])

    def ts(self, out, in0, s1, op0, s2=None, op1=None, eng="dve", accum_out=None):
        reads = [in0]
        if not isinstance(s1, (int, float)):
            reads.append(s1)
        if s2 is not None and not isinstance(s2, (int, float)):
            reads.append(s2)
        kw = {}
        writes = [out]
        if op1 is not None:
            kw["op1"] = op1
        if accum_out is not None:
            kw["accum_out"] = accum_out
            writes.append(accum_out)
        return self.add(eng, lambda e: e.tensor_scalar(out=out, in0=in0, scalar1=s1, scalar2=s2,
                                                       op0=op0, **kw),
                        reads=reads, writes=writes)

    def stt(self, out, in0, scalar, in1, op0, op1, eng="dve"):
        reads = [in0, in1]
        if not isinstance(scalar, (int, float)):
            reads.append(scalar)
        return self.add(eng, lambda e: e.scalar_tensor_tensor(out=out, in0=in0, scalar=scalar,
                                                              in1=in1, op0=op0, op1=op1),
                        reads=reads, writes=[out])

    def copy(self, out, in_, eng="dve"):
        if eng == "act":
            return self.add(eng, lambda e: e.activation(out=out, in_=in_, func=AF.Identity),
                            reads=[in_], writes=[out])
        return self.add(eng, lambda e: e.tensor_copy(out=out, in_=in_), reads=[in_], writes=[out])

    def memset(self, ap, val, eng="dve"):
        return self.add(eng, lambda e: e.memset(ap, val), reads=[], writes=[ap])

    def reduce(self, out, in_, op, axis=AX.X, eng="dve", **kw):
        return self.add(eng, lambda e: e.tensor_reduce(out=out, in_=in_, op=op, axis=axis, **kw),
                        reads=[in_], writes=[out])

    def recip(self, out, in_, eng="dve"):
        return self.add(eng, lambda e: e.reciprocal(out=out, in_=in_), reads=[in_], writes=[out])


def simulate_trace(trace):
    pos = {e: 0 for e in trace}
    sem = {}
    progress = True
    while progress:
        progress = False
        for e, ops in trace.items():
            while pos[e] < len(ops):
                kind, k, v, i = ops[pos[e]]
                if kind == "w":
                    if sem.get(k, 0) >= v:
                        pos[e] += 1
                        progress = True
                    else:
                        break
                else:
                    sem[k] = sem.get(k, 0) + v
                    pos[e] += 1
                    progress = True
    stuck = {e: ops[pos[e]] for e, ops in trace.items() if pos[e] < len(ops)}
    return stuck, sem

import contextlib
import os
_SK = set(os.environ.get('DBGSKIP', '').split(','))
import math
import ml_dtypes
from concourse.bass_utils import run_bass_kernel_spmd

S = 2048
D = 1024
L = 2
INW = 2592
FF = 2816
NT = 16
NFC = FF // 128
ALPHA = float((2 * L) ** 0.25)
EPS = 1e-5
C_DQ, C_DK, C_DV, C_FU, C_GQ, C_GK, C_GV, C_GR, C_GZ = 0, 512, 1024, 1536, 1792, 1920, 2048, 2304, 2560

CF_ID, CF_COS, CF_SIN, CF_HM, CF_CC, CF_SC, CF_ONES, CF_NEGH, CF_N = 0, 128, 640, 1152, 1156, 1284, 1412, 1540, 1548
CB_ID, CB_TRIF, CB_TRIB, CB_BD, CB_ONE, CB_N = 0, 128, 256, 384, 640, 768


def host_consts():
    p = np.arange(128)
    cf = np.zeros((128, CF_N), np.float32)
    cf[:, CF_ID:CF_ID + 128] = np.eye(128, dtype=np.float32)
    inv_freq = (10000.0 ** (-np.arange(0, 64, 2, dtype=np.float32) / 64)).astype(np.float32)
    pos = (np.arange(NT)[None, :] * 128 + p[:, None]).astype(np.float32)
    ang = pos[:, :, None] * inv_freq[None, None, :]
    cf[:, CF_COS:CF_COS + 512] = np.cos(ang).reshape(128, 512)
    cf[:, CF_SIN:CF_SIN + 512] = np.sin(ang).reshape(128, 512)
    cf[:, CF_HM:CF_HM + 4] = (p[:, None] // 32 == np.arange(4)[None, :]).astype(np.float32)
    c = np.arange(64)
    a = 2 * np.pi * np.outer(c, c) / 64.0
    cc = np.cos(a) / 8.0
    sc = np.sin(a) / 8.0
    z = np.zeros((64, 64))
    cf[:, CF_CC:CF_CC + 128] = np.block([[cc, z], [z, cc]])
    cf[:, CF_SC:CF_SC + 128] = np.block([[sc, z], [z, sc]])
    cf[:, CF_ONES:CF_ONES + 128] = 1.0
    cf[:, CF_NEGH:CF_NEGH + 8] = -0.5
    cb = np.zeros((128, CB_N), np.float32)
    cb[:, CB_ID:CB_ID + 128] = np.eye(128)
    cb[:, CB_TRIF:CB_TRIF + 128] = (p[:, None] <= p[None, :])
    cb[:, CB_TRIB:CB_TRIB + 128] = (p[:, None] >= p[None, :])
    cb[:, CB_BD:CB_BD + 256] = (p[:, None] // 32 == np.arange(256)[None, :] // 64)
    cb[:, CB_ONE:CB_ONE + 128] = 1.0
    s = np.arange(S, dtype=np.float64)
    sk = np.outer(s, s) % S
    ang = 2 * np.pi * sk / S
    dc = (np.cos(ang) / math.sqrt(S)).astype(np.float32).astype(ml_dtypes.bfloat16)
    ds = (-np.sin(ang) / math.sqrt(S)).astype(np.float32).astype(ml_dtypes.bfloat16)
    return cf, cb.astype(ml_dtypes.bfloat16), dc, ds


class Arena:
    def __init__(self, t, nbytes):
        self.t = t
        self.n = nbytes
        self.top = 0

    def mark(self):
        return self.top

    def release(self, m):
        self.top = m

    def alloc(self, shape, dt):
        n = 1
        for s_ in shape:
            n *= s_
        b = n * _DT_SIZE[dt]
        off = self.top
        self.top += (b + 63) // 64 * 64
        assert self.top <= self.n, ("arena overflow", self.top, self.n)
        ap = self.t[:, off // 2:(off + b) // 2]
        if dt != BF16:
            ap = ap.bitcast(dt)
        if len(shape) == 2:
            ap = ap.rearrange("p (a b) -> p a b", a=shape[0])
        elif len(shape) == 3:
            ap = ap.rearrange("p (a b c) -> p a b c", a=shape[0], b=shape[1])
        return ap


class Rot:
    def __init__(self, items):
        self.items = items
        self.i = 0

    def next(self):
        r = self.items[self.i % len(self.items)]
        self.i += 1
        return r


_CACHE = {}


class _Stop(Exception):
    pass


def build(n_layers=L, dbg=None, upto=None):
    nc = bass.Bass("TRN2", target_bir_lowering=False)
    dram = lambda name, shape, dt, kind="ExternalInput": nc.dram_tensor(name, shape, dt, kind=kind).ap()
    x_d = dram("x", [S, D], F32)
    w_in_d = dram("w_in", [L, D, INW], F32)
    lam_d = dram("diff_lambda", [L, 256], F32)
    dng_d = dram("diff_norm_g", [L, 128], F32)
    fw_d = dram("fourier_w", [L, 4, 64, 64], F32)
    w2_d = dram("gla_gate_w2", [L, 2, 16, 128], F32)
    b2_d = dram("gla_gate_b2", [L, 2, 128], F32)
    gng_d = dram("gla_norm_g", [L, 64], F32)
    wout_d = dram("w_out", [L, D, D], F32)
    ln_d = {k: dram(k, [L, D], F32) for k in ("ln1_g", "ln1_b", "ln2_g", "ln2_b")}
    wg_d = dram("ffn_w_gate", [L, D, FF], F32)
    wu_d = dram("ffn_w_up", [L, D, FF], F32)
    wd_d = dram("ffn_w_down", [L, FF, D], F32)
    cf_d = dram("c_f32", [128, CF_N], F32)
    cb_d = dram("c_bf", [128, CB_N], BF16)
    dftc_d = dram("dft_c", [S, S], BF16)
    dfts_d = dram("dft_s", [S, S], BF16)
    y_d = dram("y", [S, D], F32, kind="ExternalOutput")
    xs_d = dram("xs_scr", [S, D], F32, kind="Internal")
    dbg_outs = {}

    ARENA = 207 * 1024
    XTB = NT * D * 4
    with contextlib.ExitStack() as st:
        arena_t = st.enter_context(nc.sbuf_tensor("arena", [128, ARENA // 2], BF16))
        ps = st.enter_context(nc.psum_tensor("ps", [128, 4096], F32))
        P = Prog(nc)
        A = Arena(arena_t, ARENA)
        x_tok = arena_t[:, (ARENA - XTB) // 2:ARENA // 2].bitcast(F32).rearrange("p (t d) -> p t d", t=NT)

        def bank(b, n=512, off=0):
            return ps[:, b * 512 + off:b * 512 + off + n]

        def bankbf(b):
            return ps[:, b * 512:(b + 1) * 512].bitcast(BF16)

        def dump(name, ap, shape, dt=F32):
            if dbg is None or name not in dbg:
                return
            d_ = nc.dram_tensor("dbg_" + name, shape, dt, kind="ExternalOutput").ap()
            dbg_outs[name] = d_
            P.dma("sp", d_, ap)

        cf = A.alloc([CF_N], F32)
        cb = A.alloc([CB_N], BF16)
        xT = A.alloc([8, S], BF16)
        lnbuf = A.alloc([2, D], F32)
        small = A.alloc([640], F32)
        P.dma("sp", cf, cf_d)
        P.dma("sp", cb, cb_d)
        ident_f = cf[:, CF_ID:CF_ID + 128]
        ident_b = cb[:, CB_ID:CB_ID + 128]
        cos_t = cf[:, CF_COS:CF_COS + 512].rearrange("p (t i) -> p t i", t=NT)
        sin_t = cf[:, CF_SIN:CF_SIN + 512].rearrange("p (t i) -> p t i", t=NT)
        hmask4 = cf[:, CF_HM:CF_HM + 4]
        ccbd = cf[:, CF_CC:CF_CC + 128]
        scbd = cf[:, CF_SC:CF_SC + 128]
        ones_f = cf[:, CF_ONES:CF_ONES + 128]
        negh8 = cf[:, CF_NEGH:CF_NEGH + 8]
        tri = [cb[:, CB_TRIF:CB_TRIF + 128], cb[:, CB_TRIB:CB_TRIB + 128]]
        bdmask = cb[:, CB_BD:CB_BD + 256]
        lp_bc = small[:, 0:256]
        gdiff = small[:, 256:384]
        ggla = small[:, 384:448]
        negb2 = small[:, 448:450]
        neglam = small[:, 450:451]
        negM = small[:, 451:452]
        sc_tmp = small[:, 452:500]
        nrm = small[:, 500:504]
        epsc = small[:, 504:505]
        base_mark = A.mark()

        win = lambda l: w_in_d[l].rearrange("(k p) n -> p k n", p=128)

        def rstd_from(out, ss, n, width):
            tmp = sc_tmp[:, 40:40 + width]
            P.ts(tmp, ss, 1.0 / n, ALU.mult, EPS, ALU.add)
            P.tt(out, tmp, negh8[:, 0:width], ALU.pow, eng="pool")

        xb_rot = None

        def make_xT(x_tile_f32, t, xb):
            P.copy(xb, x_tile_f32, eng="act")
            pst = bankbf(7)
            for k in range(8):
                P.transpose(pst[:, k * 128:(k + 1) * 128], xb[:, k * 128:(k + 1) * 128], ident_b)
            P.copy(xT[:, :, t * 128:(t + 1) * 128], pst.rearrange("p (k n) -> p k n", k=8), eng="act")

        def ln_a(psum2, xres, gb, ytmp, par, alpha=ALPHA):
            sct = sc_tmp[:, 0:16] if par == 0 else small[:, 540:556]
            P.stt(ytmp, xres, alpha, psum2, ALU.mult, ALU.add)
            stats = sct[:, 0:12]
            mv = sct[:, 12:14]
            for c_ in range(2):
                P.add("dve", lambda e, c_=c_: e.bn_stats(out=stats[:, c_ * 6:(c_ + 1) * 6],
                                                          in_=ytmp[:, c_ * 512:(c_ + 1) * 512]),
                      reads=[ytmp[:, c_ * 512:(c_ + 1) * 512]], writes=[stats[:, c_ * 6:(c_ + 1) * 6]])
            P.add("dve", lambda e: e.bn_aggr(out=mv, in_=stats), reads=[stats], writes=[mv])
            rs = sct[:, 14:15]
            nmr = sct[:, 15:16]
            P.act(rs, mv[:, 1:2], AF.Ln, bias=epsc)
            P.act(rs, rs, AF.Exp, scale=-0.5)
            P.stt(nmr, mv[:, 0:1], -1.0, rs, ALU.mult, ALU.mult)
            P.act(ytmp, ytmp, AF.Identity, bias=nmr, scale=rs)
            P.tt(ytmp, ytmp, gb[:, 0, :], ALU.mult, eng="pool")

        def ln_b(gb, out_tok, ytmp):
            P.tt(out_tok, ytmp, gb[:, 1, :], ALU.add)

        def ck(name):
            if upto == name:
                raise _Stop()

        try:
          for l in range(n_layers):
              lam_init = 0.8 - 0.6 * math.exp(-0.3 * l)
              A.release(base_mark)
              P.dma("sp", lp_bc, lam_d[l].partition_broadcast(128))
              P.dma("sp", gdiff, dng_d[l].partition_broadcast(128))
              P.dma("sp", ggla, gng_d[l].partition_broadcast(128))
              for d_ in range(2):
                  P.dma("sp", negb2[:, d_:d_ + 1], b2_d[l, d_].rearrange("(p o) -> p o", o=1))
              P.ts(negb2, negb2, -1.0, ALU.mult)
              P.ts(gdiff, gdiff, 1.0 - lam_init, ALU.mult)
              P.memset(epsc, EPS)
              pr = sc_tmp[:, 16:18]
              prod = A.alloc([128], F32)
              lp4 = lp_bc.rearrange("p (a d) -> p a d", a=4)
              for i_ in range(2):
                  P.tt(prod[:, 0:64], lp4[:, 2 * i_, :], lp4[:, 2 * i_ + 1, :], ALU.mult)
                  P.reduce(pr[:, i_:i_ + 1], prod[:, 0:64], ALU.add)
              P.act(pr, pr, AF.Exp)
              P.tt(neglam, pr[:, 1:2], pr[:, 0:1], ALU.subtract)
              P.ts(neglam, neglam, -lam_init, ALU.add)
              A.release(base_mark)
              ck('params')

              if l == 0:
                  m_ = A.mark()
                  xin = [A.alloc([D], F32) for _ in range(3)]
                  xbs = [A.alloc([D], BF16) for _ in range(2)]
                  for t in range(NT):
                      xi = xin[t % 3]
                      P.dma("sp", xi, x_d[t * 128:(t + 1) * 128, :])
                      make_xT(xi, t, xbs[t % 2])
                  A.release(m_)
              ck('xT')
              xres_d = x_d if l == 0 else xs_d

              ocat = A.alloc([8, S], BF16)
              Wout = A.alloc([8, D], BF16)
              mixer_mark = A.mark()

              QTz = A.alloc([2, S], BF16)
              KT = A.alloc([S], BF16)
              Vaug = A.alloc([NT, 132], BF16)
              Wh = [A.alloc([8, 384], BF16) for _ in range(2)]
              PT = [A.alloc([1024], BF16) for _ in range(3)]
              O1n = A.alloc([8, 128], F32)
              O2t = A.alloc([8, 128], F32)
              otoks = [A.alloc([8, 128], BF16) for _ in range(2)]
              qkr = [A.alloc([256], BF16) for _ in range(3)]
              tmpAs = [A.alloc([128], F32) for _ in range(2)]
              tmpBs = [A.alloc([128], F32) for _ in range(2)]
              sq = A.alloc([256], F32)
              D2 = A.alloc([256], F32)
              sqs = [sq, A.alloc([256], F32)]
              red_all = A.alloc([NT, 4], F32)
              red4 = sc_tmp[:, 20:24]
              gm = sc_tmp[:, 24:26]
              nrm2 = sc_tmp[:, 26:28]
              rz = sc_tmp[:, 28:36]
              ss8 = small[:, 512:520]
              rstd8 = small[:, 520:528]
              P.memset(Vaug[:, :, 128:129], 1.0)
              P.memset(QTz[64:128, 0, :], 0.0)
              P.memset(QTz[0:64, 1, :], 0.0, eng="pool")
              pso = [ps[:, (4 + qi // 3) * 512 + (qi % 3) * 160:(4 + qi // 3) * 512 + (qi % 3) * 160 + 129]
                     for qi in range(8)]
              grp = [(0, 3), (3, 3), (6, 2)]

              def pso_grp(gi, c0, c1):
                  q0, n_ = grp[gi]
                  base_ = (4 + gi) * 512
                  return ps[:, base_:base_ + n_ * 160].rearrange("p (a b) -> p a b", b=160)[:, :, c0:c1]

              def load_wh(h_):
                  for j_, c0 in enumerate((C_DQ, C_DK, C_DV)):
                      P.dma("pool", Wh[h_ % 2][:, :, j_ * 128:(j_ + 1) * 128],
                            win(l)[:, :, c0 + h_ * 128:c0 + (h_ + 1) * 128])

              load_wh(0)
              fin_pending = []
              for h in range(4):
                  wh = Wh[h % 2]
                  P.memset(nrm, 0.0)
                  tr_pending = []
                  for t in range(NT):
                      pb = bank(t % 4, 256)
                      pv_ = bank(4 + t % 2, 128)
                      for k in range(8):
                          P.matmul(pb, xT[:, k, t * 128:(t + 1) * 128], wh[:, k, 0:256], start=(k == 0), stop=(k == 7))
                      for k in range(8):
                          P.matmul(pv_, xT[:, k, t * 128:(t + 1) * 128], wh[:, k, 256:384], start=(k == 0), stop=(k == 7))
                      qk4 = pb.rearrange("p (g h d) -> p g h d", g=4, h=2)
                      t1 = qk4[:, :, 0, :]
                      t2 = qk4[:, :, 1, :]
                      cbt = cos_t[:, t:t + 1, :].broadcast_to([128, 4, 32])
                      sbt = sin_t[:, t:t + 1, :].broadcast_to([128, 4, 32])
                      q_ = qkr[t % 3]
                      q4 = q_.rearrange("p (g h d) -> p g h d", g=4, h=2)
                      ta = tmpAs[t % 2].rearrange("p (g d) -> p g d", g=4)
                      tb_ = tmpBs[t % 2].rearrange("p (g d) -> p g d", g=4)
                      P.tt(ta, t1, cbt, ALU.mult)
                      P.tt(tb_, t2, sbt, ALU.mult)
                      P.tt(q4[:, :, 0, :], ta, tb_, ALU.subtract)
                      P.tt(ta, t2, cbt, ALU.mult)
                      P.tt(tb_, t1, sbt, ALU.mult)
                      P.tt(q4[:, :, 1, :], ta, tb_, ALU.add)
                      P.copy(Vaug[:, t, 0:128], pv_, eng="act")
                      P.tt(sqs[t % 2], q_, q_, ALU.mult, eng="pool")
                      def tr_(t=t, q_=q_, sq_=sqs[t % 2]):
                          P.reduce(red_all[:, t, :], sq_.rearrange("p (g d) -> p g d", g=4), ALU.add)
                          pst = bankbf(7 if t % 2 else 6)
                          P.transpose(pst[:, 0:128], q_[:, 0:128], ident_b)
                          P.transpose(pst[:, 128:256], q_[:, 128:256], ident_b)
                          P.copy(QTz[0:64, 0, t * 128:(t + 1) * 128], pst[0:64, 0:128], eng="act")
                          P.copy(QTz[64:128, 1, t * 128:(t + 1) * 128], pst[64:128, 0:128], eng="act")
                          P.copy(KT[:, t * 128:(t + 1) * 128], pst[:, 128:256], eng="act")
                      tr_pending.append(tr_)
                      if len(tr_pending) > 1:
                          tr_pending.pop(0)()
                  while tr_pending:
                      tr_pending.pop(0)()
                  P.reduce(nrm, red_all.rearrange("p t g -> p g t"), ALU.max)
                  ck('h0proj')
                  if h + 1 < 4:
                      load_wh(h + 1)
                  P.reduce(nrm2, nrm.rearrange("p (a b) -> p a b", a=2), ALU.max)
                  P.ts(D2[:, 0:128], ident_f, nrm2[:, 0:1], ALU.mult)
                  P.ts(D2[:, 128:256], ident_f, nrm2[:, 1:2], ALU.mult)
                  pbm = bank(6, 256)
                  P.matmul(pbm, ones_f, D2, start=True, stop=True)
                  P.reduce(gm, pbm.rearrange("p (a b) -> p a b", a=2), ALU.max)
                  P.tt(negM, gm[:, 0:1], gm[:, 1:2], ALU.add)
                  P.ts(negM, negM, -0.5 * 0.125, ALU.mult)
                  if l == 0 and h == 0:
                      dump("xT", xT, [128, 8, S], BF16)
                      dump("negM", negM, [128, 1])
                      ck('qkt')
                  for qh in range(2):
                      otok = otoks[qh]
                      steps = [(m, kt) for m in range(2) for kt in range(NT)]

                      def emit_scores(i):
                          m, kt = steps[i]
                          sb_i = (i % 2) * 2
                          for j_ in range(2):
                              P.matmul(ps[:, (sb_i + j_) * 512:(sb_i + j_ + 1) * 512],
                                       KT[:, kt * 128:(kt + 1) * 128],
                                       QTz[:, m, qh * 1024 + j_ * 512:qh * 1024 + (j_ + 1) * 512],
                                       start=True, stop=True)

                      emit_scores(0)
                      for i, (m, kt) in enumerate(steps):
                          if i + 1 < len(steps):
                              emit_scores(i + 1)
                          if i == 8 and fin_pending:
                              fin_pending.pop(0)()
                          sb_i = (i % 2) * 2
                          pss = ps[:, sb_i * 512:(sb_i + 2) * 512]
                          pt = PT[i % 3]
                          P.act(pt, pss, AF.Exp, bias=negM, scale=0.125)
                          for qi in range(8):
                              P.matmul(pso[qi], pt[:, qi * 128:(qi + 1) * 128], Vaug[:, kt, 0:129],
                                       start=(kt == 0 and qi % 3 == 0), stop=(kt == NT - 1), skip_group_check=True)
                          if kt != NT - 1:
                              continue
                          if m == 0:
                              for gi, (q0, n_) in enumerate(grp):
                                  P.recip(rz[:, q0:q0 + n_], pso_grp(gi, 128, 129).rearrange("p a b -> p (a b)"))
                                  P.tt(O1n[:, q0:q0 + n_, :], pso_grp(gi, 0, 128),
                                       rz[:, q0:q0 + n_].unsqueeze(2).broadcast_to([128, n_, 128]), ALU.mult)
                          else:
                              for gi, (q0, n_) in enumerate(grp):
                                  P.recip(rz[:, q0:q0 + n_], pso_grp(gi, 128, 129).rearrange("p a b -> p (a b)"))
                              P.ts(rz, rz, neglam, ALU.mult)
                              for gi, (q0, n_) in enumerate(grp):
                                  P.tt(O2t[:, q0:q0 + n_, :], pso_grp(gi, 0, 128),
                                       rz[:, q0:q0 + n_].unsqueeze(2).broadcast_to([128, n_, 128]), ALU.mult)
                              P.tt(O1n, O1n, O2t, ALU.add)
                              P.tt(O2t, O1n, O1n, ALU.mult, eng="pool")
                              P.reduce(ss8, O2t, ALU.add)
                              rstd_from(rstd8, ss8, 128.0, 8)
                              P.tt(O1n, O1n, rstd8.unsqueeze(2).broadcast_to([128, 8, 128]), ALU.mult)
                              P.tt(otok, O1n, gdiff.unsqueeze(1).broadcast_to([128, 8, 128]), ALU.mult, eng="pool")
                              for qi in range(8):
                                  dst_ = ocat[:, h, qh * 1024 + qi * 128:qh * 1024 + (qi + 1) * 128]
                                  P.add("sp", lambda e, dst_=dst_, src_=otok[:, qi, :]: e.dma_start(
                                      out=dst_, in_=src_, transpose=True),
                                      reads=[otok[:, qi, :]], writes=[dst_], dma=True)
              while fin_pending:
                  fin_pending.pop(0)()
              A.release(mixer_mark)
              if l == 0:
                  dump("odiff", ocat[:, 0:4, :], [128, 4, S], BF16)
              ck('att')

              for k in range(8):
                  P.dma("pool", Wout[:, k, :], wout_d[l, k * 128:(k + 1) * 128, :])
              P.dma("sp", lnbuf[:, 0, :], ln_d["ln1_g"][l].partition_broadcast(128))
              P.dma("sp", lnbuf[:, 1, :], ln_d["ln1_b"][l].partition_broadcast(128))
              Wf = A.alloc([8, 256], BF16)
              uT = A.alloc([2, S], BF16)
              wbd = A.alloc([2, 128], F32)
              Wcs = A.alloc([2, 256], BF16)
              ucs = A.alloc([NT, 512], BF16)
              NDB = 4
              dbuf = [A.alloc([8, 512], BF16) for _ in range(NDB)]
              P.dma("pool", Wf, win(l)[:, :, C_FU:C_FU + 256])
              P.memset(wbd, 0.0)
              for g_ in range(4):
                  c_, gl = g_ // 2, g_ % 2
                  P.dma("sp", wbd[gl * 64:(gl + 1) * 64, c_, gl * 64:(gl + 1) * 64], fw_d[l, g_])
              for c_ in range(2):
                  pb = bank(4, 256)
                  P.matmul(pb[:, 0:128], ccbd, wbd[:, c_, :], start=True, stop=True)
                  P.matmul(pb[:, 128:256], scbd, wbd[:, c_, :], start=True, stop=True)
                  P.copy(Wcs[:, c_, :], pb, eng="dve")
              for tb in range(4):
                  for c_ in range(2):
                      pb = bank((tb * 2 + c_) % 4)
                      for k in range(8):
                          P.matmul(pb, Wf[:, k, c_ * 128:(c_ + 1) * 128], xT[:, k, tb * 512:(tb + 1) * 512],
                                   start=(k == 0), stop=(k == 7))
                      P.copy(uT[:, c_, tb * 512:(tb + 1) * 512], pb, eng=("act" if c_ else "dve"))
              for t in range(NT):
                  pb = bank(4 + t % 2)
                  for c_ in range(2):
                      P.matmul(pb[:, c_ * 256:(c_ + 1) * 256], uT[:, c_, t * 128:(t + 1) * 128], Wcs[:, c_, :],
                               start=True, stop=True)
                  P.copy(ucs[:, t, :], pb, eng=("act" if t % 2 else "dve"))
              dft_v = [dftc_d.rearrange("(t p) k -> p t k", p=128), dfts_d.rearrange("(t p) k -> p t k", p=128)]
              di = 0
              for kb in range(4):
                  pbs = [bank(0 + (kb % 2) * 2), bank(1 + (kb % 2) * 2)]
                  first = True
                  for which in range(2):
                      for tg in range(2):
                          db = dbuf[di % NDB]
                          di += 1
                          P.dma("sp", db, dft_v[which][:, tg * 8:(tg + 1) * 8, kb * 512:(kb + 1) * 512])
                          for tt_ in range(8):
                              t = tg * 8 + tt_
                              last = (which == 1 and t == NT - 1)
                              for c_ in range(2):
                                  P.matmul(pbs[c_], ucs[:, t, c_ * 256 + which * 128:c_ * 256 + (which + 1) * 128],
                                           db[:, tt_, :], start=first, stop=last)
                              first = False
                  for c_ in range(2):
                      P.copy(ocat[:, 4 + c_, kb * 512:(kb + 1) * 512], pbs[c_], eng=("act" if c_ else "dve"))
              A.release(mixer_mark)
              if l == 0:
                  dump("ofour", ocat[:, 4:6, :], [128, 2, S], BF16)
              ck('four')

              gqk = A.alloc([2, S], F32)
              gzT = A.alloc([S], BF16)
              gv = A.alloc([NT, 256], BF16)
              gate = A.alloc([NT, 256], BF16)
              w2pad = A.alloc([2, 128], BF16)
              qt = [A.alloc([S], BF16) for _ in range(2)]
              kt_ = [A.alloc([S], BF16) for _ in range(2)]
              Sbf = [A.alloc([NT, 256], BF16) for _ in range(2)]
              gla_mark = A.mark()
              Wgf = A.alloc([8, 288], BF16)
              Wgt = A.alloc([8, 512], BF16)
              P.dma("pool", Wgf[:, :, 0:256], win(l)[:, :, C_GQ:C_GQ + 256])
              P.dma("pool", Wgf[:, :, 256:288], win(l)[:, :, C_GZ:C_GZ + 32])
              P.dma("pool", Wgt, win(l)[:, :, C_GV:C_GV + 512])
              P.memset(w2pad, 0.0)
              for d_ in range(2):
                  P.dma("pool", w2pad[d_ * 16:(d_ + 1) * 16, d_, :], w2_d[l, d_])
              for tb in range(4):
                  for c_ in range(2):
                      pb = bank((tb * 3 + c_) % 4)
                      for k in range(8):
                          P.matmul(pb, Wgf[:, k, c_ * 128:(c_ + 1) * 128], xT[:, k, tb * 512:(tb + 1) * 512],
                                   start=(k == 0), stop=(k == 7))
                      P.copy(gqk[:, c_, tb * 512:(tb + 1) * 512], pb, eng=("act" if c_ else "dve"))
                  pb = bank((tb * 3 + 2) % 4)
                  for k in range(8):
                      P.matmul(pb[0:32, :], Wgf[:, k, 256:288], xT[:, k, tb * 512:(tb + 1) * 512],
                               start=(k == 0), stop=(k == 7))
                  P.copy(gzT[0:32, tb * 512:(tb + 1) * 512], pb[0:32, :], eng="dve")
              for t in range(NT):
                  pb = bank(4 + t % 2)
                  for k in range(8):
                      P.matmul(pb, xT[:, k, t * 128:(t + 1) * 128], Wgt[:, k, :], start=(k == 0), stop=(k == 7))
                  P.copy(gv[:, t, :], pb[:, 0:256], eng="act")
                  P.act(gate[:, t, :], pb[:, 256:512], AF.Silu)
              A.release(gla_mark)
              Bc = A.alloc([S], F32)
              Ec = A.alloc([S], F32)
              kdec_tok = A.alloc([NT, 128], BF16)
              kdT = [A.alloc([128], BF16) for _ in range(2)]
              Srot = [A.alloc([256], F32) for _ in range(2)]
              for d_ in range(2):
                  for tb in range(4):
                      pb = bank(tb % 4)
                      P.matmul(pb, w2pad[0:32, d_, :], gzT[0:32, tb * 512:(tb + 1) * 512], start=True, stop=True)
                      P.act(Ec[:, tb * 512:(tb + 1) * 512], pb, AF.Exp, bias=negb2[:, d_:d_ + 1], scale=-1.0)
                  P.act(Ec, Ec, AF.Ln, bias=1.0)
                  for n in range(NT):
                      o_ = Bc[:, n * 128:(n + 1) * 128]
                      i_ = Ec[:, n * 128:(n + 1) * 128]
                      if d_ == 1:
                          o_ = o_[:, ::-1]
                          i_ = i_[:, ::-1]
                      P.add("dve", lambda e, o_=o_, i_=i_: e.tensor_tensor_scan(
                          out=o_, data0=ones_f, data1=i_, initial=0.0, op0=ALU.mult, op1=ALU.add),
                          reads=[ones_f, Ec[:, n * 128:(n + 1) * 128]], writes=[Bc[:, n * 128:(n + 1) * 128]])
                  P.act(Ec, Bc, AF.Exp, scale=-1.0 / 16.0)
                  P.act(Bc, Bc, AF.Exp, scale=1.0 / 16.0)
                  P.stt(qt[d_], gqk[:, 0, :], 32.0 ** -0.5, Ec, ALU.mult, ALU.mult)
                  P.tt(kt_[d_], gqk[:, 1, :], Bc, ALU.mult)
                  elast = [Ec[:, n * 128 + (127 if d_ == 0 else 0):n * 128 + (127 if d_ == 0 else 0) + 1]
                           for n in range(NT)]
                  pst = bankbf(7)
                  for n in range(NT):
                      kd = kdT[n % 2]
                      P.stt(kd, gqk[:, 1, n * 128:(n + 1) * 128], elast[n], Bc[:, n * 128:(n + 1) * 128],
                            ALU.mult, ALU.mult)
                      P.transpose(pst[:, (n % 8) * 128:(n % 8 + 1) * 128], kd, ident_b)
                      if n % 8 == 7:
                          P.copy(kdec_tok[:, n - 7:n + 1, :], pst.rearrange("p (a b) -> p a b", a=8), eng="act")
                  order = list(range(NT)) if d_ == 0 else list(range(NT - 1, -1, -1))
                  prev = Srot[0]
                  P.memset(prev, 0.0)
                  P.memset(Sbf[d_][:, order[0], :], 0.0)
                  for i_, n in enumerate(order[:-1]):
                      pb = bank(4 + i_ % 2, 256)
                      P.matmul(pb, kdec_tok[:, n, :], gv[:, n, :], start=True, stop=True)
                      cur = Srot[(i_ + 1) % 2]
                      P.stt(cur, prev, elast[n], pb, ALU.mult, ALU.add)
                      P.tt(Sbf[d_][:, order[i_ + 1], :], cur, bdmask, ALU.mult, eng="pool")
                      prev = cur
              A.release(gla_mark)
              Qbd = [A.alloc([4, 128], BF16) for _ in range(4)]
              Asb = [A.alloc([4, 128], BF16) for _ in range(4)]
              ogf = [A.alloc([256], F32) for _ in range(2)]
              ogb = [A.alloc([256], BF16) for _ in range(3)]
              junk = A.alloc([64], F32)
              ss4s = [small[:, 528:532], small[:, 560:564]]
              rs4s = [small[:, 532:536], small[:, 564:568]]
              P.tt(gate.rearrange("p t (h v) -> p (t h) v", h=4), gate.rearrange("p t (h v) -> p (t h) v", h=4),
                   ggla.unsqueeze(1).broadcast_to([128, NT * 4, 64]), ALU.mult, eng="pool")

              def gla_m(n):
                  po = bank(4 + n % 2, 256)
                  P.matmul(po, qt[0][:, n * 128:(n + 1) * 128], Sbf[0][:, n, :], start=True, stop=False)
                  P.matmul(po, qt[1][:, n * 128:(n + 1) * 128], Sbf[1][:, n, :], start=False, stop=False)
                  for d_ in range(2):
                      ci = 2 * n + d_
                      qb = Qbd[ci % 4]
                      P.tt(qb, qt[d_][:, n * 128:(n + 1) * 128].unsqueeze(1).broadcast_to([128, 4, 128]),
                           hmask4.unsqueeze(2).broadcast_to([128, 4, 128]), ALU.mult, eng="pool")
                      P.matmul(bank(ci % 4), kt_[d_][:, n * 128:(n + 1) * 128], qb.rearrange("p a b -> p (a b)"),
                               start=True, stop=True)
                  for d_ in range(2):
                      ci = 2 * n + d_
                      asb = Asb[ci % 4]
                      P.tt(asb, bank(ci % 4).rearrange("p (a b) -> p a b", a=4),
                           tri[d_].unsqueeze(1).broadcast_to([128, 4, 128]), ALU.mult)
                      for h in range(4):
                          P.matmul(po[:, h * 64:(h + 1) * 64], asb[:, h, :], gv[:, n, h * 64:(h + 1) * 64],
                                   start=False, stop=(d_ == 1 and h == 3))

              def gla_a(n):
                  po = bank(4 + n % 2, 256)
                  ss4 = ss4s[n % 2]
                  rs4 = rs4s[n % 2]
                  for h in range(4):
                      P.act(junk, po[:, h * 64:(h + 1) * 64], AF.Square, accum_out=ss4[:, h:h + 1])
                  P.act(rs4, ss4, AF.Ln, bias=epsc, scale=1.0 / 64.0)
                  P.act(rs4, rs4, AF.Exp, scale=-0.5)
                  of = ogf[n % 2]
                  of3 = of.rearrange("p (h v) -> p h v", h=4)
                  P.tt(of3, po.rearrange("p (h v) -> p h v", h=4), rs4.unsqueeze(2).broadcast_to([128, 4, 64]),
                       ALU.mult)
                  ob = ogb[n % 3]
                  P.tt(ob, of, gate[:, n, :], ALU.mult)
                  for c_ in range(2):
                      dst_ = ocat[:, 6 + c_, n * 128:(n + 1) * 128]
                      P.add("sp", lambda e, dst_=dst_, src_=ob[:, c_ * 128:(c_ + 1) * 128]: e.dma_start(
                          out=dst_, in_=src_, transpose=True),
                          reads=[ob[:, c_ * 128:(c_ + 1) * 128]], writes=[dst_], dma=True)

              for s_ in range(NT + 1):
                  if s_ < NT:
                      gla_m(s_)
                  if s_ >= 1:
                      gla_a(s_ - 1)
              A.release(mixer_mark)
              if l == 0:
                  dump("ogla", ocat[:, 6:8, :], [128, 2, S], BF16)
              ck('gla')

              A.n = ARENA - XTB
              assert A.top <= A.n
              post_mark = A.mark()
              ytmp = [A.alloc([D], F32) for _ in range(2)]
              xbs = [A.alloc([D], BF16) for _ in range(2)]
              for t in range(NT):
                  P.dma("sp", x_tok[:, t, :], xres_d[t * 128:(t + 1) * 128, :])
              def mm1_(t):
                  b0 = (t % 3) * 2
                  p2 = ps[:, b0 * 512:(b0 + 2) * 512]
                  for hf in range(2):
                      for e_ in range(8):
                          P.matmul(p2[:, hf * 512:(hf + 1) * 512], ocat[:, e_, t * 128:(t + 1) * 128],
                                   Wout[:, e_, hf * 512:(hf + 1) * 512], start=(e_ == 0), stop=(e_ == 7))

              def a1_(t):
                  b0 = (t % 3) * 2
                  ln_a(ps[:, b0 * 512:(b0 + 2) * 512], x_tok[:, t, :], lnbuf, ytmp[t % 2], t % 2)

              def b1_(t):
                  ln_b(lnbuf, x_tok[:, t, :], ytmp[t % 2])
                  make_xT(x_tok[:, t, :], t, xbs[t % 2])

              for s_ in range(NT + 2):
                  if s_ < NT:
                      mm1_(s_)
                  if 0 <= s_ - 1 < NT:
                      a1_(s_ - 1)
                  if 0 <= s_ - 2 < NT:
                      b1_(s_ - 2)
              A.release(base_mark)
              if l == 0:
                  dump("x1", x_tok, [128, NT, D])
              ck('ln1')

              GF = NFC // 2
              Wdg = A.alloc([GF, D], BF16)
              hT = A.alloc([GF, S], BF16)
              NWB = 3
              wgu = [A.alloc([2, 8, 128], BF16) for _ in range(NWB)]
              sg = [A.alloc([512], BF16) for _ in range(2)]
              ytmp = [A.alloc([D], F32) for _ in range(1)]
              xbs = [A.alloc([D], BF16) for _ in range(2)]
              wdv = wd_d[l].rearrange("(f p) n -> p f n", p=128)
              wgv = wg_d[l].rearrange("(k p) n -> p k n", p=128)
              wuv = wu_d[l].rearrange("(k p) n -> p k n", p=128)
              P.dma("sp", lnbuf[:, 0, :], ln_d["ln2_g"][l].partition_broadcast(128))
              P.dma("sp", lnbuf[:, 1, :], ln_d["ln2_b"][l].partition_broadcast(128))
              wi = 0
              ui = 0
              for g_ in range(2):
                  for fi in range(GF):
                      f = g_ * GF + fi
                      wb = wgu[wi % NWB]
                      wi += 1
                      P.dma("pool", wb[:, 0, :, :], wgv[:, :, f * 128:(f + 1) * 128])
                      P.dma("pool", wb[:, 1, :, :], wuv[:, :, f * 128:(f + 1) * 128])
                      P.dma("pool", Wdg[:, fi, :], wdv[:, f, :])
                      for tb in range(4):
                          pg = bank((ui % 2) * 2)
                          pu = bank((ui % 2) * 2 + 1)
                          ui += 1
                          for k in range(8):
                              P.matmul(pg, wb[:, 0, k, :], xT[:, k, tb * 512:(tb + 1) * 512],
                                       start=(k == 0), stop=(k == 7))
                          for k in range(8):
                              P.matmul(pu, wb[:, 1, k, :], xT[:, k, tb * 512:(tb + 1) * 512],
                                       start=(k == 0), stop=(k == 7))
                          s_ = sg[ui % 2]
                          P.act(s_, pg, AF.Silu)
                          P.tt(hT[:, fi, tb * 512:(tb + 1) * 512], pu, s_, ALU.mult)
                  ytmps = [ytmp[0], wgu[0].rearrange("p a b c -> p (a b c)").bitcast(F32)]

                  def mm2_(t):
                      b0 = (t % 3) * 2
                      p2 = ps[:, b0 * 512:(b0 + 2) * 512]
                      for hf in range(2):
                          for fi in range(GF):
                              P.matmul(p2[:, hf * 512:(hf + 1) * 512], hT[:, fi, t * 128:(t + 1) * 128],
                                       Wdg[:, fi, hf * 512:(hf + 1) * 512], start=(fi == 0), stop=(fi == GF - 1))

                  def a2_(t):
                      b0 = (t % 3) * 2
                      p2 = ps[:, b0 * 512:(b0 + 2) * 512]
                      if g_ == 0:
                          P.stt(x_tok[:, t, :], x_tok[:, t, :], ALPHA, p2, ALU.mult, ALU.add)
                      else:
                          ln_a(p2, x_tok[:, t, :], lnbuf, ytmps[t % 2], t % 2, alpha=1.0)

                  def b2_(t):
                      if g_ == 0:
                          return
                      ln_b(lnbuf, x_tok[:, t, :], ytmps[t % 2])
                      if l == n_layers - 1:
                          P.dma("sp", y_d[t * 128:(t + 1) * 128, :], x_tok[:, t, :])
                      else:
                          P.dma("sp", xs_d[t * 128:(t + 1) * 128, :], x_tok[:, t, :])
                          make_xT(x_tok[:, t, :], t, xbs[t % 2])

                  for s_ in range(NT + 2):
                      if s_ < NT:
                          mm2_(s_)
                      if 0 <= s_ - 1 < NT:
                          a2_(s_ - 1)
                      if 0 <= s_ - 2 < NT:
                          b2_(s_ - 2)
              A.release(base_mark)
              A.n = ARENA

        except _Stop:
            pass
        P.emit()
    _CACHE['P'] = P
    return nc, dbg_outs


def kernel(**inputs):
    if "c" not in _CACHE:
        _CACHE["c"] = host_consts()
    cf, cb, dc, ds = _CACHE["c"]
    nc, _ = build()
    x = np.ascontiguousarray(inputs["x"], dtype=np.float32)
    common = {
        "w_in": np.ascontiguousarray(inputs["w_in"], dtype=np.float32),
        "diff_lambda": np.ascontiguousarray(inputs["diff_lambda"], dtype=np.float32).reshape(L, 256),
        "diff_norm_g": np.ascontiguousarray(inputs["diff_norm_g"], dtype=np.float32),
        "fourier_w": np.ascontiguousarray(inputs["fourier_w"], dtype=np.float32),
        "gla_gate_w2": np.ascontiguousarray(inputs["gla_gate_w2"], dtype=np.float32),
        "gla_gate_b2": np.ascontiguousarray(inputs["gla_gate_b2"], dtype=np.float32),
        "gla_norm_g": np.ascontiguousarray(inputs["gla_norm_g"], dtype=np.float32),
        "w_out": np.ascontiguousarray(inputs["w_out"], dtype=np.float32),
        "ln1_g": np.ascontiguousarray(inputs["ln1_g"], dtype=np.float32),
        "ln1_b": np.ascontiguousarray(inputs["ln1_b"], dtype=np.float32),
        "ln2_g": np.ascontiguousarray(inputs["ln2_g"], dtype=np.float32),
        "ln2_b": np.ascontiguousarray(inputs["ln2_b"], dtype=np.float32),
        "ffn_w_gate": np.ascontiguousarray(inputs["ffn_w_gate"], dtype=np.float32),
        "ffn_w_up": np.ascontiguousarray(inputs["ffn_w_up"], dtype=np.float32),
        "ffn_w_down": np.ascontiguousarray(inputs["ffn_w_down"], dtype=np.float32),
        "c_f32": cf, "c_bf": cb, "dft_c": dc, "dft_s": ds,
    }
    in_maps = [dict(common, x=x[b]) for b in range(8)]
    res = run_bass_kernel_spmd(nc, in_maps, core_ids=list(range(8)))
    return np.stack([np.asarray(r["y"], dtype=np.float32) for r in res.results], axis=0)
```

```python
import numpy as np
import concourse.bass as bass
import concourse.mybir as mybir

F32 = mybir.dt.float32
BF16 = mybir.dt.bfloat16
ALU = mybir.AluOpType
AF = mybir.ActivationFunctionType
AX = mybir.AxisListType

_DT_SIZE = {F32: 4, BF16: 2, mybir.dt.float32r: 4, mybir.dt.int32: 4,
            mybir.dt.uint32: 4, mybir.dt.float16: 2, mybir.dt.uint16: 2,
            mybir.dt.int16: 2, mybir.dt.uint8: 1, mybir.dt.int8: 1}

ENGS = ("pe", "act", "dve", "pool", "sp")
N_DMA_SEMS = 24
TINY_BYTES = 256
SAME_ENGINE_SYNC = False


def ap_box(ap):
    t = ap.tensor
    name = t.name
    esz = _DT_SIZE[ap.dtype]
    dims = list(ap.ap)
    off = ap.offset
    space = str(ap.space)
    if "DRAM" in space.upper() or "HBM" in space.upper():
        lo = off
        hi = off
        for (st, n) in dims:
            if st >= 0:
                hi += st * (n - 1)
            else:
                lo += st * (n - 1)
        return (name, 0, 1, lo * esz, (hi + 1) * esz)
    pstep, pcnt = dims[0]
    if pstep == 0:
        pstep = 1 << 40
    p0 = off // pstep if pstep < (1 << 40) else 0
    f = off - p0 * pstep if pstep < (1 << 40) else off
    lo = f
    hi = f
    for (st, n) in dims[1:]:
        if st >= 0:
            hi += st * (n - 1)
        else:
            lo += st * (n - 1)
    if "PSUM" in space.upper():
        b0 = (lo * esz) // 2048
        b1 = ((hi + 1) * esz - 1) // 2048
        return (name, 0, 128, b0 * 2048, (b1 + 1) * 2048, True)
    return (name, p0, p0 + pcnt, lo * esz, (hi + 1) * esz)


def _overlap(a, b):
    return a[1] < b[2] and b[1] < a[2] and a[3] < b[4] and b[3] < a[4]


def _contains(a, b):
    return a[1] <= b[1] and a[2] >= b[2] and a[3] <= b[3] and a[4] >= b[4]


class Prog:
    def __init__(self, nc):
        self.nc = nc
        self.ins = []
        self.recs = {}
        self.dma_rr = {e: 0 for e in ENGS}
        self.trace = {}

    def add(self, eng, fn, reads=(), writes=(), dma=False):
        idx = len(self.ins)
        rb = list(dict.fromkeys(ap_box(a) for a in reads))
        wb = list(dict.fromkeys(ap_box(a) for a in writes))
        deps = set()
        tiny_deps = set()
        for b in rb:
            psum = len(b) > 5
            tiny = (not psum) and (b[4] - b[3]) <= TINY_BYTES
            for rec in self.recs.get(b[0], ()):
                if (rec[1] or (psum and rec[3] != eng)) and _overlap(rec[0], b):
                    deps.add(rec[2])
                    if tiny and rec[1]:
                        tiny_deps.add(rec[2])
        for b in wb:
            for rec in self.recs.get(b[0], ()):
                if _overlap(rec[0], b):
                    deps.add(rec[2])
        for b in wb:
            lst = self.recs.setdefault(b[0], [])
            lst[:] = [r for r in lst if not _contains(b, r[0])]
            lst.append([b, True, idx, eng])
        for b in rb:
            lst = self.recs.setdefault(b[0], [])
            found = False
            for r in lst:
                if (not r[1]) and r[3] == eng and r[0] == b and not dma \
                        and not self.ins[r[2]]["dma"]:
                    r[2] = idx
                    found = True
                    break
            if not found:
                lst.append([b, False, idx, eng])
        real = set()
        for d in deps:
            p = self.ins[d]
            if p["eng"] == "pe" and eng == "pe" and not p["dma"] and not dma:
                continue
            if (not SAME_ENGINE_SYNC) and p["eng"] == eng and eng != "pool" and not p["dma"] and not dma \
                    and d not in tiny_deps:
                continue
            real.add(d)
        self.ins.append(dict(eng=eng, fn=fn, deps=real, dma=dma, needed=False,
                             sem=None, val=None))
        return idx

    def emit(self, final_wait_eng="sp"):
        nc = self.nc
        ins = self.ins
        for r in ins:
            for d in r["deps"]:
                ins[d]["needed"] = True
        last_dmas = [i for i, r in enumerate(ins) if r["dma"]]
        import contextlib
        with contextlib.ExitStack() as st:
            esem = {e: st.enter_context(nc.semaphore("s_" + e)) for e in ENGS}
            dsem = {e: [st.enter_context(nc.semaphore("d_%s_%d" % (e, i)))
                        for i in range(N_DMA_SEMS)] for e in ("sp", "act", "pool")}
            cnt = {e: 0 for e in ENGS}
            dcnt = {e: [0] * N_DMA_SEMS for e in dsem}
            drr = {e: 0 for e in dsem}
            prev_use = {}
            for i, r in enumerate(ins):
                e = r["eng"]
                if r["dma"]:
                    s = drr[e]
                    drr[e] = (s + 1) % N_DMA_SEMS
                    if dcnt[e][s] > 0:
                        r["prev"] = (dsem[e][s], dcnt[e][s], ("d", e, s))
                    else:
                        r["prev"] = None
                    dcnt[e][s] += 16
                    r["sem"] = dsem[e][s]
                    r["val"] = dcnt[e][s]
                    r["semkey"] = ("d", e, s)
                elif r["needed"]:
                    cnt[e] += 1
                    r["sem"] = esem[e]
                    r["val"] = cnt[e]
                    r["semkey"] = ("e", e)
            block = st.enter_context(nc.Block())
            per_eng = {e: [i for i, r in enumerate(ins) if r["eng"] == e] for e in ENGS}
            final = {}
            for i in last_dmas:
                r = ins[i]
                final[r["semkey"]] = (r["sem"], max(r["val"], final.get(r["semkey"], (None, 0))[1]))

            def body(ename):
                def run(engobj):
                    waited = {}
                    for i in per_eng[ename]:
                        r = ins[i]
                        need = {}
                        for d in r["deps"]:
                            p = ins[d]
                            k = p["semkey"]
                            if need.get(k, (None, 0))[1] < p["val"]:
                                need[k] = (p["sem"], p["val"])
                        if r["dma"] and r["prev"] is not None:
                            s, v, k = r["prev"]
                            if need.get(k, (None, 0))[1] < v:
                                need[k] = (s, v)
                        for k, (s, v) in need.items():
                            if waited.get(k, 0) < v:
                                engobj.wait_ge(s, v)
                                waited[k] = v
                                self.trace.setdefault(ename, []).append(("w", k, v, i))
                        h = r["fn"](engobj)
                        if r["dma"]:
                            h.then_inc(r["sem"], 16)
                            self.trace.setdefault(ename, []).append(("i", r["semkey"], 16, i))
                        elif r["needed"]:
                            h.then_inc(r["sem"], 1)
                            self.trace.setdefault(ename, []).append(("i", r["semkey"], 1, i))
                    if ename == final_wait_eng:
                        for k, (s, v) in final.items():
                            if waited.get(k, 0) < v:
                                engobj.wait_ge(s, v)
                return run

            block.tensor(body("pe"))
            block.scalar(body("act"))
            block.vector(body("dve"))
            block.gpsimd(body("pool"))
            block.sync(body("sp"))

    def dma(self, eng, out, in_, **kw):
        return self.add(eng, lambda e: e.dma_start(out=out, in_=in_, **kw),
                        reads=[in_], writes=[out], dma=True)

    def matmul(self, out, lhsT, rhs, start=True, stop=True, **kw):
        return self.add("pe", lambda e: e.matmul(out, lhsT, rhs, start=start, stop=stop, **kw),
                        reads=[lhsT, rhs], writes=[out])

    def transpose(self, out, in_, ident):
        return self.add("pe", lambda e: e.transpose(out, in_, ident),
                        reads=[in_, ident], writes=[out])

    def act(self, out, in_, func, bias=None, scale=1.0, accum_out=None, eng="act"):
        reads = [in_]
        writes = [out]
        kw = {}
        if bias is not None:
            kw["bias"] = bias
            if not isinstance(bias, (int, float)):
                reads.append(bias)
        if not isinstance(scale, (int, float)):
            reads.append(scale)
        if accum_out is not None:
            kw["accum_out"] = accum_out
            writes.append(accum_out)
        return self.add(eng, lambda e: e.activation(out=out, in_=in_, func=func, scale=scale, **kw),
                        reads=reads, writes=writes)

    def tt(self, out, in0, in1, op, eng="dve"):
        return self.add(eng, lambda e: e.tensor_tensor(out=out, in0=in0, in1=in1, op=op),
                        reads=[in0, in1], writes=[out])

    def ts(self, out, in0, s1, op0, s2=None, op1=None, eng="dve", accum_out=None):
        reads = [in0]
        if not isinstance(s1, (int, float)):
            reads.append(s1)
        if s2 is not None and not isinstance(s2, (int, float)):
            reads.append(s2)
        kw = {}
        writes = [out]
        if op1 is not None:
            kw["op1"] = op1
        if accum_out is not None:
            kw["accum_out"] = accum_out
            writes.append(accum_out)
        return self.add(eng, lambda e: e.tensor_scalar(out=out, in0=in0, scalar1=s1, scalar2=s2,
                                                       op0=op0, **kw),
                        reads=reads, writes=writes)

    def stt(self, out, in0, scalar, in1, op0, op1, eng="dve"):
        reads = [in0, in1]
        if not isinstance(scalar, (int, float)):
            reads.append(scalar)
        return self.add(eng, lambda e: e.scalar_tensor_tensor(out=out, in0=in0, scalar=scalar,
                                                              in1=in1, op0=op0, op1=op1),
                        reads=reads, writes=[out])

    def copy(self, out, in_, eng="dve"):
        if eng == "act":
            return self.add(eng, lambda e: e.activation(out=out, in_=in_, func=AF.Identity),
                            reads=[in_], writes=[out])
        return self.add(eng, lambda e: e.tensor_copy(out=out, in_=in_), reads=[in_], writes=[out])

    def memset(self, ap, val, eng="dve"):
        return self.add(eng, lambda e: e.memset(ap, val), reads=[], writes=[ap])

    def reduce(self, out, in_, op, axis=AX.X, eng="dve", **kw):
        return self.add(eng, lambda e: e.tensor_reduce(out=out, in_=in_, op=op, axis=axis, **kw),
                        reads=[in_], writes=[out])

    def recip(self, out, in_, eng="dve"):
        return self.add(eng, lambda e: e.reciprocal(out=out, in_=in_), reads=[in_], writes=[out])


def simulate_trace(trace):
    pos = {e: 0 for e in trace}
    sem = {}
    progress = True
    while progress:
        progress = False
        for e, ops in trace.items():
            while pos[e] < len(ops):
                kind, k, v, i = ops[pos[e]]
                if kind == "w":
                    if sem.get(k, 0) >= v:
                        pos[e] += 1
                        progress = True
                    else:
                        break
                else:
                    sem[k] = sem.get(k, 0) + v
                    pos[e] += 1
                    progress = True
    stuck = {e: ops[pos[e]] for e, ops in trace.items() if pos[e] < len(ops)}
    return stuck, sem

import contextlib
import os
_SK = set(os.environ.get('DBGSKIP', '').split(','))
import math
import ml_dtypes
from concourse.bass_utils import run_bass_kernel_spmd

S = 2048
D = 1024
L = 2
INW = 2592
FF = 2816
NT = 16
NFC = FF // 128
ALPHA = float((2 * L) ** 0.25)
EPS = 1e-5
C_DQ, C_DK, C_DV, C_FU, C_GQ, C_GK, C_GV, C_GR, C_GZ = 0, 512, 1024, 1536, 1792, 1920, 2048, 2304, 2560

CF_ID, CF_COS, CF_SIN, CF_HM, CF_CC, CF_SC, CF_ONES, CF_NEGH, CF_N = 0, 128, 640, 1152, 1156, 1284, 1412, 1540, 1548
CB_ID, CB_TRIF, CB_TRIB, CB_BD, CB_ONE, CB_N = 0, 128, 256, 384, 640, 768


def host_consts():
    p = np.arange(128)
    cf = np.zeros((128, CF_N), np.float32)
    cf[:, CF_ID:CF_ID + 128] = np.eye(128, dtype=np.float32)
    inv_freq = (10000.0 ** (-np.arange(0, 64, 2, dtype=np.float32) / 64)).astype(np.float32)
    pos = (np.arange(NT)[None, :] * 128 + p[:, None]).astype(np.float32)
    ang = pos[:, :, None] * inv_freq[None, None, :]
    cf[:, CF_COS:CF_COS + 512] = np.cos(ang).reshape(128, 512)
    cf[:, CF_SIN:CF_SIN + 512] = np.sin(ang).reshape(128, 512)
    cf[:, CF_HM:CF_HM + 4] = (p[:, None] // 32 == np.arange(4)[None, :]).astype(np.float32)
    c = np.arange(64)
    a = 2 * np.pi * np.outer(c, c) / 64.0
    cc = np.cos(a) / 8.0
    sc = np.sin(a) / 8.0
    z = np.zeros((64, 64))
    cf[:, CF_CC:CF_CC + 128] = np.block([[cc, z], [z, cc]])
    cf[:, CF_SC:CF_SC + 128] = np.block([[sc, z], [z, sc]])
    cf[:, CF_ONES:CF_ONES + 128] = 1.0
    cf[:, CF_NEGH:CF_NEGH + 8] = -0.5
    cb = np.zeros((128, CB_N), np.float32)
    cb[:, CB_ID:CB_ID + 128] = np.eye(128)
    cb[:, CB_TRIF:CB_TRIF + 128] = (p[:, None] <= p[None, :])
    cb[:, CB_TRIB:CB_TRIB + 128] = (p[:, None] >= p[None, :])
    cb[:, CB_BD:CB_BD + 256] = (p[:, None] // 32 == np.arange(256)[None, :] // 64)
    cb[:, CB_ONE:CB_ONE + 128] = 1.0
    s = np.arange(S, dtype=np.float64)
    sk = np.outer(s, s) % S
    ang = 2 * np.pi * sk / S
    dc = (np.cos(ang) / math.sqrt(S)).astype(np.float32).astype(ml_dtypes.bfloat16)
    ds = (-np.sin(ang) / math.sqrt(S)).astype(np.float32).astype(ml_dtypes.bfloat16)
    return cf, cb.astype(ml_dtypes.bfloat16), dc, ds


class Arena:
    def __init__(self, t, nbytes):
        self.t = t
        self.n = nbytes
        self.top = 0

    def mark(self):
        return self.top

    def release(self, m):
        self.top = m

    def alloc(self, shape, dt):
        n = 1
        for s_ in shape:
            n *= s_
        b = n * _DT_SIZE[dt]
        off = self.top
        self.top += (b + 63) // 64 * 64
        assert self.top <= self.n, ("arena overflow", self.top, self.n)
        ap = self.t[:, off // 2:(off + b) // 2]
        if dt != BF16:
            ap = ap.bitcast(dt)
        if len(shape) == 2:
            ap = ap.rearrange("p (a b) -> p a b", a=shape[0])
        elif len(shape) == 3:
            ap = ap.rearrange("p (a b c) -> p a b c", a=shape[0], b=shape[1])
        return ap


class Rot:
    def __init__(self, items):
        self.items = items
        self.i = 0

    def next(self):
        r = self.items[self.i % len(self.items)]
        self.i += 1
        return r


_CACHE = {}


class _Stop(Exception):
    pass


def build(n_layers=L, dbg=None, upto=None):
    nc = bass.Bass("TRN2", target_bir_lowering=False)
    dram = lambda name, shape, dt, kind="ExternalInput": nc.dram_tensor(name, shape, dt, kind=kind).ap()
    x_d = dram("x", [S, D], F32)
    w_in_d = dram("w_in", [L, D, INW], F32)
    lam_d = dram("diff_lambda", [L, 256], F32)
    dng_d = dram("diff_norm_g", [L, 128], F32)
    fw_d = dram("fourier_w", [L, 4, 64, 64], F32)
    w2_d = dram("gla_gate_w2", [L, 2, 16, 128], F32)
    b2_d = dram("gla_gate_b2", [L, 2, 128], F32)
    gng_d = dram("gla_norm_g", [L, 64], F32)
    wout_d = dram("w_out", [L, D, D], F32)
    ln_d = {k: dram(k, [L, D], F32) for k in ("ln1_g", "ln1_b", "ln2_g", "ln2_b")}
    wg_d = dram("ffn_w_gate", [L, D, FF], F32)
    wu_d = dram("ffn_w_up", [L, D, FF], F32)
    wd_d = dram("ffn_w_down", [L, FF, D], F32)
    cf_d = dram("c_f32", [128, CF_N], F32)
    cb_d = dram("c_bf", [128, CB_N], BF16)
    dftc_d = dram("dft_c", [S, S], BF16)
    dfts_d = dram("dft_s", [S, S], BF16)
    y_d = dram("y", [S, D], F32, kind="ExternalOutput")
    xs_d = dram("xs_scr", [S, D], F32, kind="Internal")
    dbg_outs = {}

    ARENA = 207 * 1024
    XTB = NT * D * 4
    with contextlib.ExitStack() as st:
        arena_t = st.enter_context(nc.sbuf_tensor("arena", [128, ARENA // 2], BF16))
        ps = st.enter_context(nc.psum_tensor("ps", [128, 4096], F32))
        P = Prog(nc)
        A = Arena(arena_t, ARENA)
        x_tok = arena_t[:, (ARENA - XTB) // 2:ARENA // 2].bitcast(F32).rearrange("p (t d) -> p t d", t=NT)

        def bank(b, n=512, off=0):
            return ps[:, b * 512 + off:b * 512 + off + n]

        def bankbf(b):
            return ps[:, b * 512:(b + 1) * 512].bitcast(BF16)

        def dump(name, ap, shape, dt=F32):
            if dbg is None or name not in dbg:
                return
            d_ = nc.dram_tensor("dbg_" + name, shape, dt, kind="ExternalOutput").ap()
            dbg_outs[name] = d_
            P.dma("sp", d_, ap)

        cf = A.alloc([CF_N], F32)
        cb = A.alloc([CB_N], BF16)
        xT = A.alloc([8, S], BF16)
        lnbuf = A.alloc([2, D], F32)
        small = A.alloc([640], F32)
        P.dma("sp", cf, cf_d)
        P.dma("sp", cb, cb_d)
        ident_f = cf[:, CF_ID:CF_ID + 128]
        ident_b = cb[:, CB_ID:CB_ID + 128]
        cos_t = cf[:, CF_COS:CF_COS + 512].rearrange("p (t i) -> p t i", t=NT)
        sin_t = cf[:, CF_SIN:CF_SIN + 512].rearrange("p (t i) -> p t i", t=NT)
        hmask4 = cf[:, CF_HM:CF_HM + 4]
        ccbd = cf[:, CF_CC:CF_CC + 128]
        scbd = cf[:, CF_SC:CF_SC + 128]
        ones_f = cf[:, CF_ONES:CF_ONES + 128]
        negh8 = cf[:, CF_NEGH:CF_NEGH + 8]
        tri = [cb[:, CB_TRIF:CB_TRIF + 128], cb[:, CB_TRIB:CB_TRIB + 128]]
        bdmask = cb[:, CB_BD:CB_BD + 256]
        lp_bc = small[:, 0:256]
        gdiff = small[:, 256:384]
        ggla = small[:, 384:448]
        negb2 = small[:, 448:450]
        neglam = small[:, 450:451]
        negM = small[:, 451:452]
        sc_tmp = small[:, 452:500]
        nrm = small[:, 500:504]
        epsc = small[:, 504:505]
        base_mark = A.mark()

        win = lambda l: w_in_d[l].rearrange("(k p) n -> p k n", p=128)

        def rstd_from(out, ss, n, width):
            tmp = sc_tmp[:, 40:40 + width]
            P.ts(tmp, ss, 1.0 / n, ALU.mult, EPS, ALU.add)
            P.tt(out, tmp, negh8[:, 0:width], ALU.pow, eng="pool")

        xb_rot = None

        def make_xT(x_tile_f32, t, xb):
            P.copy(xb, x_tile_f32, eng="act")
            pst = bankbf(7)
            for k in range(8):
                P.transpose(pst[:, k * 128:(k + 1) * 128], xb[:, k * 128:(k + 1) * 128], ident_b)
            P.copy(xT[:, :, t * 128:(t + 1) * 128], pst.rearrange("p (k n) -> p k n", k=8), eng="act")

        def ln_a(psum2, xres, gb, ytmp, par, alpha=ALPHA):
            sct = sc_tmp[:, 0:16] if par == 0 else small[:, 540:556]
            P.stt(ytmp, xres, alpha, psum2, ALU.mult, ALU.add)
            stats = sct[:, 0:12]
            mv = sct[:, 12:14]
            for c_ in range(2):
                P.add("dve", lambda e, c_=c_: e.bn_stats(out=stats[:, c_ * 6:(c_ + 1) * 6],
                                                          in_=ytmp[:, c_ * 512:(c_ + 1) * 512]),
                      reads=[ytmp[:, c_ * 512:(c_ + 1) * 512]], writes=[stats[:, c_ * 6:(c_ + 1) * 6]])
            P.add("dve", lambda e: e.bn_aggr(out=mv, in_=stats), reads=[stats], writes=[mv])
            rs = sct[:, 14:15]
            nmr = sct[:, 15:16]
            P.act(rs, mv[:, 1:2], AF.Ln, bias=epsc)
            P.act(rs, rs, AF.Exp, scale=-0.5)
            P.stt(nmr, mv[:, 0:1], -1.0, rs, ALU.mult, ALU.mult)
            P.act(ytmp, ytmp, AF.Identity, bias=nmr, scale=rs)
            P.tt(ytmp, ytmp, gb[:, 0, :], ALU.mult, eng="pool")

        def ln_b(gb, out_tok, ytmp):
            P.tt(out_tok, ytmp, gb[:, 1, :], ALU.add)

        def ck(name):
            if upto == name:
                raise _Stop()

        try:
          for l in range(n_layers):
              lam_init = 0.8 - 0.6 * math.exp(-0.3 * l)
              A.release(base_mark)
              P.dma("sp", lp_bc, lam_d[l].partition_broadcast(128))
              P.dma("sp", gdiff, dng_d[l].partition_broadcast(128))
              P.dma("sp", ggla, gng_d[l].partition_broadcast(128))
              for d_ in range(2):
                  P.dma("sp", negb2[:, d_:d_ + 1], b2_d[l, d_].rearrange("(p o) -> p o", o=1))
              P.ts(negb2, negb2, -1.0, ALU.mult)
              P.ts(gdiff, gdiff, 1.0 - lam_init, ALU.mult)
              P.memset(epsc, EPS)
              pr = sc_tmp[:, 16:18]
              prod = A.alloc([128], F32)
              lp4 = lp_bc.rearrange("p (a d) -> p a d", a=4)
              for i_ in range(2):
                  P.tt(prod[:, 0:64], lp4[:, 2 * i_, :], lp4[:, 2 * i_ + 1, :], ALU.mult)
                  P.reduce(pr[:, i_:i_ + 1], prod[:, 0:64], ALU.add)
              P.act(pr, pr, AF.Exp)
              P.tt(neglam, pr[:, 1:2], pr[:, 0:1], ALU.subtract)
              P.ts(neglam, neglam, -lam_init, ALU.add)
              A.release(base_mark)
              ck('params')

              if l == 0:
                  m_ = A.mark()
                  xin = [A.alloc([D], F32) for _ in range(3)]
                  xbs = [A.alloc([D], BF16) for _ in range(2)]
                  for t in range(NT):
                      xi = xin[t % 3]
                      P.dma("sp", xi, x_d[t * 128:(t + 1) * 128, :])
                      make_xT(xi, t, xbs[t % 2])
                  A.release(m_)
              ck('xT')
              xres_d = x_d if l == 0 else xs_d

              ocat = A.alloc([8, S], BF16)
              Wout = A.alloc([8, D], BF16)
              Wgf = A.alloc([8, 288], BF16)
              Wgt = A.alloc([8, 512], BF16)
              mixer_mark = A.mark()

              Wh = [A.alloc([8, 384], BF16) for _ in range(2)]
              QTz = A.alloc([2, S], BF16)
              KT = A.alloc([S], BF16)
              Vaug = A.alloc([NT, 132], BF16)
              PT = [A.alloc([1024], BF16) for _ in range(3)]
              O1n = A.alloc([8, 128], F32)
              O2t = A.alloc([8, 128], F32)
              otoks = [A.alloc([8, 128], BF16) for _ in range(2)]
              qkr = [A.alloc([256], BF16) for _ in range(3)]
              tmpAs = [A.alloc([128], F32) for _ in range(2)]
              tmpBs = [A.alloc([128], F32) for _ in range(2)]
              sq = A.alloc([256], F32)
              D2 = A.alloc([256], F32)
              sqs = [sq, A.alloc([256], F32)]
              red_all = A.alloc([NT, 4], F32)
              red4 = sc_tmp[:, 20:24]
              gm = sc_tmp[:, 24:26]
              nrm2 = sc_tmp[:, 26:28]
              rz = sc_tmp[:, 28:36]
              ss8 = small[:, 512:520]
              rstd8 = small[:, 520:528]
              P.memset(Vaug[:, :, 128:129], 1.0)
              P.memset(QTz[64:128, 0, :], 0.0)
              P.memset(QTz[0:64, 1, :], 0.0, eng="pool")
              pso = [ps[:, (4 + qi // 3) * 512 + (qi % 3) * 160:(4 + qi // 3) * 512 + (qi % 3) * 160 + 129]
                     for qi in range(8)]
              grp = [(0, 3), (3, 3), (6, 2)]

              def pso_grp(gi, c0, c1):
                  q0, n_ = grp[gi]
                  base_ = (4 + gi) * 512
                  return ps[:, base_:base_ + n_ * 160].rearrange("p (a b) -> p a b", b=160)[:, :, c0:c1]

              def load_wh(h_):
                  for j_, c0 in enumerate((C_DQ, C_DK, C_DV)):
                      P.dma("pool", Wh[h_ % 2][:, :, j_ * 128:(j_ + 1) * 128],
                            win(l)[:, :, c0 + h_ * 128:c0 + (h_ + 1) * 128])

              load_wh(0)
              fin_pending = []
              for h in range(4):
                  wh = Wh[h % 2]
                  P.memset(nrm, 0.0)
                  tr_pending = []
                  for t in range(NT):
                      pb = bank(t % 4, 256)
                      pv_ = bank(4 + t % 2, 128)
                      for k in range(8):
                          P.matmul(pb, xT[:, k, t * 128:(t + 1) * 128], wh[:, k, 0:256], start=(k == 0), stop=(k == 7))
                      for k in range(8):
                          P.matmul(pv_, xT[:, k, t * 128:(t + 1) * 128], wh[:, k, 256:384], start=(k == 0), stop=(k == 7))
                      qk4 = pb.rearrange("p (g h d) -> p g h d", g=4, h=2)
                      t1 = qk4[:, :, 0, :]
                      t2 = qk4[:, :, 1, :]
                      cbt = cos_t[:, t:t + 1, :].broadcast_to([128, 4, 32])
                      sbt = sin_t[:, t:t + 1, :].broadcast_to([128, 4, 32])
                      q_ = qkr[t % 3]
                      q4 = q_.rearrange("p (g h d) -> p g h d", g=4, h=2)
                      ta = tmpAs[t % 2].rearrange("p (g d) -> p g d", g=4)
                      tb_ = tmpBs[t % 2].rearrange("p (g d) -> p g d", g=4)
                      P.tt(ta, t1, cbt, ALU.mult)
                      P.tt(tb_, t2, sbt, ALU.mult)
                      P.tt(q4[:, :, 0, :], ta, tb_, ALU.subtract)
                      P.tt(ta, t2, cbt, ALU.mult)
                      P.tt(tb_, t1, sbt, ALU.mult)
                      P.tt(q4[:, :, 1, :], ta, tb_, ALU.add)
                      P.copy(Vaug[:, t, 0:128], pv_, eng="act")
                      P.tt(sqs[t % 2], q_, q_, ALU.mult, eng="pool")
                      def tr_(t=t, q_=q_, sq_=sqs[t % 2]):
                          P.reduce(red_all[:, t, :], sq_.rearrange("p (g d) -> p g d", g=4), ALU.add)
                          pst = bankbf(7 if t % 2 else 6)
                          P.transpose(pst[:, 0:128], q_[:, 0:128], ident_b)
                          P.transpose(pst[:, 128:256], q_[:, 128:256], ident_b)
                          P.copy(QTz[0:64, 0, t * 128:(t + 1) * 128], pst[0:64, 0:128], eng="act")
                          P.copy(QTz[64:128, 1, t * 128:(t + 1) * 128], pst[64:128, 0:128], eng="act")
                          P.copy(KT[:, t * 128:(t + 1) * 128], pst[:, 128:256], eng="act")
                      tr_pending.append(tr_)
                      if len(tr_pending) > 1:
                          tr_pending.pop(0)()
                  while tr_pending:
                      tr_pending.pop(0)()
                  P.reduce(nrm, red_all.rearrange("p t g -> p g t"), ALU.max)
                  ck('h0proj')
                  if h + 1 < 4:
                      load_wh(h + 1)
                  else:
                      Wf = Wh[0][:, :, 0:256]
                      P.dma("pool", Wf, win(l)[:, :, C_FU:C_FU + 256])
                      P.dma("pool", Wgf[:, :, 0:256], win(l)[:, :, C_GQ:C_GQ + 256])
                      P.dma("pool", Wgf[:, :, 256:288], win(l)[:, :, C_GZ:C_GZ + 32])
                      P.dma("pool", Wgt, win(l)[:, :, C_GV:C_GV + 512])
                      for k in range(8):
                          P.dma("pool", Wout[:, k, :], wout_d[l, k * 128:(k + 1) * 128, :])
                  P.reduce(nrm2, nrm.rearrange("p (a b) -> p a b", a=2), ALU.max)
                  P.ts(D2[:, 0:128], ident_f, nrm2[:, 0:1], ALU.mult)
                  P.ts(D2[:, 128:256], ident_f, nrm2[:, 1:2], ALU.mult)
                  pbm = bank(6, 256)
                  P.matmul(pbm, ones_f, D2, start=True, stop=True)
                  P.reduce(gm, pbm.rearrange("p (a b) -> p a b", a=2), ALU.max)
                  P.tt(negM, gm[:, 0:1], gm[:, 1:2], ALU.add)
                  P.ts(negM, negM, -0.5 * 0.125, ALU.mult)
                  if l == 0 and h == 0:
                      dump("xT", xT, [128, 8, S], BF16)
                      dump("negM", negM, [128, 1])
                      ck('qkt')
                  for qh in range(2):
                      otok = otoks[qh]
                      steps = [(m, kt) for m in range(2) for kt in range(NT)]

                      def emit_scores(i):
                          m, kt = steps[i]
                          sb_i = (i % 2) * 2
                          for j_ in range(2):
                              P.matmul(ps[:, (sb_i + j_) * 512:(sb_i + j_ + 1) * 512],
                                       KT[:, kt * 128:(kt + 1) * 128],
                                       QTz[:, m, qh * 1024 + j_ * 512:qh * 1024 + (j_ + 1) * 512],
                                       start=True, stop=True)

                      emit_scores(0)
                      for i, (m, kt) in enumerate(steps):
                          if i + 1 < len(steps):
                              emit_scores(i + 1)
                          if i == 8 and fin_pending:
                              fin_pending.pop(0)()
                          sb_i = (i % 2) * 2
                          pss = ps[:, sb_i * 512:(sb_i + 2) * 512]
                          pt = PT[i % 3]
                          P.act(pt, pss, AF.Exp, bias=negM, scale=0.125)
                          for qi in range(8):
                              P.matmul(pso[qi], pt[:, qi * 128:(qi + 1) * 128], Vaug[:, kt, 0:129],
                                       start=(kt == 0 and qi % 3 == 0), stop=(kt == NT - 1), skip_group_check=True)
                          if kt != NT - 1:
                              continue
                          if m == 0:
                              for gi, (q0, n_) in enumerate(grp):
                                  P.recip(rz[:, q0:q0 + n_], pso_grp(gi, 128, 129).rearrange("p a b -> p (a b)"))
                                  P.tt(O1n[:, q0:q0 + n_, :], pso_grp(gi, 0, 128),
                                       rz[:, q0:q0 + n_].unsqueeze(2).broadcast_to([128, n_, 128]), ALU.mult)
                          else:
                              for gi, (q0, n_) in enumerate(grp):
                                  P.recip(rz[:, q0:q0 + n_], pso_grp(gi, 128, 129).rearrange("p a b -> p (a b)"))
                              P.ts(rz, rz, neglam, ALU.mult)
                              for gi, (q0, n_) in enumerate(grp):
                                  P.tt(O2t[:, q0:q0 + n_, :], pso_grp(gi, 0, 128),
                                       rz[:, q0:q0 + n_].unsqueeze(2).broadcast_to([128, n_, 128]), ALU.mult)
                              P.tt(O1n, O1n, O2t, ALU.add)
                              P.tt(O2t, O1n, O1n, ALU.mult, eng="pool")
                              P.reduce(ss8, O2t, ALU.add)
                              rstd_from(rstd8, ss8, 128.0, 8)
                              P.tt(O1n, O1n, rstd8.unsqueeze(2).broadcast_to([128, 8, 128]), ALU.mult)
                              P.tt(otok, O1n, gdiff.unsqueeze(1).broadcast_to([128, 8, 128]), ALU.mult, eng="pool")
                              for qi in range(8):
                                  dst_ = ocat[:, h, qh * 1024 + qi * 128:qh * 1024 + (qi + 1) * 128]
                                  P.add("sp", lambda e, dst_=dst_, src_=otok[:, qi, :]: e.dma_start(
                                      out=dst_, in_=src_, transpose=True),
                                      reads=[otok[:, qi, :]], writes=[dst_], dma=True)
              while fin_pending:
                  fin_pending.pop(0)()
              A.release(mixer_mark)
              if l == 0:
                  dump("odiff", ocat[:, 0:4, :], [128, 4, S], BF16)
              ck('att')

              _skip = A.alloc([2 * 8 * 384], BF16)
              uT = A.alloc([2, S], BF16)
              wbd = A.alloc([2, 128], F32)
              Wcs = A.alloc([2, 256], BF16)
              ucs = A.alloc([NT, 512], BF16)
              NDB = 4
              dbuf = [A.alloc([8, 512], BF16) for _ in range(NDB)]
              P.dma("sp", lnbuf[:, 0, :], ln_d["ln1_g"][l].partition_broadcast(128))
              P.dma("sp", lnbuf[:, 1, :], ln_d["ln1_b"][l].partition_broadcast(128))
              P.memset(wbd, 0.0)
              for g_ in range(4):
                  c_, gl = g_ // 2, g_ % 2
                  P.dma("sp", wbd[gl * 64:(gl + 1) * 64, c_, gl * 64:(gl + 1) * 64], fw_d[l, g_])
              for c_ in range(2):
                  pb = bank(4, 256)
                  P.matmul(pb[:, 0:128], ccbd, wbd[:, c_, :], start=True, stop=True)
                  P.matmul(pb[:, 128:256], scbd, wbd[:, c_, :], start=True, stop=True)
                  P.copy(Wcs[:, c_, :], pb, eng="dve")
              for tb in range(4):
                  for c_ in range(2):
                      pb = bank((tb * 2 + c_) % 4)
                      for k in range(8):
                          P.matmul(pb, Wf[:, k, c_ * 128:(c_ + 1) * 128], xT[:, k, tb * 512:(tb + 1) * 512],
                                   start=(k == 0), stop=(k == 7))
                      P.copy(uT[:, c_, tb * 512:(tb + 1) * 512], pb, eng=("act" if c_ else "dve"))
              for t in range(NT):
                  pb = bank(4 + t % 2)
                  for c_ in range(2):
                      P.matmul(pb[:, c_ * 256:(c_ + 1) * 256], uT[:, c_, t * 128:(t + 1) * 128], Wcs[:, c_, :],
                               start=True, stop=True)
                  P.copy(ucs[:, t, :], pb, eng=("act" if t % 2 else "dve"))
              dft_v = [dftc_d.rearrange("(t p) k -> p t k", p=128), dfts_d.rearrange("(t p) k -> p t k", p=128)]
              di = 0
              for kb in range(4):
                  pbs = [bank(0 + (kb % 2) * 2), bank(1 + (kb % 2) * 2)]
                  first = True
                  for which in range(2):
                      for tg in range(2):
                          db = dbuf[di % NDB]
                          di += 1
                          P.dma("sp", db, dft_v[which][:, tg * 8:(tg + 1) * 8, kb * 512:(kb + 1) * 512])
                          for tt_ in range(8):
                              t = tg * 8 + tt_
                              last = (which == 1 and t == NT - 1)
                              for c_ in range(2):
                                  P.matmul(pbs[c_], ucs[:, t, c_ * 256 + which * 128:c_ * 256 + (which + 1) * 128],
                                           db[:, tt_, :], start=first, stop=last)
                              first = False
                  for c_ in range(2):
                      P.copy(ocat[:, 4 + c_, kb * 512:(kb + 1) * 512], pbs[c_], eng=("act" if c_ else "dve"))
              A.release(mixer_mark)
              if l == 0:
                  dump("ofour", ocat[:, 4:6, :], [128, 2, S], BF16)
              ck('four')

              gqk = A.alloc([2, S], F32)
              gzT = A.alloc([S], BF16)
              gv = A.alloc([NT, 256], BF16)
              gate = A.alloc([NT, 256], BF16)
              w2pad = A.alloc([2, 128], BF16)
              qt = [A.alloc([S], BF16) for _ in range(2)]
              kt_ = [A.alloc([S], BF16) for _ in range(2)]
              Sbf = [A.alloc([NT, 256], BF16) for _ in range(2)]
              gla_mark = A.mark()
              P.memset(w2pad, 0.0)
              for d_ in range(2):
                  P.dma("pool", w2pad[d_ * 16:(d_ + 1) * 16, d_, :], w2_d[l, d_])
              for tb in range(4):
                  for c_ in range(2):
                      pb = bank((tb * 3 + c_) % 4)
                      for k in range(8):
                          P.matmul(pb, Wgf[:, k, c_ * 128:(c_ + 1) * 128], xT[:, k, tb * 512:(tb + 1) * 512],
                                   start=(k == 0), stop=(k == 7))
                      P.copy(gqk[:, c_, tb * 512:(tb + 1) * 512], pb, eng=("act" if c_ else "dve"))
                  pb = bank((tb * 3 + 2) % 4)
                  for k in range(8):
                      P.matmul(pb[0:32, :], Wgf[:, k, 256:288], xT[:, k, tb * 512:(tb + 1) * 512],
                               start=(k == 0), stop=(k == 7))
                  P.copy(gzT[0:32, tb * 512:(tb + 1) * 512], pb[0:32, :], eng="dve")
              for t in range(NT):
                  pb = bank(4 + t % 2)
                  for k in range(8):
                      P.matmul(pb, xT[:, k, t * 128:(t + 1) * 128], Wgt[:, k, :], start=(k == 0), stop=(k == 7))
                  P.copy(gv[:, t, :], pb[:, 0:256], eng="act")
                  P.act(gate[:, t, :], pb[:, 256:512], AF.Silu)
              A.release(gla_mark)
              Bc = A.alloc([S], F32)
              Ec = A.alloc([S], F32)
              kdec_tok = A.alloc([NT, 128], BF16)
              kdT = [A.alloc([128], BF16) for _ in range(2)]
              Srot = [A.alloc([256], F32) for _ in range(2)]
              for d_ in range(2):
                  for tb in range(4):
                      pb = bank(tb % 4)
                      P.matmul(pb, w2pad[0:32, d_, :], gzT[0:32, tb * 512:(tb + 1) * 512], start=True, stop=True)
                      P.act(Ec[:, tb * 512:(tb + 1) * 512], pb, AF.Exp, bias=negb2[:, d_:d_ + 1], scale=-1.0)
                  P.act(Ec, Ec, AF.Ln, bias=1.0)
                  for n in range(NT):
                      o_ = Bc[:, n * 128:(n + 1) * 128]
                      i_ = Ec[:, n * 128:(n + 1) * 128]
                      if d_ == 1:
                          o_ = o_[:, ::-1]
                          i_ = i_[:, ::-1]
                      P.add("dve", lambda e, o_=o_, i_=i_: e.tensor_tensor_scan(
                          out=o_, data0=ones_f, data1=i_, initial=0.0, op0=ALU.mult, op1=ALU.add),
                          reads=[ones_f, Ec[:, n * 128:(n + 1) * 128]], writes=[Bc[:, n * 128:(n + 1) * 128]])
                  P.act(Ec, Bc, AF.Exp, scale=-1.0 / 16.0)
                  P.act(Bc, Bc, AF.Exp, scale=1.0 / 16.0)
                  P.stt(qt[d_], gqk[:, 0, :], 32.0 ** -0.5, Ec, ALU.mult, ALU.mult)
                  P.tt(kt_[d_], gqk[:, 1, :], Bc, ALU.mult)
                  elast = [Ec[:, n * 128 + (127 if d_ == 0 else 0):n * 128 + (127 if d_ == 0 else 0) + 1]
                           for n in range(NT)]
                  pst = bankbf(7)
                  for n in range(NT):
                      kd = kdT[n % 2]
                      P.stt(kd, gqk[:, 1, n * 128:(n + 1) * 128], elast[n], Bc[:, n * 128:(n + 1) * 128],
                            ALU.mult, ALU.mult)
                      P.transpose(pst[:, (n % 8) * 128:(n % 8 + 1) * 128], kd, ident_b)
                      if n % 8 == 7:
                          P.copy(kdec_tok[:, n - 7:n + 1, :], pst.rearrange("p (a b) -> p a b", a=8), eng="act")
                  order = list(range(NT)) if d_ == 0 else list(range(NT - 1, -1, -1))
                  prev = Srot[0]
                  P.memset(prev, 0.0)
                  P.memset(Sbf[d_][:, order[0], :], 0.0)
                  for i_, n in enumerate(order[:-1]):
                      pb = bank(4 + i_ % 2, 256)
                      P.matmul(pb, kdec_tok[:, n, :], gv[:, n, :], start=True, stop=True)
                      cur = Srot[(i_ + 1) % 2]
                      P.stt(cur, prev, elast[n], pb, ALU.mult, ALU.add)
                      P.tt(Sbf[d_][:, order[i_ + 1], :], cur, bdmask, ALU.mult, eng="pool")
                      prev = cur
              A.release(gla_mark)
              Qbd = [A.alloc([4, 128], BF16) for _ in range(4)]
              Asb = [A.alloc([4, 128], BF16) for _ in range(4)]
              ogf = [A.alloc([256], F32) for _ in range(2)]
              ogb = [A.alloc([256], BF16) for _ in range(3)]
              junk = A.alloc([64], F32)
              ss4s = [small[:, 528:532], small[:, 560:564]]
              rs4s = [small[:, 532:536], small[:, 564:568]]
              P.tt(gate.rearrange("p t (h v) -> p (t h) v", h=4), gate.rearrange("p t (h v) -> p (t h) v", h=4),
                   ggla.unsqueeze(1).broadcast_to([128, NT * 4, 64]), ALU.mult, eng="pool")

              def gla_m(n):
                  po = bank(4 + n % 2, 256)
                  P.matmul(po, qt[0][:, n * 128:(n + 1) * 128], Sbf[0][:, n, :], start=True, stop=False)
                  P.matmul(po, qt[1][:, n * 128:(n + 1) * 128], Sbf[1][:, n, :], start=False, stop=False)
                  for d_ in range(2):
                      ci = 2 * n + d_
                      qb = Qbd[ci % 4]
                      P.tt(qb, qt[d_][:, n * 128:(n + 1) * 128].unsqueeze(1).broadcast_to([128, 4, 128]),
                           hmask4.unsqueeze(2).broadcast_to([128, 4, 128]), ALU.mult, eng="pool")
                      P.matmul(bank(ci % 4), kt_[d_][:, n * 128:(n + 1) * 128], qb.rearrange("p a b -> p (a b)"),
                               start=True, stop=True)
                  for d_ in range(2):
                      ci = 2 * n + d_
                      asb = Asb[ci % 4]
                      P.tt(asb, bank(ci % 4).rearrange("p (a b) -> p a b", a=4),
                           tri[d_].unsqueeze(1).broadcast_to([128, 4, 128]), ALU.mult)
                      for h in range(4):
                          P.matmul(po[:, h * 64:(h + 1) * 64], asb[:, h, :], gv[:, n, h * 64:(h + 1) * 64],
                                   start=False, stop=(d_ == 1 and h == 3))

              def gla_a(n):
                  po = bank(4 + n % 2, 256)
                  ss4 = ss4s[n % 2]
                  rs4 = rs4s[n % 2]
                  for h in range(4):
                      P.act(junk, po[:, h * 64:(h + 1) * 64], AF.Square, accum_out=ss4[:, h:h + 1])
                  P.act(rs4, ss4, AF.Ln, bias=epsc, scale=1.0 / 64.0)
                  P.act(rs4, rs4, AF.Exp, scale=-0.5)
                  of = ogf[n % 2]
                  of3 = of.rearrange("p (h v) -> p h v", h=4)
                  P.tt(of3, po.rearrange("p (h v) -> p h v", h=4), rs4.unsqueeze(2).broadcast_to([128, 4, 64]),
                       ALU.mult)
                  ob = ogb[n % 3]
                  P.tt(ob, of, gate[:, n, :], ALU.mult)
                  for c_ in range(2):
                      dst_ = ocat[:, 6 + c_, n * 128:(n + 1) * 128]
                      P.add("sp", lambda e, dst_=dst_, src_=ob[:, c_ * 128:(c_ + 1) * 128]: e.dma_start(
                          out=dst_, in_=src_, transpose=True),
                          reads=[ob[:, c_ * 128:(c_ + 1) * 128]], writes=[dst_], dma=True)

              for s_ in range(NT + 1):
                  if s_ < NT:
                      gla_m(s_)
                  if s_ >= 1:
                      gla_a(s_ - 1)
              A.release(mixer_mark)
              if l == 0:
                  dump("ogla", ocat[:, 6:8, :], [128, 2, S], BF16)
              ck('gla')

              A.n = ARENA - XTB
              assert A.top <= A.n
              post_mark = A.mark()
              ytmp = [A.alloc([D], F32) for _ in range(2)]
              xbs = [A.alloc([D], BF16) for _ in range(2)]
              for t in range(NT):
                  P.dma("sp", x_tok[:, t, :], xres_d[t * 128:(t + 1) * 128, :])
              def mm1_(t):
                  b0 = (t % 3) * 2
                  p2 = ps[:, b0 * 512:(b0 + 2) * 512]
                  for hf in range(2):
                      for e_ in range(8):
                          P.matmul(p2[:, hf * 512:(hf + 1) * 512], ocat[:, e_, t * 128:(t + 1) * 128],
                                   Wout[:, e_, hf * 512:(hf + 1) * 512], start=(e_ == 0), stop=(e_ == 7))

              def a1_(t):
                  b0 = (t % 3) * 2
                  ln_a(ps[:, b0 * 512:(b0 + 2) * 512], x_tok[:, t, :], lnbuf, ytmp[t % 2], t % 2)

              def b1_(t):
                  ln_b(lnbuf, x_tok[:, t, :], ytmp[t % 2])
                  make_xT(x_tok[:, t, :], t, xbs[t % 2])

              for s_ in range(NT + 2):
                  if s_ < NT:
                      mm1_(s_)
                  if 0 <= s_ - 1 < NT:
                      a1_(s_ - 1)
                  if 0 <= s_ - 2 < NT:
                      b1_(s_ - 2)
              A.release(base_mark)
              if l == 0:
                  dump("x1", x_tok, [128, NT, D])
              ck('ln1')

              GF = NFC // 2
              Wdg = A.alloc([GF, D], BF16)
              hT = A.alloc([GF, S], BF16)
              NWB = 3
              wgu = [A.alloc([2, 8, 128], BF16) for _ in range(NWB)]
              sg = [A.alloc([512], BF16) for _ in range(2)]
              ytmp = [A.alloc([D], F32) for _ in range(1)]
              xbs = [A.alloc([D], BF16) for _ in range(2)]
              wdv = wd_d[l].rearrange("(f p) n -> p f n", p=128)
              wgv = wg_d[l].rearrange("(k p) n -> p k n", p=128)
              wuv = wu_d[l].rearrange("(k p) n -> p k n", p=128)
              P.dma("sp", lnbuf[:, 0, :], ln_d["ln2_g"][l].partition_broadcast(128))
              P.dma("sp", lnbuf[:, 1, :], ln_d["ln2_b"][l].partition_broadcast(128))
              wi = 0
              ui = 0
              for g_ in range(2):
                  for fi in range(GF):
                      f = g_ * GF + fi
                      wb = wgu[wi % NWB]
                      wi += 1
                      P.dma("pool", wb[:, 0, :, :], wgv[:, :, f * 128:(f + 1) * 128])
                      P.dma("pool", wb[:, 1, :, :], wuv[:, :, f * 128:(f + 1) * 128])
                      P.dma("pool", Wdg[:, fi, :], wdv[:, f, :])
                      for tb in range(4):
                          pg = bank((ui % 2) * 2)
                          pu = bank((ui % 2) * 2 + 1)
                          ui += 1
                          for k in range(8):
                              P.matmul(pg, wb[:, 0, k, :], xT[:, k, tb * 512:(tb + 1) * 512],
                                       start=(k == 0), stop=(k == 7))
                          for k in range(8):
                              P.matmul(pu, wb[:, 1, k, :], xT[:, k, tb * 512:(tb + 1) * 512],
                                       start=(k == 0), stop=(k == 7))
                          s_ = sg[ui % 2]
                          P.act(s_, pg, AF.Silu)
                          P.tt(hT[:, fi, tb * 512:(tb + 1) * 512], pu, s_, ALU.mult)
                  ytmps = [ytmp[0], wgu[0].rearrange("p a b c -> p (a b c)").bitcast(F32)]

                  def mm2_(t):
                      b0 = (t % 3) * 2
                      p2 = ps[:, b0 * 512:(b0 + 2) * 512]
                      for hf in range(2):
                          for fi in range(GF):
                              P.matmul(p2[:, hf * 512:(hf + 1) * 512], hT[:, fi, t * 128:(t + 1) * 128],
                                       Wdg[:, fi, hf * 512:(hf + 1) * 512], start=(fi == 0), stop=(fi == GF - 1))

                  def a2_(t):
                      b0 = (t % 3) * 2
                      p2 = ps[:, b0 * 512:(b0 + 2) * 512]
                      if g_ == 0:
                          P.stt(x_tok[:, t, :], x_tok[:, t, :], ALPHA, p2, ALU.mult, ALU.add)
                      else:
                          ln_a(p2, x_tok[:, t, :], lnbuf, ytmps[t % 2], t % 2, alpha=1.0)

                  def b2_(t):
                      if g_ == 0:
                          return
                      ln_b(lnbuf, x_tok[:, t, :], ytmps[t % 2])
                      if l == n_layers - 1:
                          P.dma("sp", y_d[t * 128:(t + 1) * 128, :], x_tok[:, t, :])
                      else:
                          P.dma("sp", xs_d[t * 128:(t + 1) * 128, :], x_tok[:, t, :])
                          make_xT(x_tok[:, t, :], t, xbs[t % 2])

                  for s_ in range(NT + 2):
                      if s_ < NT:
                          mm2_(s_)
                      if 0 <= s_ - 1 < NT:
                          a2_(s_ - 1)
                      if 0 <= s_ - 2 < NT:
                          b2_(s_ - 2)
              A.release(base_mark)
              A.n = ARENA

        except _Stop:
            pass
        P.emit()
    _CACHE['P'] = P
    return nc, dbg_outs


def kernel(**inputs):
    if "c" not in _CACHE:
        _CACHE["c"] = host_consts()
    cf, cb, dc, ds = _CACHE["c"]
    nc, _ = build()
    x = np.ascontiguousarray(inputs["x"], dtype=np.float32)
    common = {
        "w_in": np.ascontiguousarray(inputs["w_in"], dtype=np.float32),
        "diff_lambda": np.ascontiguousarray(inputs["diff_lambda"], dtype=np.float32).reshape(L, 256),
        "diff_norm_g": np.ascontiguousarray(inputs["diff_norm_g"], dtype=np.float32),
        "fourier_w": np.ascontiguousarray(inputs["fourier_w"], dtype=np.float32),
        "gla_gate_w2": np.ascontiguousarray(inputs["gla_gate_w2"], dtype=np.float32),
        "gla_gate_b2": np.ascontiguousarray(inputs["gla_gate_b2"], dtype=np.float32),
        "gla_norm_g": np.ascontiguousarray(inputs["gla_norm_g"], dtype=np.float32),
        "w_out": np.ascontiguousarray(inputs["w_out"], dtype=np.float32),
        "ln1_g": np.ascontiguousarray(inputs["ln1_g"], dtype=np.float32),
        "ln1_b": np.ascontiguousarray(inputs["ln1_b"], dtype=np.float32),
        "ln2_g": np.ascontiguousarray(inputs["ln2_g"], dtype=np.float32),
        "ln2_b": np.ascontiguousarray(inputs["ln2_b"], dtype=np.float32),
        "ffn_w_gate": np.ascontiguousarray(inputs["ffn_w_gate"], dtype=np.float32),
        "ffn_w_up": np.ascontiguousarray(inputs["ffn_w_up"], dtype=np.float32),
        "ffn_w_down": np.ascontiguousarray(inputs["ffn_w_down"], dtype=np.float32),
        "c_f32": cf, "c_bf": cb, "dft_c": dc, "dft_s": ds,
    }
    in_maps = [dict(common, x=x[b]) for b in range(8)]
    res = run_bass_kernel_spmd(nc, in_maps, core_ids=list(range(8)))
    return np.stack([np.asarray(r["y"], dtype=np.float32) for r in res.results], axis=0)
```

```python
import numpy as np
import concourse.bass as bass
import concourse.mybir as mybir

F32 = mybir.dt.float32
BF16 = mybir.dt.bfloat16
ALU = mybir.AluOpType
AF = mybir.ActivationFunctionType
AX = mybir.AxisListType

_DT_SIZE = {F32: 4, BF16: 2, mybir.dt.float32r: 4, mybir.dt.int32: 4,
            mybir.dt.uint32: 4, mybir.dt.float16: 2, mybir.dt.uint16: 2,
            mybir.dt.int16: 2, mybir.dt.uint8: 1, mybir.dt.int8: 1}

ENGS = ("pe", "act", "dve", "pool", "sp")
N_DMA_SEMS = 24
TINY_BYTES = 256
SAME_ENGINE_SYNC = False


def ap_box(ap):
    t = ap.tensor
    name = t.name
    esz = _DT_SIZE[ap.dtype]
    dims = list(ap.ap)
    off = ap.offset
    space = str(ap.space)
    if "DRAM" in space.upper() or "HBM" in space.upper():
        lo = off
        hi = off
        for (st, n) in dims:
            if st >= 0:
                hi += st * (n - 1)
            else:
                lo += st * (n - 1)
        return (name, 0, 1, lo * esz, (hi + 1) * esz)
    pstep, pcnt = dims[0]
    if pstep == 0:
        pstep = 1 << 40
    p0 = off // pstep if pstep < (1 << 40) else 0
    f = off - p0 * pstep if pstep < (1 << 40) else off
    lo = f
    hi = f
    for (st, n) in dims[1:]:
        if st >= 0:
            hi += st * (n - 1)
        else:
            lo += st * (n - 1)
    if "PSUM" in space.upper():
        b0 = (lo * esz) // 2048
        b1 = ((hi + 1) * esz - 1) // 2048
        return (name, 0, 128, b0 * 2048, (b1 + 1) * 2048, True)
    return (name, p0, p0 + pcnt, lo * esz, (hi + 1) * esz)


def _overlap(a, b):
    return a[1] < b[2] and b[1] < a[2] and a[3] < b[4] and b[3] < a[4]


def _contains(a, b):
    return a[1] <= b[1] and a[2] >= b[2] and a[3] <= b[3] and a[4] >= b[4]


class Prog:
    def __init__(self, nc):
        self.nc = nc
        self.ins = []
        self.recs = {}
        self.dma_rr = {e: 0 for e in ENGS}
        self.trace = {}

    def add(self, eng, fn, reads=(), writes=(), dma=False):
        idx = len(self.ins)
        rb = list(dict.fromkeys(ap_box(a) for a in reads))
        wb = list(dict.fromkeys(ap_box(a) for a in writes))
        deps = set()
        tiny_deps = set()
        for b in rb:
            psum = len(b) > 5
            tiny = (not psum) and (b[4] - b[3]) <= TINY_BYTES
            for rec in self.recs.get(b[0], ()):
                if (rec[1] or (psum and rec[3] != eng)) and _overlap(rec[0], b):
                    deps.add(rec[2])
                    if tiny and rec[1]:
                        tiny_deps.add(rec[2])
        for b in wb:
            for rec in self.recs.get(b[0], ()):
                if _overlap(rec[0], b):
                    deps.add(rec[2])
        for b in wb:
            lst = self.recs.setdefault(b[0], [])
            lst[:] = [r for r in lst if not _contains(b, r[0])]
            lst.append([b, True, idx, eng])
        for b in rb:
            lst = self.recs.setdefault(b[0], [])
            found = False
            for r in lst:
                if (not r[1]) and r[3] == eng and r[0] == b and not dma \
                        and not self.ins[r[2]]["dma"]:
                    r[2] = idx
                    found = True
                    break
            if not found:
                lst.append([b, False, idx, eng])
        real = set()
        for d in deps:
            p = self.ins[d]
            if p["eng"] == "pe" and eng == "pe" and not p["dma"] and not dma:
                continue
            if (not SAME_ENGINE_SYNC) and p["eng"] == eng and eng != "pool" and not p["dma"] and not dma \
                    and d not in tiny_deps:
                continue
            real.add(d)
        self.ins.append(dict(eng=eng, fn=fn, deps=real, dma=dma, needed=False,
                             sem=None, val=None))
        return idx

    def emit(self, final_wait_eng="sp"):
        nc = self.nc
        ins = self.ins
        for r in ins:
            for d in r["deps"]:
                ins[d]["needed"] = True
        last_dmas = [i for i, r in enumerate(ins) if r["dma"]]
        import contextlib
        with contextlib.ExitStack() as st:
            esem = {e: st.enter_context(nc.semaphore("s_" + e)) for e in ENGS}
            dsem = {e: [st.enter_context(nc.semaphore("d_%s_%d" % (e, i)))
                        for i in range(N_DMA_SEMS)] for e in ("sp", "act", "pool")}
            cnt = {e: 0 for e in ENGS}
            dcnt = {e: [0] * N_DMA_SEMS for e in dsem}
            drr = {e: 0 for e in dsem}
            prev_use = {}
            for i, r in enumerate(ins):
                e = r["eng"]
                if r["dma"]:
                    s = drr[e]
                    drr[e] = (s + 1) % N_DMA_SEMS
                    if dcnt[e][s] > 0:
                        r["prev"] = (dsem[e][s], dcnt[e][s], ("d", e, s))
                    else:
                        r["prev"] = None
                    dcnt[e][s] += 16
                    r["sem"] = dsem[e][s]
                    r["val"] = dcnt[e][s]
                    r["semkey"] = ("d", e, s)
                elif r["needed"]:
                    cnt[e] += 1
                    r["sem"] = esem[e]
                    r["val"] = cnt[e]
                    r["semkey"] = ("e", e)
            block = st.enter_context(nc.Block())
            per_eng = {e: [i for i, r in enumerate(ins) if r["eng"] == e] for e in ENGS}
            final = {}
            for i in last_dmas:
                r = ins[i]
                final[r["semkey"]] = (r["sem"], max(r["val"], final.get(r["semkey"], (None, 0))[1]))

            def body(ename):
                def run(engobj):
                    waited = {}
                    for i in per_eng[ename]:
                        r = ins[i]
                        need = {}
                        for d in r["deps"]:
                            p = ins[d]
                            k = p["semkey"]
                            if need.get(k, (None, 0))[1] < p["val"]:
                                need[k] = (p["sem"], p["val"])
                        if r["dma"] and r["prev"] is not None:
                            s, v, k = r["prev"]
                            if need.get(k, (None, 0))[1] < v:
                                need[k] = (s, v)
                        for k, (s, v) in need.items():
                            if waited.get(k, 0) < v:
                                engobj.wait_ge(s, v)
                                waited[k] = v
                                self.trace.setdefault(ename, []).append(("w", k, v, i))
                        h = r["fn"](engobj)
                        if r["dma"]:
                            h.then_inc(r["sem"], 16)
                            self.trace.setdefault(ename, []).append(("i", r["semkey"], 16, i))
                        elif r["needed"]:
                            h.then_inc(r["sem"], 1)
                            self.trace.setdefault(ename, []).append(("i", r["semkey"], 1, i))
                    if ename == final_wait_eng:
                        for k, (s, v) in final.items():
                            if waited.get(k, 0) < v:
                                engobj.wait_ge(s, v)
                return run

            block.tensor(body("pe"))
            block.scalar(body("act"))
            block.vector(body("dve"))
            block.gpsimd(body("pool"))
            block.sync(body("sp"))

    def dma(self, eng, out, in_, **kw):
        return self.add(eng, lambda e: e.dma_start(out=out, in_=in_, **kw),
                        reads=[in_], writes=[out], dma=True)

    def matmul(self, out, lhsT, rhs, start=True, stop=True, **kw):
        return self.add("pe", lambda e: e.matmul(out, lhsT, rhs, start=start, stop=stop, **kw),
                        reads=[lhsT, rhs], writes=[out])

    def transpose(self, out, in_, ident):
        return self.add("pe", lambda e: e.transpose(out, in_, ident),
                        reads=[in_, ident], writes=[out])

    def act(self, out, in_, func, bias=None, scale=1.0, accum_out=None, eng="act"):
        reads = [in_]
        writes = [out]
        kw = {}
        if bias is not None:
            kw["bias"] = bias
            if not isinstance(bias, (int, float)):
                reads.append(bias)
        if not isinstance(scale, (int, float)):
            reads.append(scale)
        if accum_out is not None:
            kw["accum_out"] = accum_out
            writes.append(accum_out)
        return self.add(eng, lambda e: e.activation(out=out, in_=in_, func=func, scale=scale, **kw),
                        reads=reads, writes=writes)

    def tt(self, out, in0, in1, op, eng="dve"):
        return self.add(eng, lambda e: e.tensor_tensor(out=out, in0=in0, in1=in1, op=op),
                        reads=[in0, in1], writes=[out])

    def ts(self, out, in0, s1, op0, s2=None, op1=None, eng="dve", accum_out=None):
        reads = [in0]
        if not isinstance(s1, (int, float)):
            reads.append(s1)
        if s2 is not None and not isinstance(s2, (int, float)):
            reads.append(s2)
        kw = {}
        writes = [out]
        if op1 is not None:
            kw["op1"] = op1
        if accum_out is not None:
            kw["accum_out"] = accum_out
            writes.append(accum_out)
        return self.add(eng, lambda e: e.tensor_scalar(out=out, in0=in0, scalar1=s1, scalar2=s2,
                                                       op0=op0, **kw),
                        reads=reads, writes=writes)

    def stt(self, out, in0, scalar, in1, op0, op1, eng="dve"):
        reads = [in0, in1]
        if not isinstance(scalar, (int, float)):
            reads.append(scalar)
        return self.add(eng, lambda e: e.scalar_tensor_tensor(out=out, in0=in0, scalar=scalar,
                                                              in1=in1, op0=op0, op1=op1),
                        reads=reads, writes=[out])

    def copy(self, out, in_, eng="dve"):
        if eng == "act":
            return self.add(eng, lambda e: e.activation(out=out, in_=in_, func=AF.Identity),
                            reads=[in_], writes=[out])
        return self.add(eng, lambda e: e.tensor_copy(out=out, in_=in_), reads=[in_], writes=[out])

    def memset(self, ap, val, eng="dve"):
        return self.add(eng, lambda e: e.memset(ap, val), reads=[], writes=[ap])

    def reduce(self, out, in_, op, axis=AX.X, eng="dve", **kw):
        return self.add(eng, lambda e: e.tensor_reduce(out=out, in_=in_, op=op, axis=axis, **kw),
                        reads=[in_], writes=[out])

    def recip(self, out, in_, eng="dve"):
        return self.add(eng, lambda e: e.reciprocal(out=out, in_=in_), reads=[in_], writes=[out])


def simulate_trace(trace):
    pos = {e: 0 for e in trace}
    sem = {}
    progress = True
    while progress:
        progress = False
        for e, ops in trace.items():
            while pos[e] < len(ops):
                kind, k, v, i = ops[pos[e]]
                if kind == "w":
                    if sem.get(k, 0) >= v:
                        pos[e] += 1
                        progress = True
                    else:
                        break
                else:
                    sem[k] = sem.get(k, 0) + v
                    pos[e] += 1
                    progress = True
    stuck = {e: ops[pos[e]] for e, ops in trace.items() if pos[e] < len(ops)}
    return stuck, sem

import contextlib
import os
_SK = set(os.environ.get('DBGSKIP', '').split(','))
import math
import ml_dtypes
from concourse.bass_utils import run_bass_kernel_spmd

S = 2048
D = 1024
L = 2
INW = 2592
FF = 2816
NT = 16
NFC = FF // 128
ALPHA = float((2 * L) ** 0.25)
EPS = 1e-5
C_DQ, C_DK, C_DV, C_FU, C_GQ, C_GK, C_GV, C_GR, C_GZ = 0, 512, 1024, 1536, 1792, 1920, 2048, 2304, 2560

CF_ID, CF_COS, CF_SIN, CF_HM, CF_CC, CF_SC, CF_ONES, CF_NEGH, CF_N = 0, 128, 640, 1152, 1156, 1284, 1412, 1540, 1548
CB_ID, CB_TRIF, CB_TRIB, CB_BD, CB_ONE, CB_N = 0, 128, 256, 384, 640, 768


def host_consts():
    p = np.arange(128)
    cf = np.zeros((128, CF_N), np.float32)
    cf[:, CF_ID:CF_ID + 128] = np.eye(128, dtype=np.float32)
    inv_freq = (10000.0 ** (-np.arange(0, 64, 2, dtype=np.float32) / 64)).astype(np.float32)
    pos = (np.arange(NT)[None, :] * 128 + p[:, None]).astype(np.float32)
    ang = pos[:, :, None] * inv_freq[None, None, :]
    cf[:, CF_COS:CF_COS + 512] = np.cos(ang).reshape(128, 512)
    cf[:, CF_SIN:CF_SIN + 512] = np.sin(ang).reshape(128, 512)
    cf[:, CF_HM:CF_HM + 4] = (p[:, None] // 32 == np.arange(4)[None, :]).astype(np.float32)
    c = np.arange(64)
    a = 2 * np.pi * np.outer(c, c) / 64.0
    cc = np.cos(a) / 8.0
    sc = np.sin(a) / 8.0
    z = np.zeros((64, 64))
    cf[:, CF_CC:CF_CC + 128] = np.block([[cc, z], [z, cc]])
    cf[:, CF_SC:CF_SC + 128] = np.block([[sc, z], [z, sc]])
    cf[:, CF_ONES:CF_ONES + 128] = 1.0
    cf[:, CF_NEGH:CF_NEGH + 8] = -0.5
    cb = np.zeros((128, CB_N), np.float32)
    cb[:, CB_ID:CB_ID + 128] = np.eye(128)
    cb[:, CB_TRIF:CB_TRIF + 128] = (p[:, None] <= p[None, :])
    cb[:, CB_TRIB:CB_TRIB + 128] = (p[:, None] >= p[None, :])
    cb[:, CB_BD:CB_BD + 256] = (p[:, None] // 32 == np.arange(256)[None, :] // 64)
    cb[:, CB_ONE:CB_ONE + 128] = 1.0
    s = np.arange(S, dtype=np.float64)
    sk = np.outer(s, s) % S
    ang = 2 * np.pi * sk / S
    dc = (np.cos(ang) / math.sqrt(S)).astype(np.float32).astype(ml_dtypes.bfloat16)
    ds = (-np.sin(ang) / math.sqrt(S)).astype(np.float32).astype(ml_dtypes.bfloat16)
    return cf, cb.astype(ml_dtypes.bfloat16), dc, ds


class Arena:
    def __init__(self, t, nbytes):
        self.t = t
        self.n = nbytes
        self.top = 0

    def mark(self):
        return self.top

    def release(self, m):
        self.top = m

    def alloc(self, shape, dt):
        n = 1
        for s_ in shape:
            n *= s_
        b = n * _DT_SIZE[dt]
        off = self.top
        self.top += (b + 63) // 64 * 64
        assert self.top <= self.n, ("arena overflow", self.top, self.n)
        ap = self.t[:, off // 2:(off + b) // 2]
        if dt != BF16:
            ap = ap.bitcast(dt)
        if len(shape) == 2:
            ap = ap.rearrange("p (a b) -> p a b", a=shape[0])
        elif len(shape) == 3:
            ap = ap.rearrange("p (a b c) -> p a b c", a=shape[0], b=shape[1])
        return ap


class Rot:
    def __init__(self, items):
        self.items = items
        self.i = 0

    def next(self):
        r = self.items[self.i % len(self.items)]
        self.i += 1
        return r


_CACHE = {}


class _Stop(Exception):
    pass


def build(n_layers=L, dbg=None, upto=None):
    nc = bass.Bass("TRN2", target_bir_lowering=False)
    dram = lambda name, shape, dt, kind="ExternalInput": nc.dram_tensor(name, shape, dt, kind=kind).ap()
    x_d = dram("x", [S, D], F32)
    w_in_d = dram("w_in", [L, D, INW], F32)
    lam_d = dram("diff_lambda", [L, 256], F32)
    dng_d = dram("diff_norm_g", [L, 128], F32)
    fw_d = dram("fourier_w", [L, 4, 64, 64], F32)
    w2_d = dram("gla_gate_w2", [L, 2, 16, 128], F32)
    b2_d = dram("gla_gate_b2", [L, 2, 128], F32)
    gng_d = dram("gla_norm_g", [L, 64], F32)
    wout_d = dram("w_out", [L, D, D], F32)
    ln_d = {k: dram(k, [L, D], F32) for k in ("ln1_g", "ln1_b", "ln2_g", "ln2_b")}
    wg_d = dram("ffn_w_gate", [L, D, FF], F32)
    wu_d = dram("ffn_w_up", [L, D, FF], F32)
    wd_d = dram("ffn_w_down", [L, FF, D], F32)
    cf_d = dram("c_f32", [128, CF_N], F32)
    cb_d = dram("c_bf", [128, CB_N], BF16)
    dftc_d = dram("dft_c", [S, S], BF16)
    dfts_d = dram("dft_s", [S, S], BF16)
    y_d = dram("y", [S, D], F32, kind="ExternalOutput")
    xs_d = dram("xs_scr", [S, D], F32, kind="Internal")
    dbg_outs = {}

    ARENA = 207 * 1024
    XTB = NT * D * 4
    with contextlib.ExitStack() as st:
        arena_t = st.enter_context(nc.sbuf_tensor("arena", [128, ARENA // 2], BF16))
        ps = st.enter_context(nc.psum_tensor("ps", [128, 4096], F32))
        P = Prog(nc)
        A = Arena(arena_t, ARENA)
        x_tok = arena_t[:, (ARENA - XTB) // 2:ARENA // 2].bitcast(F32).rearrange("p (t d) -> p t d", t=NT)

        def bank(b, n=512, off=0):
            return ps[:, b * 512 + off:b * 512 + off + n]

        def bankbf(b):
            return ps[:, b * 512:(b + 1) * 512].bitcast(BF16)

        def dump(name, ap, shape, dt=F32):
            if dbg is None or name not in dbg:
                return
            d_ = nc.dram_tensor("dbg_" + name, shape, dt, kind="ExternalOutput").ap()
            dbg_outs[name] = d_
            P.dma("sp", d_, ap)

        cf = A.alloc([CF_N], F32)
        cb = A.alloc([CB_N], BF16)
        xT = A.alloc([8, S], BF16)
        lnbuf = A.alloc([2, D], F32)
        small = A.alloc([640], F32)
        P.dma("sp", cf, cf_d)
        P.dma("sp", cb, cb_d)
        ident_f = cf[:, CF_ID:CF_ID + 128]
        ident_b = cb[:, CB_ID:CB_ID + 128]
        cos_t = cf[:, CF_COS:CF_COS + 512].rearrange("p (t i) -> p t i", t=NT)
        sin_t = cf[:, CF_SIN:CF_SIN + 512].rearrange("p (t i) -> p t i", t=NT)
        hmask4 = cf[:, CF_HM:CF_HM + 4]
        ccbd = cf[:, CF_CC:CF_CC + 128]
        scbd = cf[:, CF_SC:CF_SC + 128]
        ones_f = cf[:, CF_ONES:CF_ONES + 128]
        negh8 = cf[:, CF_NEGH:CF_NEGH + 8]
        tri = [cb[:, CB_TRIF:CB_TRIF + 128], cb[:, CB_TRIB:CB_TRIB + 128]]
        bdmask = cb[:, CB_BD:CB_BD + 256]
        lp_bc = small[:, 0:256]
        gdiff = small[:, 256:384]
        ggla = small[:, 384:448]
        negb2 = small[:, 448:450]
        neglam = small[:, 450:451]
        negM = small[:, 451:452]
        sc_tmp = small[:, 452:500]
        nrm = small[:, 500:504]
        epsc = small[:, 504:505]
        base_mark = A.mark()

        win = lambda l: w_in_d[l].rearrange("(k p) n -> p k n", p=128)

        def rstd_from(out, ss, n, width):
            tmp = sc_tmp[:, 40:40 + width]
            P.ts(tmp, ss, 1.0 / n, ALU.mult, EPS, ALU.add)
            P.tt(out, tmp, negh8[:, 0:width], ALU.pow, eng="pool")

        xb_rot = None

        def make_xT(x_tile_f32, t, xb, evac_eng="act", pbank=7):
            P.copy(xb, x_tile_f32, eng="act")
            pst = bankbf(pbank)
            for k in range(8):
                P.transpose(pst[:, k * 128:(k + 1) * 128], xb[:, k * 128:(k + 1) * 128], ident_b)
            P.copy(xT[:, :, t * 128:(t + 1) * 128], pst.rearrange("p (k n) -> p k n", k=8), eng=evac_eng)

        def ln_a(psum2, xres, gb, ytmp, par, alpha=ALPHA):
            sct = sc_tmp[:, 0:16] if par == 0 else small[:, 540:556]
            P.stt(ytmp, xres, alpha, psum2, ALU.mult, ALU.add)
            stats = sct[:, 0:12]
            mv = sct[:, 12:14]
            for c_ in range(2):
                P.add("dve", lambda e, c_=c_: e.bn_stats(out=stats[:, c_ * 6:(c_ + 1) * 6],
                                                          in_=ytmp[:, c_ * 512:(c_ + 1) * 512]),
                      reads=[ytmp[:, c_ * 512:(c_ + 1) * 512]], writes=[stats[:, c_ * 6:(c_ + 1) * 6]])
            P.add("dve", lambda e: e.bn_aggr(out=mv, in_=stats), reads=[stats], writes=[mv])
            rs = sct[:, 14:15]
            nmr = sct[:, 15:16]
            P.act(rs, mv[:, 1:2], AF.Ln, bias=epsc)
            P.act(rs, rs, AF.Exp, scale=-0.5)
            P.stt(nmr, mv[:, 0:1], -1.0, rs, ALU.mult, ALU.mult)
            P.act(ytmp, ytmp, AF.Identity, bias=nmr, scale=rs)
            P.tt(ytmp, ytmp, gb[:, 0, :], ALU.mult, eng="pool")

        def ln_b(gb, out_tok, ytmp):
            P.tt(out_tok, ytmp, gb[:, 1, :], ALU.add)

        def ck(name):
            if upto == name:
                raise _Stop()

        try:
          for l in range(n_layers):
              lam_init = 0.8 - 0.6 * math.exp(-0.3 * l)
              A.release(base_mark)
              P.dma("sp", lp_bc, lam_d[l].partition_broadcast(128))
              P.dma("sp", gdiff, dng_d[l].partition_broadcast(128))
              P.dma("sp", ggla, gng_d[l].partition_broadcast(128))
              for d_ in range(2):
                  P.dma("sp", negb2[:, d_:d_ + 1], b2_d[l, d_].rearrange("(p o) -> p o", o=1))
              P.ts(negb2, negb2, -1.0, ALU.mult)
              P.ts(gdiff, gdiff, 1.0 - lam_init, ALU.mult)
              P.memset(epsc, EPS)
              pr = sc_tmp[:, 16:18]
              prod = A.alloc([128], F32)
              lp4 = lp_bc.rearrange("p (a d) -> p a d", a=4)
              for i_ in range(2):
                  P.tt(prod[:, 0:64], lp4[:, 2 * i_, :], lp4[:, 2 * i_ + 1, :], ALU.mult)
                  P.reduce(pr[:, i_:i_ + 1], prod[:, 0:64], ALU.add)
              P.act(pr, pr, AF.Exp)
              P.tt(neglam, pr[:, 1:2], pr[:, 0:1], ALU.subtract)
              P.ts(neglam, neglam, -lam_init, ALU.add)
              A.release(base_mark)
              ck('params')

              if l == 0:
                  m_ = A.mark()
                  xin = [A.alloc([D], F32) for _ in range(3)]
                  xbs = [A.alloc([D], BF16) for _ in range(2)]
                  for t in range(NT):
                      xi = xin[t % 3]
                      P.dma("sp", xi, x_d[t * 128:(t + 1) * 128, :])
                      make_xT(xi, t, xbs[t % 2], evac_eng="dve", pbank=(7 if t % 2 else 6))
                  A.release(m_)
              ck('xT')
              xres_d = x_d if l == 0 else xs_d

              ocat = A.alloc([8, S], BF16)
              Wout = A.alloc([8, D], BF16)
              Wgf = A.alloc([8, 288], BF16)
              Wgt = A.alloc([8, 512], BF16)
              mixer_mark = A.mark()

              Wh = [A.alloc([8, 384], BF16) for _ in range(2)]
              QTz = A.alloc([2, S], BF16)
              KT = A.alloc([S], BF16)
              Vaug = A.alloc([NT, 132], BF16)
              PT = [A.alloc([1024], BF16) for _ in range(3)]
              O1n = A.alloc([8, 128], F32)
              O2t = A.alloc([8, 128], F32)
              otoks = [A.alloc([8, 128], BF16) for _ in range(2)]
              qkr = [A.alloc([256], BF16) for _ in range(3)]
              tmpAs = [A.alloc([128], F32) for _ in range(2)]
              tmpBs = [A.alloc([128], F32) for _ in range(2)]
              sq = A.alloc([256], F32)
              D2 = A.alloc([256], F32)
              sqs = [sq, A.alloc([256], F32)]
              red_all = A.alloc([NT, 4], F32)
              red4 = sc_tmp[:, 20:24]
              gm = sc_tmp[:, 24:26]
              nrm2 = sc_tmp[:, 26:28]
              rz = sc_tmp[:, 28:36]
              ss8 = small[:, 512:520]
              rstd8 = small[:, 520:528]
              P.memset(Vaug[:, :, 128:129], 1.0)
              P.memset(QTz[64:128, 0, :], 0.0)
              P.memset(QTz[0:64, 1, :], 0.0, eng="pool")
              pso = [ps[:, (4 + qi // 3) * 512 + (qi % 3) * 160:(4 + qi // 3) * 512 + (qi % 3) * 160 + 129]
                     for qi in range(8)]
              grp = [(0, 3), (3, 3), (6, 2)]

              def pso_grp(gi, c0, c1):
                  q0, n_ = grp[gi]
                  base_ = (4 + gi) * 512
                  return ps[:, base_:base_ + n_ * 160].rearrange("p (a b) -> p a b", b=160)[:, :, c0:c1]

              def load_wh(h_):
                  for j_, c0 in enumerate((C_DQ, C_DK, C_DV)):
                      P.dma("pool", Wh[h_ % 2][:, :, j_ * 128:(j_ + 1) * 128],
                            win(l)[:, :, c0 + h_ * 128:c0 + (h_ + 1) * 128])

              load_wh(0)
              fin_pending = []
              for h in range(4):
                  wh = Wh[h % 2]
                  P.memset(nrm, 0.0)
                  tr_pending = []
                  for t in range(NT):
                      pb = bank(t % 4, 256)
                      pv_ = bank(4 + t % 2, 128)
                      for k in range(8):
                          P.matmul(pb, xT[:, k, t * 128:(t + 1) * 128], wh[:, k, 0:256], start=(k == 0), stop=(k == 7))
                      for k in range(8):
                          P.matmul(pv_, xT[:, k, t * 128:(t + 1) * 128], wh[:, k, 256:384], start=(k == 0), stop=(k == 7))
                      qk4 = pb.rearrange("p (g h d) -> p g h d", g=4, h=2)
                      t1 = qk4[:, :, 0, :]
                      t2 = qk4[:, :, 1, :]
                      cbt = cos_t[:, t:t + 1, :].broadcast_to([128, 4, 32])
                      sbt = sin_t[:, t:t + 1, :].broadcast_to([128, 4, 32])
                      q_ = qkr[t % 3]
                      q4 = q_.rearrange("p (g h d) -> p g h d", g=4, h=2)
                      ta = tmpAs[t % 2].rearrange("p (g d) -> p g d", g=4)
                      tb_ = tmpBs[t % 2].rearrange("p (g d) -> p g d", g=4)
                      P.tt(ta, t1, cbt, ALU.mult)
                      P.tt(tb_, t2, sbt, ALU.mult)
                      P.tt(q4[:, :, 0, :], ta, tb_, ALU.subtract)
                      P.tt(ta, t2, cbt, ALU.mult)
                      P.tt(tb_, t1, sbt, ALU.mult)
                      P.tt(q4[:, :, 1, :], ta, tb_, ALU.add)
                      P.copy(Vaug[:, t, 0:128], pv_, eng="act")
                      P.tt(sqs[t % 2], q_, q_, ALU.mult, eng="pool")
                      def tr_(t=t, q_=q_, sq_=sqs[t % 2]):
                          P.reduce(red_all[:, t, :], sq_.rearrange("p (g d) -> p g d", g=4), ALU.add)
                          pst = bankbf(7 if t % 2 else 6)
                          P.transpose(pst[:, 0:128], q_[:, 0:128], ident_b)
                          P.transpose(pst[:, 128:256], q_[:, 128:256], ident_b)
                          P.copy(QTz[0:64, 0, t * 128:(t + 1) * 128], pst[0:64, 0:128], eng="act")
                          P.copy(QTz[64:128, 1, t * 128:(t + 1) * 128], pst[64:128, 0:128], eng="act")
                          P.copy(KT[:, t * 128:(t + 1) * 128], pst[:, 128:256], eng="act")
                      tr_pending.append(tr_)
                      if len(tr_pending) > 1:
                          tr_pending.pop(0)()
                  while tr_pending:
                      tr_pending.pop(0)()
                  P.reduce(nrm, red_all.rearrange("p t g -> p g t"), ALU.max)
                  ck('h0proj')
                  if h + 1 < 4:
                      load_wh(h + 1)
                  else:
                      Wf = Wh[0][:, :, 0:256]
                      P.dma("pool", Wf, win(l)[:, :, C_FU:C_FU + 256])
                      P.dma("pool", Wgf[:, :, 0:256], win(l)[:, :, C_GQ:C_GQ + 256])
                      P.dma("pool", Wgf[:, :, 256:288], win(l)[:, :, C_GZ:C_GZ + 32])
                      P.dma("pool", Wgt, win(l)[:, :, C_GV:C_GV + 512])
                      for k in range(8):
                          P.dma("pool", Wout[:, k, :], wout_d[l, k * 128:(k + 1) * 128, :])
                  P.reduce(nrm2, nrm.rearrange("p (a b) -> p a b", a=2), ALU.max)
                  P.ts(D2[:, 0:128], ident_f, nrm2[:, 0:1], ALU.mult)
                  P.ts(D2[:, 128:256], ident_f, nrm2[:, 1:2], ALU.mult)
                  pbm = bank(6, 256)
                  P.matmul(pbm, ones_f, D2, start=True, stop=True)
                  P.reduce(gm, pbm.rearrange("p (a b) -> p a b", a=2), ALU.max)
                  P.tt(negM, gm[:, 0:1], gm[:, 1:2], ALU.add)
                  P.ts(negM, negM, -0.5 * 0.125, ALU.mult)
                  if l == 0 and h == 0:
                      dump("xT", xT, [128, 8, S], BF16)
                      dump("negM", negM, [128, 1])
                      ck('qkt')
                  for qh in range(2):
                      otok = otoks[qh]
                      steps = [(m, kt) for m in range(2) for kt in range(NT)]

                      def emit_scores(i):
                          m, kt = steps[i]
                          sb_i = (i % 2) * 2
                          for j_ in range(2):
                              P.matmul(ps[:, (sb_i + j_) * 512:(sb_i + j_ + 1) * 512],
                                       KT[:, kt * 128:(kt + 1) * 128],
                                       QTz[:, m, qh * 1024 + j_ * 512:qh * 1024 + (j_ + 1) * 512],
                                       start=True, stop=True)

                      emit_scores(0)
                      for i, (m, kt) in enumerate(steps):
                          if i + 1 < len(steps):
                              emit_scores(i + 1)
                          if i == 8 and fin_pending:
                              fin_pending.pop(0)()
                          sb_i = (i % 2) * 2
                          pss = ps[:, sb_i * 512:(sb_i + 2) * 512]
                          pt = PT[i % 3]
                          P.act(pt, pss, AF.Exp, bias=negM, scale=0.125)
                          for qi in range(8):
                              P.matmul(pso[qi], pt[:, qi * 128:(qi + 1) * 128], Vaug[:, kt, 0:129],
                                       start=(kt == 0 and qi % 3 == 0), stop=(kt == NT - 1), skip_group_check=True)
                          if kt != NT - 1:
                              continue
                          if m == 0:
                              for gi, (q0, n_) in enumerate(grp):
                                  P.recip(rz[:, q0:q0 + n_], pso_grp(gi, 128, 129).rearrange("p a b -> p (a b)"))
                                  P.tt(O1n[:, q0:q0 + n_, :], pso_grp(gi, 0, 128),
                                       rz[:, q0:q0 + n_].unsqueeze(2).broadcast_to([128, n_, 128]), ALU.mult)
                          else:
                              for gi, (q0, n_) in enumerate(grp):
                                  P.recip(rz[:, q0:q0 + n_], pso_grp(gi, 128, 129).rearrange("p a b -> p (a b)"))
                              P.ts(rz, rz, neglam, ALU.mult)
                              for gi, (q0, n_) in enumerate(grp):
                                  P.tt(O2t[:, q0:q0 + n_, :], pso_grp(gi, 0, 128),
                                       rz[:, q0:q0 + n_].unsqueeze(2).broadcast_to([128, n_, 128]), ALU.mult)
                              P.tt(O1n, O1n, O2t, ALU.add)
                              P.tt(O2t, O1n, O1n, ALU.mult, eng="pool")
                              P.reduce(ss8, O2t, ALU.add)
                              rstd_from(rstd8, ss8, 128.0, 8)
                              P.tt(O1n, O1n, rstd8.unsqueeze(2).broadcast_to([128, 8, 128]), ALU.mult)
                              P.tt(otok, O1n, gdiff.unsqueeze(1).broadcast_to([128, 8, 128]), ALU.mult, eng="pool")
                              for qi in range(8):
                                  dst_ = ocat[:, h, qh * 1024 + qi * 128:qh * 1024 + (qi + 1) * 128]
                                  P.add("sp", lambda e, dst_=dst_, src_=otok[:, qi, :]: e.dma_start(
                                      out=dst_, in_=src_, transpose=True),
                                      reads=[otok[:, qi, :]], writes=[dst_], dma=True)
              while fin_pending:
                  fin_pending.pop(0)()
              A.release(mixer_mark)
              if l == 0:
                  dump("odiff", ocat[:, 0:4, :], [128, 4, S], BF16)
              ck('att')

              _skip = A.alloc([2 * 8 * 384], BF16)
              uT = A.alloc([2, S], BF16)
              wbd = A.alloc([2, 128], F32)
              Wcs = A.alloc([2, 256], BF16)
              ucs = A.alloc([NT, 512], BF16)
              NDB = 4
              dbuf = [A.alloc([8, 512], BF16) for _ in range(NDB)]
              P.dma("sp", lnbuf[:, 0, :], ln_d["ln1_g"][l].partition_broadcast(128))
              P.dma("sp", lnbuf[:, 1, :], ln_d["ln1_b"][l].partition_broadcast(128))
              P.memset(wbd, 0.0)
              for g_ in range(4):
                  c_, gl = g_ // 2, g_ % 2
                  P.dma("sp", wbd[gl * 64:(gl + 1) * 64, c_, gl * 64:(gl + 1) * 64], fw_d[l, g_])
              for c_ in range(2):
                  pb = bank(4, 256)
                  P.matmul(pb[:, 0:128], ccbd, wbd[:, c_, :], start=True, stop=True)
                  P.matmul(pb[:, 128:256], scbd, wbd[:, c_, :], start=True, stop=True)
                  P.copy(Wcs[:, c_, :], pb, eng="dve")
              for tb in range(4):
                  for c_ in range(2):
                      pb = bank((tb * 2 + c_) % 4)
                      for k in range(8):
                          P.matmul(pb, Wf[:, k, c_ * 128:(c_ + 1) * 128], xT[:, k, tb * 512:(tb + 1) * 512],
                                   start=(k == 0), stop=(k == 7))
                      P.copy(uT[:, c_, tb * 512:(tb + 1) * 512], pb, eng=("act" if c_ else "dve"))
              for t in range(NT):
                  pb = bank(4 + t % 2)
                  for c_ in range(2):
                      P.matmul(pb[:, c_ * 256:(c_ + 1) * 256], uT[:, c_, t * 128:(t + 1) * 128], Wcs[:, c_, :],
                               start=True, stop=True)
                  P.copy(ucs[:, t, :], pb, eng=("act" if t % 2 else "dve"))
              dft_v = [dftc_d.rearrange("(t p) k -> p t k", p=128), dfts_d.rearrange("(t p) k -> p t k", p=128)]
              di = 0
              for kb in range(4):
                  pbs = [bank(0 + (kb % 2) * 2), bank(1 + (kb % 2) * 2)]
                  first = True
                  for which in range(2):
                      for tg in range(2):
                          db = dbuf[di % NDB]
                          di += 1
                          P.dma("sp", db, dft_v[which][:, tg * 8:(tg + 1) * 8, kb * 512:(kb + 1) * 512])
                          for tt_ in range(8):
                              t = tg * 8 + tt_
                              last = (which == 1 and t == NT - 1)
                              for c_ in range(2):
                                  P.matmul(pbs[c_], ucs[:, t, c_ * 256 + which * 128:c_ * 256 + (which + 1) * 128],
                                           db[:, tt_, :], start=first, stop=last)
                              first = False
                  for c_ in range(2):
                      P.copy(ocat[:, 4 + c_, kb * 512:(kb + 1) * 512], pbs[c_], eng=("act" if c_ else "dve"))
              A.release(mixer_mark)
              if l == 0:
                  dump("ofour", ocat[:, 4:6, :], [128, 2, S], BF16)
              ck('four')

              gqk = A.alloc([2, S], F32)
              gzT = A.alloc([S], BF16)
              gv = A.alloc([NT, 256], BF16)
              gate = A.alloc([NT, 256], BF16)
              w2pad = A.alloc([2, 128], BF16)
              qt = [A.alloc([S], BF16) for _ in range(2)]
              kt_ = [A.alloc([S], BF16) for _ in range(2)]
              Sbf = [A.alloc([NT, 256], BF16) for _ in range(2)]
              gla_mark = A.mark()
              P.memset(w2pad, 0.0)
              for d_ in range(2):
                  P.dma("pool", w2pad[d_ * 16:(d_ + 1) * 16, d_, :], w2_d[l, d_])
              for tb in range(4):
                  for c_ in range(2):
                      pb = bank((tb * 3 + c_) % 4)
                      for k in range(8):
                          P.matmul(pb, Wgf[:, k, c_ * 128:(c_ + 1) * 128], xT[:, k, tb * 512:(tb + 1) * 512],
                                   start=(k == 0), stop=(k == 7))
                      P.copy(gqk[:, c_, tb * 512:(tb + 1) * 512], pb, eng=("act" if c_ else "dve"))
                  pb = bank((tb * 3 + 2) % 4)
                  for k in range(8):
                      P.matmul(pb[0:32, :], Wgf[:, k, 256:288], xT[:, k, tb * 512:(tb + 1) * 512],
                               start=(k == 0), stop=(k == 7))
                  P.copy(gzT[0:32, tb * 512:(tb + 1) * 512], pb[0:32, :], eng="dve")
              for t in range(NT):
                  pb = bank(4 + t % 2)
                  for k in range(8):
                      P.matmul(pb, xT[:, k, t * 128:(t + 1) * 128], Wgt[:, k, :], start=(k == 0), stop=(k == 7))
                  P.copy(gv[:, t, :], pb[:, 0:256], eng="act")
                  P.act(gate[:, t, :], pb[:, 256:512], AF.Silu)
              A.release(gla_mark)
              Bc = A.alloc([S], F32)
              Ec = A.alloc([S], F32)
              kdec_tok = A.alloc([NT, 128], BF16)
              kdT = [A.alloc([128], BF16) for _ in range(2)]
              Srot = [A.alloc([256], F32) for _ in range(2)]
              for d_ in range(2):
                  for tb in range(4):
                      pb = bank(tb % 4)
                      P.matmul(pb, w2pad[0:32, d_, :], gzT[0:32, tb * 512:(tb + 1) * 512], start=True, stop=True)
                      P.act(Ec[:, tb * 512:(tb + 1) * 512], pb, AF.Exp, bias=negb2[:, d_:d_ + 1], scale=-1.0)
                  P.act(Ec, Ec, AF.Ln, bias=1.0)
                  for n in range(NT):
                      o_ = Bc[:, n * 128:(n + 1) * 128]
                      i_ = Ec[:, n * 128:(n + 1) * 128]
                      if d_ == 1:
                          o_ = o_[:, ::-1]
                          i_ = i_[:, ::-1]
                      P.add("dve", lambda e, o_=o_, i_=i_: e.tensor_tensor_scan(
                          out=o_, data0=ones_f, data1=i_, initial=0.0, op0=ALU.mult, op1=ALU.add),
                          reads=[ones_f, Ec[:, n * 128:(n + 1) * 128]], writes=[Bc[:, n * 128:(n + 1) * 128]])
                  P.act(Ec, Bc, AF.Exp, scale=-1.0 / 16.0)
                  P.act(Bc, Bc, AF.Exp, scale=1.0 / 16.0)
                  P.stt(qt[d_], gqk[:, 0, :], 32.0 ** -0.5, Ec, ALU.mult, ALU.mult)
                  P.tt(kt_[d_], gqk[:, 1, :], Bc, ALU.mult)
                  elast = [Ec[:, n * 128 + (127 if d_ == 0 else 0):n * 128 + (127 if d_ == 0 else 0) + 1]
                           for n in range(NT)]
                  pst = bankbf(7)
                  for n in range(NT):
                      kd = kdT[n % 2]
                      P.stt(kd, gqk[:, 1, n * 128:(n + 1) * 128], elast[n], Bc[:, n * 128:(n + 1) * 128],
                            ALU.mult, ALU.mult)
                      P.transpose(pst[:, (n % 8) * 128:(n % 8 + 1) * 128], kd, ident_b)
                      if n % 8 == 7:
                          P.copy(kdec_tok[:, n - 7:n + 1, :], pst.rearrange("p (a b) -> p a b", a=8), eng="act")
                  order = list(range(NT)) if d_ == 0 else list(range(NT - 1, -1, -1))
                  prev = Srot[0]
                  P.memset(prev, 0.0)
                  P.memset(Sbf[d_][:, order[0], :], 0.0)
                  for i_, n in enumerate(order[:-1]):
                      pb = bank(4 + i_ % 2, 256)
                      P.matmul(pb, kdec_tok[:, n, :], gv[:, n, :], start=True, stop=True)
                      cur = Srot[(i_ + 1) % 2]
                      P.stt(cur, prev, elast[n], pb, ALU.mult, ALU.add)
                      P.tt(Sbf[d_][:, order[i_ + 1], :], cur, bdmask, ALU.mult, eng="pool")
                      prev = cur
              A.release(gla_mark)
              Qbd = [A.alloc([4, 128], BF16) for _ in range(4)]
              Asb = [A.alloc([4, 128], BF16) for _ in range(4)]
              ogf = [A.alloc([256], F32) for _ in range(2)]
              ogb = [A.alloc([256], BF16) for _ in range(3)]
              junk = A.alloc([64], F32)
              ss4s = [small[:, 528:532], small[:, 560:564]]
              rs4s = [small[:, 532:536], small[:, 564:568]]
              P.tt(gate.rearrange("p t (h v) -> p (t h) v", h=4), gate.rearrange("p t (h v) -> p (t h) v", h=4),
                   ggla.unsqueeze(1).broadcast_to([128, NT * 4, 64]), ALU.mult, eng="pool")

              def gla_m(n):
                  po = bank(4 + n % 2, 256)
                  P.matmul(po, qt[0][:, n * 128:(n + 1) * 128], Sbf[0][:, n, :], start=True, stop=False)
                  P.matmul(po, qt[1][:, n * 128:(n + 1) * 128], Sbf[1][:, n, :], start=False, stop=False)
                  for d_ in range(2):
                      ci = 2 * n + d_
                      qb = Qbd[ci % 4]
                      P.tt(qb, qt[d_][:, n * 128:(n + 1) * 128].unsqueeze(1).broadcast_to([128, 4, 128]),
                           hmask4.unsqueeze(2).broadcast_to([128, 4, 128]), ALU.mult, eng="pool")
                      P.matmul(bank(ci % 4), kt_[d_][:, n * 128:(n + 1) * 128], qb.rearrange("p a b -> p (a b)"),
                               start=True, stop=True)
                  for d_ in range(2):
                      ci = 2 * n + d_
                      asb = Asb[ci % 4]
                      P.tt(asb, bank(ci % 4).rearrange("p (a b) -> p a b", a=4),
                           tri[d_].unsqueeze(1).broadcast_to([128, 4, 128]), ALU.mult)
                      for h in range(4):
                          P.matmul(po[:, h * 64:(h + 1) * 64], asb[:, h, :], gv[:, n, h * 64:(h + 1) * 64],
                                   start=False, stop=(d_ == 1 and h == 3))

              def gla_a(n):
                  po = bank(4 + n % 2, 256)
                  ss4 = ss4s[n % 2]
                  rs4 = rs4s[n % 2]
                  for h in range(4):
                      P.act(junk, po[:, h * 64:(h + 1) * 64], AF.Square, accum_out=ss4[:, h:h + 1])
                  P.act(rs4, ss4, AF.Ln, bias=epsc, scale=1.0 / 64.0)
                  P.act(rs4, rs4, AF.Exp, scale=-0.5)
                  of = ogf[n % 2]
                  of3 = of.rearrange("p (h v) -> p h v", h=4)
                  P.tt(of3, po.rearrange("p (h v) -> p h v", h=4), rs4.unsqueeze(2).broadcast_to([128, 4, 64]),
                       ALU.mult)
                  ob = ogb[n % 3]
                  P.tt(ob, of, gate[:, n, :], ALU.mult)
                  for c_ in range(2):
                      dst_ = ocat[:, 6 + c_, n * 128:(n + 1) * 128]
                      P.add("sp", lambda e, dst_=dst_, src_=ob[:, c_ * 128:(c_ + 1) * 128]: e.dma_start(
                          out=dst_, in_=src_, transpose=True),
                          reads=[ob[:, c_ * 128:(c_ + 1) * 128]], writes=[dst_], dma=True)

              for s_ in range(NT + 1):
                  if s_ < NT:
                      gla_m(s_)
                  if s_ >= 1:
                      gla_a(s_ - 1)
              A.release(mixer_mark)
              if l == 0:
                  dump("ogla", ocat[:, 6:8, :], [128, 2, S], BF16)
              ck('gla')

              A.n = ARENA - XTB
              assert A.top <= A.n
              post_mark = A.mark()
              ytmp = [A.alloc([D], F32) for _ in range(2)]
              xbs = [A.alloc([D], BF16) for _ in range(2)]
              for t in range(NT):
                  P.dma("sp", x_tok[:, t, :], xres_d[t * 128:(t + 1) * 128, :])
              def mm1_(t):
                  b0 = (t % 3) * 2
                  p2 = ps[:, b0 * 512:(b0 + 2) * 512]
                  for hf in range(2):
                      for e_ in range(8):
                          P.matmul(p2[:, hf * 512:(hf + 1) * 512], ocat[:, e_, t * 128:(t + 1) * 128],
                                   Wout[:, e_, hf * 512:(hf + 1) * 512], start=(e_ == 0), stop=(e_ == 7))

              def a1_(t):
                  b0 = (t % 3) * 2
                  ln_a(ps[:, b0 * 512:(b0 + 2) * 512], x_tok[:, t, :], lnbuf, ytmp[t % 2], t % 2)

              def b1_(t):
                  ln_b(lnbuf, x_tok[:, t, :], ytmp[t % 2])
                  make_xT(x_tok[:, t, :], t, xbs[t % 2])

              for s_ in range(NT + 2):
                  if s_ < NT:
                      mm1_(s_)
                  if 0 <= s_ - 1 < NT:
                      a1_(s_ - 1)
                  if 0 <= s_ - 2 < NT:
                      b1_(s_ - 2)
              A.release(base_mark)
              if l == 0:
                  dump("x1", x_tok, [128, NT, D])
              ck('ln1')

              GF = NFC // 2
              Wdg = A.alloc([GF, D], BF16)
              hT = A.alloc([GF, S], BF16)
              NWB = 3
              wgu = [A.alloc([2, 8, 128], BF16) for _ in range(NWB)]
              sg = [A.alloc([512], BF16) for _ in range(2)]
              ytmp = [A.alloc([D], F32) for _ in range(1)]
              xbs = [A.alloc([D], BF16) for _ in range(2)]
              wdv = wd_d[l].rearrange("(f p) n -> p f n", p=128)
              wgv = wg_d[l].rearrange("(k p) n -> p k n", p=128)
              wuv = wu_d[l].rearrange("(k p) n -> p k n", p=128)
              P.dma("sp", lnbuf[:, 0, :], ln_d["ln2_g"][l].partition_broadcast(128))
              P.dma("sp", lnbuf[:, 1, :], ln_d["ln2_b"][l].partition_broadcast(128))
              wi = 0
              ui = 0
              for g_ in range(2):
                  for fi in range(GF):
                      f = g_ * GF + fi
                      wb = wgu[wi % NWB]
                      wi += 1
                      P.dma("pool", wb[:, 0, :, :], wgv[:, :, f * 128:(f + 1) * 128])
                      P.dma("pool", wb[:, 1, :, :], wuv[:, :, f * 128:(f + 1) * 128])
                      P.dma("pool", Wdg[:, fi, :], wdv[:, f, :])
                      for tb in range(4):
                          pg = bank((ui % 2) * 2)
                          pu = bank((ui % 2) * 2 + 1)
                          ui += 1
                          for k in range(8):
                              P.matmul(pg, wb[:, 0, k, :], xT[:, k, tb * 512:(tb + 1) * 512],
                                       start=(k == 0), stop=(k == 7))
                          for k in range(8):
                              P.matmul(pu, wb[:, 1, k, :], xT[:, k, tb * 512:(tb + 1) * 512],
                                       start=(k == 0), stop=(k == 7))
                          s_ = sg[ui % 2]
                          P.act(s_, pg, AF.Silu)
                          P.tt(hT[:, fi, tb * 512:(tb + 1) * 512], pu, s_, ALU.mult)
                  ytmps = [ytmp[0], wgu[0].rearrange("p a b c -> p (a b c)").bitcast(F32)]

                  def mm2_(t):
                      b0 = (t % 3) * 2
                      p2 = ps[:, b0 * 512:(b0 + 2) * 512]
                      for hf in range(2):
                          for fi in range(GF):
                              P.matmul(p2[:, hf * 512:(hf + 1) * 512], hT[:, fi, t * 128:(t + 1) * 128],
                                       Wdg[:, fi, hf * 512:(hf + 1) * 512], start=(fi == 0), stop=(fi == GF - 1))

                  def a2_(t):
                      b0 = (t % 3) * 2
                      p2 = ps[:, b0 * 512:(b0 + 2) * 512]
                      if g_ == 0:
                          P.stt(x_tok[:, t, :], x_tok[:, t, :], ALPHA, p2, ALU.mult, ALU.add)
                      else:
                          ln_a(p2, x_tok[:, t, :], lnbuf, ytmps[t % 2], t % 2, alpha=1.0)

                  def b2_(t):
                      if g_ == 0:
                          return
                      ln_b(lnbuf, x_tok[:, t, :], ytmps[t % 2])
                      if l == n_layers - 1:
                          P.dma("sp", y_d[t * 128:(t + 1) * 128, :], x_tok[:, t, :])
                      else:
                          P.dma("sp", xs_d[t * 128:(t + 1) * 128, :], x_tok[:, t, :])
                          make_xT(x_tok[:, t, :], t, xbs[t % 2])

                  for s_ in range(NT + 2):
                      if s_ < NT:
                          mm2_(s_)
                      if 0 <= s_ - 1 < NT:
                          a2_(s_ - 1)
                      if 0 <= s_ - 2 < NT:
                          b2_(s_ - 2)
              A.release(base_mark)
              A.n = ARENA

        except _Stop:
            pass
        P.emit()
    _CACHE['P'] = P
    return nc, dbg_outs


def kernel(**inputs):
    if "c" not in _CACHE:
        _CACHE["c"] = host_consts()
    cf, cb, dc, ds = _CACHE["c"]
    nc, _ = build()
    x = np.ascontiguousarray(inputs["x"], dtype=np.float32)
    common = {
        "w_in": np.ascontiguousarray(inputs["w_in"], dtype=np.float32),
        "diff_lambda": np.ascontiguousarray(inputs["diff_lambda"], dtype=np.float32).reshape(L, 256),
        "diff_norm_g": np.ascontiguousarray(inputs["diff_norm_g"], dtype=np.float32),
        "fourier_w": np.ascontiguousarray(inputs["fourier_w"], dtype=np.float32),
        "gla_gate_w2": np.ascontiguousarray(inputs["gla_gate_w2"], dtype=np.float32),
        "gla_gate_b2": np.ascontiguousarray(inputs["gla_gate_b2"], dtype=np.float32),
        "gla_norm_g": np.ascontiguousarray(inputs["gla_norm_g"], dtype=np.float32),
        "w_out": np.ascontiguousarray(inputs["w_out"], dtype=np.float32),
        "ln1_g": np.ascontiguousarray(inputs["ln1_g"], dtype=np.float32),
        "ln1_b": np.ascontiguousarray(inputs["ln1_b"], dtype=np.float32),
        "ln2_g": np.ascontiguousarray(inputs["ln2_g"], dtype=np.float32),
        "ln2_b": np.ascontiguousarray(inputs["ln2_b"], dtype=np.float32),
        "ffn_w_gate": np.ascontiguousarray(inputs["ffn_w_gate"], dtype=np.float32),
        "ffn_w_up": np.ascontiguousarray(inputs["ffn_w_up"], dtype=np.float32),
        "ffn_w_down": np.ascontiguousarray(inputs["ffn_w_down"], dtype=np.float32),
        "c_f32": cf, "c_bf": cb, "dft_c": dc, "dft_s": ds,
    }
    in_maps = [dict(common, x=x[b]) for b in range(8)]
    res = run_bass_kernel_spmd(nc, in_maps, core_ids=list(range(8)))
    return np.stack([np.asarray(r["y"], dtype=np.float32) for r in res.results], axis=0)
```

```python
import numpy as np
import concourse.bass as bass
import concourse.mybir as mybir

F32 = mybir.dt.float32
BF16 = mybir.dt.bfloat16
ALU = mybir.AluOpType
AF = mybir.ActivationFunctionType
AX = mybir.AxisListType

_DT_SIZE = {F32: 4, BF16: 2, mybir.dt.float32r: 4, mybir.dt.int32: 4,
            mybir.dt.uint32: 4, mybir.dt.float16: 2, mybir.dt.uint16: 2,
            mybir.dt.int16: 2, mybir.dt.uint8: 1, mybir.dt.int8: 1}

ENGS = ("pe", "act", "dve", "pool", "sp")
N_DMA_SEMS = 24
TINY_BYTES = 256
SAME_ENGINE_SYNC = False


def ap_box(ap):
    t = ap.tensor
    name = t.name
    esz = _DT_SIZE[ap.dtype]
    dims = list(ap.ap)
    off = ap.offset
    space = str(ap.space)
    if "DRAM" in space.upper() or "HBM" in space.upper():
        lo = off
        hi = off
        for (st, n) in dims:
            if st >= 0:
                hi += st * (n - 1)
            else:
                lo += st * (n - 1)
        return (name, 0, 1, lo * esz, (hi + 1) * esz)
    pstep, pcnt = dims[0]
    if pstep == 0:
        pstep = 1 << 40
    p0 = off // pstep if pstep < (1 << 40) else 0
    f = off - p0 * pstep if pstep < (1 << 40) else off
    lo = f
    hi = f
    for (st, n) in dims[1:]:
        if st >= 0:
            hi += st * (n - 1)
        else:
            lo += st * (n - 1)
    if "PSUM" in space.upper():
        b0 = (lo * esz) // 2048
        b1 = ((hi + 1) * esz - 1) // 2048
        return (name, 0, 128, b0 * 2048, (b1 + 1) * 2048, True)
    return (name, p0, p0 + pcnt, lo * esz, (hi + 1) * esz)


def _overlap(a, b):
    return a[1] < b[2] and b[1] < a[2] and a[3] < b[4] and b[3] < a[4]


def _contains(a, b):
    return a[1] <= b[1] and a[2] >= b[2] and a[3] <= b[3] and a[4] >= b[4]


class Prog:
    def __init__(self, nc):
        self.nc = nc
        self.ins = []
        self.recs = {}
        self.dma_rr = {e: 0 for e in ENGS}
        self.trace = {}

    def add(self, eng, fn, reads=(), writes=(), dma=False):
        idx = len(self.ins)
        rb = list(dict.fromkeys(ap_box(a) for a in reads))
        wb = list(dict.fromkeys(ap_box(a) for a in writes))
        deps = set()
        tiny_deps = set()
        for b in rb:
            psum = len(b) > 5
            tiny = (not psum) and (b[4] - b[3]) <= TINY_BYTES
            for rec in self.recs.get(b[0], ()):
                if (rec[1] or (psum and rec[3] != eng)) and _overlap(rec[0], b):
                    deps.add(rec[2])
                    if tiny and rec[1]:
                        tiny_deps.add(rec[2])
        for b in wb:
            for rec in self.recs.get(b[0], ()):
                if _overlap(rec[0], b):
                    deps.add(rec[2])
        for b in wb:
            lst = self.recs.setdefault(b[0], [])
            lst[:] = [r for r in lst if not _contains(b, r[0])]
            lst.append([b, True, idx, eng])
        for b in rb:
            lst = self.recs.setdefault(b[0], [])
            found = False
            for r in lst:
                if (not r[1]) and r[3] == eng and r[0] == b and not dma \
                        and not self.ins[r[2]]["dma"]:
                    r[2] = idx
                    found = True
                    break
            if not found:
                lst.append([b, False, idx, eng])
        real = set()
        for d in deps:
            p = self.ins[d]
            if p["eng"] == "pe" and eng == "pe" and not p["dma"] and not dma:
                continue
            if (not SAME_ENGINE_SYNC) and p["eng"] == eng and eng != "pool" and not p["dma"] and not dma \
                    and d not in tiny_deps:
                continue
            real.add(d)
        self.ins.append(dict(eng=eng, fn=fn, deps=real, dma=dma, needed=False,
                             sem=None, val=None))
        return idx

    def emit(self, final_wait_eng="sp"):
        nc = self.nc
        ins = self.ins
        for r in ins:
            for d in r["deps"]:
                ins[d]["needed"] = True
        last_dmas = [i for i, r in enumerate(ins) if r["dma"]]
        import contextlib
        with contextlib.ExitStack() as st:
            esem = {e: st.enter_context(nc.semaphore("s_" + e)) for e in ENGS}
            dsem = {e: [st.enter_context(nc.semaphore("d_%s_%d" % (e, i)))
                        for i in range(N_DMA_SEMS)] for e in ("sp", "act", "pool")}
            cnt = {e: 0 for e in ENGS}
            dcnt = {e: [0] * N_DMA_SEMS for e in dsem}
            drr = {e: 0 for e in dsem}
            prev_use = {}
            for i, r in enumerate(ins):
                e = r["eng"]
                if r["dma"]:
                    s = drr[e]
                    drr[e] = (s + 1) % N_DMA_SEMS
                    if dcnt[e][s] > 0:
                        r["prev"] = (dsem[e][s], dcnt[e][s], ("d", e, s))
                    else:
                        r["prev"] = None
                    dcnt[e][s] += 16
                    r["sem"] = dsem[e][s]
                    r["val"] = dcnt[e][s]
                    r["semkey"] = ("d", e, s)
                elif r["needed"]:
                    cnt[e] += 1
                    r["sem"] = esem[e]
                    r["val"] = cnt[e]
                    r["semkey"] = ("e", e)
            block = st.enter_context(nc.Block())
            per_eng = {e: [i for i, r in enumerate(ins) if r["eng"] == e] for e in ENGS}
            final = {}
            for i in last_dmas:
                r = ins[i]
                final[r["semkey"]] = (r["sem"], max(r["val"], final.get(r["semkey"], (None, 0))[1]))

            def body(ename):
                def run(engobj):
                    waited = {}
                    for i in per_eng[ename]:
                        r = ins[i]
                        need = {}
                        for d in r["deps"]:
                            p = ins[d]
                            k = p["semkey"]
                            if need.get(k, (None, 0))[1] < p["val"]:
                                need[k] = (p["sem"], p["val"])
                        if r["dma"] and r["prev"] is not None:
                            s, v, k = r["prev"]
                            if need.get(k, (None, 0))[1] < v:
                                need[k] = (s, v)
                        for k, (s, v) in need.items():
                            if waited.get(k, 0) < v:
                                engobj.wait_ge(s, v)
                                waited[k] = v
                                self.trace.setdefault(ename, []).append(("w", k, v, i))
                        h = r["fn"](engobj)
                        if r["dma"]:
                            h.then_inc(r["sem"], 16)
                            self.trace.setdefault(ename, []).append(("i", r["semkey"], 16, i))
                        elif r["needed"]:
                            h.then_inc(r["sem"], 1)
                            self.trace.setdefault(ename, []).append(("i", r["semkey"], 1, i))
                    if ename == final_wait_eng:
                        for k, (s, v) in final.items():
                            if waited.get(k, 0) < v:
                                engobj.wait_ge(s, v)
                return run

            block.tensor(body("pe"))
            block.scalar(body("act"))
            block.vector(body("dve"))
            block.gpsimd(body("pool"))
            block.sync(body("sp"))

    def dma(self, eng, out, in_, **kw):
        return self.add(eng, lambda e: e.dma_start(out=out, in_=in_, **kw),
                        reads=[in_], writes=[out], dma=True)

    def matmul(self, out, lhsT, rhs, start=True, stop=True, **kw):
        return self.add("pe", lambda e: e.matmul(out, lhsT, rhs, start=start, stop=stop, **kw),
                        reads=[lhsT, rhs], writes=[out])

    def transpose(self, out, in_, ident):
        return self.add("pe", lambda e: e.transpose(out, in_, ident),
                        reads=[in_, ident], writes=[out])

    def act(self, out, in_, func, bias=None, scale=1.0, accum_out=None, eng="act"):
        reads = [in_]
        writes = [out]
        kw = {}
        if bias is not None:
            kw["bias"] = bias
            if not isinstance(bias, (int, float)):
                reads.append(bias)
        if not isinstance(scale, (int, float)):
            reads.append(scale)
        if accum_out is not None:
            kw["accum_out"] = accum_out
            writes.append(accum_out)
        return self.add(eng, lambda e: e.activation(out=out, in_=in_, func=func, scale=scale, **kw),
                        reads=reads, writes=writes)

    def tt(self, out, in0, in1, op, eng="dve"):
        return self.add(eng, lambda e: e.tensor_tensor(out=out, in0=in0, in1=in1, op=op),
                        reads=[in0, in1], writes=[out])

    def ts(self, out, in0, s1, op0, s2=None, op1=None, eng="dve", accum_out=None):
        reads = [in0]
        if not isinstance(s1, (int, float)):
            reads.append(s1)
        if s2 is not None and not isinstance(s2, (int, float)):
            reads.append(s2)
        kw = {}
        writes = [out]
        if op1 is not None:
            kw["op1"] = op1
        if accum_out is not None:
            kw["accum_out"] = accum_out
            writes.append(accum_out)
        return self.add(eng, lambda e: e.tensor_scalar(out=out, in0=in0, scalar1=s1, scalar2=s2,
                                                       op0=op0, **kw),
                        reads=reads, writes=writes)

    def stt(self, out, in0, scalar, in1, op0, op1, eng="dve"):
        reads = [in0, in1]
        if not isinstance(scalar, (int, float)):
            reads.append(scalar)
        return self.add(eng, lambda e: e.scalar_tensor_tensor(out=out, in0=in0, scalar=scalar,
                                                              in1=in1, op0=op0, op1=op1),
                        reads=reads, writes=[out])

    def copy(self, out, in_, eng="dve"):
        if eng == "act":
            return self.add(eng, lambda e: e.activation(out=out, in_=in_, func=AF.Identity),
                            reads=[in_], writes=[out])
        return self.add(eng, lambda e: e.tensor_copy(out=out, in_=in_), reads=[in_], writes=[out])

    def memset(self, ap, val, eng="dve"):
        return self.add(eng, lambda e: e.memset(ap, val), reads=[], writes=[ap])

    def reduce(self, out, in_, op, axis=AX.X, eng="dve", **kw):
        return self.add(eng, lambda e: e.tensor_reduce(out=out, in_=in_, op=op, axis=axis, **kw),
                        reads=[in_], writes=[out])

    def recip(self, out, in_, eng="dve"):
        return self.add(eng, lambda e: e.reciprocal(out=out, in_=in_), reads=[in_], writes=[out])


def simulate_trace(trace):
    pos = {e: 0 for e in trace}
    sem = {}
    progress = True
    while progress:
        progress = False
        for e, ops in trace.items():
            while pos[e] < len(ops):
                kind, k, v, i = ops[pos[e]]
                if kind == "w":
                    if sem.get(k, 0) >= v:
                        pos[e] += 1
                        progress = True
                    else:
                        break
                else:
                    sem[k] = sem.get(k, 0) + v
                    pos[e] += 1
                    progress = True
    stuck = {e: ops[pos[e]] for e, ops in trace.items() if pos[e] < len(ops)}
    return stuck, sem

import contextlib
import os
_SK = set(os.environ.get('DBGSKIP', '').split(','))
import math
import ml_dtypes
from concourse.bass_utils import run_bass_kernel_spmd

S = 2048
D = 1024
L = 2
INW = 2592
FF = 2816
NT = 16
NFC = FF // 128
ALPHA = float((2 * L) ** 0.25)
EPS = 1e-5
C_DQ, C_DK, C_DV, C_FU, C_GQ, C_GK, C_GV, C_GR, C_GZ = 0, 512, 1024, 1536, 1792, 1920, 2048, 2304, 2560

CF_ID, CF_COS, CF_SIN, CF_HM, CF_CC, CF_SC, CF_ONES, CF_NEGH, CF_N = 0, 128, 640, 1152, 1156, 1284, 1412, 1540, 1548
CB_ID, CB_TRIF, CB_TRIB, CB_BD, CB_ONE, CB_N = 0, 128, 256, 384, 640, 768


def host_consts():
    p = np.arange(128)
    cf = np.zeros((128, CF_N), np.float32)
    cf[:, CF_ID:CF_ID + 128] = np.eye(128, dtype=np.float32)
    inv_freq = (10000.0 ** (-np.arange(0, 64, 2, dtype=np.float32) / 64)).astype(np.float32)
    pos = (np.arange(NT)[None, :] * 128 + p[:, None]).astype(np.float32)
    ang = pos[:, :, None] * inv_freq[None, None, :]
    cf[:, CF_COS:CF_COS + 512] = np.cos(ang).reshape(128, 512)
    cf[:, CF_SIN:CF_SIN + 512] = np.sin(ang).reshape(128, 512)
    cf[:, CF_HM:CF_HM + 4] = (p[:, None] // 32 == np.arange(4)[None, :]).astype(np.float32)
    c = np.arange(64)
    a = 2 * np.pi * np.outer(c, c) / 64.0
    cc = np.cos(a) / 8.0
    sc = np.sin(a) / 8.0
    z = np.zeros((64, 64))
    cf[:, CF_CC:CF_CC + 128] = np.block([[cc, z], [z, cc]])
    cf[:, CF_SC:CF_SC + 128] = np.block([[sc, z], [z, sc]])
    cf[:, CF_ONES:CF_ONES + 128] = 1.0
    cf[:, CF_NEGH:CF_NEGH + 8] = -0.5
    cb = np.zeros((128, CB_N), np.float32)
    cb[:, CB_ID:CB_ID + 128] = np.eye(128)
    cb[:, CB_TRIF:CB_TRIF + 128] = (p[:, None] <= p[None, :])
    cb[:, CB_TRIB:CB_TRIB + 128] = (p[:, None] >= p[None, :])
    cb[:, CB_BD:CB_BD + 256] = (p[:, None] // 32 == np.arange(256)[None, :] // 64)
    cb[:, CB_ONE:CB_ONE + 128] = 1.0
    s = np.arange(S, dtype=np.float64)
    sk = np.outer(s, s) % S
    ang = 2 * np.pi * sk / S
    dc = (np.cos(ang) / math.sqrt(S)).astype(np.float32).astype(ml_dtypes.bfloat16)
    ds = (-np.sin(ang) / math.sqrt(S)).astype(np.float32).astype(ml_dtypes.bfloat16)
    return cf, cb.astype(ml_dtypes.bfloat16), dc, ds


class Arena:
    def __init__(self, t, nbytes):
        self.t = t
        self.n = nbytes
        self.top = 0

    def mark(self):
        return self.top

    def release(self, m):
        self.top = m

    def alloc(self, shape, dt):
        n = 1
        for s_ in shape:
            n *= s_
        b = n * _DT_SIZE[dt]
        off = self.top
        self.top += (b + 63) // 64 * 64
        assert self.top <= self.n, ("arena overflow", self.top, self.n)
        ap = self.t[:, off // 2:(off + b) // 2]
        if dt != BF16:
            ap = ap.bitcast(dt)
        if len(shape) == 2:
            ap = ap.rearrange("p (a b) -> p a b", a=shape[0])
        elif len(shape) == 3:
            ap = ap.rearrange("p (a b c) -> p a b c", a=shape[0], b=shape[1])
        return ap


class Rot:
    def __init__(self, items):
        self.items = items
        self.i = 0

    def next(self):
        r = self.items[self.i % len(self.items)]
        self.i += 1
        return r


_CACHE = {}


class _Stop(Exception):
    pass


def build(n_layers=L, dbg=None, upto=None):
    nc = bass.Bass("TRN2", target_bir_lowering=False)
    dram = lambda name, shape, dt, kind="ExternalInput": nc.dram_tensor(name, shape, dt, kind=kind).ap()
    x_d = dram("x", [S, D], F32)
    w_in_d = dram("w_in", [L, D, INW], F32)
    lam_d = dram("diff_lambda", [L, 256], F32)
    dng_d = dram("diff_norm_g", [L, 128], F32)
    fw_d = dram("fourier_w", [L, 4, 64, 64], F32)
    w2_d = dram("gla_gate_w2", [L, 2, 16, 128], F32)
    b2_d = dram("gla_gate_b2", [L, 2, 128], F32)
    gng_d = dram("gla_norm_g", [L, 64], F32)
    wout_d = dram("w_out", [L, D, D], F32)
    ln_d = {k: dram(k, [L, D], F32) for k in ("ln1_g", "ln1_b", "ln2_g", "ln2_b")}
    wg_d = dram("ffn_w_gate", [L, D, FF], F32)
    wu_d = dram("ffn_w_up", [L, D, FF], F32)
    wd_d = dram("ffn_w_down", [L, FF, D], F32)
    cf_d = dram("c_f32", [128, CF_N], F32)
    cb_d = dram("c_bf", [128, CB_N], BF16)
    dftc_d = dram("dft_c", [S, S], BF16)
    dfts_d = dram("dft_s", [S, S], BF16)
    y_d = dram("y", [S, D], F32, kind="ExternalOutput")
    xs_d = dram("xs_scr", [S, D], F32, kind="Internal")
    dbg_outs = {}

    ARENA = 207 * 1024
    XTB = NT * D * 4
    with contextlib.ExitStack() as st:
        arena_t = st.enter_context(nc.sbuf_tensor("arena", [128, ARENA // 2], BF16))
        ps = st.enter_context(nc.psum_tensor("ps", [128, 4096], F32))
        P = Prog(nc)
        A = Arena(arena_t, ARENA)
        x_tok = arena_t[:, (ARENA - XTB) // 2:ARENA // 2].bitcast(F32).rearrange("p (t d) -> p t d", t=NT)

        def bank(b, n=512, off=0):
            return ps[:, b * 512 + off:b * 512 + off + n]

        def bankbf(b):
            return ps[:, b * 512:(b + 1) * 512].bitcast(BF16)

        def dump(name, ap, shape, dt=F32):
            if dbg is None or name not in dbg:
                return
            d_ = nc.dram_tensor("dbg_" + name, shape, dt, kind="ExternalOutput").ap()
            dbg_outs[name] = d_
            P.dma("sp", d_, ap)

        cf = A.alloc([CF_N], F32)
        cb = A.alloc([CB_N], BF16)
        xT = A.alloc([8, S], BF16)
        lnbuf = A.alloc([2, D], F32)
        small = A.alloc([640], F32)
        P.dma("sp", cf, cf_d)
        P.dma("sp", cb, cb_d)
        ident_f = cf[:, CF_ID:CF_ID + 128]
        ident_b = cb[:, CB_ID:CB_ID + 128]
        cos_t = cf[:, CF_COS:CF_COS + 512].rearrange("p (t i) -> p t i", t=NT)
        sin_t = cf[:, CF_SIN:CF_SIN + 512].rearrange("p (t i) -> p t i", t=NT)
        hmask4 = cf[:, CF_HM:CF_HM + 4]
        ccbd = cf[:, CF_CC:CF_CC + 128]
        scbd = cf[:, CF_SC:CF_SC + 128]
        ones_f = cf[:, CF_ONES:CF_ONES + 128]
        negh8 = cf[:, CF_NEGH:CF_NEGH + 8]
        tri = [cb[:, CB_TRIF:CB_TRIF + 128], cb[:, CB_TRIB:CB_TRIB + 128]]
        bdmask = cb[:, CB_BD:CB_BD + 256]
        lp_bc = small[:, 0:256]
        gdiff = small[:, 256:384]
        ggla = small[:, 384:448]
        negb2 = small[:, 448:450]
        neglam = small[:, 450:451]
        negM = small[:, 451:452]
        sc_tmp = small[:, 452:500]
        nrm = small[:, 500:504]
        epsc = small[:, 504:505]
        base_mark = A.mark()

        win = lambda l: w_in_d[l].rearrange("(k p) n -> p k n", p=128)

        def rstd_from(out, ss, n, width):
            tmp = sc_tmp[:, 40:40 + width]
            P.ts(tmp, ss, 1.0 / n, ALU.mult, EPS, ALU.add)
            P.tt(out, tmp, negh8[:, 0:width], ALU.pow, eng="pool")

        xb_rot = None

        def make_xT(x_tile_f32, t, xb, evac_eng="act", pbank=7):
            P.copy(xb, x_tile_f32, eng="act")
            pst = bankbf(pbank)
            for k in range(8):
                P.transpose(pst[:, k * 128:(k + 1) * 128], xb[:, k * 128:(k + 1) * 128], ident_b)
            P.copy(xT[:, :, t * 128:(t + 1) * 128], pst.rearrange("p (k n) -> p k n", k=8), eng=evac_eng)

        def ln_a(psum2, xres, gb, ytmp, par, alpha=ALPHA, mid=None):
            sct = sc_tmp[:, 0:16] if par == 0 else small[:, 540:556]
            P.stt(ytmp, xres, alpha, psum2, ALU.mult, ALU.add)
            stats = sct[:, 0:12]
            mv = sct[:, 12:14]
            for c_ in range(2):
                P.add("dve", lambda e, c_=c_: e.bn_stats(out=stats[:, c_ * 6:(c_ + 1) * 6],
                                                          in_=ytmp[:, c_ * 512:(c_ + 1) * 512]),
                      reads=[ytmp[:, c_ * 512:(c_ + 1) * 512]], writes=[stats[:, c_ * 6:(c_ + 1) * 6]])
            P.add("dve", lambda e: e.bn_aggr(out=mv, in_=stats), reads=[stats], writes=[mv])
            rs = sct[:, 14:15]
            nmr = sct[:, 15:16]
            P.act(rs, mv[:, 1:2], AF.Ln, bias=epsc)
            P.act(rs, rs, AF.Exp, scale=-0.5)
            if mid is not None:
                mid()
            P.stt(nmr, mv[:, 0:1], -1.0, rs, ALU.mult, ALU.mult)
            P.act(ytmp, ytmp, AF.Identity, bias=nmr, scale=rs)
            P.tt(ytmp, ytmp, gb[:, 0, :], ALU.mult, eng="pool")

        def ln_b(gb, out_tok, ytmp):
            P.tt(out_tok, ytmp, gb[:, 1, :], ALU.add)

        def ck(name):
            if upto == name:
                raise _Stop()

        try:
          for l in range(n_layers):
              lam_init = 0.8 - 0.6 * math.exp(-0.3 * l)
              A.release(base_mark)
              P.dma("sp", lp_bc, lam_d[l].partition_broadcast(128))
              P.dma("sp", gdiff, dng_d[l].partition_broadcast(128))
              P.dma("sp", ggla, gng_d[l].partition_broadcast(128))
              for d_ in range(2):
                  P.dma("sp", negb2[:, d_:d_ + 1], b2_d[l, d_].rearrange("(p o) -> p o", o=1))
              P.ts(negb2, negb2, -1.0, ALU.mult)
              P.ts(gdiff, gdiff, 1.0 - lam_init, ALU.mult)
              P.memset(epsc, EPS)
              pr = sc_tmp[:, 16:18]
              prod = A.alloc([128], F32)
              lp4 = lp_bc.rearrange("p (a d) -> p a d", a=4)
              for i_ in range(2):
                  P.tt(prod[:, 0:64], lp4[:, 2 * i_, :], lp4[:, 2 * i_ + 1, :], ALU.mult)
                  P.reduce(pr[:, i_:i_ + 1], prod[:, 0:64], ALU.add)
              P.act(pr, pr, AF.Exp)
              P.tt(neglam, pr[:, 1:2], pr[:, 0:1], ALU.subtract)
              P.ts(neglam, neglam, -lam_init, ALU.add)
              A.release(base_mark)
              ck('params')

              if l == 0:
                  m_ = A.mark()
                  xin = [A.alloc([D], F32) for _ in range(3)]
                  xbs = [A.alloc([D], BF16) for _ in range(2)]
                  for t in range(NT):
                      xi = xin[t % 3]
                      P.dma("sp", xi, x_d[t * 128:(t + 1) * 128, :])
                      make_xT(xi, t, xbs[t % 2], evac_eng="dve", pbank=(7 if t % 2 else 6))
                  A.release(m_)
              ck('xT')
              xres_d = x_d if l == 0 else xs_d

              ocat = A.alloc([8, S], BF16)
              Wout = A.alloc([8, D], BF16)
              Wgf = A.alloc([8, 288], BF16)
              Wgt = A.alloc([8, 512], BF16)
              wbd = A.alloc([2, 128], F32)
              Wcs = A.alloc([2, 256], BF16)
              mixer_mark = A.mark()
              NDB = 4
              DBB = 8 * 512 * 2
              dbuf = [arena_t[:, (ARENA - (i_ + 1) * DBB) // 2:(ARENA - i_ * DBB) // 2].rearrange(
                  "p (a b) -> p a b", a=8) for i_ in range(NDB)]
              dft_v = [dftc_d.rearrange("(t p) k -> p t k", p=128), dfts_d.rearrange("(t p) k -> p t k", p=128)]
              dft_chunks = [(kb, which, tg) for kb in range(4) for which in range(2) for tg in range(2)]

              def dft_load(j):
                  kb, which, tg = dft_chunks[j]
                  P.dma("sp", dbuf[j % NDB], dft_v[which][:, tg * 8:(tg + 1) * 8, kb * 512:(kb + 1) * 512])

              Wh = [A.alloc([8, 384], BF16) for _ in range(2)]
              QTz = A.alloc([2, S], BF16)
              KT = A.alloc([S], BF16)
              Vaug = A.alloc([NT, 132], BF16)
              PT = [A.alloc([1024], BF16) for _ in range(3)]
              O1n = A.alloc([8, 128], F32)
              O2t = A.alloc([8, 128], F32)
              otoks = [A.alloc([8, 128], BF16) for _ in range(2)]
              qkr = [A.alloc([256], BF16) for _ in range(3)]
              tmpAs = [A.alloc([128], F32) for _ in range(2)]
              tmpBs = [A.alloc([128], F32) for _ in range(2)]
              sq = A.alloc([256], F32)
              D2 = A.alloc([256], F32)
              sqs = [sq, A.alloc([256], F32)]
              red_all = A.alloc([NT, 4], F32)
              red4 = sc_tmp[:, 20:24]
              gm = sc_tmp[:, 24:26]
              nrm2 = sc_tmp[:, 26:28]
              rz = sc_tmp[:, 28:36]
              ss8 = small[:, 512:520]
              rstd8 = small[:, 520:528]
              P.memset(Vaug[:, :, 128:129], 1.0)
              P.memset(QTz[64:128, 0, :], 0.0)
              P.memset(QTz[0:64, 1, :], 0.0, eng="pool")
              pso = [ps[:, (4 + qi // 3) * 512 + (qi % 3) * 160:(4 + qi // 3) * 512 + (qi % 3) * 160 + 129]
                     for qi in range(8)]
              grp = [(0, 3), (3, 3), (6, 2)]

              def pso_grp(gi, c0, c1):
                  q0, n_ = grp[gi]
                  base_ = (4 + gi) * 512
                  return ps[:, base_:base_ + n_ * 160].rearrange("p (a b) -> p a b", b=160)[:, :, c0:c1]

              def load_wh(h_):
                  for j_, c0 in enumerate((C_DQ, C_DK, C_DV)):
                      P.dma("pool", Wh[h_ % 2][:, :, j_ * 128:(j_ + 1) * 128],
                            win(l)[:, :, c0 + h_ * 128:c0 + (h_ + 1) * 128])

              load_wh(0)
              fin_pending = []
              for h in range(4):
                  wh = Wh[h % 2]
                  P.memset(nrm, 0.0)
                  tr_pending = []
                  for t in range(NT):
                      pb = bank(t % 4, 256)
                      pv_ = bank(4 + t % 2, 128)
                      for k in range(8):
                          P.matmul(pb, xT[:, k, t * 128:(t + 1) * 128], wh[:, k, 0:256], start=(k == 0), stop=(k == 7))
                      for k in range(8):
                          P.matmul(pv_, xT[:, k, t * 128:(t + 1) * 128], wh[:, k, 256:384], start=(k == 0), stop=(k == 7))
                      qk4 = pb.rearrange("p (g h d) -> p g h d", g=4, h=2)
                      t1 = qk4[:, :, 0, :]
                      t2 = qk4[:, :, 1, :]
                      cbt = cos_t[:, t:t + 1, :].broadcast_to([128, 4, 32])
                      sbt = sin_t[:, t:t + 1, :].broadcast_to([128, 4, 32])
                      q_ = qkr[t % 3]
                      q4 = q_.rearrange("p (g h d) -> p g h d", g=4, h=2)
                      ta = tmpAs[t % 2].rearrange("p (g d) -> p g d", g=4)
                      tb_ = tmpBs[t % 2].rearrange("p (g d) -> p g d", g=4)
                      P.tt(ta, t1, cbt, ALU.mult)
                      P.tt(tb_, t2, sbt, ALU.mult)
                      P.tt(q4[:, :, 0, :], ta, tb_, ALU.subtract)
                      P.tt(ta, t2, cbt, ALU.mult)
                      P.tt(tb_, t1, sbt, ALU.mult)
                      P.tt(q4[:, :, 1, :], ta, tb_, ALU.add)
                      P.copy(Vaug[:, t, 0:128], pv_, eng="act")
                      P.tt(sqs[t % 2], q_, q_, ALU.mult, eng="pool")
                      def tr_(t=t, q_=q_, sq_=sqs[t % 2]):
                          P.reduce(red_all[:, t, :], sq_.rearrange("p (g d) -> p g d", g=4), ALU.add)
                          pst = bankbf(7 if t % 2 else 6)
                          P.transpose(pst[:, 0:128], q_[:, 0:128], ident_b)
                          P.transpose(pst[:, 128:256], q_[:, 128:256], ident_b)
                          P.copy(QTz[0:64, 0, t * 128:(t + 1) * 128], pst[0:64, 0:128], eng="act")
                          P.copy(QTz[64:128, 1, t * 128:(t + 1) * 128], pst[64:128, 0:128], eng="act")
                          P.copy(KT[:, t * 128:(t + 1) * 128], pst[:, 128:256], eng="act")
                      tr_pending.append(tr_)
                      if len(tr_pending) > 1:
                          tr_pending.pop(0)()
                  while tr_pending:
                      tr_pending.pop(0)()
                  P.reduce(nrm, red_all.rearrange("p t g -> p g t"), ALU.max)
                  ck('h0proj')
                  if h + 1 < 4:
                      load_wh(h + 1)
                  else:
                      Wf = Wh[0][:, :, 0:256]
                      P.dma("pool", Wf, win(l)[:, :, C_FU:C_FU + 256])
                      P.dma("pool", Wgf[:, :, 0:256], win(l)[:, :, C_GQ:C_GQ + 256])
                      P.dma("pool", Wgf[:, :, 256:288], win(l)[:, :, C_GZ:C_GZ + 32])
                      P.dma("pool", Wgt, win(l)[:, :, C_GV:C_GV + 512])
                      for k in range(8):
                          P.dma("pool", Wout[:, k, :], wout_d[l, k * 128:(k + 1) * 128, :])
                      P.memset(wbd, 0.0, eng="pool")
                      for g_ in range(4):
                          c_, gl = g_ // 2, g_ % 2
                          P.dma("sp", wbd[gl * 64:(gl + 1) * 64, c_, gl * 64:(gl + 1) * 64], fw_d[l, g_])
                      for c_ in range(2):
                          pb = bank(7, 256)
                          P.matmul(pb[:, 0:128], ccbd, wbd[:, c_, :], start=True, stop=True)
                          P.matmul(pb[:, 128:256], scbd, wbd[:, c_, :], start=True, stop=True)
                          P.copy(Wcs[:, c_, :], pb, eng="dve")
                      for j_ in range(NDB):
                          dft_load(j_)
                  P.reduce(nrm2, nrm.rearrange("p (a b) -> p a b", a=2), ALU.max)
                  P.ts(D2[:, 0:128], ident_f, nrm2[:, 0:1], ALU.mult)
                  P.ts(D2[:, 128:256], ident_f, nrm2[:, 1:2], ALU.mult)
                  pbm = bank(6, 256)
                  P.matmul(pbm, ones_f, D2, start=True, stop=True)
                  P.reduce(gm, pbm.rearrange("p (a b) -> p a b", a=2), ALU.max)
                  P.tt(negM, gm[:, 0:1], gm[:, 1:2], ALU.add)
                  P.ts(negM, negM, -0.5 * 0.125, ALU.mult)
                  if l == 0 and h == 0:
                      dump("xT", xT, [128, 8, S], BF16)
                      dump("negM", negM, [128, 1])
                      ck('qkt')
                  for qh in range(2):
                      otok = otoks[qh]
                      steps = [(m, kt) for m in range(2) for kt in range(NT)]

                      def emit_scores(i):
                          m, kt = steps[i]
                          sb_i = (i % 2) * 2
                          for j_ in range(2):
                              P.matmul(ps[:, (sb_i + j_) * 512:(sb_i + j_ + 1) * 512],
                                       KT[:, kt * 128:(kt + 1) * 128],
                                       QTz[:, m, qh * 1024 + j_ * 512:qh * 1024 + (j_ + 1) * 512],
                                       start=True, stop=True)

                      emit_scores(0)
                      for i, (m, kt) in enumerate(steps):
                          if i + 1 < len(steps):
                              emit_scores(i + 1)
                          if i == 8 and fin_pending:
                              fin_pending.pop(0)()
                          sb_i = (i % 2) * 2
                          pss = ps[:, sb_i * 512:(sb_i + 2) * 512]
                          pt = PT[i % 3]
                          P.act(pt, pss, AF.Exp, bias=negM, scale=0.125)
                          for qi in range(8):
                              P.matmul(pso[qi], pt[:, qi * 128:(qi + 1) * 128], Vaug[:, kt, 0:129],
                                       start=(kt == 0 and qi % 3 == 0), stop=(kt == NT - 1), skip_group_check=True)
                          if kt != NT - 1:
                              continue
                          if m == 0:
                              for gi, (q0, n_) in enumerate(grp):
                                  P.recip(rz[:, q0:q0 + n_], pso_grp(gi, 128, 129).rearrange("p a b -> p (a b)"))
                                  P.tt(O1n[:, q0:q0 + n_, :], pso_grp(gi, 0, 128),
                                       rz[:, q0:q0 + n_].unsqueeze(2).broadcast_to([128, n_, 128]), ALU.mult)
                          else:
                              for gi, (q0, n_) in enumerate(grp):
                                  P.recip(rz[:, q0:q0 + n_], pso_grp(gi, 128, 129).rearrange("p a b -> p (a b)"))
                              P.ts(rz, rz, neglam, ALU.mult)
                              for gi, (q0, n_) in enumerate(grp):
                                  P.tt(O2t[:, q0:q0 + n_, :], pso_grp(gi, 0, 128),
                                       rz[:, q0:q0 + n_].unsqueeze(2).broadcast_to([128, n_, 128]), ALU.mult)
                              P.tt(O1n, O1n, O2t, ALU.add)
                              P.tt(O2t, O1n, O1n, ALU.mult, eng="pool")
                              P.reduce(ss8, O2t, ALU.add)
                              rstd_from(rstd8, ss8, 128.0, 8)
                              P.tt(O1n, O1n, rstd8.unsqueeze(2).broadcast_to([128, 8, 128]), ALU.mult)
                              P.tt(otok, O1n, gdiff.unsqueeze(1).broadcast_to([128, 8, 128]), ALU.mult, eng="pool")
                              for qi in range(8):
                                  dst_ = ocat[:, h, qh * 1024 + qi * 128:qh * 1024 + (qi + 1) * 128]
                                  P.add("sp", lambda e, dst_=dst_, src_=otok[:, qi, :]: e.dma_start(
                                      out=dst_, in_=src_, transpose=True),
                                      reads=[otok[:, qi, :]], writes=[dst_], dma=True)
              while fin_pending:
                  fin_pending.pop(0)()
              A.release(mixer_mark)
              if l == 0:
                  dump("odiff", ocat[:, 0:4, :], [128, 4, S], BF16)
              ck('att')

              _skip = A.alloc([2 * 8 * 384], BF16)
              uT = A.alloc([2, S], BF16)
              ucs = A.alloc([NT, 512], BF16)
              P.dma("sp", lnbuf[:, 0, :], ln_d["ln1_g"][l].partition_broadcast(128))
              P.dma("sp", lnbuf[:, 1, :], ln_d["ln1_b"][l].partition_broadcast(128))
              for tb in range(4):
                  for c_ in range(2):
                      pb = bank((tb * 2 + c_) % 4)
                      for k in range(8):
                          P.matmul(pb, Wf[:, k, c_ * 128:(c_ + 1) * 128], xT[:, k, tb * 512:(tb + 1) * 512],
                                   start=(k == 0), stop=(k == 7))
                      P.copy(uT[:, c_, tb * 512:(tb + 1) * 512], pb, eng="act")
              for t in range(NT):
                  pb = bank(4 + t % 2)
                  for c_ in range(2):
                      P.matmul(pb[:, c_ * 256:(c_ + 1) * 256], uT[:, c_, t * 128:(t + 1) * 128], Wcs[:, c_, :],
                               start=True, stop=True)
                  P.copy(ucs[:, t, :], pb, eng="act")
                  if t == 7:
                      ucs_half = True
              for j_, (kb, which, tg) in enumerate(dft_chunks):
                  pbs = [bank(0 + (kb % 2) * 2), bank(1 + (kb % 2) * 2)]
                  db = dbuf[j_ % NDB]
                  for tt_ in range(8):
                      t = tg * 8 + tt_
                      first = (which == 0 and t == 0)
                      last = (which == 1 and t == NT - 1)
                      for c_ in range(2):
                          P.matmul(pbs[c_], ucs[:, t, c_ * 256 + which * 128:c_ * 256 + (which + 1) * 128],
                                   db[:, tt_, :], start=first, stop=last)
                  if j_ + NDB < len(dft_chunks):
                      dft_load(j_ + NDB)
                  if which == 1 and tg == 1:
                      for c_ in range(2):
                          P.copy(ocat[:, 4 + c_, kb * 512:(kb + 1) * 512], pbs[c_], eng=("act" if c_ else "dve"))
              A.release(mixer_mark)
              if l == 0:
                  dump("ofour", ocat[:, 4:6, :], [128, 2, S], BF16)
              ck('four')

              gqk = A.alloc([2, S], F32)
              gzT = A.alloc([S], BF16)
              gv = A.alloc([NT, 256], BF16)
              gate = A.alloc([NT, 256], BF16)
              w2pad = A.alloc([2, 128], BF16)
              qt = [A.alloc([S], BF16) for _ in range(2)]
              kt_ = [A.alloc([S], BF16) for _ in range(2)]
              Sbf = [A.alloc([NT, 256], BF16) for _ in range(2)]
              gla_mark = A.mark()
              P.memset(w2pad, 0.0)
              for d_ in range(2):
                  P.dma("pool", w2pad[d_ * 16:(d_ + 1) * 16, d_, :], w2_d[l, d_])
              for tb in range(4):
                  for c_ in range(2):
                      pb = bank((tb * 3 + c_) % 4)
                      for k in range(8):
                          P.matmul(pb, Wgf[:, k, c_ * 128:(c_ + 1) * 128], xT[:, k, tb * 512:(tb + 1) * 512],
                                   start=(k == 0), stop=(k == 7))
                      P.copy(gqk[:, c_, tb * 512:(tb + 1) * 512], pb, eng=("act" if c_ else "dve"))
                  pb = bank((tb * 3 + 2) % 4)
                  for k in range(8):
                      P.matmul(pb[0:32, :], Wgf[:, k, 256:288], xT[:, k, tb * 512:(tb + 1) * 512],
                               start=(k == 0), stop=(k == 7))
                  P.copy(gzT[0:32, tb * 512:(tb + 1) * 512], pb[0:32, :], eng="dve")
              for t in range(NT):
                  pb = bank(4 + t % 2)
                  for k in range(8):
                      P.matmul(pb, xT[:, k, t * 128:(t + 1) * 128], Wgt[:, k, :], start=(k == 0), stop=(k == 7))
                  P.copy(gv[:, t, :], pb[:, 0:256], eng="act")
                  P.act(gate[:, t, :], pb[:, 256:512], AF.Silu)
              A.release(gla_mark)
              Bc = A.alloc([S], F32)
              Ec = A.alloc([S], F32)
              kdec_tok = A.alloc([NT, 128], BF16)
              kdT = [A.alloc([128], BF16) for _ in range(2)]
              Srot = [A.alloc([256], F32) for _ in range(2)]
              for d_ in range(2):
                  for tb in range(4):
                      pb = bank(tb % 4)
                      P.matmul(pb, w2pad[0:32, d_, :], gzT[0:32, tb * 512:(tb + 1) * 512], start=True, stop=True)
                      P.act(Ec[:, tb * 512:(tb + 1) * 512], pb, AF.Exp, bias=negb2[:, d_:d_ + 1], scale=-1.0)
                  P.act(Ec, Ec, AF.Ln, bias=1.0)
                  for n in range(NT):
                      o_ = Bc[:, n * 128:(n + 1) * 128]
                      i_ = Ec[:, n * 128:(n + 1) * 128]
                      if d_ == 1:
                          o_ = o_[:, ::-1]
                          i_ = i_[:, ::-1]
                      P.add("dve", lambda e, o_=o_, i_=i_: e.tensor_tensor_scan(
                          out=o_, data0=ones_f, data1=i_, initial=0.0, op0=ALU.mult, op1=ALU.add),
                          reads=[ones_f, Ec[:, n * 128:(n + 1) * 128]], writes=[Bc[:, n * 128:(n + 1) * 128]])
                  P.act(Ec, Bc, AF.Exp, scale=-1.0 / 16.0)
                  P.act(Bc, Bc, AF.Exp, scale=1.0 / 16.0)
                  P.stt(qt[d_], gqk[:, 0, :], 32.0 ** -0.5, Ec, ALU.mult, ALU.mult)
                  P.tt(kt_[d_], gqk[:, 1, :], Bc, ALU.mult)
                  elast = [Ec[:, n * 128 + (127 if d_ == 0 else 0):n * 128 + (127 if d_ == 0 else 0) + 1]
                           for n in range(NT)]
                  pst = bankbf(7)
                  for n in range(NT):
                      kd = kdT[n % 2]
                      P.stt(kd, gqk[:, 1, n * 128:(n + 1) * 128], elast[n], Bc[:, n * 128:(n + 1) * 128],
                            ALU.mult, ALU.mult)
                      P.transpose(pst[:, (n % 8) * 128:(n % 8 + 1) * 128], kd, ident_b)
                      if n % 8 == 7:
                          P.copy(kdec_tok[:, n - 7:n + 1, :], pst.rearrange("p (a b) -> p a b", a=8), eng="act")
                  order = list(range(NT)) if d_ == 0 else list(range(NT - 1, -1, -1))
                  prev = Srot[0]
                  P.memset(prev, 0.0)
                  P.memset(Sbf[d_][:, order[0], :], 0.0)
                  for i_, n in enumerate(order[:-1]):
                      pb = bank(4 + i_ % 2, 256)
                      P.matmul(pb, kdec_tok[:, n, :], gv[:, n, :], start=True, stop=True)
                      cur = Srot[(i_ + 1) % 2]
                      P.stt(cur, prev, elast[n], pb, ALU.mult, ALU.add)
                      P.tt(Sbf[d_][:, order[i_ + 1], :], cur, bdmask, ALU.mult, eng="pool")
                      prev = cur
              A.release(gla_mark)
              Qbd = [A.alloc([4, 128], BF16) for _ in range(4)]
              Asb = [A.alloc([4, 128], BF16) for _ in range(4)]
              ogf = [A.alloc([256], F32) for _ in range(2)]
              ogb = [A.alloc([256], BF16) for _ in range(3)]
              junk = A.alloc([64], F32)
              ss4s = [small[:, 528:532], small[:, 560:564]]
              rs4s = [small[:, 532:536], small[:, 564:568]]
              P.tt(gate.rearrange("p t (h v) -> p (t h) v", h=4), gate.rearrange("p t (h v) -> p (t h) v", h=4),
                   ggla.unsqueeze(1).broadcast_to([128, NT * 4, 64]), ALU.mult, eng="pool")

              def gla_m(n):
                  po = bank(4 + n % 2, 256)
                  P.matmul(po, qt[0][:, n * 128:(n + 1) * 128], Sbf[0][:, n, :], start=True, stop=False)
                  P.matmul(po, qt[1][:, n * 128:(n + 1) * 128], Sbf[1][:, n, :], start=False, stop=False)
                  for d_ in range(2):
                      ci = 2 * n + d_
                      qb = Qbd[ci % 4]
                      P.tt(qb, qt[d_][:, n * 128:(n + 1) * 128].unsqueeze(1).broadcast_to([128, 4, 128]),
                           hmask4.unsqueeze(2).broadcast_to([128, 4, 128]), ALU.mult, eng="pool")
                      P.matmul(bank(ci % 4), kt_[d_][:, n * 128:(n + 1) * 128], qb.rearrange("p a b -> p (a b)"),
                               start=True, stop=True)
                  for d_ in range(2):
                      ci = 2 * n + d_
                      asb = Asb[ci % 4]
                      P.tt(asb, bank(ci % 4).rearrange("p (a b) -> p a b", a=4),
                           tri[d_].unsqueeze(1).broadcast_to([128, 4, 128]), ALU.mult)
                      for h in range(4):
                          P.matmul(po[:, h * 64:(h + 1) * 64], asb[:, h, :], gv[:, n, h * 64:(h + 1) * 64],
                                   start=False, stop=(d_ == 1 and h == 3))

              def gla_a(n):
                  po = bank(4 + n % 2, 256)
                  ss4 = ss4s[n % 2]
                  rs4 = rs4s[n % 2]
                  for h in range(4):
                      P.act(junk, po[:, h * 64:(h + 1) * 64], AF.Square, accum_out=ss4[:, h:h + 1])
                  P.act(rs4, ss4, AF.Ln, bias=epsc, scale=1.0 / 64.0)
                  P.act(rs4, rs4, AF.Exp, scale=-0.5)
                  of = ogf[n % 2]
                  of3 = of.rearrange("p (h v) -> p h v", h=4)
                  P.tt(of3, po.rearrange("p (h v) -> p h v", h=4), rs4.unsqueeze(2).broadcast_to([128, 4, 64]),
                       ALU.mult)
                  ob = ogb[n % 3]
                  P.tt(ob, of, gate[:, n, :], ALU.mult)
                  for c_ in range(2):
                      dst_ = ocat[:, 6 + c_, n * 128:(n + 1) * 128]
                      P.add("sp", lambda e, dst_=dst_, src_=ob[:, c_ * 128:(c_ + 1) * 128]: e.dma_start(
                          out=dst_, in_=src_, transpose=True),
                          reads=[ob[:, c_ * 128:(c_ + 1) * 128]], writes=[dst_], dma=True)

              for s_ in range(NT + 1):
                  if s_ < NT:
                      gla_m(s_)
                  if s_ >= 1:
                      gla_a(s_ - 1)
              A.release(mixer_mark)
              if l == 0:
                  dump("ogla", ocat[:, 6:8, :], [128, 2, S], BF16)
              ck('gla')

              A.n = ARENA - XTB
              assert A.top <= A.n
              post_mark = A.mark()
              ytmp = [A.alloc([D], F32) for _ in range(3)]
              xbs = [A.alloc([D], BF16) for _ in range(2)]
              for t in range(NT):
                  P.dma("sp", x_tok[:, t, :], xres_d[t * 128:(t + 1) * 128, :])
              def mm1_(t):
                  b0 = (t % 3) * 2
                  p2 = ps[:, b0 * 512:(b0 + 2) * 512]
                  for hf in range(2):
                      for e_ in range(8):
                          P.matmul(p2[:, hf * 512:(hf + 1) * 512], ocat[:, e_, t * 128:(t + 1) * 128],
                                   Wout[:, e_, hf * 512:(hf + 1) * 512], start=(e_ == 0), stop=(e_ == 7))

              def a1_(t, mid=None):
                  b0 = (t % 3) * 2
                  ln_a(ps[:, b0 * 512:(b0 + 2) * 512], x_tok[:, t, :], lnbuf, ytmp[t % 3], t % 2, mid=mid)

              def b1_(t):
                  ln_b(lnbuf, x_tok[:, t, :], ytmp[t % 3])

              def c1_(t):
                  make_xT(x_tok[:, t, :], t, xbs[t % 2])

              for s_ in range(NT + 3):
                  if s_ < NT:
                      mm1_(s_)
                  midf = (lambda t_=s_ - 3: b1_(t_)) if 0 <= s_ - 3 < NT else None
                  if 0 <= s_ - 1 < NT:
                      a1_(s_ - 1, mid=midf)
                  elif midf is not None:
                      midf()
                  if 0 <= s_ - 3 < NT:
                      c1_(s_ - 3)
              A.release(base_mark)
              if l == 0:
                  dump("x1", x_tok, [128, NT, D])
              ck('ln1')

              GF = NFC // 2
              Wdg = A.alloc([GF, D], BF16)
              hT = A.alloc([GF, S], BF16)
              NWB = 3
              wgu = [A.alloc([2, 8, 128], BF16) for _ in range(NWB)]
              sg = [A.alloc([512], BF16) for _ in range(2)]
              ytmp = [A.alloc([D], F32) for _ in range(1)]
              xbs = [A.alloc([D], BF16) for _ in range(2)]
              wdv = wd_d[l].rearrange("(f p) n -> p f n", p=128)
              wgv = wg_d[l].rearrange("(k p) n -> p k n", p=128)
              wuv = wu_d[l].rearrange("(k p) n -> p k n", p=128)
              P.dma("sp", lnbuf[:, 0, :], ln_d["ln2_g"][l].partition_broadcast(128))
              P.dma("sp", lnbuf[:, 1, :], ln_d["ln2_b"][l].partition_broadcast(128))
              wi = 0
              ui = 0
              for g_ in range(2):
                  for fi in range(GF):
                      f = g_ * GF + fi
                      wb = wgu[wi % NWB]
                      wi += 1
                      P.dma("pool", wb[:, 0, :, :], wgv[:, :, f * 128:(f + 1) * 128])
                      P.dma("pool", wb[:, 1, :, :], wuv[:, :, f * 128:(f + 1) * 128])
                      P.dma("pool", Wdg[:, fi, :], wdv[:, f, :])
                      for tb in range(4):
                          pg = bank((ui % 2) * 2)
                          pu = bank((ui % 2) * 2 + 1)
                          ui += 1
                          for k in range(8):
                              P.matmul(pg, wb[:, 0, k, :], xT[:, k, tb * 512:(tb + 1) * 512],
                                       start=(k == 0), stop=(k == 7))
                          for k in range(8):
                              P.matmul(pu, wb[:, 1, k, :], xT[:, k, tb * 512:(tb + 1) * 512],
                                       start=(k == 0), stop=(k == 7))
                          s_ = sg[ui % 2]
                          P.act(s_, pg, AF.Silu)
                          P.tt(hT[:, fi, tb * 512:(tb + 1) * 512], pu, s_, ALU.mult)
                  ytmps = [ytmp[0], wgu[0].rearrange("p a b c -> p (a b c)").bitcast(F32),
                           wgu[1].rearrange("p a b c -> p (a b c)").bitcast(F32)]

                  def mm2_(t):
                      b0 = (t % 3) * 2
                      p2 = ps[:, b0 * 512:(b0 + 2) * 512]
                      for hf in range(2):
                          for fi in range(GF):
                              P.matmul(p2[:, hf * 512:(hf + 1) * 512], hT[:, fi, t * 128:(t + 1) * 128],
                                       Wdg[:, fi, hf * 512:(hf + 1) * 512], start=(fi == 0), stop=(fi == GF - 1))

                  def a2_(t, mid=None):
                      b0 = (t % 3) * 2
                      p2 = ps[:, b0 * 512:(b0 + 2) * 512]
                      if g_ == 0:
                          P.stt(x_tok[:, t, :], x_tok[:, t, :], ALPHA, p2, ALU.mult, ALU.add)
                          if mid is not None:
                              mid()
                      else:
                          ln_a(p2, x_tok[:, t, :], lnbuf, ytmps[t % 3], t % 2, alpha=1.0, mid=mid)

                  def b2_(t):
                      if g_ == 0:
                          return
                      ln_b(lnbuf, x_tok[:, t, :], ytmps[t % 3])

                  def c2_(t):
                      if g_ == 0:
                          return
                      if l == n_layers - 1:
                          P.dma("sp", y_d[t * 128:(t + 1) * 128, :], x_tok[:, t, :])
                      else:
                          P.dma("sp", xs_d[t * 128:(t + 1) * 128, :], x_tok[:, t, :])
                          make_xT(x_tok[:, t, :], t, xbs[t % 2])

                  for s_ in range(NT + 3):
                      if s_ < NT:
                          mm2_(s_)
                      midf = (lambda t_=s_ - 3: b2_(t_)) if 0 <= s_ - 3 < NT else None
                      if 0 <= s_ - 1 < NT:
                          a2_(s_ - 1, mid=midf)
                      elif midf is not None:
                          midf()
                      if 0 <= s_ - 3 < NT:
                          c2_(s_ - 3)
              A.release(base_mark)
              A.n = ARENA

        except _Stop:
            pass
        P.emit()
    _CACHE['P'] = P
    return nc, dbg_outs


def kernel(**inputs):
    if "c" not in _CACHE:
        _CACHE["c"] = host_consts()
    cf, cb, dc, ds = _CACHE["c"]
    nc, _ = build()
    x = np.ascontiguousarray(inputs["x"], dtype=np.float32)
    common = {
        "w_in": np.ascontiguousarray(inputs["w_in"], dtype=np.float32),
        "diff_lambda": np.ascontiguousarray(inputs["diff_lambda"], dtype=np.float32).reshape(L, 256),
        "diff_norm_g": np.ascontiguousarray(inputs["diff_norm_g"], dtype=np.float32),
        "fourier_w": np.ascontiguousarray(inputs["fourier_w"], dtype=np.float32),
        "gla_gate_w2": np.ascontiguousarray(inputs["gla_gate_w2"], dtype=np.float32),
        "gla_gate_b2": np.ascontiguousarray(inputs["gla_gate_b2"], dtype=np.float32),
        "gla_norm_g": np.ascontiguousarray(inputs["gla_norm_g"], dtype=np.float32),
        "w_out": np.ascontiguousarray(inputs["w_out"], dtype=np.float32),
        "ln1_g": np.ascontiguousarray(inputs["ln1_g"], dtype=np.float32),
        "ln1_b": np.ascontiguousarray(inputs["ln1_b"], dtype=np.float32),
        "ln2_g": np.ascontiguousarray(inputs["ln2_g"], dtype=np.float32),
        "ln2_b": np.ascontiguousarray(inputs["ln2_b"], dtype=np.float32),
        "ffn_w_gate": np.ascontiguousarray(inputs["ffn_w_gate"], dtype=np.float32),
        "ffn_w_up": np.ascontiguousarray(inputs["ffn_w_up"], dtype=np.float32),
        "ffn_w_down": np.ascontiguousarray(inputs["ffn_w_down"], dtype=np.float32),
        "c_f32": cf, "c_bf": cb, "dft_c": dc, "dft_s": ds,
    }
    in_maps = [dict(common, x=x[b]) for b in range(8)]
    res = run_bass_kernel_spmd(nc, in_maps, core_ids=list(range(8)))
    return np.stack([np.asarray(r["y"], dtype=np.float32) for r in res.results], axis=0)
```

```python
import numpy as np
import concourse.bass as bass
import concourse.mybir as mybir

F32 = mybir.dt.float32
BF16 = mybir.dt.bfloat16
ALU = mybir.AluOpType
AF = mybir.ActivationFunctionType
AX = mybir.AxisListType

_DT_SIZE = {F32: 4, BF16: 2, mybir.dt.float32r: 4, mybir.dt.int32: 4,
            mybir.dt.uint32: 4, mybir.dt.float16: 2, mybir.dt.uint16: 2,
            mybir.dt.int16: 2, mybir.dt.uint8: 1, mybir.dt.int8: 1}

ENGS = ("pe", "act", "dve", "pool", "sp")
N_DMA_SEMS = 24
TINY_BYTES = 256
SAME_ENGINE_SYNC = False


def ap_box(ap):
    t = ap.tensor
    name = t.name
    esz = _DT_SIZE[ap.dtype]
    dims = list(ap.ap)
    off = ap.offset
    space = str(ap.space)
    if "DRAM" in space.upper() or "HBM" in space.upper():
        lo = off
        hi = off
        for (st, n) in dims:
            if st >= 0:
                hi += st * (n - 1)
            else:
                lo += st * (n - 1)
        return (name, 0, 1, lo * esz, (hi + 1) * esz)
    pstep, pcnt = dims[0]
    if pstep == 0:
        pstep = 1 << 40
    p0 = off // pstep if pstep < (1 << 40) else 0
    f = off - p0 * pstep if pstep < (1 << 40) else off
    lo = f
    hi = f
    for (st, n) in dims[1:]:
        if st >= 0:
            hi += st * (n - 1)
        else:
            lo += st * (n - 1)
    if "PSUM" in space.upper():
        b0 = (lo * esz) // 2048
        b1 = ((hi + 1) * esz - 1) // 2048
        return (name, 0, 128, b0 * 2048, (b1 + 1) * 2048, True)
    return (name, p0, p0 + pcnt, lo * esz, (hi + 1) * esz)


def _overlap(a, b):
    return a[1] < b[2] and b[1] < a[2] and a[3] < b[4] and b[3] < a[4]


def _contains(a, b):
    return a[1] <= b[1] and a[2] >= b[2] and a[3] <= b[3] and a[4] >= b[4]


class Prog:
    def __init__(self, nc):
        self.nc = nc
        self.ins = []
        self.recs = {}
        self.dma_rr = {e: 0 for e in ENGS}
        self.trace = {}

    def add(self, eng, fn, reads=(), writes=(), dma=False):
        idx = len(self.ins)
        rb = list(dict.fromkeys(ap_box(a) for a in reads))
        wb = list(dict.fromkeys(ap_box(a) for a in writes))
        deps = set()
        tiny_deps = set()
        for b in rb:
            psum = len(b) > 5
            tiny = (not psum) and (b[4] - b[3]) <= TINY_BYTES
            for rec in self.recs.get(b[0], ()):
                if (rec[1] or (psum and rec[3] != eng)) and _overlap(rec[0], b):
                    deps.add(rec[2])
                    if tiny and rec[1]:
                        tiny_deps.add(rec[2])
        for b in wb:
            for rec in self.recs.get(b[0], ()):
                if _overlap(rec[0], b):
                    deps.add(rec[2])
        for b in wb:
            lst = self.recs.setdefault(b[0], [])
            lst[:] = [r for r in lst if not _contains(b, r[0])]
            lst.append([b, True, idx, eng])
        for b in rb:
            lst = self.recs.setdefault(b[0], [])
            found = False
            for r in lst:
                if (not r[1]) and r[3] == eng and r[0] == b and not dma \
                        and not self.ins[r[2]]["dma"]:
                    r[2] = idx
                    found = True
                    break
            if not found:
                lst.append([b, False, idx, eng])
        real = set()
        for d in deps:
            p = self.ins[d]
            if p["eng"] == "pe" and eng == "pe" and not p["dma"] and not dma:
                continue
            if (not SAME_ENGINE_SYNC) and p["eng"] == eng and eng != "pool" and not p["dma"] and not dma \
                    and d not in tiny_deps:
                continue
            real.add(d)
        self.ins.append(dict(eng=eng, fn=fn, deps=real, dma=dma, needed=False,
                             sem=None, val=None))
        return idx

    def emit(self, final_wait_eng="sp"):
        nc = self.nc
        ins = self.ins
        for r in ins:
            for d in r["deps"]:
                ins[d]["needed"] = True
        last_dmas = [i for i, r in enumerate(ins) if r["dma"]]
        import contextlib
        with contextlib.ExitStack() as st:
            esem = {e: st.enter_context(nc.semaphore("s_" + e)) for e in ENGS}
            dsem = {e: [st.enter_context(nc.semaphore("d_%s_%d" % (e, i)))
                        for i in range(N_DMA_SEMS)] for e in ("sp", "act", "pool")}
            cnt = {e: 0 for e in ENGS}
            dcnt = {e: [0] * N_DMA_SEMS for e in dsem}
            drr = {e: 0 for e in dsem}
            prev_use = {}
            for i, r in enumerate(ins):
                e = r["eng"]
                if r["dma"]:
                    s = drr[e]
                    drr[e] = (s + 1) % N_DMA_SEMS
                    if dcnt[e][s] > 0:
                        r["prev"] = (dsem[e][s], dcnt[e][s], ("d", e, s))
                    else:
                        r["prev"] = None
                    dcnt[e][s] += 16
                    r["sem"] = dsem[e][s]
                    r["val"] = dcnt[e][s]
                    r["semkey"] = ("d", e, s)
                elif r["needed"]:
                    cnt[e] += 1
                    r["sem"] = esem[e]
                    r["val"] = cnt[e]
                    r["semkey"] = ("e", e)
            block = st.enter_context(nc.Block())
            per_eng = {e: [i for i, r in enumerate(ins) if r["eng"] == e] for e in ENGS}
            final = {}
            for i in last_dmas:
                r = ins[i]
                final[r["semkey"]] = (r["sem"], max(r["val"], final.get(r["semkey"], (None, 0))[1]))

            def body(ename):
                def run(engobj):
                    waited = {}
                    for i in per_eng[ename]:
                        r = ins[i]
                        need = {}
                        for d in r["deps"]:
                            p = ins[d]
                            k = p["semkey"]
                            if need.get(k, (None, 0))[1] < p["val"]:
                                need[k] = (p["sem"], p["val"])
                        if r["dma"] and r["prev"] is not None:
                            s, v, k = r["prev"]
                            if need.get(k, (None, 0))[1] < v:
                                need[k] = (s, v)
                        for k, (s, v) in need.items():
                            if waited.get(k, 0) < v:
                                engobj.wait_ge(s, v)
                                waited[k] = v
                                self.trace.setdefault(ename, []).append(("w", k, v, i))
                        h = r["fn"](engobj)
                        if r["dma"]:
                            h.then_inc(r["sem"], 16)
                            self.trace.setdefault(ename, []).append(("i", r["semkey"], 16, i))
                        elif r["needed"]:
                            h.then_inc(r["sem"], 1)
                            self.trace.setdefault(ename, []).append(("i", r["semkey"], 1, i))
                    if ename == final_wait_eng:
                        for k, (s, v) in final.items():
                            if waited.get(k, 0) < v:
                                engobj.wait_ge(s, v)
                return run

            block.tensor(body("pe"))
            block.scalar(body("act"))
            block.vector(body("dve"))
            block.gpsimd(body("pool"))
            block.sync(body("sp"))

    def dma(self, eng, out, in_, **kw):
        return self.add(eng, lambda e: e.dma_start(out=out, in_=in_, **kw),
                        reads=[in_], writes=[out], dma=True)

    def matmul(self, out, lhsT, rhs, start=True, stop=True, **kw):
        return self.add("pe", lambda e: e.matmul(out, lhsT, rhs, start=start, stop=stop, **kw),
                        reads=[lhsT, rhs], writes=[out])

    def transpose(self, out, in_, ident):
        return self.add("pe", lambda e: e.transpose(out, in_, ident),
                        reads=[in_, ident], writes=[out])

    def act(self, out, in_, func, bias=None, scale=1.0, accum_out=None, eng="act"):
        reads = [in_]
        writes = [out]
        kw = {}
        if bias is not None:
            kw["bias"] = bias
            if not isinstance(bias, (int, float)):
                reads.append(bias)
        if not isinstance(scale, (int, float)):
            reads.append(scale)
        if accum_out is not None:
            kw["accum_out"] = accum_out
            writes.append(accum_out)
        return self.add(eng, lambda e: e.activation(out=out, in_=in_, func=func, scale=scale, **kw),
                        reads=reads, writes=writes)

    def tt(self, out, in0, in1, op, eng="dve"):
        return self.add(eng, lambda e: e.tensor_tensor(out=out, in0=in0, in1=in1, op=op),
                        reads=[in0, in1], writes=[out])

    def ts(self, out, in0, s1, op0, s2=None, op1=None, eng="dve", accum_out=None):
        reads = [in0]
        if not isinstance(s1, (int, float)):
            reads.append(s1)
        if s2 is not None and not isinstance(s2, (int, float)):
            reads.append(s2)
        kw = {}
        writes = [out]
        if op1 is not None:
            kw["op1"] = op1
        if accum_out is not None:
            kw["accum_out"] = accum_out
            writes.append(accum_out)
        return self.add(eng, lambda e: e.tensor_scalar(out=out, in0=in0, scalar1=s1, scalar2=s2,
                                                       op0=op0, **kw),
                        reads=reads, writes=writes)

    def stt(self, out, in0, scalar, in1, op0, op1, eng="dve"):
        reads = [in0, in1]
        if not isinstance(scalar, (int, float)):
            reads.append(scalar)
        return self.add(eng, lambda e: e.scalar_tensor_tensor(out=out, in0=in0, scalar=scalar,
                                                              in1=in1, op0=op0, op1=op1),
                        reads=reads, writes=[out])

    def copy(self, out, in_, eng="dve"):
        if eng == "act":
            return self.add(eng, lambda e: e.activation(out=out, in_=in_, func=AF.Identity),
                            reads=[in_], writes=[out])
        return self.add(eng, lambda e: e.tensor_copy(out=out, in_=in_), reads=[in_], writes=[out])

    def memset(self, ap, val, eng="dve"):
        return self.add(eng, lambda e: e.memset(ap, val), reads=[], writes=[ap])

    def reduce(self, out, in_, op, axis=AX.X, eng="dve", **kw):
        return self.add(eng, lambda e: e.tensor_reduce(out=out, in_=in_, op=op, axis=axis, **kw),
                        reads=[in_], writes=[out])

    def recip(self, out, in_, eng="dve"):
        return self.add(eng, lambda e: e.reciprocal(out=out, in_=in_), reads=[in_], writes=[out])


def simulate_trace(trace):
    pos = {e: 0 for e in trace}
    sem = {}
    progress = True
    while progress:
        progress = False
        for e, ops in trace.items():
            while pos[e] < len(ops):
                kind, k, v, i = ops[pos[e]]
                if kind == "w":
                    if sem.get(k, 0) >= v:
                        pos[e] += 1
                        progress = True
                    else:
                        break
                else:
                    sem[k] = sem.get(k, 0) + v
                    pos[e] += 1
                    progress = True
    stuck = {e: ops[pos[e]] for e, ops in trace.items() if pos[e] < len(ops)}
    return stuck, sem

import contextlib
import os
_SK = set(os.environ.get('DBGSKIP', '').split(','))
import math
import ml_dtypes
from concourse.bass_utils import run_bass_kernel_spmd

S = 2048
D = 1024
L = 2
INW = 2592
FF = 2816
NT = 16
NFC = FF // 128
ALPHA = float((2 * L) ** 0.25)
EPS = 1e-5
C_DQ, C_DK, C_DV, C_FU, C_GQ, C_GK, C_GV, C_GR, C_GZ = 0, 512, 1024, 1536, 1792, 1920, 2048, 2304, 2560

CF_ID, CF_COS, CF_SIN, CF_HM, CF_CC, CF_SC, CF_ONES, CF_NEGH, CF_N = 0, 128, 640, 1152, 1156, 1284, 1412, 1540, 1548
CB_ID, CB_TRIF, CB_TRIB, CB_BD, CB_ONE, CB_N = 0, 128, 256, 384, 640, 768


def host_consts():
    p = np.arange(128)
    cf = np.zeros((128, CF_N), np.float32)
    cf[:, CF_ID:CF_ID + 128] = np.eye(128, dtype=np.float32)
    inv_freq = (10000.0 ** (-np.arange(0, 64, 2, dtype=np.float32) / 64)).astype(np.float32)
    pos = (np.arange(NT)[None, :] * 128 + p[:, None]).astype(np.float32)
    ang = pos[:, :, None] * inv_freq[None, None, :]
    cf[:, CF_COS:CF_COS + 512] = np.cos(ang).reshape(128, 512)
    cf[:, CF_SIN:CF_SIN + 512] = np.sin(ang).reshape(128, 512)
    cf[:, CF_HM:CF_HM + 4] = (p[:, None] // 32 == np.arange(4)[None, :]).astype(np.float32)
    c = np.arange(64)
    a = 2 * np.pi * np.outer(c, c) / 64.0
    cc = np.cos(a) / 8.0
    sc = np.sin(a) / 8.0
    z = np.zeros((64, 64))
    cf[:, CF_CC:CF_CC + 128] = np.block([[cc, z], [z, cc]])
    cf[:, CF_SC:CF_SC + 128] = np.block([[sc, z], [z, sc]])
    cf[:, CF_ONES:CF_ONES + 128] = 1.0
    cf[:, CF_NEGH:CF_NEGH + 8] = -0.5
    cb = np.zeros((128, CB_N), np.float32)
    cb[:, CB_ID:CB_ID + 128] = np.eye(128)
    cb[:, CB_TRIF:CB_TRIF + 128] = (p[:, None] <= p[None, :])
    cb[:, CB_TRIB:CB_TRIB + 128] = (p[:, None] >= p[None, :])
    cb[:, CB_BD:CB_BD + 256] = (p[:, None] // 32 == np.arange(256)[None, :] // 64)
    cb[:, CB_ONE:CB_ONE + 128] = 1.0
    s = np.arange(S, dtype=np.float64)
    sk = np.outer(s, s) % S
    ang = 2 * np.pi * sk / S
    dc = (np.cos(ang) / math.sqrt(S)).astype(np.float32).astype(ml_dtypes.bfloat16)
    ds = (-np.sin(ang) / math.sqrt(S)).astype(np.float32).astype(ml_dtypes.bfloat16)
    return cf, cb.astype(ml_dtypes.bfloat16), dc, ds


class Arena:
    def __init__(self, t, nbytes):
        self.t = t
        self.n = nbytes
        self.top = 0

    def mark(self):
        return self.top

    def release(self, m):
        self.top = m

    def alloc(self, shape, dt):
        n = 1
        for s_ in shape:
            n *= s_
        b = n * _DT_SIZE[dt]
        off = self.top
        self.top += (b + 63) // 64 * 64
        assert self.top <= self.n, ("arena overflow", self.top, self.n)
        ap = self.t[:, off // 2:(off + b) // 2]
        if dt != BF16:
            ap = ap.bitcast(dt)
        if len(shape) == 2:
            ap = ap.rearrange("p (a b) -> p a b", a=shape[0])
        elif len(shape) == 3:
            ap = ap.rearrange("p (a b c) -> p a b c", a=shape[0], b=shape[1])
        return ap


class Rot:
    def __init__(self, items):
        self.items = items
        self.i = 0

    def next(self):
        r = self.items[self.i % len(self.items)]
        self.i += 1
        return r


_CACHE = {}


class _Stop(Exception):
    pass


def build(n_layers=L, dbg=None, upto=None):
    nc = bass.Bass("TRN2", target_bir_lowering=False)
    dram = lambda name, shape, dt, kind="ExternalInput": nc.dram_tensor(name, shape, dt, kind=kind).ap()
    x_d = dram("x", [S, D], F32)
    w_in_d = dram("w_in", [L, D, INW], F32)
    lam_d = dram("diff_lambda", [L, 256], F32)
    dng_d = dram("diff_norm_g", [L, 128], F32)
    fw_d = dram("fourier_w", [L, 4, 64, 64], F32)
    w2_d = dram("gla_gate_w2", [L, 2, 16, 128], F32)
    b2_d = dram("gla_gate_b2", [L, 2, 128], F32)
    gng_d = dram("gla_norm_g", [L, 64], F32)
    wout_d = dram("w_out", [L, D, D], F32)
    ln_d = {k: dram(k, [L, D], F32) for k in ("ln1_g", "ln1_b", "ln2_g", "ln2_b")}
    wg_d = dram("ffn_w_gate", [L, D, FF], F32)
    wu_d = dram("ffn_w_up", [L, D, FF], F32)
    wd_d = dram("ffn_w_down", [L, FF, D], F32)
    cf_d = dram("c_f32", [128, CF_N], F32)
    cb_d = dram("c_bf", [128, CB_N], BF16)
    dftc_d = dram("dft_c", [S, S], BF16)
    dfts_d = dram("dft_s", [S, S], BF16)
    y_d = dram("y", [S, D], F32, kind="ExternalOutput")
    xs_d = dram("xs_scr", [S, D], F32, kind="Internal")
    dbg_outs = {}

    ARENA = 207 * 1024
    XTB = NT * D * 4
    with contextlib.ExitStack() as st:
        arena_t = st.enter_context(nc.sbuf_tensor("arena", [128, ARENA // 2], BF16))
        ps = st.enter_context(nc.psum_tensor("ps", [128, 4096], F32))
        P = Prog(nc)
        A = Arena(arena_t, ARENA)
        x_tok = arena_t[:, (ARENA - XTB) // 2:ARENA // 2].bitcast(F32).rearrange("p (t d) -> p t d", t=NT)

        def bank(b, n=512, off=0):
            return ps[:, b * 512 + off:b * 512 + off + n]

        def bankbf(b):
            return ps[:, b * 512:(b + 1) * 512].bitcast(BF16)

        def dump(name, ap, shape, dt=F32):
            if dbg is None or name not in dbg:
                return
            d_ = nc.dram_tensor("dbg_" + name, shape, dt, kind="ExternalOutput").ap()
            dbg_outs[name] = d_
            P.dma("sp", d_, ap)

        cf = A.alloc([CF_N], F32)
        cb = A.alloc([CB_N], BF16)
        xT = A.alloc([8, S], BF16)
        lnbuf = A.alloc([2, D], F32)
        small = A.alloc([640], F32)
        P.dma("sp", cf, cf_d)
        P.dma("sp", cb, cb_d)
        ident_f = cf[:, CF_ID:CF_ID + 128]
        ident_b = cb[:, CB_ID:CB_ID + 128]
        cos_t = cf[:, CF_COS:CF_COS + 512].rearrange("p (t i) -> p t i", t=NT)
        sin_t = cf[:, CF_SIN:CF_SIN + 512].rearrange("p (t i) -> p t i", t=NT)
        hmask4 = cf[:, CF_HM:CF_HM + 4]
        ccbd = cf[:, CF_CC:CF_CC + 128]
        scbd = cf[:, CF_SC:CF_SC + 128]
        ones_f = cf[:, CF_ONES:CF_ONES + 128]
        negh8 = cf[:, CF_NEGH:CF_NEGH + 8]
        tri = [cb[:, CB_TRIF:CB_TRIF + 128], cb[:, CB_TRIB:CB_TRIB + 128]]
        bdmask = cb[:, CB_BD:CB_BD + 256]
        lp_bc = small[:, 0:256]
        gdiff = small[:, 256:384]
        ggla = small[:, 384:448]
        negb2 = small[:, 448:450]
        neglam = small[:, 450:451]
        negM = small[:, 451:452]
        sc_tmp = small[:, 452:500]
        nrm = small[:, 500:504]
        epsc = small[:, 504:505]
        base_mark = A.mark()

        win = lambda l: w_in_d[l].rearrange("(k p) n -> p k n", p=128)

        def rstd_from(out, ss, n, width):
            tmp = sc_tmp[:, 40:40 + width]
            P.ts(tmp, ss, 1.0 / n, ALU.mult, EPS, ALU.add)
            P.tt(out, tmp, negh8[:, 0:width], ALU.pow, eng="pool")

        xb_rot = None

        def make_xT(x_tile_f32, t, xb, evac_eng="act", pbank=7):
            P.copy(xb, x_tile_f32, eng="act")
            pst = bankbf(pbank)
            for k in range(8):
                P.transpose(pst[:, k * 128:(k + 1) * 128], xb[:, k * 128:(k + 1) * 128], ident_b)
            P.copy(xT[:, :, t * 128:(t + 1) * 128], pst.rearrange("p (k n) -> p k n", k=8), eng=evac_eng)

        def ln_a(psum2, xres, gb, ytmp, par, alpha=ALPHA, mid=None):
            sct = sc_tmp[:, 0:16] if par == 0 else small[:, 540:556]
            P.stt(ytmp, xres, alpha, psum2, ALU.mult, ALU.add)
            stats = sct[:, 0:12]
            mv = sct[:, 12:14]
            for c_ in range(2):
                P.add("dve", lambda e, c_=c_: e.bn_stats(out=stats[:, c_ * 6:(c_ + 1) * 6],
                                                          in_=ytmp[:, c_ * 512:(c_ + 1) * 512]),
                      reads=[ytmp[:, c_ * 512:(c_ + 1) * 512]], writes=[stats[:, c_ * 6:(c_ + 1) * 6]])
            P.add("dve", lambda e: e.bn_aggr(out=mv, in_=stats), reads=[stats], writes=[mv])
            rs = sct[:, 14:15]
            nmr = sct[:, 15:16]
            P.act(rs, mv[:, 1:2], AF.Ln, bias=epsc)
            P.act(rs, rs, AF.Exp, scale=-0.5)
            if mid is not None:
                mid()
            P.stt(nmr, mv[:, 0:1], -1.0, rs, ALU.mult, ALU.mult)
            P.act(ytmp, ytmp, AF.Identity, bias=nmr, scale=rs)
            P.tt(ytmp, ytmp, gb[:, 0, :], ALU.mult, eng="pool")

        def ln_b(gb, out_tok, ytmp):
            P.tt(out_tok, ytmp, gb[:, 1, :], ALU.add)

        def ck(name):
            if upto == name:
                raise _Stop()

        try:
          for l in range(n_layers):
              lam_init = 0.8 - 0.6 * math.exp(-0.3 * l)
              A.release(base_mark)
              P.dma("sp", lp_bc, lam_d[l].partition_broadcast(128))
              P.dma("sp", gdiff, dng_d[l].partition_broadcast(128))
              P.dma("sp", ggla, gng_d[l].partition_broadcast(128))
              for d_ in range(2):
                  P.dma("sp", negb2[:, d_:d_ + 1], b2_d[l, d_].rearrange("(p o) -> p o", o=1))
              P.ts(negb2, negb2, -1.0, ALU.mult)
              P.ts(gdiff, gdiff, 1.0 - lam_init, ALU.mult)
              P.memset(epsc, EPS)
              pr = sc_tmp[:, 16:18]
              prod = A.alloc([128], F32)
              lp4 = lp_bc.rearrange("p (a d) -> p a d", a=4)
              for i_ in range(2):
                  P.tt(prod[:, 0:64], lp4[:, 2 * i_, :], lp4[:, 2 * i_ + 1, :], ALU.mult)
                  P.reduce(pr[:, i_:i_ + 1], prod[:, 0:64], ALU.add)
              P.act(pr, pr, AF.Exp)
              P.tt(neglam, pr[:, 1:2], pr[:, 0:1], ALU.subtract)
              P.ts(neglam, neglam, -lam_init, ALU.add)
              A.release(base_mark)
              ck('params')

              if l == 0:
                  m_ = A.mark()
                  xin = [A.alloc([D], F32) for _ in range(3)]
                  xbs = [A.alloc([D], BF16) for _ in range(2)]
                  for t in range(NT):
                      xi = xin[t % 3]
                      P.dma("sp", xi, x_d[t * 128:(t + 1) * 128, :])
                      make_xT(xi, t, xbs[t % 2], evac_eng="dve", pbank=(7 if t % 2 else 6))
                  A.release(m_)
              ck('xT')
              xres_d = x_d if l == 0 else xs_d

              ocat = A.alloc([8, S], BF16)
              Wout = A.alloc([8, D], BF16)
              Wgf = A.alloc([8, 288], BF16)
              Wgt = A.alloc([8, 512], BF16)
              wbd = A.alloc([2, 128], F32)
              Wcs = A.alloc([2, 256], BF16)
              mixer_mark = A.mark()
              NDB = 4
              DBB = 8 * 512 * 2
              dbuf = [arena_t[:, (ARENA - (i_ + 1) * DBB) // 2:(ARENA - i_ * DBB) // 2].rearrange(
                  "p (a b) -> p a b", a=8) for i_ in range(NDB)]
              dft_v = [dftc_d.rearrange("(t p) k -> p t k", p=128), dfts_d.rearrange("(t p) k -> p t k", p=128)]
              dft_chunks = [(kb, which, tg) for kb in range(4) for which in range(2) for tg in range(2)]

              def dft_load(j):
                  kb, which, tg = dft_chunks[j]
                  P.dma("sp", dbuf[j % NDB], dft_v[which][:, tg * 8:(tg + 1) * 8, kb * 512:(kb + 1) * 512])

              Wh = [A.alloc([8, 384], BF16) for _ in range(2)]
              QTz = A.alloc([2, S], BF16)
              KT = A.alloc([S], BF16)
              Vaug = A.alloc([NT, 132], BF16)
              PT = [A.alloc([1024], BF16) for _ in range(3)]
              O1n = A.alloc([8, 128], F32)
              O2t = A.alloc([8, 128], F32)
              otoks = [A.alloc([8, 128], BF16) for _ in range(2)]
              qkr = [A.alloc([256], BF16) for _ in range(3)]
              tmpAs = [A.alloc([128], F32) for _ in range(2)]
              tmpBs = [A.alloc([128], F32) for _ in range(2)]
              sq = A.alloc([256], F32)
              D2 = A.alloc([256], F32)
              sqs = [sq, A.alloc([256], F32)]
              red_all = A.alloc([NT, 4], F32)
              red4 = sc_tmp[:, 20:24]
              gm = sc_tmp[:, 24:26]
              nrm2 = sc_tmp[:, 26:28]
              rz = sc_tmp[:, 28:36]
              ss8 = small[:, 512:520]
              rstd8 = small[:, 520:528]
              P.memset(Vaug[:, :, 128:129], 1.0)
              P.memset(QTz[64:128, 0, :], 0.0)
              P.memset(QTz[0:64, 1, :], 0.0, eng="pool")
              pso = [ps[:, (4 + qi // 3) * 512 + (qi % 3) * 160:(4 + qi // 3) * 512 + (qi % 3) * 160 + 129]
                     for qi in range(8)]
              grp = [(0, 3), (3, 3), (6, 2)]

              def pso_grp(gi, c0, c1):
                  q0, n_ = grp[gi]
                  base_ = (4 + gi) * 512
                  return ps[:, base_:base_ + n_ * 160].rearrange("p (a b) -> p a b", b=160)[:, :, c0:c1]

              def load_wh(h_):
                  for j_, c0 in enumerate((C_DQ, C_DK, C_DV)):
                      P.dma("pool", Wh[h_ % 2][:, :, j_ * 128:(j_ + 1) * 128],
                            win(l)[:, :, c0 + h_ * 128:c0 + (h_ + 1) * 128])

              load_wh(0)
              fin_pending = []
              for h in range(4):
                  wh = Wh[h % 2]
                  P.memset(nrm, 0.0)
                  tr_pending = []
                  for t in range(NT):
                      pb = bank(t % 4, 256)
                      pv_ = bank(4 + t % 2, 128)
                      for k in range(8):
                          P.matmul(pb, xT[:, k, t * 128:(t + 1) * 128], wh[:, k, 0:256], start=(k == 0), stop=(k == 7))
                      for k in range(8):
                          P.matmul(pv_, xT[:, k, t * 128:(t + 1) * 128], wh[:, k, 256:384], start=(k == 0), stop=(k == 7))
                      qk4 = pb.rearrange("p (g h d) -> p g h d", g=4, h=2)
                      t1 = qk4[:, :, 0, :]
                      t2 = qk4[:, :, 1, :]
                      cbt = cos_t[:, t:t + 1, :].broadcast_to([128, 4, 32])
                      sbt = sin_t[:, t:t + 1, :].broadcast_to([128, 4, 32])
                      q_ = qkr[t % 3]
                      q4 = q_.rearrange("p (g h d) -> p g h d", g=4, h=2)
                      ta = tmpAs[t % 2].rearrange("p (g d) -> p g d", g=4)
                      tb_ = tmpBs[t % 2].rearrange("p (g d) -> p g d", g=4)
                      P.tt(ta, t1, cbt, ALU.mult)
                      P.tt(tb_, t2, sbt, ALU.mult)
                      P.tt(q4[:, :, 0, :], ta, tb_, ALU.subtract)
                      P.tt(ta, t2, cbt, ALU.mult)
                      P.tt(tb_, t1, sbt, ALU.mult)
                      P.tt(q4[:, :, 1, :], ta, tb_, ALU.add)
                      P.copy(Vaug[:, t, 0:128], pv_, eng="act")
                      P.tt(sqs[t % 2], q_, q_, ALU.mult, eng="pool")
                      def tr_(t=t, q_=q_, sq_=sqs[t % 2]):
                          P.reduce(red_all[:, t, :], sq_.rearrange("p (g d) -> p g d", g=4), ALU.add)
                          pst = bankbf(7 if t % 2 else 6)
                          P.transpose(pst[:, 0:128], q_[:, 0:128], ident_b)
                          P.transpose(pst[:, 128:256], q_[:, 128:256], ident_b)
                          P.copy(QTz[0:64, 0, t * 128:(t + 1) * 128], pst[0:64, 0:128], eng="act")
                          P.copy(QTz[64:128, 1, t * 128:(t + 1) * 128], pst[64:128, 0:128], eng="act")
                          P.copy(KT[:, t * 128:(t + 1) * 128], pst[:, 128:256], eng="act")
                      tr_pending.append(tr_)
                      if len(tr_pending) > 1:
                          tr_pending.pop(0)()
                  while tr_pending:
                      tr_pending.pop(0)()
                  P.reduce(nrm, red_all.rearrange("p t g -> p g t"), ALU.max)
                  ck('h0proj')
                  if h + 1 < 4:
                      load_wh(h + 1)
                  else:
                      Wf = Wh[0][:, :, 0:256]
                      P.dma("pool", Wf, win(l)[:, :, C_FU:C_FU + 256])
                      P.dma("pool", Wgf[:, :, 0:256], win(l)[:, :, C_GQ:C_GQ + 256])
                      P.dma("pool", Wgf[:, :, 256:288], win(l)[:, :, C_GZ:C_GZ + 32])
                      P.dma("pool", Wgt, win(l)[:, :, C_GV:C_GV + 512])
                      for k in range(8):
                          P.dma("pool", Wout[:, k, :], wout_d[l, k * 128:(k + 1) * 128, :])
                      P.memset(wbd, 0.0, eng="pool")
                      for g_ in range(4):
                          c_, gl = g_ // 2, g_ % 2
                          P.dma("sp", wbd[gl * 64:(gl + 1) * 64, c_, gl * 64:(gl + 1) * 64], fw_d[l, g_])
                      for j_ in range(NDB):
                          dft_load(j_)
                  P.reduce(nrm2, nrm.rearrange("p (a b) -> p a b", a=2), ALU.max)
                  P.ts(D2[:, 0:128], ident_f, nrm2[:, 0:1], ALU.mult)
                  P.ts(D2[:, 128:256], ident_f, nrm2[:, 1:2], ALU.mult)
                  pbm = bank(6, 256)
                  P.matmul(pbm, ones_f, D2, start=True, stop=True)
                  P.reduce(gm, pbm.rearrange("p (a b) -> p a b", a=2), ALU.max)
                  P.tt(negM, gm[:, 0:1], gm[:, 1:2], ALU.add)
                  P.ts(negM, negM, -0.5 * 0.125, ALU.mult)
                  if l == 0 and h == 0:
                      dump("xT", xT, [128, 8, S], BF16)
                      dump("negM", negM, [128, 1])
                      ck('qkt')
                  for qh in range(2):
                      otok = otoks[qh]
                      steps = [(m, kt) for m in range(2) for kt in range(NT)]

                      def emit_scores(i):
                          m, kt = steps[i]
                          sb_i = (i % 2) * 2
                          for j_ in range(2):
                              P.matmul(ps[:, (sb_i + j_) * 512:(sb_i + j_ + 1) * 512],
                                       KT[:, kt * 128:(kt + 1) * 128],
                                       QTz[:, m, qh * 1024 + j_ * 512:qh * 1024 + (j_ + 1) * 512],
                                       start=True, stop=True)

                      emit_scores(0)
                      for i, (m, kt) in enumerate(steps):
                          if i + 1 < len(steps):
                              emit_scores(i + 1)
                          if i == 8 and fin_pending:
                              fin_pending.pop(0)()
                          sb_i = (i % 2) * 2
                          pss = ps[:, sb_i * 512:(sb_i + 2) * 512]
                          pt = PT[i % 3]
                          P.act(pt, pss, AF.Exp, bias=negM, scale=0.125)
                          for qi in range(8):
                              P.matmul(pso[qi], pt[:, qi * 128:(qi + 1) * 128], Vaug[:, kt, 0:129],
                                       start=(kt == 0 and qi % 3 == 0), stop=(kt == NT - 1), skip_group_check=True)
                          if kt != NT - 1:
                              continue
                          if m == 0:
                              for gi, (q0, n_) in enumerate(grp):
                                  P.recip(rz[:, q0:q0 + n_], pso_grp(gi, 128, 129).rearrange("p a b -> p (a b)"))
                                  P.tt(O1n[:, q0:q0 + n_, :], pso_grp(gi, 0, 128),
                                       rz[:, q0:q0 + n_].unsqueeze(2).broadcast_to([128, n_, 128]), ALU.mult)
                          else:
                              for gi, (q0, n_) in enumerate(grp):
                                  P.recip(rz[:, q0:q0 + n_], pso_grp(gi, 128, 129).rearrange("p a b -> p (a b)"))
                              P.ts(rz, rz, neglam, ALU.mult)
                              for gi, (q0, n_) in enumerate(grp):
                                  P.tt(O2t[:, q0:q0 + n_, :], pso_grp(gi, 0, 128),
                                       rz[:, q0:q0 + n_].unsqueeze(2).broadcast_to([128, n_, 128]), ALU.mult)
                              P.tt(O1n, O1n, O2t, ALU.add)
                              P.tt(O2t, O1n, O1n, ALU.mult, eng="pool")
                              P.reduce(ss8, O2t, ALU.add)
                              rstd_from(rstd8, ss8, 128.0, 8)
                              P.tt(O1n, O1n, rstd8.unsqueeze(2).broadcast_to([128, 8, 128]), ALU.mult)
                              P.tt(otok, O1n, gdiff.unsqueeze(1).broadcast_to([128, 8, 128]), ALU.mult, eng="pool")
                              for qi in range(8):
                                  dst_ = ocat[:, h, qh * 1024 + qi * 128:qh * 1024 + (qi + 1) * 128]
                                  P.add("sp", lambda e, dst_=dst_, src_=otok[:, qi, :]: e.dma_start(
                                      out=dst_, in_=src_, transpose=True),
                                      reads=[otok[:, qi, :]], writes=[dst_], dma=True)
              while fin_pending:
                  fin_pending.pop(0)()
              A.release(mixer_mark)
              if l == 0:
                  dump("odiff", ocat[:, 0:4, :], [128, 4, S], BF16)
              ck('att')

              _skip = A.alloc([2 * 8 * 384], BF16)
              uT = A.alloc([2, S], BF16)
              ucs = A.alloc([NT, 512], BF16)
              P.dma("sp", lnbuf[:, 0, :], ln_d["ln1_g"][l].partition_broadcast(128))
              P.dma("sp", lnbuf[:, 1, :], ln_d["ln1_b"][l].partition_broadcast(128))
              for c_ in range(2):
                  pb = bank(4 + c_, 256)
                  P.matmul(pb[:, 0:128], ccbd, wbd[:, c_, :], start=True, stop=True)
                  P.matmul(pb[:, 128:256], scbd, wbd[:, c_, :], start=True, stop=True)
                  P.copy(Wcs[:, c_, :], pb, eng="act")
              for tb in range(4):
                  for c_ in range(2):
                      pb = bank((tb * 2 + c_) % 4)
                      for k in range(8):
                          P.matmul(pb, Wf[:, k, c_ * 128:(c_ + 1) * 128], xT[:, k, tb * 512:(tb + 1) * 512],
                                   start=(k == 0), stop=(k == 7))
                      P.copy(uT[:, c_, tb * 512:(tb + 1) * 512], pb, eng="act")
              for t in range(NT):
                  pb = bank(4 + t % 2)
                  for c_ in range(2):
                      P.matmul(pb[:, c_ * 256:(c_ + 1) * 256], uT[:, c_, t * 128:(t + 1) * 128], Wcs[:, c_, :],
                               start=True, stop=True)
                  P.copy(ucs[:, t, :], pb, eng="act")
                  if t == 7:
                      ucs_half = True
              for j_, (kb, which, tg) in enumerate(dft_chunks):
                  pbs = [bank(0 + (kb % 2) * 2), bank(1 + (kb % 2) * 2)]
                  db = dbuf[j_ % NDB]
                  for tt_ in range(8):
                      t = tg * 8 + tt_
                      first = (which == 0 and t == 0)
                      last = (which == 1 and t == NT - 1)
                      for c_ in range(2):
                          P.matmul(pbs[c_], ucs[:, t, c_ * 256 + which * 128:c_ * 256 + (which + 1) * 128],
                                   db[:, tt_, :], start=first, stop=last)
                  if j_ + NDB < len(dft_chunks):
                      dft_load(j_ + NDB)
                  if which == 1 and tg == 1:
                      for c_ in range(2):
                          P.copy(ocat[:, 4 + c_, kb * 512:(kb + 1) * 512], pbs[c_], eng=("act" if c_ else "dve"))
              A.release(mixer_mark)
              if l == 0:
                  dump("ofour", ocat[:, 4:6, :], [128, 2, S], BF16)
              ck('four')

              gqk = A.alloc([2, S], F32)
              gzT = A.alloc([S], BF16)
              gv = A.alloc([NT, 256], BF16)
              gate = A.alloc([NT, 256], BF16)
              w2pad = A.alloc([2, 128], BF16)
              qt = [A.alloc([S], BF16) for _ in range(2)]
              kt_ = [A.alloc([S], BF16) for _ in range(2)]
              Sbf = [A.alloc([NT, 256], BF16) for _ in range(2)]
              gla_mark = A.mark()
              P.memset(w2pad, 0.0)
              for d_ in range(2):
                  P.dma("pool", w2pad[d_ * 16:(d_ + 1) * 16, d_, :], w2_d[l, d_])
              for tb in range(4):
                  for c_ in range(2):
                      pb = bank((tb * 3 + c_) % 4)
                      for k in range(8):
                          P.matmul(pb, Wgf[:, k, c_ * 128:(c_ + 1) * 128], xT[:, k, tb * 512:(tb + 1) * 512],
                                   start=(k == 0), stop=(k == 7))
                      P.copy(gqk[:, c_, tb * 512:(tb + 1) * 512], pb, eng=("act" if c_ else "dve"))
                  pb = bank((tb * 3 + 2) % 4)
                  for k in range(8):
                      P.matmul(pb[0:32, :], Wgf[:, k, 256:288], xT[:, k, tb * 512:(tb + 1) * 512],
                               start=(k == 0), stop=(k == 7))
                  P.copy(gzT[0:32, tb * 512:(tb + 1) * 512], pb[0:32, :], eng="dve")
              for t in range(NT):
                  pb = bank(4 + t % 2)
                  for k in range(8):
                      P.matmul(pb, xT[:, k, t * 128:(t + 1) * 128], Wgt[:, k, :], start=(k == 0), stop=(k == 7))
                  P.copy(gv[:, t, :], pb[:, 0:256], eng="act")
                  P.act(gate[:, t, :], pb[:, 256:512], AF.Silu)
              A.release(gla_mark)
              Bc = A.alloc([S], F32)
              Ec = A.alloc([S], F32)
              kdec_tok = A.alloc([NT, 128], BF16)
              kdT = [A.alloc([128], BF16) for _ in range(2)]
              Srot = [A.alloc([256], F32) for _ in range(2)]
              for d_ in range(2):
                  for tb in range(4):
                      pb = bank(tb % 4)
                      P.matmul(pb, w2pad[0:32, d_, :], gzT[0:32, tb * 512:(tb + 1) * 512], start=True, stop=True)
                      P.act(Ec[:, tb * 512:(tb + 1) * 512], pb, AF.Exp, bias=negb2[:, d_:d_ + 1], scale=-1.0)
                  P.act(Ec, Ec, AF.Ln, bias=1.0)
                  for n in range(NT):
                      o_ = Bc[:, n * 128:(n + 1) * 128]
                      i_ = Ec[:, n * 128:(n + 1) * 128]
                      if d_ == 1:
                          o_ = o_[:, ::-1]
                          i_ = i_[:, ::-1]
                      P.add("dve", lambda e, o_=o_, i_=i_: e.tensor_tensor_scan(
                          out=o_, data0=ones_f, data1=i_, initial=0.0, op0=ALU.mult, op1=ALU.add),
                          reads=[ones_f, Ec[:, n * 128:(n + 1) * 128]], writes=[Bc[:, n * 128:(n + 1) * 128]])
                  P.act(Ec, Bc, AF.Exp, scale=-1.0 / 16.0)
                  P.act(Bc, Bc, AF.Exp, scale=1.0 / 16.0)
                  P.stt(qt[d_], gqk[:, 0, :], 32.0 ** -0.5, Ec, ALU.mult, ALU.mult)
                  P.tt(kt_[d_], gqk[:, 1, :], Bc, ALU.mult)
                  elast = [Ec[:, n * 128 + (127 if d_ == 0 else 0):n * 128 + (127 if d_ == 0 else 0) + 1]
                           for n in range(NT)]
                  pst = bankbf(7)
                  for n in range(NT):
                      kd = kdT[n % 2]
                      P.stt(kd, gqk[:, 1, n * 128:(n + 1) * 128], elast[n], Bc[:, n * 128:(n + 1) * 128],
                            ALU.mult, ALU.mult)
                      P.transpose(pst[:, (n % 8) * 128:(n % 8 + 1) * 128], kd, ident_b)
                      if n % 8 == 7:
                          P.copy(kdec_tok[:, n - 7:n + 1, :], pst.rearrange("p (a b) -> p a b", a=8), eng="act")
                  order = list(range(NT)) if d_ == 0 else list(range(NT - 1, -1, -1))
                  prev = Srot[0]
                  P.memset(prev, 0.0)
                  P.memset(Sbf[d_][:, order[0], :], 0.0)
                  for i_, n in enumerate(order[:-1]):
                      pb = bank(4 + i_ % 2, 256)
                      P.matmul(pb, kdec_tok[:, n, :], gv[:, n, :], start=True, stop=True)
                      cur = Srot[(i_ + 1) % 2]
                      P.stt(cur, prev, elast[n], pb, ALU.mult, ALU.add)
                      P.tt(Sbf[d_][:, order[i_ + 1], :], cur, bdmask, ALU.mult, eng="pool")
                      prev = cur
              A.release(gla_mark)
              Qbd = [A.alloc([4, 128], BF16) for _ in range(4)]
              Asb = [A.alloc([4, 128], BF16) for _ in range(4)]
              ogf = [A.alloc([256], F32) for _ in range(2)]
              ogb = [A.alloc([256], BF16) for _ in range(3)]
              junk = A.alloc([64], F32)
              ss4s = [small[:, 528:532], small[:, 560:564]]
              rs4s = [small[:, 532:536], small[:, 564:568]]
              P.tt(gate.rearrange("p t (h v) -> p (t h) v", h=4), gate.rearrange("p t (h v) -> p (t h) v", h=4),
                   ggla.unsqueeze(1).broadcast_to([128, NT * 4, 64]), ALU.mult, eng="pool")

              def gla_m(n):
                  po = bank(4 + n % 2, 256)
                  P.matmul(po, qt[0][:, n * 128:(n + 1) * 128], Sbf[0][:, n, :], start=True, stop=False)
                  P.matmul(po, qt[1][:, n * 128:(n + 1) * 128], Sbf[1][:, n, :], start=False, stop=False)
                  for d_ in range(2):
                      ci = 2 * n + d_
                      qb = Qbd[ci % 4]
                      P.tt(qb, qt[d_][:, n * 128:(n + 1) * 128].unsqueeze(1).broadcast_to([128, 4, 128]),
                           hmask4.unsqueeze(2).broadcast_to([128, 4, 128]), ALU.mult, eng="pool")
                      P.matmul(bank(ci % 4), kt_[d_][:, n * 128:(n + 1) * 128], qb.rearrange("p a b -> p (a b)"),
                               start=True, stop=True)
                  for d_ in range(2):
                      ci = 2 * n + d_
                      asb = Asb[ci % 4]
                      P.tt(asb, bank(ci % 4).rearrange("p (a b) -> p a b", a=4),
                           tri[d_].unsqueeze(1).broadcast_to([128, 4, 128]), ALU.mult)
                      for h in range(4):
                          P.matmul(po[:, h * 64:(h + 1) * 64], asb[:, h, :], gv[:, n, h * 64:(h + 1) * 64],
                                   start=False, stop=(d_ == 1 and h == 3))

              def gla_a(n):
                  po = bank(4 + n % 2, 256)
                  ss4 = ss4s[n % 2]
                  rs4 = rs4s[n % 2]
                  for h in range(4):
                      P.act(junk, po[:, h * 64:(h + 1) * 64], AF.Square, accum_out=ss4[:, h:h + 1])
                  P.act(rs4, ss4, AF.Ln, bias=epsc, scale=1.0 / 64.0)
                  P.act(rs4, rs4, AF.Exp, scale=-0.5)
                  of = ogf[n % 2]
                  of3 = of.rearrange("p (h v) -> p h v", h=4)
                  P.tt(of3, po.rearrange("p (h v) -> p h v", h=4), rs4.unsqueeze(2).broadcast_to([128, 4, 64]),
                       ALU.mult)
                  ob = ogb[n % 3]
                  P.tt(ob, of, gate[:, n, :], ALU.mult)
                  for c_ in range(2):
                      dst_ = ocat[:, 6 + c_, n * 128:(n + 1) * 128]
                      P.add("sp", lambda e, dst_=dst_, src_=ob[:, c_ * 128:(c_ + 1) * 128]: e.dma_start(
                          out=dst_, in_=src_, transpose=True),
                          reads=[ob[:, c_ * 128:(c_ + 1) * 128]], writes=[dst_], dma=True)

              for s_ in range(NT + 1):
                  if s_ < NT:
                      gla_m(s_)
                  if s_ >= 1:
                      gla_a(s_ - 1)
              A.release(mixer_mark)
              if l == 0:
                  dump("ogla", ocat[:, 6:8, :], [128, 2, S], BF16)
              ck('gla')

              A.n = ARENA - XTB
              assert A.top <= A.n
              post_mark = A.mark()
              ytmp = [A.alloc([D], F32) for _ in range(3)]
              xbs = [A.alloc([D], BF16) for _ in range(2)]
              for t in range(NT):
                  P.dma("sp", x_tok[:, t, :], xres_d[t * 128:(t + 1) * 128, :])
              def mm1_(t):
                  b0 = (t % 3) * 2
                  p2 = ps[:, b0 * 512:(b0 + 2) * 512]
                  for hf in range(2):
                      for e_ in range(8):
                          P.matmul(p2[:, hf * 512:(hf + 1) * 512], ocat[:, e_, t * 128:(t + 1) * 128],
                                   Wout[:, e_, hf * 512:(hf + 1) * 512], start=(e_ == 0), stop=(e_ == 7))

              def a1_(t, mid=None):
                  b0 = (t % 3) * 2
                  ln_a(ps[:, b0 * 512:(b0 + 2) * 512], x_tok[:, t, :], lnbuf, ytmp[t % 3], t % 2, mid=mid)

              def b1_(t):
                  ln_b(lnbuf, x_tok[:, t, :], ytmp[t % 3])

              def c1_(t):
                  make_xT(x_tok[:, t, :], t, xbs[t % 2])

              for s_ in range(NT + 3):
                  if s_ < NT:
                      mm1_(s_)
                  midf = (lambda t_=s_ - 3: b1_(t_)) if 0 <= s_ - 3 < NT else None
                  if 0 <= s_ - 1 < NT:
                      a1_(s_ - 1, mid=midf)
                  elif midf is not None:
                      midf()
                  if 0 <= s_ - 3 < NT:
                      c1_(s_ - 3)
              A.release(base_mark)
              if l == 0:
                  dump("x1", x_tok, [128, NT, D])
              ck('ln1')

              GF = NFC // 2
              Wdg = A.alloc([GF, D], BF16)
              hT = A.alloc([GF, S], BF16)
              NWB = 3
              wgu = [A.alloc([2, 8, 128], BF16) for _ in range(NWB)]
              sg = [A.alloc([512], BF16) for _ in range(2)]
              ytmp = [A.alloc([D], F32) for _ in range(1)]
              xbs = [A.alloc([D], BF16) for _ in range(2)]
              wdv = wd_d[l].rearrange("(f p) n -> p f n", p=128)
              wgv = wg_d[l].rearrange("(k p) n -> p k n", p=128)
              wuv = wu_d[l].rearrange("(k p) n -> p k n", p=128)
              P.dma("sp", lnbuf[:, 0, :], ln_d["ln2_g"][l].partition_broadcast(128))
              P.dma("sp", lnbuf[:, 1, :], ln_d["ln2_b"][l].partition_broadcast(128))
              wi = 0
              ui = 0
              for g_ in range(2):
                  for fi in range(GF):
                      f = g_ * GF + fi
                      wb = wgu[wi % NWB]
                      wi += 1
                      P.dma("pool", wb[:, 0, :, :], wgv[:, :, f * 128:(f + 1) * 128])
                      P.dma("pool", wb[:, 1, :, :], wuv[:, :, f * 128:(f + 1) * 128])
                      P.dma("pool", Wdg[:, fi, :], wdv[:, f, :])
                      for tb in range(4):
                          pg = bank((ui % 2) * 2)
                          pu = bank((ui % 2) * 2 + 1)
                          ui += 1
                          for k in range(8):
                              P.matmul(pg, wb[:, 0, k, :], xT[:, k, tb * 512:(tb + 1) * 512],
                                       start=(k == 0), stop=(k == 7))
                          for k in range(8):
                              P.matmul(pu, wb[:, 1, k, :], xT[:, k, tb * 512:(tb + 1) * 512],
                                       start=(k == 0), stop=(k == 7))
                          s_ = sg[ui % 2]
                          P.act(s_, pg, AF.Silu)
                          P.tt(hT[:, fi, tb * 512:(tb + 1) * 512], pu, s_, ALU.mult)
                  ytmps = [ytmp[0], wgu[0].rearrange("p a b c -> p (a b c)").bitcast(F32),
                           wgu[1].rearrange("p a b c -> p (a b c)").bitcast(F32)]

                  def mm2_(t):
                      b0 = (t % 3) * 2
                      p2 = ps[:, b0 * 512:(b0 + 2) * 512]
                      for hf in range(2):
                          for fi in range(GF):
                              P.matmul(p2[:, hf * 512:(hf + 1) * 512], hT[:, fi, t * 128:(t + 1) * 128],
                                       Wdg[:, fi, hf * 512:(hf + 1) * 512], start=(fi == 0), stop=(fi == GF - 1))

                  def a2_(t, mid=None):
                      b0 = (t % 3) * 2
                      p2 = ps[:, b0 * 512:(b0 + 2) * 512]
                      if g_ == 0:
                          P.stt(x_tok[:, t, :], x_tok[:, t, :], ALPHA, p2, ALU.mult, ALU.add)
                          if mid is not None:
                              mid()
                      else:
                          ln_a(p2, x_tok[:, t, :], lnbuf, ytmps[t % 3], t % 2, alpha=1.0, mid=mid)

                  def b2_(t):
                      if g_ == 0:
                          return
                      ln_b(lnbuf, x_tok[:, t, :], ytmps[t % 3])

                  def c2_(t):
                      if g_ == 0:
                          return
                      if l == n_layers - 1:
                          P.dma("sp", y_d[t * 128:(t + 1) * 128, :], x_tok[:, t, :])
                      else:
                          P.dma("sp", xs_d[t * 128:(t + 1) * 128, :], x_tok[:, t, :])
                          make_xT(x_tok[:, t, :], t, xbs[t % 2])

                  for s_ in range(NT + 3):
                      if s_ < NT:
                          mm2_(s_)
                      midf = (lambda t_=s_ - 3: b2_(t_)) if 0 <= s_ - 3 < NT else None
                      if 0 <= s_ - 1 < NT:
                          a2_(s_ - 1, mid=midf)
                      elif midf is not None:
                          midf()
                      if 0 <= s_ - 3 < NT:
                          c2_(s_ - 3)
              A.release(base_mark)
              A.n = ARENA

        except _Stop:
            pass
        P.emit()
    _CACHE['P'] = P
    return nc, dbg_outs


def kernel(**inputs):
    if "c" not in _CACHE:
        _CACHE["c"] = host_consts()
    cf, cb, dc, ds = _CACHE["c"]
    nc, _ = build()
    x = np.ascontiguousarray(inputs["x"], dtype=np.float32)
    common = {
        "w_in": np.ascontiguousarray(inputs["w_in"], dtype=np.float32),
        "diff_lambda": np.ascontiguousarray(inputs["diff_lambda"], dtype=np.float32).reshape(L, 256),
        "diff_norm_g": np.ascontiguousarray(inputs["diff_norm_g"], dtype=np.float32),
        "fourier_w": np.ascontiguousarray(inputs["fourier_w"], dtype=np.float32),
        "gla_gate_w2": np.ascontiguousarray(inputs["gla_gate_w2"], dtype=np.float32),
        "gla_gate_b2": np.ascontiguousarray(inputs["gla_gate_b2"], dtype=np.float32),
        "gla_norm_g": np.ascontiguousarray(inputs["gla_norm_g"], dtype=np.float32),
        "w_out": np.ascontiguousarray(inputs["w_out"], dtype=np.float32),
        "ln1_g": np.ascontiguousarray(inputs["ln1_g"], dtype=np.float32),
        "ln1_b": np.ascontiguousarray(inputs["ln1_b"], dtype=np.float32),
        "ln2_g": np.ascontiguousarray(inputs["ln2_g"], dtype=np.float32),
        "ln2_b": np.ascontiguousarray(inputs["ln2_b"], dtype=np.float32),
        "ffn_w_gate": np.ascontiguousarray(inputs["ffn_w_gate"], dtype=np.float32),
        "ffn_w_up": np.ascontiguousarray(inputs["ffn_w_up"], dtype=np.float32),
        "ffn_w_down": np.ascontiguousarray(inputs["ffn_w_down"], dtype=np.float32),
        "c_f32": cf, "c_bf": cb, "dft_c": dc, "dft_s": ds,
    }
    in_maps = [dict(common, x=x[b]) for b in range(8)]
    res = run_bass_kernel_spmd(nc, in_maps, core_ids=list(range(8)))
    return np.stack([np.asarray(r["y"], dtype=np.float32) for r in res.results], axis=0)
```

```python
import numpy as np
import concourse.bass as bass
import concourse.mybir as mybir

F32 = mybir.dt.float32
BF16 = mybir.dt.bfloat16
ALU = mybir.AluOpType
AF = mybir.ActivationFunctionType
AX = mybir.AxisListType

_DT_SIZE = {F32: 4, BF16: 2, mybir.dt.float32r: 4, mybir.dt.int32: 4,
            mybir.dt.uint32: 4, mybir.dt.float16: 2, mybir.dt.uint16: 2,
            mybir.dt.int16: 2, mybir.dt.uint8: 1, mybir.dt.int8: 1}

ENGS = ("pe", "act", "dve", "pool", "sp")
N_DMA_SEMS = 24
TINY_BYTES = 256
SAME_ENGINE_SYNC = False


def ap_box(ap):
    t = ap.tensor
    name = t.name
    esz = _DT_SIZE[ap.dtype]
    dims = list(ap.ap)
    off = ap.offset
    space = str(ap.space)
    if "DRAM" in space.upper() or "HBM" in space.upper():
        lo = off
        hi = off
        for (st, n) in dims:
            if st >= 0:
                hi += st * (n - 1)
            else:
                lo += st * (n - 1)
        return (name, 0, 1, lo * esz, (hi + 1) * esz)
    pstep, pcnt = dims[0]
    if pstep == 0:
        pstep = 1 << 40
    p0 = off // pstep if pstep < (1 << 40) else 0
    f = off - p0 * pstep if pstep < (1 << 40) else off
    lo = f
    hi = f
    for (st, n) in dims[1:]:
        if st >= 0:
            hi += st * (n - 1)
        else:
            lo += st * (n - 1)
    if "PSUM" in space.upper():
        b0 = (lo * esz) // 2048
        b1 = ((hi + 1) * esz - 1) // 2048
        return (name, 0, 128, b0 * 2048, (b1 + 1) * 2048, True)
    return (name, p0, p0 + pcnt, lo * esz, (hi + 1) * esz)


def _overlap(a, b):
    return a[1] < b[2] and b[1] < a[2] and a[3] < b[4] and b[3] < a[4]


def _contains(a, b):
    return a[1] <= b[1] and a[2] >= b[2] and a[3] <= b[3] and a[4] >= b[4]


class Prog:
    def __init__(self, nc):
        self.nc = nc
        self.ins = []
        self.recs = {}
        self.dma_rr = {e: 0 for e in ENGS}
        self.trace = {}

    def add(self, eng, fn, reads=(), writes=(), dma=False):
        idx = len(self.ins)
        rb = list(dict.fromkeys(ap_box(a) for a in reads))
        wb = list(dict.fromkeys(ap_box(a) for a in writes))
        deps = set()
        tiny_deps = set()
        for b in rb:
            psum = len(b) > 5
            tiny = (not psum) and (b[4] - b[3]) <= TINY_BYTES
            for rec in self.recs.get(b[0], ()):
                if (rec[1] or (psum and rec[3] != eng)) and _overlap(rec[0], b):
                    deps.add(rec[2])
                    if tiny and rec[1]:
                        tiny_deps.add(rec[2])
        for b in wb:
            for rec in self.recs.get(b[0], ()):
                if _overlap(rec[0], b):
                    deps.add(rec[2])
        for b in wb:
            lst = self.recs.setdefault(b[0], [])
            lst[:] = [r for r in lst if not _contains(b, r[0])]
            lst.append([b, True, idx, eng])
        for b in rb:
            lst = self.recs.setdefault(b[0], [])
            found = False
            for r in lst:
                if (not r[1]) and r[3] == eng and r[0] == b and not dma \
                        and not self.ins[r[2]]["dma"]:
                    r[2] = idx
                    found = True
                    break
            if not found:
                lst.append([b, False, idx, eng])
        real = set()
        for d in deps:
            p = self.ins[d]
            if p["eng"] == "pe" and eng == "pe" and not p["dma"] and not dma:
                continue
            if (not SAME_ENGINE_SYNC) and p["eng"] == eng and eng != "pool" and not p["dma"] and not dma \
                    and d not in tiny_deps:
                continue
            real.add(d)
        self.ins.append(dict(eng=eng, fn=fn, deps=real, dma=dma, needed=False,
                             sem=None, val=None))
        return idx

    def emit(self, final_wait_eng="sp"):
        nc = self.nc
        ins = self.ins
        for r in ins:
            for d in r["deps"]:
                ins[d]["needed"] = True
        last_dmas = [i for i, r in enumerate(ins) if r["dma"]]
        import contextlib
        with contextlib.ExitStack() as st:
            esem = {e: st.enter_context(nc.semaphore("s_" + e)) for e in ENGS}
            dsem = {e: [st.enter_context(nc.semaphore("d_%s_%d" % (e, i)))
                        for i in range(N_DMA_SEMS)] for e in ("sp", "act", "pool")}
            cnt = {e: 0 for e in ENGS}
            dcnt = {e: [0] * N_DMA_SEMS for e in dsem}
            drr = {e: 0 for e in dsem}
            prev_use = {}
            for i, r in enumerate(ins):
                e = r["eng"]
                if r["dma"]:
                    s = drr[e]
                    drr[e] = (s + 1) % N_DMA_SEMS
                    if dcnt[e][s] > 0:
                        r["prev"] = (dsem[e][s], dcnt[e][s], ("d", e, s))
                    else:
                        r["prev"] = None
                    dcnt[e][s] += 16
                    r["sem"] = dsem[e][s]
                    r["val"] = dcnt[e][s]
                    r["semkey"] = ("d", e, s)
                elif r["needed"]:
                    cnt[e] += 1
                    r["sem"] = esem[e]
                    r["val"] = cnt[e]
                    r["semkey"] = ("e", e)
            block = st.enter_context(nc.Block())
            per_eng = {e: [i for i, r in enumerate(ins) if r["eng"] == e] for e in ENGS}
            final = {}
            for i in last_dmas:
                r = ins[i]
                final[r["semkey"]] = (r["sem"], max(r["val"], final.get(r["semkey"], (None, 0))[1]))

            def body(ename):
                def run(engobj):
                    waited = {}
                    for i in per_eng[ename]:
                        r = ins[i]
                        need = {}
                        for d in r["deps"]:
                            p = ins[d]
                            k = p["semkey"]
                            if need.get(k, (None, 0))[1] < p["val"]:
                                need[k] = (p["sem"], p["val"])
                        if r["dma"] and r["prev"] is not None:
                            s, v, k = r["prev"]
                            if need.get(k, (None, 0))[1] < v:
                                need[k] = (s, v)
                        for k, (s, v) in need.items():
                            if waited.get(k, 0) < v:
                                engobj.wait_ge(s, v)
                                waited[k] = v
                                self.trace.setdefault(ename, []).append(("w", k, v, i))
                        h = r["fn"](engobj)
                        if r["dma"]:
                            h.then_inc(r["sem"], 16)
                            self.trace.setdefault(ename, []).append(("i", r["semkey"], 16, i))
                        elif r["needed"]:
                            h.then_inc(r["sem"], 1)
                            self.trace.setdefault(ename, []).append(("i", r["semkey"], 1, i))
                    if ename == final_wait_eng:
                        for k, (s, v) in final.items():
                            if waited.get(k, 0) < v:
                                engobj.wait_ge(s, v)
                return run

            block.tensor(body("pe"))
            block.scalar(body("act"))
            block.vector(body("dve"))
            block.gpsimd(body("pool"))
            block.sync(body("sp"))

    def dma(self, eng, out, in_, **kw):
        return self.add(eng, lambda e: e.dma_start(out=out, in_=in_, **kw),
                        reads=[in_], writes=[out], dma=True)

    def matmul(self, out, lhsT, rhs, start=True, stop=True, **kw):
        return self.add("pe", lambda e: e.matmul(out, lhsT, rhs, start=start, stop=stop, **kw),
                        reads=[lhsT, rhs], writes=[out])

    def transpose(self, out, in_, ident):
        return self.add("pe", lambda e: e.transpose(out, in_, ident),
                        reads=[in_, ident], writes=[out])

    def act(self, out, in_, func, bias=None, scale=1.0, accum_out=None, eng="act"):
        reads = [in_]
        writes = [out]
        kw = {}
        if bias is not None:
            kw["bias"] = bias
            if not isinstance(bias, (int, float)):
                reads.append(bias)
        if not isinstance(scale, (int, float)):
            reads.append(scale)
        if accum_out is not None:
            kw["accum_out"] = accum_out
            writes.append(accum_out)
        return self.add(eng, lambda e: e.activation(out=out, in_=in_, func=func, scale=scale, **kw),
                        reads=reads, writes=writes)

    def tt(self, out, in0, in1, op, eng="dve"):
        return self.add(eng, lambda e: e.tensor_tensor(out=out, in0=in0, in1=in1, op=op),
                        reads=[in0, in1], writes=[out])

    def ts(self, out, in0, s1, op0, s2=None, op1=None, eng="dve", accum_out=None):
        reads = [in0]
        if not isinstance(s1, (int, float)):
            reads.append(s1)
        if s2 is not None and not isinstance(s2, (int, float)):
            reads.append(s2)
        kw = {}
        writes = [out]
        if op1 is not None:
            kw["op1"] = op1
        if accum_out is not None:
            kw["accum_out"] = accum_out
            writes.append(accum_out)
        return self.add(eng, lambda e: e.tensor_scalar(out=out, in0=in0, scalar1=s1, scalar2=s2,
                                                       op0=op0, **kw),
                        reads=reads, writes=writes)

    def stt(self, out, in0, scalar, in1, op0, op1, eng="dve"):
        reads = [in0, in1]
        if not isinstance(scalar, (int, float)):
            reads.append(scalar)
        return self.add(eng, lambda e: e.scalar_tensor_tensor(out=out, in0=in0, scalar=scalar,
                                                              in1=in1, op0=op0, op1=op1),
                        reads=reads, writes=[out])

    def copy(self, out, in_, eng="dve"):
        if eng == "act":
            return self.add(eng, lambda e: e.activation(out=out, in_=in_, func=AF.Identity),
                            reads=[in_], writes=[out])
        return self.add(eng, lambda e: e.tensor_copy(out=out, in_=in_), reads=[in_], writes=[out])

    def memset(self, ap, val, eng="dve"):
        return self.add(eng, lambda e: e.memset(ap, val), reads=[], writes=[ap])

    def reduce(self, out, in_, op, axis=AX.X, eng="dve", **kw):
        return self.add(eng, lambda e: e.tensor_reduce(out=out, in_=in_, op=op, axis=axis, **kw),
                        reads=[in_], writes=[out])

    def recip(self, out, in_, eng="dve"):
        return self.add(eng, lambda e: e.reciprocal(out=out, in_=in_), reads=[in_], writes=[out])


def simulate_trace(trace):
    pos = {e: 0 for e in trace}
    sem = {}
    progress = True
    while progress:
        progress = False
        for e, ops in trace.items():
            while pos[e] < len(ops):
                kind, k, v, i = ops[pos[e]]
                if kind == "w":
                    if sem.get(k, 0) >= v:
                        pos[e] += 1
                        progress = True
                    else:
                        break
                else:
                    sem[k] = sem.get(k, 0) + v
                    pos[e] += 1
                    progress = True
    stuck = {e: ops[pos[e]] for e, ops in trace.items() if pos[e] < len(ops)}
    return stuck, sem

import contextlib
import os
_SK = set(os.environ.get('DBGSKIP', '').split(','))
import math
import ml_dtypes
from concourse.bass_utils import run_bass_kernel_spmd

S = 2048
D = 1024
L = 2
INW = 2592
FF = 2816
NT = 16
NFC = FF // 128
ALPHA = float((2 * L) ** 0.25)
EPS = 1e-5
C_DQ, C_DK, C_DV, C_FU, C_GQ, C_GK, C_GV, C_GR, C_GZ = 0, 512, 1024, 1536, 1792, 1920, 2048, 2304, 2560

CF_ID, CF_COS, CF_SIN, CF_HM, CF_CC, CF_SC, CF_ONES, CF_NEGH, CF_N = 0, 128, 640, 1152, 1156, 1284, 1412, 1540, 1548
CB_ID, CB_TRIF, CB_TRIB, CB_BD, CB_ONE, CB_N = 0, 128, 256, 384, 640, 768


def host_consts():
    p = np.arange(128)
    cf = np.zeros((128, CF_N), np.float32)
    cf[:, CF_ID:CF_ID + 128] = np.eye(128, dtype=np.float32)
    inv_freq = (10000.0 ** (-np.arange(0, 64, 2, dtype=np.float32) / 64)).astype(np.float32)
    pos = (np.arange(NT)[None, :] * 128 + p[:, None]).astype(np.float32)
    ang = pos[:, :, None] * inv_freq[None, None, :]
    cf[:, CF_COS:CF_COS + 512] = np.cos(ang).reshape(128, 512)
    cf[:, CF_SIN:CF_SIN + 512] = np.sin(ang).reshape(128, 512)
    cf[:, CF_HM:CF_HM + 4] = (p[:, None] // 32 == np.arange(4)[None, :]).astype(np.float32)
    c = np.arange(64)
    a = 2 * np.pi * np.outer(c, c) / 64.0
    cc = np.cos(a) / 8.0
    sc = np.sin(a) / 8.0
    z = np.zeros((64, 64))
    cf[:, CF_CC:CF_CC + 128] = np.block([[cc, z], [z, cc]])
    cf[:, CF_SC:CF_SC + 128] = np.block([[sc, z], [z, sc]])
    cf[:, CF_ONES:CF_ONES + 128] = 1.0
    cf[:, CF_NEGH:CF_NEGH + 8] = -0.5
    cb = np.zeros((128, CB_N), np.float32)
    cb[:, CB_ID:CB_ID + 128] = np.eye(128)
    cb[:, CB_TRIF:CB_TRIF + 128] = (p[:, None] <= p[None, :])
    cb[:, CB_TRIB:CB_TRIB + 128] = (p[:, None] >= p[None, :])
    cb[:, CB_BD:CB_BD + 256] = (p[:, None] // 32 == np.arange(256)[None, :] // 64)
    cb[:, CB_ONE:CB_ONE + 128] = 1.0
    s = np.arange(S, dtype=np.float64)
    sk = np.outer(s, s) % S
    ang = 2 * np.pi * sk / S
    dc = (np.cos(ang) / math.sqrt(S)).astype(np.float32).astype(ml_dtypes.bfloat16)
    ds = (-np.sin(ang) / math.sqrt(S)).astype(np.float32).astype(ml_dtypes.bfloat16)
    return cf, cb.astype(ml_dtypes.bfloat16), dc, ds


class Arena:
    def __init__(self, t, nbytes):
        self.t = t
        self.n = nbytes
        self.top = 0

    def mark(self):
        return self.top

    def release(self, m):
        self.top = m

    def alloc(self, shape, dt):
        n = 1
        for s_ in shape:
            n *= s_
        b = n * _DT_SIZE[dt]
        off = self.top
        self.top += (b + 63) // 64 * 64
        assert self.top <= self.n, ("arena overflow", self.top, self.n)
        ap = self.t[:, off // 2:(off + b) // 2]
        if dt != BF16:
            ap = ap.bitcast(dt)
        if len(shape) == 2:
            ap = ap.rearrange("p (a b) -> p a b", a=shape[0])
        elif len(shape) == 3:
            ap = ap.rearrange("p (a b c) -> p a b c", a=shape[0], b=shape[1])
        return ap


class Rot:
    def __init__(self, items):
        self.items = items
        self.i = 0

    def next(self):
        r = self.items[self.i % len(self.items)]
        self.i += 1
        return r


_CACHE = {}


class _Stop(Exception):
    pass


def build(n_layers=L, dbg=None, upto=None):
    nc = bass.Bass("TRN2", target_bir_lowering=False)
    dram = lambda name, shape, dt, kind="ExternalInput": nc.dram_tensor(name, shape, dt, kind=kind).ap()
    x_d = dram("x", [S, D], F32)
    w_in_d = dram("w_in", [L, D, INW], F32)
    lam_d = dram("diff_lambda", [L, 256], F32)
    dng_d = dram("diff_norm_g", [L, 128], F32)
    fw_d = dram("fourier_w", [L, 4, 64, 64], F32)
    w2_d = dram("gla_gate_w2", [L, 2, 16, 128], F32)
    b2_d = dram("gla_gate_b2", [L, 2, 128], F32)
    gng_d = dram("gla_norm_g", [L, 64], F32)
    wout_d = dram("w_out", [L, D, D], F32)
    ln_d = {k: dram(k, [L, D], F32) for k in ("ln1_g", "ln1_b", "ln2_g", "ln2_b")}
    wg_d = dram("ffn_w_gate", [L, D, FF], F32)
    wu_d = dram("ffn_w_up", [L, D, FF], F32)
    wd_d = dram("ffn_w_down", [L, FF, D], F32)
    cf_d = dram("c_f32", [128, CF_N], F32)
    cb_d = dram("c_bf", [128, CB_N], BF16)
    dftc_d = dram("dft_c", [S, S], BF16)
    dfts_d = dram("dft_s", [S, S], BF16)
    y_d = dram("y", [S, D], F32, kind="ExternalOutput")
    xs_d = dram("xs_scr", [S, D], F32, kind="Internal")
    dbg_outs = {}

    ARENA = 207 * 1024
    XTB = NT * D * 4
    with contextlib.ExitStack() as st:
        arena_t = st.enter_context(nc.sbuf_tensor("arena", [128, ARENA // 2], BF16))
        ps = st.enter_context(nc.psum_tensor("ps", [128, 4096], F32))
        P = Prog(nc)
        A = Arena(arena_t, ARENA)
        x_tok = arena_t[:, (ARENA - XTB) // 2:ARENA // 2].bitcast(F32).rearrange("p (t d) -> p t d", t=NT)

        def bank(b, n=512, off=0):
            return ps[:, b * 512 + off:b * 512 + off + n]

        def bankbf(b):
            return ps[:, b * 512:(b + 1) * 512].bitcast(BF16)

        def dump(name, ap, shape, dt=F32):
            if dbg is None or name not in dbg:
                return
            d_ = nc.dram_tensor("dbg_" + name, shape, dt, kind="ExternalOutput").ap()
            dbg_outs[name] = d_
            P.dma("sp", d_, ap)

        cf = A.alloc([CF_N], F32)
        cb = A.alloc([CB_N], BF16)
        xT = A.alloc([8, S], BF16)
        lnbuf = A.alloc([2, D], F32)
        small = A.alloc([640], F32)
        P.dma("sp", cf, cf_d)
        P.dma("sp", cb, cb_d)
        ident_f = cf[:, CF_ID:CF_ID + 128]
        ident_b = cb[:, CB_ID:CB_ID + 128]
        cos_t = cf[:, CF_COS:CF_COS + 512].rearrange("p (t i) -> p t i", t=NT)
        sin_t = cf[:, CF_SIN:CF_SIN + 512].rearrange("p (t i) -> p t i", t=NT)
        hmask4 = cf[:, CF_HM:CF_HM + 4]
        ccbd = cf[:, CF_CC:CF_CC + 128]
        scbd = cf[:, CF_SC:CF_SC + 128]
        ones_f = cf[:, CF_ONES:CF_ONES + 128]
        negh8 = cf[:, CF_NEGH:CF_NEGH + 8]
        tri = [cb[:, CB_TRIF:CB_TRIF + 128], cb[:, CB_TRIB:CB_TRIB + 128]]
        bdmask = cb[:, CB_BD:CB_BD + 256]
        lp_bc = small[:, 0:256]
        gdiff = small[:, 256:384]
        ggla = small[:, 384:448]
        negb2 = small[:, 448:450]
        neglam = small[:, 450:451]
        negM = small[:, 451:452]
        sc_tmp = small[:, 452:500]
        nrm = small[:, 500:504]
        epsc = small[:, 504:505]
        base_mark = A.mark()

        win = lambda l: w_in_d[l].rearrange("(k p) n -> p k n", p=128)

        def rstd_from(out, ss, n, width):
            tmp = sc_tmp[:, 40:40 + width]
            P.ts(tmp, ss, 1.0 / n, ALU.mult, EPS, ALU.add)
            P.tt(out, tmp, negh8[:, 0:width], ALU.pow, eng="pool")

        xb_rot = None

        def make_xT(x_tile_f32, t, xb, evac_eng="act", pbank=7):
            P.copy(xb, x_tile_f32, eng="act")
            pst = bankbf(pbank)
            for k in range(8):
                P.transpose(pst[:, k * 128:(k + 1) * 128], xb[:, k * 128:(k + 1) * 128], ident_b)
            P.copy(xT[:, :, t * 128:(t + 1) * 128], pst.rearrange("p (k n) -> p k n", k=8), eng=evac_eng)

        def ln_a(psum2, xres, gb, ytmp, par, alpha=ALPHA, mid=None):
            sct = sc_tmp[:, 0:16] if par == 0 else small[:, 540:556]
            P.stt(ytmp, xres, alpha, psum2, ALU.mult, ALU.add)
            stats = sct[:, 0:12]
            mv = sct[:, 12:14]
            for c_ in range(2):
                P.add("dve", lambda e, c_=c_: e.bn_stats(out=stats[:, c_ * 6:(c_ + 1) * 6],
                                                          in_=ytmp[:, c_ * 512:(c_ + 1) * 512]),
                      reads=[ytmp[:, c_ * 512:(c_ + 1) * 512]], writes=[stats[:, c_ * 6:(c_ + 1) * 6]])
            P.add("dve", lambda e: e.bn_aggr(out=mv, in_=stats), reads=[stats], writes=[mv])
            rs = sct[:, 14:15]
            nmr = sct[:, 15:16]
            P.act(rs, mv[:, 1:2], AF.Ln, bias=epsc)
            P.act(rs, rs, AF.Exp, scale=-0.5)
            if mid is not None:
                mid()
            P.stt(nmr, mv[:, 0:1], -1.0, rs, ALU.mult, ALU.mult)
            P.act(ytmp, ytmp, AF.Identity, bias=nmr, scale=rs)
            P.tt(ytmp, ytmp, gb[:, 0, :], ALU.mult, eng="pool")

        def ln_b(gb, out_tok, ytmp):
            P.tt(out_tok, ytmp, gb[:, 1, :], ALU.add)

        def ck(name):
            if upto == name:
                raise _Stop()

        try:
          for l in range(n_layers):
              lam_init = 0.8 - 0.6 * math.exp(-0.3 * l)
              A.release(base_mark)
              P.dma("sp", lp_bc, lam_d[l].partition_broadcast(128))
              P.dma("sp", gdiff, dng_d[l].partition_broadcast(128))
              P.dma("sp", ggla, gng_d[l].partition_broadcast(128))
              for d_ in range(2):
                  P.dma("sp", negb2[:, d_:d_ + 1], b2_d[l, d_].rearrange("(p o) -> p o", o=1))
              P.ts(negb2, negb2, -1.0, ALU.mult)
              P.ts(gdiff, gdiff, 1.0 - lam_init, ALU.mult)
              P.memset(epsc, EPS)
              pr = sc_tmp[:, 16:18]
              prod = A.alloc([128], F32)
              lp4 = lp_bc.rearrange("p (a d) -> p a d", a=4)
              for i_ in range(2):
                  P.tt(prod[:, 0:64], lp4[:, 2 * i_, :], lp4[:, 2 * i_ + 1, :], ALU.mult)
                  P.reduce(pr[:, i_:i_ + 1], prod[:, 0:64], ALU.add)
              P.act(pr, pr, AF.Exp)
              P.tt(neglam, pr[:, 1:2], pr[:, 0:1], ALU.subtract)
              P.ts(neglam, neglam, -lam_init, ALU.add)
              A.release(base_mark)
              ck('params')

              if l == 0:
                  m_ = A.mark()
                  xin = [A.alloc([D], F32) for _ in range(3)]
                  xbs = [A.alloc([D], BF16) for _ in range(2)]
                  for t in range(NT):
                      xi = xin[t % 3]
                      P.dma("sp", xi, x_d[t * 128:(t + 1) * 128, :])
                      make_xT(xi, t, xbs[t % 2], evac_eng="dve", pbank=(7 if t % 2 else 6))
                  A.release(m_)
              ck('xT')
              xres_d = x_d if l == 0 else xs_d

              ocat = A.alloc([8, S], BF16)
              Wout = A.alloc([8, D], BF16)
              Wgf = A.alloc([8, 288], BF16)
              Wgt = A.alloc([8, 512], BF16)
              wbd = A.alloc([2, 128], F32)
              Wcs = A.alloc([2, 256], BF16)
              mixer_mark = A.mark()
              NDB = 4
              DBB = 8 * 512 * 2
              dbuf = [arena_t[:, (ARENA - (i_ + 1) * DBB) // 2:(ARENA - i_ * DBB) // 2].rearrange(
                  "p (a b) -> p a b", a=8) for i_ in range(NDB)]
              dft_v = [dftc_d.rearrange("(t p) k -> p t k", p=128), dfts_d.rearrange("(t p) k -> p t k", p=128)]
              dft_chunks = [(kb, which, tg) for kb in range(4) for which in range(2) for tg in range(2)]

              def dft_load(j):
                  kb, which, tg = dft_chunks[j]
                  P.dma("sp", dbuf[j % NDB], dft_v[which][:, tg * 8:(tg + 1) * 8, kb * 512:(kb + 1) * 512])

              Wh = [A.alloc([8, 384], BF16) for _ in range(2)]
              QTzs = [A.alloc([2, S], BF16) for _ in range(2)]
              KTs = [A.alloc([S], BF16) for _ in range(2)]
              Vaugs = [A.alloc([NT, 132], BF16) for _ in range(2)]
              negMs = [negM, small[:, 570:571]]
              Wf = Wh[0][:, :, 0:256]
              PT = [A.alloc([1024], BF16) for _ in range(3)]
              O1n = A.alloc([8, 128], F32)
              O2t = A.alloc([8, 128], F32)
              otoks = [A.alloc([8, 128], BF16) for _ in range(2)]
              qkr = [A.alloc([256], BF16) for _ in range(3)]
              tmpAs = [A.alloc([128], F32) for _ in range(2)]
              tmpBs = [A.alloc([128], F32) for _ in range(2)]
              sq = A.alloc([256], F32)
              D2 = A.alloc([256], F32)
              sqs = [sq, A.alloc([256], F32)]
              red_all = A.alloc([NT, 4], F32)
              red4 = sc_tmp[:, 20:24]
              gm = sc_tmp[:, 24:26]
              nrm2 = sc_tmp[:, 26:28]
              rz = sc_tmp[:, 28:36]
              ss8 = small[:, 512:520]
              rstd8 = small[:, 520:528]
              for i_ in range(2):
                  P.memset(Vaugs[i_][:, :, 128:129], 1.0)
                  P.memset(QTzs[i_][64:128, 0, :], 0.0)
                  P.memset(QTzs[i_][0:64, 1, :], 0.0, eng="pool")
              assert A.top <= ARENA - 3 * DBB, A.top
              pso = [ps[:, (4 + qi // 3) * 512 + (qi % 3) * 160:(4 + qi // 3) * 512 + (qi % 3) * 160 + 129]
                     for qi in range(8)]
              grp = [(0, 3), (3, 3), (6, 2)]

              def pso_grp(gi, c0, c1):
                  q0, n_ = grp[gi]
                  base_ = (4 + gi) * 512
                  return ps[:, base_:base_ + n_ * 160].rearrange("p (a b) -> p a b", b=160)[:, :, c0:c1]

              def load_wh(h_):
                  for j_, c0 in enumerate((C_DQ, C_DK, C_DV)):
                      P.dma("pool", Wh[h_ % 2][:, :, j_ * 128:(j_ + 1) * 128],
                            win(l)[:, :, c0 + h_ * 128:c0 + (h_ + 1) * 128])

              load_wh(0)
              load_wh(1)
              fin_pending = []

              def att_inproj(h):
                  wh = Wh[h % 2]
                  QTz, KT, Vaug, negM_h = QTzs[h % 2], KTs[h % 2], Vaugs[h % 2], negMs[h % 2]
                  P.memset(nrm, 0.0)
                  tr_pending = []
                  for t in range(NT):
                      pb = bank(t % 4, 256)
                      pv_ = bank(4 + t % 2, 128)
                      for k in range(8):
                          P.matmul(pb, xT[:, k, t * 128:(t + 1) * 128], wh[:, k, 0:256], start=(k == 0), stop=(k == 7))
                      for k in range(8):
                          P.matmul(pv_, xT[:, k, t * 128:(t + 1) * 128], wh[:, k, 256:384], start=(k == 0), stop=(k == 7))
                      qk4 = pb.rearrange("p (g h d) -> p g h d", g=4, h=2)
                      t1 = qk4[:, :, 0, :]
                      t2 = qk4[:, :, 1, :]
                      cbt = cos_t[:, t:t + 1, :].broadcast_to([128, 4, 32])
                      sbt = sin_t[:, t:t + 1, :].broadcast_to([128, 4, 32])
                      q_ = qkr[t % 3]
                      q4 = q_.rearrange("p (g h d) -> p g h d", g=4, h=2)
                      ta = tmpAs[t % 2].rearrange("p (g d) -> p g d", g=4)
                      tb_ = tmpBs[t % 2].rearrange("p (g d) -> p g d", g=4)
                      P.tt(ta, t1, cbt, ALU.mult)
                      P.tt(tb_, t2, sbt, ALU.mult)
                      P.tt(q4[:, :, 0, :], ta, tb_, ALU.subtract)
                      P.tt(ta, t2, cbt, ALU.mult)
                      P.tt(tb_, t1, sbt, ALU.mult)
                      P.tt(q4[:, :, 1, :], ta, tb_, ALU.add)
                      P.copy(Vaug[:, t, 0:128], pv_, eng="act")
                      P.tt(sqs[t % 2], q_, q_, ALU.mult, eng="pool")
                      def tr_(t=t, q_=q_, sq_=sqs[t % 2]):
                          P.reduce(red_all[:, t, :], sq_.rearrange("p (g d) -> p g d", g=4), ALU.add)
                          pst = bankbf(7 if t % 2 else 6)
                          P.transpose(pst[:, 0:128], q_[:, 0:128], ident_b)
                          P.transpose(pst[:, 128:256], q_[:, 128:256], ident_b)
                          P.copy(QTz[0:64, 0, t * 128:(t + 1) * 128], pst[0:64, 0:128], eng="act")
                          P.copy(QTz[64:128, 1, t * 128:(t + 1) * 128], pst[64:128, 0:128], eng="act")
                          P.copy(KT[:, t * 128:(t + 1) * 128], pst[:, 128:256], eng="act")
                      tr_pending.append(tr_)
                      if len(tr_pending) > 1:
                          tr_pending.pop(0)()
                  while tr_pending:
                      tr_pending.pop(0)()
                  P.reduce(nrm, red_all.rearrange("p t g -> p g t"), ALU.max)
                  ck('h0proj')
                  if h + 2 < 4:
                      load_wh(h + 2)
                  if h == 3:
                      P.dma("pool", Wf, win(l)[:, :, C_FU:C_FU + 256])
                      P.dma("pool", Wgf[:, :, 0:256], win(l)[:, :, C_GQ:C_GQ + 256])
                      P.dma("pool", Wgf[:, :, 256:288], win(l)[:, :, C_GZ:C_GZ + 32])
                      P.dma("pool", Wgt, win(l)[:, :, C_GV:C_GV + 512])
                      for k in range(8):
                          P.dma("pool", Wout[:, k, :], wout_d[l, k * 128:(k + 1) * 128, :])
                      P.memset(wbd, 0.0, eng="pool")
                      for g_ in range(4):
                          c_, gl = g_ // 2, g_ % 2
                          P.dma("sp", wbd[gl * 64:(gl + 1) * 64, c_, gl * 64:(gl + 1) * 64], fw_d[l, g_])
                      for j_ in range(NDB - 1):
                          dft_load(j_)
                  P.reduce(nrm2, nrm.rearrange("p (a b) -> p a b", a=2), ALU.max)
                  P.ts(D2[:, 0:128], ident_f, nrm2[:, 0:1], ALU.mult)
                  P.ts(D2[:, 128:256], ident_f, nrm2[:, 1:2], ALU.mult)
                  pbm = bank(6, 256)
                  P.matmul(pbm, ones_f, D2, start=True, stop=True)
                  P.reduce(gm, pbm.rearrange("p (a b) -> p a b", a=2), ALU.max)
                  P.tt(negM_h, gm[:, 0:1], gm[:, 1:2], ALU.add)
                  P.ts(negM_h, negM_h, -0.5 * 0.125, ALU.mult)
                  if l == 0 and h == 0:
                      dump("xT", xT, [128, 8, S], BF16)
                      dump("negM", negM_h, [128, 1])
                      ck('qkt')

              def att_steps(h):
                  QTz, KT, Vaug, negM_h = QTzs[h % 2], KTs[h % 2], Vaugs[h % 2], negMs[h % 2]
                  for qh in range(2):
                      otok = otoks[qh]
                      steps = [(m, kt) for m in range(2) for kt in range(NT)]

                      def emit_scores(i):
                          m, kt = steps[i]
                          sb_i = (i % 2) * 2
                          for j_ in range(2):
                              P.matmul(ps[:, (sb_i + j_) * 512:(sb_i + j_ + 1) * 512],
                                       KT[:, kt * 128:(kt + 1) * 128],
                                       QTz[:, m, qh * 1024 + j_ * 512:qh * 1024 + (j_ + 1) * 512],
                                       start=True, stop=True)

                      emit_scores(0)
                      for i, (m, kt) in enumerate(steps):
                          if i + 1 < len(steps):
                              emit_scores(i + 1)
                          if i == 4 and fin_pending:
                              fin_pending.pop(0)()
                          sb_i = (i % 2) * 2
                          pss = ps[:, sb_i * 512:(sb_i + 2) * 512]
                          pt = PT[i % 3]
                          P.act(pt, pss, AF.Exp, bias=negM_h, scale=0.125)
                          for qi in range(8):
                              P.matmul(pso[qi], pt[:, qi * 128:(qi + 1) * 128], Vaug[:, kt, 0:129],
                                       start=(kt == 0 and qi % 3 == 0), stop=(kt == NT - 1), skip_group_check=True)
                          if kt != NT - 1:
                              continue
                          if m == 0:
                              for gi, (q0, n_) in enumerate(grp):
                                  P.recip(rz[:, q0:q0 + n_], pso_grp(gi, 128, 129).rearrange("p a b -> p (a b)"))
                                  P.tt(O1n[:, q0:q0 + n_, :], pso_grp(gi, 0, 128),
                                       rz[:, q0:q0 + n_].unsqueeze(2).broadcast_to([128, n_, 128]), ALU.mult)
                          else:
                              for gi, (q0, n_) in enumerate(grp):
                                  P.recip(rz[:, q0:q0 + n_], pso_grp(gi, 128, 129).rearrange("p a b -> p (a b)"))
                              P.ts(rz, rz, neglam, ALU.mult)
                              for gi, (q0, n_) in enumerate(grp):
                                  P.tt(O2t[:, q0:q0 + n_, :], pso_grp(gi, 0, 128),
                                       rz[:, q0:q0 + n_].unsqueeze(2).broadcast_to([128, n_, 128]), ALU.mult)
                              P.tt(O1n, O1n, O2t, ALU.add)

                              def chain_(h=h, qh=qh, otok=otok):
                                  P.tt(O2t, O1n, O1n, ALU.mult, eng="pool")
                                  P.reduce(ss8, O2t, ALU.add)
                                  rstd_from(rstd8, ss8, 128.0, 8)
                                  P.tt(O1n, O1n, rstd8.unsqueeze(2).broadcast_to([128, 8, 128]), ALU.mult)
                                  P.tt(otok, O1n, gdiff.unsqueeze(1).broadcast_to([128, 8, 128]), ALU.mult,
                                       eng="pool")
                                  for qi in range(8):
                                      dst_ = ocat[:, h, qh * 1024 + qi * 128:qh * 1024 + (qi + 1) * 128]
                                      P.add("sp", lambda e, dst_=dst_, src_=otok[:, qi, :]: e.dma_start(
                                          out=dst_, in_=src_, transpose=True),
                                          reads=[otok[:, qi, :]], writes=[dst_], dma=True)
                              fin_pending.append(chain_)
              att_inproj(0)
              att_inproj(1)
              for h in range(4):
                  att_steps(h)
                  if h + 2 < 4:
                      att_inproj(h + 2)
              while fin_pending:
                  fin_pending.pop(0)()
              A.release(mixer_mark)
              if l == 0:
                  dump("odiff", ocat[:, 0:4, :], [128, 4, S], BF16)
              ck('att')

              _skip = A.alloc([2 * 8 * 384], BF16)
              uT = A.alloc([2, S], BF16)
              ucs = A.alloc([NT, 512], BF16)
              P.dma("sp", lnbuf[:, 0, :], ln_d["ln1_g"][l].partition_broadcast(128))
              P.dma("sp", lnbuf[:, 1, :], ln_d["ln1_b"][l].partition_broadcast(128))
              dft_load(NDB - 1)
              for c_ in range(2):
                  pb = bank(4 + c_, 256)
                  P.matmul(pb[:, 0:128], ccbd, wbd[:, c_, :], start=True, stop=True)
                  P.matmul(pb[:, 128:256], scbd, wbd[:, c_, :], start=True, stop=True)
                  P.copy(Wcs[:, c_, :], pb, eng="act")
              for tb in range(4):
                  for c_ in range(2):
                      pb = bank((tb * 2 + c_) % 4)
                      for k in range(8):
                          P.matmul(pb, Wf[:, k, c_ * 128:(c_ + 1) * 128], xT[:, k, tb * 512:(tb + 1) * 512],
                                   start=(k == 0), stop=(k == 7))
                      P.copy(uT[:, c_, tb * 512:(tb + 1) * 512], pb, eng="act")
              for t in range(NT):
                  pb = bank(4 + t % 2)
                  for c_ in range(2):
                      P.matmul(pb[:, c_ * 256:(c_ + 1) * 256], uT[:, c_, t * 128:(t + 1) * 128], Wcs[:, c_, :],
                               start=True, stop=True)
                  P.copy(ucs[:, t, :], pb, eng="act")
                  if t == 7:
                      ucs_half = True
              for j_, (kb, which, tg) in enumerate(dft_chunks):
                  pbs = [bank(0 + (kb % 2) * 2), bank(1 + (kb % 2) * 2)]
                  db = dbuf[j_ % NDB]
                  for tt_ in range(8):
                      t = tg * 8 + tt_
                      first = (which == 0 and t == 0)
                      last = (which == 1 and t == NT - 1)
                      for c_ in range(2):
                          P.matmul(pbs[c_], ucs[:, t, c_ * 256 + which * 128:c_ * 256 + (which + 1) * 128],
                                   db[:, tt_, :], start=first, stop=last)
                  if j_ + NDB < len(dft_chunks):
                      dft_load(j_ + NDB)
                  if which == 1 and tg == 1:
                      for c_ in range(2):
                          P.copy(ocat[:, 4 + c_, kb * 512:(kb + 1) * 512], pbs[c_], eng=("act" if c_ else "dve"))
              A.release(mixer_mark)
              if l == 0:
                  dump("ofour", ocat[:, 4:6, :], [128, 2, S], BF16)
              ck('four')

              gqk = A.alloc([2, S], F32)
              gzT = A.alloc([S], BF16)
              gv = A.alloc([NT, 256], BF16)
              gate = A.alloc([NT, 256], BF16)
              w2pad = A.alloc([2, 128], BF16)
              qt = [A.alloc([S], BF16) for _ in range(2)]
              kt_ = [A.alloc([S], BF16) for _ in range(2)]
              Sbf = [A.alloc([NT, 256], BF16) for _ in range(2)]
              gla_mark = A.mark()
              P.memset(w2pad, 0.0)
              for d_ in range(2):
                  P.dma("pool", w2pad[d_ * 16:(d_ + 1) * 16, d_, :], w2_d[l, d_])
              for tb in range(4):
                  for c_ in range(2):
                      pb = bank((tb * 3 + c_) % 4)
                      for k in range(8):
                          P.matmul(pb, Wgf[:, k, c_ * 128:(c_ + 1) * 128], xT[:, k, tb * 512:(tb + 1) * 512],
                                   start=(k == 0), stop=(k == 7))
                      P.copy(gqk[:, c_, tb * 512:(tb + 1) * 512], pb, eng=("act" if c_ else "dve"))
                  pb = bank((tb * 3 + 2) % 4)
                  for k in range(8):
                      P.matmul(pb[0:32, :], Wgf[:, k, 256:288], xT[:, k, tb * 512:(tb + 1) * 512],
                               start=(k == 0), stop=(k == 7))
                  P.copy(gzT[0:32, tb * 512:(tb + 1) * 512], pb[0:32, :], eng="dve")
              for t in range(NT):
                  pb = bank(4 + t % 2)
                  for k in range(8):
                      P.matmul(pb, xT[:, k, t * 128:(t + 1) * 128], Wgt[:, k, :], start=(k == 0), stop=(k == 7))
                  P.copy(gv[:, t, :], pb[:, 0:256], eng="act")
                  P.act(gate[:, t, :], pb[:, 256:512], AF.Silu)
              A.release(gla_mark)
              Bc = A.alloc([S], F32)
              Ec = A.alloc([S], F32)
              kdec_tok = A.alloc([NT, 128], BF16)
              kdT = [A.alloc([128], BF16) for _ in range(2)]
              Srot = [A.alloc([256], F32) for _ in range(2)]
              for d_ in range(2):
                  for tb in range(4):
                      pb = bank(tb % 4)
                      P.matmul(pb, w2pad[0:32, d_, :], gzT[0:32, tb * 512:(tb + 1) * 512], start=True, stop=True)
                      P.act(Ec[:, tb * 512:(tb + 1) * 512], pb, AF.Exp, bias=negb2[:, d_:d_ + 1], scale=-1.0)
                  P.act(Ec, Ec, AF.Ln, bias=1.0)
                  for n in range(NT):
                      o_ = Bc[:, n * 128:(n + 1) * 128]
                      i_ = Ec[:, n * 128:(n + 1) * 128]
                      if d_ == 1:
                          o_ = o_[:, ::-1]
                          i_ = i_[:, ::-1]
                      P.add("dve", lambda e, o_=o_, i_=i_: e.tensor_tensor_scan(
                          out=o_, data0=ones_f, data1=i_, initial=0.0, op0=ALU.mult, op1=ALU.add),
                          reads=[ones_f, Ec[:, n * 128:(n + 1) * 128]], writes=[Bc[:, n * 128:(n + 1) * 128]])
                  P.act(Ec, Bc, AF.Exp, scale=-1.0 / 16.0)
                  P.act(Bc, Bc, AF.Exp, scale=1.0 / 16.0)
                  P.stt(qt[d_], gqk[:, 0, :], 32.0 ** -0.5, Ec, ALU.mult, ALU.mult)
                  P.tt(kt_[d_], gqk[:, 1, :], Bc, ALU.mult)
                  elast = [Ec[:, n * 128 + (127 if d_ == 0 else 0):n * 128 + (127 if d_ == 0 else 0) + 1]
                           for n in range(NT)]
                  pst = bankbf(7)
                  for n in range(NT):
                      kd = kdT[n % 2]
                      P.stt(kd, gqk[:, 1, n * 128:(n + 1) * 128], elast[n], Bc[:, n * 128:(n + 1) * 128],
                            ALU.mult, ALU.mult)
                      P.transpose(pst[:, (n % 8) * 128:(n % 8 + 1) * 128], kd, ident_b)
                      if n % 8 == 7:
                          P.copy(kdec_tok[:, n - 7:n + 1, :], pst.rearrange("p (a b) -> p a b", a=8), eng="act")
                  order = list(range(NT)) if d_ == 0 else list(range(NT - 1, -1, -1))
                  prev = Srot[0]
                  P.memset(prev, 0.0)
                  P.memset(Sbf[d_][:, order[0], :], 0.0)
                  for i_, n in enumerate(order[:-1]):
                      pb = bank(4 + i_ % 2, 256)
                      P.matmul(pb, kdec_tok[:, n, :], gv[:, n, :], start=True, stop=True)
                      cur = Srot[(i_ + 1) % 2]
                      P.stt(cur, prev, elast[n], pb, ALU.mult, ALU.add)
                      P.tt(Sbf[d_][:, order[i_ + 1], :], cur, bdmask, ALU.mult, eng="pool")
                      prev = cur
              A.release(gla_mark)
              Qbd = [A.alloc([4, 128], BF16) for _ in range(4)]
              Asb = [A.alloc([4, 128], BF16) for _ in range(4)]
              ogf = [A.alloc([256], F32) for _ in range(2)]
              ogb = [A.alloc([256], BF16) for _ in range(3)]
              junk = A.alloc([64], F32)
              ss4s = [small[:, 528:532], small[:, 560:564]]
              rs4s = [small[:, 532:536], small[:, 564:568]]
              P.tt(gate.rearrange("p t (h v) -> p (t h) v", h=4), gate.rearrange("p t (h v) -> p (t h) v", h=4),
                   ggla.unsqueeze(1).broadcast_to([128, NT * 4, 64]), ALU.mult, eng="pool")

              def gla_m(n):
                  po = bank(4 + n % 2, 256)
                  P.matmul(po, qt[0][:, n * 128:(n + 1) * 128], Sbf[0][:, n, :], start=True, stop=False)
                  P.matmul(po, qt[1][:, n * 128:(n + 1) * 128], Sbf[1][:, n, :], start=False, stop=False)
                  for d_ in range(2):
                      ci = 2 * n + d_
                      qb = Qbd[ci % 4]
                      P.tt(qb, qt[d_][:, n * 128:(n + 1) * 128].unsqueeze(1).broadcast_to([128, 4, 128]),
                           hmask4.unsqueeze(2).broadcast_to([128, 4, 128]), ALU.mult, eng="pool")
                      P.matmul(bank(ci % 4), kt_[d_][:, n * 128:(n + 1) * 128], qb.rearrange("p a b -> p (a b)"),
                               start=True, stop=True)
                  for d_ in range(2):
                      ci = 2 * n + d_
                      asb = Asb[ci % 4]
                      P.tt(asb, bank(ci % 4).rearrange("p (a b) -> p a b", a=4),
                           tri[d_].unsqueeze(1).broadcast_to([128, 4, 128]), ALU.mult)
                      for h in range(4):
                          P.matmul(po[:, h * 64:(h + 1) * 64], asb[:, h, :], gv[:, n, h * 64:(h + 1) * 64],
                                   start=False, stop=(d_ == 1 and h == 3))

              def gla_a(n):
                  po = bank(4 + n % 2, 256)
                  ss4 = ss4s[n % 2]
                  rs4 = rs4s[n % 2]
                  for h in range(4):
                      P.act(junk, po[:, h * 64:(h + 1) * 64], AF.Square, accum_out=ss4[:, h:h + 1])
                  P.act(rs4, ss4, AF.Ln, bias=epsc, scale=1.0 / 64.0)
                  P.act(rs4, rs4, AF.Exp, scale=-0.5)
                  of = ogf[n % 2]
                  of3 = of.rearrange("p (h v) -> p h v", h=4)
                  P.tt(of3, po.rearrange("p (h v) -> p h v", h=4), rs4.unsqueeze(2).broadcast_to([128, 4, 64]),
                       ALU.mult)
                  ob = ogb[n % 3]
                  P.tt(ob, of, gate[:, n, :], ALU.mult)
                  for c_ in range(2):
                      dst_ = ocat[:, 6 + c_, n * 128:(n + 1) * 128]
                      P.add("sp", lambda e, dst_=dst_, src_=ob[:, c_ * 128:(c_ + 1) * 128]: e.dma_start(
                          out=dst_, in_=src_, transpose=True),
                          reads=[ob[:, c_ * 128:(c_ + 1) * 128]], writes=[dst_], dma=True)

              for s_ in range(NT + 1):
                  if s_ < NT:
                      gla_m(s_)
                  if s_ >= 1:
                      gla_a(s_ - 1)
              A.release(mixer_mark)
              if l == 0:
                  dump("ogla", ocat[:, 6:8, :], [128, 2, S], BF16)
              ck('gla')

              A.n = ARENA - XTB
              assert A.top <= A.n
              post_mark = A.mark()
              ytmp = [A.alloc([D], F32) for _ in range(3)]
              xbs = [A.alloc([D], BF16) for _ in range(2)]
              for t in range(NT):
                  P.dma("sp", x_tok[:, t, :], xres_d[t * 128:(t + 1) * 128, :])
              def mm1_(t):
                  b0 = (t % 3) * 2
                  p2 = ps[:, b0 * 512:(b0 + 2) * 512]
                  for hf in range(2):
                      for e_ in range(8):
                          P.matmul(p2[:, hf * 512:(hf + 1) * 512], ocat[:, e_, t * 128:(t + 1) * 128],
                                   Wout[:, e_, hf * 512:(hf + 1) * 512], start=(e_ == 0), stop=(e_ == 7))

              def a1_(t, mid=None):
                  b0 = (t % 3) * 2
                  ln_a(ps[:, b0 * 512:(b0 + 2) * 512], x_tok[:, t, :], lnbuf, ytmp[t % 3], t % 2, mid=mid)

              def b1_(t):
                  ln_b(lnbuf, x_tok[:, t, :], ytmp[t % 3])

              def c1_(t):
                  make_xT(x_tok[:, t, :], t, xbs[t % 2])

              for s_ in range(NT + 3):
                  if s_ < NT:
                      mm1_(s_)
                  midf = (lambda t_=s_ - 3: b1_(t_)) if 0 <= s_ - 3 < NT else None
                  if 0 <= s_ - 1 < NT:
                      a1_(s_ - 1, mid=midf)
                  elif midf is not None:
                      midf()
                  if 0 <= s_ - 3 < NT:
                      c1_(s_ - 3)
              A.release(base_mark)
              if l == 0:
                  dump("x1", x_tok, [128, NT, D])
              ck('ln1')

              GF = NFC // 2
              Wdg = A.alloc([GF, D], BF16)
              hT = A.alloc([GF, S], BF16)
              NWB = 3
              wgu = [A.alloc([2, 8, 128], BF16) for _ in range(NWB)]
              sg = [A.alloc([512], BF16) for _ in range(2)]
              ytmp = [A.alloc([D], F32) for _ in range(1)]
              xbs = [A.alloc([D], BF16) for _ in range(2)]
              wdv = wd_d[l].rearrange("(f p) n -> p f n", p=128)
              wgv = wg_d[l].rearrange("(k p) n -> p k n", p=128)
              wuv = wu_d[l].rearrange("(k p) n -> p k n", p=128)
              P.dma("sp", lnbuf[:, 0, :], ln_d["ln2_g"][l].partition_broadcast(128))
              P.dma("sp", lnbuf[:, 1, :], ln_d["ln2_b"][l].partition_broadcast(128))
              wi = 0
              ui = 0
              for g_ in range(2):
                  for fi in range(GF):
                      f = g_ * GF + fi
                      wb = wgu[wi % NWB]
                      wi += 1
                      P.dma("pool", wb[:, 0, :, :], wgv[:, :, f * 128:(f + 1) * 128])
                      P.dma("pool", wb[:, 1, :, :], wuv[:, :, f * 128:(f + 1) * 128])
                      P.dma("pool", Wdg[:, fi, :], wdv[:, f, :])
                      for tb in range(4):
                          pg = bank((ui % 2) * 2)
                          pu = bank((ui % 2) * 2 + 1)
                          ui += 1
                          for k in range(8):
                              P.matmul(pg, wb[:, 0, k, :], xT[:, k, tb * 512:(tb + 1) * 512],
                                       start=(k == 0), stop=(k == 7))
                          for k in range(8):
                              P.matmul(pu, wb[:, 1, k, :], xT[:, k, tb * 512:(tb + 1) * 512],
                                       start=(k == 0), stop=(k == 7))
                          s_ = sg[ui % 2]
                          P.act(s_, pg, AF.Silu)
                          P.tt(hT[:, fi, tb * 512:(tb + 1) * 512], pu, s_, ALU.mult)
                  ytmps = [ytmp[0], wgu[0].rearrange("p a b c -> p (a b c)").bitcast(F32),
                           wgu[1].rearrange("p a b c -> p (a b c)").bitcast(F32)]

                  def mm2_(t):
                      b0 = (t % 3) * 2
                      p2 = ps[:, b0 * 512:(b0 + 2) * 512]
                      for hf in range(2):
                          for fi in range(GF):
                              P.matmul(p2[:, hf * 512:(hf + 1) * 512], hT[:, fi, t * 128:(t + 1) * 128],
                                       Wdg[:, fi, hf * 512:(hf + 1) * 512], start=(fi == 0), stop=(fi == GF - 1))

                  def a2_(t, mid=None):
                      b0 = (t % 3) * 2
                      p2 = ps[:, b0 * 512:(b0 + 2) * 512]
                      if g_ == 0:
                          P.stt(x_tok[:, t, :], x_tok[:, t, :], ALPHA, p2, ALU.mult, ALU.add)
                          if mid is not None:
                              mid()
                      else:
                          ln_a(p2, x_tok[:, t, :], lnbuf, ytmps[t % 3], t % 2, alpha=1.0, mid=mid)

                  def b2_(t):
                      if g_ == 0:
                          return
                      ln_b(lnbuf, x_tok[:, t, :], ytmps[t % 3])

                  def c2_(t):
                      if g_ == 0:
                          return
                      if l == n_layers - 1:
                          P.dma("sp", y_d[t * 128:(t + 1) * 128, :], x_tok[:, t, :])
                      else:
                          P.dma("sp", xs_d[t * 128:(t + 1) * 128, :], x_tok[:, t, :])
                          make_xT(x_tok[:, t, :], t, xbs[t % 2])

                  for s_ in range(NT + 3):
                      if s_ < NT:
                          mm2_(s_)
                      midf = (lambda t_=s_ - 3: b2_(t_)) if 0 <= s_ - 3 < NT else None
                      if 0 <= s_ - 1 < NT:
                          a2_(s_ - 1, mid=midf)
                      elif midf is not None:
                          midf()
                      if 0 <= s_ - 3 < NT:
                          c2_(s_ - 3)
              A.release(base_mark)
              A.n = ARENA

        except _Stop:
            pass
        P.emit()
    _CACHE['P'] = P
    return nc, dbg_outs


def kernel(**inputs):
    if "c" not in _CACHE:
        _CACHE["c"] = host_consts()
    cf, cb, dc, ds = _CACHE["c"]
    nc, _ = build()
    x = np.ascontiguousarray(inputs["x"], dtype=np.float32)
    common = {
        "w_in": np.ascontiguousarray(inputs["w_in"], dtype=np.float32),
        "diff_lambda": np.ascontiguousarray(inputs["diff_lambda"], dtype=np.float32).reshape(L, 256),
        "diff_norm_g": np.ascontiguousarray(inputs["diff_norm_g"], dtype=np.float32),
        "fourier_w": np.ascontiguousarray(inputs["fourier_w"], dtype=np.float32),
        "gla_gate_w2": np.ascontiguousarray(inputs["gla_gate_w2"], dtype=np.float32),
        "gla_gate_b2": np.ascontiguousarray(inputs["gla_gate_b2"], dtype=np.float32),
        "gla_norm_g": np.ascontiguousarray(inputs["gla_norm_g"], dtype=np.float32),
        "w_out": np.ascontiguousarray(inputs["w_out"], dtype=np.float32),
        "ln1_g": np.ascontiguousarray(inputs["ln1_g"], dtype=np.float32),
        "ln1_b": np.ascontiguousarray(inputs["ln1_b"], dtype=np.float32),
        "ln2_g": np.ascontiguousarray(inputs["ln2_g"], dtype=np.float32),
        "ln2_b": np.ascontiguousarray(inputs["ln2_b"], dtype=np.float32),
        "ffn_w_gate": np.ascontiguousarray(inputs["ffn_w_gate"], dtype=np.float32),
        "ffn_w_up": np.ascontiguousarray(inputs["ffn_w_up"], dtype=np.float32),
        "ffn_w_down": np.ascontiguousarray(inputs["ffn_w_down"], dtype=np.float32),
        "c_f32": cf, "c_bf": cb, "dft_c": dc, "dft_s": ds,
    }
    in_maps = [dict(common, x=x[b]) for b in range(8)]
    res = run_bass_kernel_spmd(nc, in_maps, core_ids=list(range(8)))
    return np.stack([np.asarray(r["y"], dtype=np.float32) for r in res.results], axis=0)
```

```python
import numpy as np
import concourse.bass as bass
import concourse.mybir as mybir

F32 = mybir.dt.float32
BF16 = mybir.dt.bfloat16
ALU = mybir.AluOpType
AF = mybir.ActivationFunctionType
AX = mybir.AxisListType

_DT_SIZE = {F32: 4, BF16: 2, mybir.dt.float32r: 4, mybir.dt.int32: 4,
            mybir.dt.uint32: 4, mybir.dt.float16: 2, mybir.dt.uint16: 2,
            mybir.dt.int16: 2, mybir.dt.uint8: 1, mybir.dt.int8: 1}

ENGS = ("pe", "act", "dve", "pool", "sp")
N_DMA_SEMS = 24
TINY_BYTES = 256
SAME_ENGINE_SYNC = False


def ap_box(ap):
    t = ap.tensor
    name = t.name
    esz = _DT_SIZE[ap.dtype]
    dims = list(ap.ap)
    off = ap.offset
    space = str(ap.space)
    if "DRAM" in space.upper() or "HBM" in space.upper():
        lo = off
        hi = off
        for (st, n) in dims:
            if st >= 0:
                hi += st * (n - 1)
            else:
                lo += st * (n - 1)
        return (name, 0, 1, lo * esz, (hi + 1) * esz)
    pstep, pcnt = dims[0]
    if pstep == 0:
        pstep = 1 << 40
    p0 = off // pstep if pstep < (1 << 40) else 0
    f = off - p0 * pstep if pstep < (1 << 40) else off
    lo = f
    hi = f
    for (st, n) in dims[1:]:
        if st >= 0:
            hi += st * (n - 1)
        else:
            lo += st * (n - 1)
    if "PSUM" in space.upper():
        b0 = (lo * esz) // 2048
        b1 = ((hi + 1) * esz - 1) // 2048
        return (name, 0, 128, b0 * 2048, (b1 + 1) * 2048, True)
    return (name, p0, p0 + pcnt, lo * esz, (hi + 1) * esz)


def _overlap(a, b):
    return a[1] < b[2] and b[1] < a[2] and a[3] < b[4] and b[3] < a[4]


def _contains(a, b):
    return a[1] <= b[1] and a[2] >= b[2] and a[3] <= b[3] and a[4] >= b[4]


class Prog:
    def __init__(self, nc):
        self.nc = nc
        self.ins = []
        self.recs = {}
        self.dma_rr = {e: 0 for e in ENGS}
        self.trace = {}

    def add(self, eng, fn, reads=(), writes=(), dma=False):
        idx = len(self.ins)
        rb = list(dict.fromkeys(ap_box(a) for a in reads))
        wb = list(dict.fromkeys(ap_box(a) for a in writes))
        deps = set()
        tiny_deps = set()
        for b in rb:
            psum = len(b) > 5
            tiny = (not psum) and (b[4] - b[3]) <= TINY_BYTES
            for rec in self.recs.get(b[0], ()):
                if (rec[1] or (psum and rec[3] != eng)) and _overlap(rec[0], b):
                    deps.add(rec[2])
                    if tiny and rec[1]:
                        tiny_deps.add(rec[2])
        for b in wb:
            for rec in self.recs.get(b[0], ()):
                if _overlap(rec[0], b):
                    deps.add(rec[2])
        for b in wb:
            lst = self.recs.setdefault(b[0], [])
            lst[:] = [r for r in lst if not _contains(b, r[0])]
            lst.append([b, True, idx, eng])
        for b in rb:
            lst = self.recs.setdefault(b[0], [])
            found = False
            for r in lst:
                if (not r[1]) and r[3] == eng and r[0] == b and not dma \
                        and not self.ins[r[2]]["dma"]:
                    r[2] = idx
                    found = True
                    break
            if not found:
                lst.append([b, False, idx, eng])
        real = set()
        for d in deps:
            p = self.ins[d]
            if p["eng"] == "pe" and eng == "pe" and not p["dma"] and not dma:
                continue
            if (not SAME_ENGINE_SYNC) and p["eng"] == eng and eng != "pool" and not p["dma"] and not dma \
                    and d not in tiny_deps:
                continue
            real.add(d)
        self.ins.append(dict(eng=eng, fn=fn, deps=real, dma=dma, needed=False,
                             sem=None, val=None))
        return idx

    def emit(self, final_wait_eng="sp"):
        nc = self.nc
        ins = self.ins
        for r in ins:
            for d in r["deps"]:
                ins[d]["needed"] = True
        last_dmas = [i for i, r in enumerate(ins) if r["dma"]]
        import contextlib
        with contextlib.ExitStack() as st:
            esem = {e: st.enter_context(nc.semaphore("s_" + e)) for e in ENGS}
            dsem = {e: [st.enter_context(nc.semaphore("d_%s_%d" % (e, i)))
                        for i in range(N_DMA_SEMS)] for e in ("sp", "act", "pool")}
            cnt = {e: 0 for e in ENGS}
            dcnt = {e: [0] * N_DMA_SEMS for e in dsem}
            drr = {e: 0 for e in dsem}
            prev_use = {}
            for i, r in enumerate(ins):
                e = r["eng"]
                if r["dma"]:
                    s = drr[e]
                    drr[e] = (s + 1) % N_DMA_SEMS
                    if dcnt[e][s] > 0:
                        r["prev"] = (dsem[e][s], dcnt[e][s], ("d", e, s))
                    else:
                        r["prev"] = None
                    dcnt[e][s] += 16
                    r["sem"] = dsem[e][s]
                    r["val"] = dcnt[e][s]
                    r["semkey"] = ("d", e, s)
                elif r["needed"]:
                    cnt[e] += 1
                    r["sem"] = esem[e]
                    r["val"] = cnt[e]
                    r["semkey"] = ("e", e)
            block = st.enter_context(nc.Block())
            per_eng = {e: [i for i, r in enumerate(ins) if r["eng"] == e] for e in ENGS}
            final = {}
            for i in last_dmas:
                r = ins[i]
                final[r["semkey"]] = (r["sem"], max(r["val"], final.get(r["semkey"], (None, 0))[1]))

            def body(ename):
                def run(engobj):
                    waited = {}
                    for i in per_eng[ename]:
                        r = ins[i]
                        need = {}
                        for d in r["deps"]:
                            p = ins[d]
                            k = p["semkey"]
                            if need.get(k, (None, 0))[1] < p["val"]:
                                need[k] = (p["sem"], p["val"])
                        if r["dma"] and r["prev"] is not None:
                            s, v, k = r["prev"]
                            if need.get(k, (None, 0))[1] < v:
                                need[k] = (s, v)
                        for k, (s, v) in need.items():
                            if waited.get(k, 0) < v:
                                engobj.wait_ge(s, v)
                                waited[k] = v
                                self.trace.setdefault(ename, []).append(("w", k, v, i))
                        h = r["fn"](engobj)
                        if r["dma"]:
                            h.then_inc(r["sem"], 16)
                            self.trace.setdefault(ename, []).append(("i", r["semkey"], 16, i))
                        elif r["needed"]:
                            h.then_inc(r["sem"], 1)
                            self.trace.setdefault(ename, []).append(("i", r["semkey"], 1, i))
                    if ename == final_wait_eng:
                        for k, (s, v) in final.items():
                            if waited.get(k, 0) < v:
                                engobj.wait_ge(s, v)
                return run

            block.tensor(body("pe"))
            block.scalar(body("act"))
            block.vector(body("dve"))
            block.gpsimd(body("pool"))
            block.sync(body("sp"))

    def dma(self, eng, out, in_, **kw):
        return self.add(eng, lambda e: e.dma_start(out=out, in_=in_, **kw),
                        reads=[in_], writes=[out], dma=True)

    def matmul(self, out, lhsT, rhs, start=True, stop=True, **kw):
        return self.add("pe", lambda e: e.matmul(out, lhsT, rhs, start=start, stop=stop, **kw),
                        reads=[lhsT, rhs], writes=[out])

    def transpose(self, out, in_, ident):
        return self.add("pe", lambda e: e.transpose(out, in_, ident),
                        reads=[in_, ident], writes=[out])

    def act(self, out, in_, func, bias=None, scale=1.0, accum_out=None, eng="act"):
        reads = [in_]
        writes = [out]
        kw = {}
        if bias is not None:
            kw["bias"] = bias
            if not isinstance(bias, (int, float)):
                reads.append(bias)
        if not isinstance(scale, (int, float)):
            reads.append(scale)
        if accum_out is not None:
            kw["accum_out"] = accum_out
            writes.append(accum_out)
        return self.add(eng, lambda e: e.activation(out=out, in_=in_, func=func, scale=scale, **kw),
                        reads=reads, writes=writes)

    def tt(self, out, in0, in1, op, eng="dve"):
        return self.add(eng, lambda e: e.tensor_tensor(out=out, in0=in0, in1=in1, op=op),
                        reads=[in0, in1], writes=[out])

    def ts(self, out, in0, s1, op0, s2=None, op1=None, eng="dve", accum_out=None):
        reads = [in0]
        if not isinstance(s1, (int, float)):
            reads.append(s1)
        if s2 is not None and not isinstance(s2, (int, float)):
            reads.append(s2)
        kw = {}
        writes = [out]
        if op1 is not None:
            kw["op1"] = op1
        if accum_out is not None:
            kw["accum_out"] = accum_out
            writes.append(accum_out)
        return self.add(eng, lambda e: e.tensor_scalar(out=out, in0=in0, scalar1=s1, scalar2=s2,
                                                       op0=op0, **kw),
                        reads=reads, writes=writes)

    def stt(self, out, in0, scalar, in1, op0, op1, eng="dve"):
        reads = [in0, in1]
        if not isinstance(scalar, (int, float)):
            reads.append(scalar)
        return self.add(eng, lambda e: e.scalar_tensor_tensor(out=out, in0=in0, scalar=scalar,
                                                              in1=in1, op0=op0, op1=op1),
                        reads=reads, writes=[out])

    def copy(self, out, in_, eng="dve"):
        if eng == "act":
            return self.add(eng, lambda e: e.activation(out=out, in_=in_, func=AF.Identity),
                            reads=[in_], writes=[out])
        return self.add(eng, lambda e: e.tensor_copy(out=out, in_=in_), reads=[in_], writes=[out])

    def memset(self, ap, val, eng="dve"):
        return self.add(eng, lambda e: e.memset(ap, val), reads=[], writes=[ap])

    def reduce(self, out, in_, op, axis=AX.X, eng="dve", **kw):
        return self.add(eng, lambda e: e.tensor_reduce(out=out, in_=in_, op=op, axis=axis, **kw),
                        reads=[in_], writes=[out])

    def recip(self, out, in_, eng="dve"):
        return self.add(eng, lambda e: e.reciprocal(out=out, in_=in_), reads=[in_], writes=[out])


def simulate_trace(trace):
    pos = {e: 0 for e in trace}
    sem = {}
    progress = True
    while progress:
        progress = False
        for e, ops in trace.items():
            while pos[e] < len(ops):
                kind, k, v, i = ops[pos[e]]
                if kind == "w":
                    if sem.get(k, 0) >= v:
                        pos[e] += 1
                        progress = True
                    else:
                        break
                else:
                    sem[k] = sem.get(k, 0) + v
                    pos[e] += 1
                    progress = True
    stuck = {e: ops[pos[e]] for e, ops in trace.items() if pos[e] < len(ops)}
    return stuck, sem

import contextlib
import os
_SK = set(os.environ.get('DBGSKIP', '').split(','))
import math
import ml_dtypes
from concourse.bass_utils import run_bass_kernel_spmd

S = 2048
D = 1024
L = 2
INW = 2592
FF = 2816
NT = 16
NFC = FF // 128
ALPHA = float((2 * L) ** 0.25)
EPS = 1e-5
C_DQ, C_DK, C_DV, C_FU, C_GQ, C_GK, C_GV, C_GR, C_GZ = 0, 512, 1024, 1536, 1792, 1920, 2048, 2304, 2560

CF_ID, CF_COS, CF_SIN, CF_HM, CF_CC, CF_SC, CF_ONES, CF_NEGH, CF_N = 0, 128, 640, 1152, 1156, 1284, 1412, 1540, 1548
CB_ID, CB_TRIF, CB_TRIB, CB_BD, CB_ONE, CB_N = 0, 128, 256, 384, 640, 768


def host_consts():
    p = np.arange(128)
    cf = np.zeros((128, CF_N), np.float32)
    cf[:, CF_ID:CF_ID + 128] = np.eye(128, dtype=np.float32)
    inv_freq = (10000.0 ** (-np.arange(0, 64, 2, dtype=np.float32) / 64)).astype(np.float32)
    pos = (np.arange(NT)[None, :] * 128 + p[:, None]).astype(np.float32)
    ang = pos[:, :, None] * inv_freq[None, None, :]
    cf[:, CF_COS:CF_COS + 512] = np.cos(ang).reshape(128, 512)
    cf[:, CF_SIN:CF_SIN + 512] = np.sin(ang).reshape(128, 512)
    cf[:, CF_HM:CF_HM + 4] = (p[:, None] // 32 == np.arange(4)[None, :]).astype(np.float32)
    c = np.arange(64)
    a = 2 * np.pi * np.outer(c, c) / 64.0
    cc = np.cos(a) / 8.0
    sc = np.sin(a) / 8.0
    z = np.zeros((64, 64))
    cf[:, CF_CC:CF_CC + 128] = np.block([[cc, z], [z, cc]])
    cf[:, CF_SC:CF_SC + 128] = np.block([[sc, z], [z, sc]])
    cf[:, CF_ONES:CF_ONES + 128] = 1.0
    cf[:, CF_NEGH:CF_NEGH + 8] = -0.5
    cb = np.zeros((128, CB_N), np.float32)
    cb[:, CB_ID:CB_ID + 128] = np.eye(128)
    cb[:, CB_TRIF:CB_TRIF + 128] = (p[:, None] <= p[None, :])
    cb[:, CB_TRIB:CB_TRIB + 128] = (p[:, None] >= p[None, :])
    cb[:, CB_BD:CB_BD + 256] = (p[:, None] // 32 == np.arange(256)[None, :] // 64)
    cb[:, CB_ONE:CB_ONE + 128] = 1.0
    s = np.arange(S, dtype=np.float64)
    sk = np.outer(s, s) % S
    ang = 2 * np.pi * sk / S
    dc = (np.cos(ang) / math.sqrt(S)).astype(np.float32).astype(ml_dtypes.bfloat16)
    ds = (-np.sin(ang) / math.sqrt(S)).astype(np.float32).astype(ml_dtypes.bfloat16)
    return cf, cb.astype(ml_dtypes.bfloat16), dc, ds


class Arena:
    def __init__(self, t, nbytes):
        self.t = t
        self.n = nbytes
        self.top = 0

    def mark(self):
        return self.top

    def release(self, m):
        self.top = m

    def alloc(self, shape, dt):
        n = 1
        for s_ in shape:
            n *= s_
        b = n * _DT_SIZE[dt]
        off = self.top
        self.top += (b + 63) // 64 * 64
        assert self.top <= self.n, ("arena overflow", self.top, self.n)
        ap = self.t[:, off // 2:(off + b) // 2]
        if dt != BF16:
            ap = ap.bitcast(dt)
        if len(shape) == 2:
            ap = ap.rearrange("p (a b) -> p a b", a=shape[0])
        elif len(shape) == 3:
            ap = ap.rearrange("p (a b c) -> p a b c", a=shape[0], b=shape[1])
        return ap


class Rot:
    def __init__(self, items):
        self.items = items
        self.i = 0

    def next(self):
        r = self.items[self.i % len(self.items)]
        self.i += 1
        return r


_CACHE = {}


class _Stop(Exception):
    pass


def build(n_layers=L, dbg=None, upto=None):
    nc = bass.Bass("TRN2", target_bir_lowering=False)
    dram = lambda name, shape, dt, kind="ExternalInput": nc.dram_tensor(name, shape, dt, kind=kind).ap()
    x_d = dram("x", [S, D], F32)
    w_in_d = dram("w_in", [L, D, INW], F32)
    lam_d = dram("diff_lambda", [L, 256], F32)
    dng_d = dram("diff_norm_g", [L, 128], F32)
    fw_d = dram("fourier_w", [L, 4, 64, 64], F32)
    w2_d = dram("gla_gate_w2", [L, 2, 16, 128], F32)
    b2_d = dram("gla_gate_b2", [L, 2, 128], F32)
    gng_d = dram("gla_norm_g", [L, 64], F32)
    wout_d = dram("w_out", [L, D, D], F32)
    ln_d = {k: dram(k, [L, D], F32) for k in ("ln1_g", "ln1_b", "ln2_g", "ln2_b")}
    wg_d = dram("ffn_w_gate", [L, D, FF], F32)
    wu_d = dram("ffn_w_up", [L, D, FF], F32)
    wd_d = dram("ffn_w_down", [L, FF, D], F32)
    cf_d = dram("c_f32", [128, CF_N], F32)
    cb_d = dram("c_bf", [128, CB_N], BF16)
    dftc_d = dram("dft_c", [S, S], BF16)
    dfts_d = dram("dft_s", [S, S], BF16)
    y_d = dram("y", [S, D], F32, kind="ExternalOutput")
    xs_d = dram("xs_scr", [S, D], F32, kind="Internal")
    dbg_outs = {}

    ARENA = 207 * 1024
    XTB = NT * D * 4
    with contextlib.ExitStack() as st:
        arena_t = st.enter_context(nc.sbuf_tensor("arena", [128, ARENA // 2], BF16))
        ps = st.enter_context(nc.psum_tensor("ps", [128, 4096], F32))
        P = Prog(nc)
        A = Arena(arena_t, ARENA)
        x_tok = arena_t[:, (ARENA - XTB) // 2:ARENA // 2].bitcast(F32).rearrange("p (t d) -> p t d", t=NT)

        def bank(b, n=512, off=0):
            return ps[:, b * 512 + off:b * 512 + off + n]

        def bankbf(b):
            return ps[:, b * 512:(b + 1) * 512].bitcast(BF16)

        def dump(name, ap, shape, dt=F32):
            if dbg is None or name not in dbg:
                return
            d_ = nc.dram_tensor("dbg_" + name, shape, dt, kind="ExternalOutput").ap()
            dbg_outs[name] = d_
            P.dma("sp", d_, ap)

        cf = A.alloc([CF_N], F32)
        cb = A.alloc([CB_N], BF16)
        xT = A.alloc([8, S], BF16)
        lnbuf = A.alloc([2, D], F32)
        small = A.alloc([640], F32)
        P.dma("sp", cf, cf_d)
        P.dma("sp", cb, cb_d)
        ident_f = cf[:, CF_ID:CF_ID + 128]
        ident_b = cb[:, CB_ID:CB_ID + 128]
        cos_t = cf[:, CF_COS:CF_COS + 512].rearrange("p (t i) -> p t i", t=NT)
        sin_t = cf[:, CF_SIN:CF_SIN + 512].rearrange("p (t i) -> p t i", t=NT)
        hmask4 = cf[:, CF_HM:CF_HM + 4]
        ccbd = cf[:, CF_CC:CF_CC + 128]
        scbd = cf[:, CF_SC:CF_SC + 128]
        ones_f = cf[:, CF_ONES:CF_ONES + 128]
        negh8 = cf[:, CF_NEGH:CF_NEGH + 8]
        tri = [cb[:, CB_TRIF:CB_TRIF + 128], cb[:, CB_TRIB:CB_TRIB + 128]]
        bdmask = cb[:, CB_BD:CB_BD + 256]
        lp_bc = small[:, 0:256]
        gdiff = small[:, 256:384]
        ggla = small[:, 384:448]
        negb2 = small[:, 448:450]
        neglam = small[:, 450:451]
        negM = small[:, 451:452]
        sc_tmp = small[:, 452:500]
        nrm = small[:, 500:504]
        epsc = small[:, 504:505]
        base_mark = A.mark()

        win = lambda l: w_in_d[l].rearrange("(k p) n -> p k n", p=128)

        def rstd_from(out, ss, n, width):
            tmp = sc_tmp[:, 40:40 + width]
            P.ts(tmp, ss, 1.0 / n, ALU.mult, EPS, ALU.add)
            P.tt(out, tmp, negh8[:, 0:width], ALU.pow, eng="pool")

        xb_rot = None

        def make_xT(x_tile_f32, t, xb, evac_eng="act", pbank=7):
            P.copy(xb, x_tile_f32, eng="act")
            pst = bankbf(pbank)
            for k in range(8):
                P.transpose(pst[:, k * 128:(k + 1) * 128], xb[:, k * 128:(k + 1) * 128], ident_b)
            P.copy(xT[:, :, t * 128:(t + 1) * 128], pst.rearrange("p (k n) -> p k n", k=8), eng=evac_eng)

        def ln_a(psum2, xres, gb, ytmp, par, alpha=ALPHA, mid=None):
            sct = sc_tmp[:, 0:16] if par == 0 else small[:, 540:556]
            P.stt(ytmp, xres, alpha, psum2, ALU.mult, ALU.add)
            stats = sct[:, 0:12]
            mv = sct[:, 12:14]
            for c_ in range(2):
                P.add("dve", lambda e, c_=c_: e.bn_stats(out=stats[:, c_ * 6:(c_ + 1) * 6],
                                                          in_=ytmp[:, c_ * 512:(c_ + 1) * 512]),
                      reads=[ytmp[:, c_ * 512:(c_ + 1) * 512]], writes=[stats[:, c_ * 6:(c_ + 1) * 6]])
            P.add("dve", lambda e: e.bn_aggr(out=mv, in_=stats), reads=[stats], writes=[mv])
            rs = sct[:, 14:15]
            nmr = sct[:, 15:16]
            P.act(rs, mv[:, 1:2], AF.Ln, bias=epsc)
            P.act(rs, rs, AF.Exp, scale=-0.5)
            if mid is not None:
                mid()
            P.stt(nmr, mv[:, 0:1], -1.0, rs, ALU.mult, ALU.mult)
            P.act(ytmp, ytmp, AF.Identity, bias=nmr, scale=rs)
            P.tt(ytmp, ytmp, gb[:, 0, :], ALU.mult, eng="pool")

        def ln_b(gb, out_tok, ytmp):
            P.tt(out_tok, ytmp, gb[:, 1, :], ALU.add)

        def ck(name):
            if upto == name:
                raise _Stop()

        try:
          for l in range(n_layers):
              lam_init = 0.8 - 0.6 * math.exp(-0.3 * l)
              A.release(base_mark)
              P.dma("sp", lp_bc, lam_d[l].partition_broadcast(128))
              P.dma("sp", gdiff, dng_d[l].partition_broadcast(128))
              P.dma("sp", ggla, gng_d[l].partition_broadcast(128))
              for d_ in range(2):
                  P.dma("sp", negb2[:, d_:d_ + 1], b2_d[l, d_].rearrange("(p o) -> p o", o=1))
              P.ts(negb2, negb2, -1.0, ALU.mult)
              P.ts(gdiff, gdiff, 1.0 - lam_init, ALU.mult)
              P.memset(epsc, EPS)
              pr = sc_tmp[:, 16:18]
              prod = A.alloc([128], F32)
              lp4 = lp_bc.rearrange("p (a d) -> p a d", a=4)
              for i_ in range(2):
                  P.tt(prod[:, 0:64], lp4[:, 2 * i_, :], lp4[:, 2 * i_ + 1, :], ALU.mult)
                  P.reduce(pr[:, i_:i_ + 1], prod[:, 0:64], ALU.add)
              P.act(pr, pr, AF.Exp)
              P.tt(neglam, pr[:, 1:2], pr[:, 0:1], ALU.subtract)
              P.ts(neglam, neglam, -lam_init, ALU.add)
              A.release(base_mark)
              ck('params')

              if l == 0:
                  m_ = A.mark()
                  xin = [A.alloc([D], F32) for _ in range(3)]
                  xbs = [A.alloc([D], BF16) for _ in range(2)]
                  for t in range(NT):
                      xi = xin[t % 3]
                      P.dma("sp", xi, x_d[t * 128:(t + 1) * 128, :])
                      make_xT(xi, t, xbs[t % 2], evac_eng="dve", pbank=(7 if t % 2 else 6))
                  A.release(m_)
              ck('xT')
              xres_d = x_d if l == 0 else xs_d

              ocat = A.alloc([8, S], BF16)
              Wout = A.alloc([8, D], BF16)
              Wgf = A.alloc([8, 288], BF16)
              Wgt = A.alloc([8, 512], BF16)
              wbd = A.alloc([2, 128], F32)
              Wcs = A.alloc([2, 256], BF16)
              mixer_mark = A.mark()
              NDB = 4
              DBB = 8 * 512 * 2
              dbuf = [arena_t[:, (ARENA - (i_ + 1) * DBB) // 2:(ARENA - i_ * DBB) // 2].rearrange(
                  "p (a b) -> p a b", a=8) for i_ in range(NDB)]
              dft_v = [dftc_d.rearrange("(t p) k -> p t k", p=128), dfts_d.rearrange("(t p) k -> p t k", p=128)]
              dft_chunks = [(kb, which, tg) for kb in range(4) for which in range(2) for tg in range(2)]

              def dft_load(j):
                  kb, which, tg = dft_chunks[j]
                  P.dma("sp", dbuf[j % NDB], dft_v[which][:, tg * 8:(tg + 1) * 8, kb * 512:(kb + 1) * 512])

              Wh = [A.alloc([8, 384], BF16) for _ in range(2)]
              QTzs = [A.alloc([2, S], BF16) for _ in range(2)]
              KTs = [A.alloc([S], BF16) for _ in range(2)]
              Vaugs = [A.alloc([NT, 132], BF16) for _ in range(2)]
              negMs = [negM, small[:, 570:571]]
              Wf = Wh[0][:, :, 0:256]
              PT = [A.alloc([1024], BF16) for _ in range(3)]
              O1n = A.alloc([8, 128], F32)
              O2t = A.alloc([8, 128], F32)
              otoks = [A.alloc([8, 128], BF16) for _ in range(2)]
              qkr = [A.alloc([256], BF16) for _ in range(3)]
              tmpAs = [A.alloc([128], F32) for _ in range(2)]
              tmpBs = [A.alloc([128], F32) for _ in range(2)]
              sq = A.alloc([256], F32)
              D2 = A.alloc([256], F32)
              sqs = [sq, A.alloc([256], F32)]
              red_all = A.alloc([NT, 4], F32)
              red4 = sc_tmp[:, 20:24]
              gm = sc_tmp[:, 24:26]
              nrm2 = sc_tmp[:, 26:28]
              rz = sc_tmp[:, 28:36]
              ss8 = small[:, 512:520]
              rstd8 = small[:, 520:528]
              for i_ in range(2):
                  P.memset(Vaugs[i_][:, :, 128:129], 1.0)
                  P.memset(QTzs[i_][64:128, 0, :], 0.0)
                  P.memset(QTzs[i_][0:64, 1, :], 0.0, eng="pool")
              assert A.top <= ARENA - 3 * DBB, A.top
              pso = [ps[:, (4 + qi // 3) * 512 + (qi % 3) * 160:(4 + qi // 3) * 512 + (qi % 3) * 160 + 129]
                     for qi in range(8)]
              grp = [(0, 3), (3, 3), (6, 2)]

              def pso_grp(gi, c0, c1):
                  q0, n_ = grp[gi]
                  base_ = (4 + gi) * 512
                  return ps[:, base_:base_ + n_ * 160].rearrange("p (a b) -> p a b", b=160)[:, :, c0:c1]

              def load_wh(h_):
                  for j_, c0 in enumerate((C_DQ, C_DK, C_DV)):
                      P.dma("pool", Wh[h_ % 2][:, :, j_ * 128:(j_ + 1) * 128],
                            win(l)[:, :, c0 + h_ * 128:c0 + (h_ + 1) * 128])

              load_wh(0)
              load_wh(1)
              fin_pending = []

              def att_inproj(h):
                  wh = Wh[h % 2]
                  QTz, KT, Vaug, negM_h = QTzs[h % 2], KTs[h % 2], Vaugs[h % 2], negMs[h % 2]
                  P.memset(nrm, 0.0)
                  tr_pending = []
                  for t in range(NT):
                      pb = bank(t % 4, 256)
                      pv_ = bank(4 + t % 2, 128)
                      for k in range(8):
                          P.matmul(pb, xT[:, k, t * 128:(t + 1) * 128], wh[:, k, 0:256], start=(k == 0), stop=(k == 7))
                      for k in range(8):
                          P.matmul(pv_, xT[:, k, t * 128:(t + 1) * 128], wh[:, k, 256:384], start=(k == 0), stop=(k == 7))
                      qk4 = pb.rearrange("p (g h d) -> p g h d", g=4, h=2)
                      t1 = qk4[:, :, 0, :]
                      t2 = qk4[:, :, 1, :]
                      cbt = cos_t[:, t:t + 1, :].broadcast_to([128, 4, 32])
                      sbt = sin_t[:, t:t + 1, :].broadcast_to([128, 4, 32])
                      q_ = qkr[t % 3]
                      q4 = q_.rearrange("p (g h d) -> p g h d", g=4, h=2)
                      ta = tmpAs[t % 2].rearrange("p (g d) -> p g d", g=4)
                      tb_ = tmpBs[t % 2].rearrange("p (g d) -> p g d", g=4)
                      P.tt(ta, t1, cbt, ALU.mult)
                      P.tt(tb_, t2, sbt, ALU.mult)
                      P.tt(q4[:, :, 0, :], ta, tb_, ALU.subtract)
                      P.tt(ta, t2, cbt, ALU.mult)
                      P.tt(tb_, t1, sbt, ALU.mult)
                      P.tt(q4[:, :, 1, :], ta, tb_, ALU.add)
                      P.copy(Vaug[:, t, 0:128], pv_, eng="act")
                      P.tt(sqs[t % 2], q_, q_, ALU.mult, eng="pool")
                      def tr_(t=t, q_=q_, sq_=sqs[t % 2]):
                          P.reduce(red_all[:, t, :], sq_.rearrange("p (g d) -> p g d", g=4), ALU.add)
                          pst = bankbf(7 if t % 2 else 6)
                          P.transpose(pst[:, 0:128], q_[:, 0:128], ident_b)
                          P.transpose(pst[:, 128:256], q_[:, 128:256], ident_b)
                          P.copy(QTz[0:64, 0, t * 128:(t + 1) * 128], pst[0:64, 0:128], eng="act")
                          P.copy(QTz[64:128, 1, t * 128:(t + 1) * 128], pst[64:128, 0:128], eng="act")
                          P.copy(KT[:, t * 128:(t + 1) * 128], pst[:, 128:256], eng="act")
                      tr_pending.append(tr_)
                      if len(tr_pending) > 1:
                          tr_pending.pop(0)()
                  while tr_pending:
                      tr_pending.pop(0)()
                  P.reduce(nrm, red_all.rearrange("p t g -> p g t"), ALU.max)
                  ck('h0proj')
                  if h + 2 < 4:
                      load_wh(h + 2)
                  if h == 3:
                      P.dma("pool", Wf, win(l)[:, :, C_FU:C_FU + 256])
                      P.dma("pool", Wgf[:, :, 0:256], win(l)[:, :, C_GQ:C_GQ + 256])
                      P.dma("pool", Wgf[:, :, 256:288], win(l)[:, :, C_GZ:C_GZ + 32])
                      P.dma("pool", Wgt, win(l)[:, :, C_GV:C_GV + 512])
                      for k in range(8):
                          P.dma("pool", Wout[:, k, :], wout_d[l, k * 128:(k + 1) * 128, :])
                      P.memset(wbd, 0.0, eng="pool")
                      for g_ in range(4):
                          c_, gl = g_ // 2, g_ % 2
                          P.dma("sp", wbd[gl * 64:(gl + 1) * 64, c_, gl * 64:(gl + 1) * 64], fw_d[l, g_])
                      for j_ in range(NDB - 1):
                          dft_load(j_)
                  P.reduce(nrm2, nrm.rearrange("p (a b) -> p a b", a=2), ALU.max)
                  P.ts(D2[:, 0:128], ident_f, nrm2[:, 0:1], ALU.mult)
                  P.ts(D2[:, 128:256], ident_f, nrm2[:, 1:2], ALU.mult)
                  pbm = bank(6, 256)
                  P.matmul(pbm, ones_f, D2, start=True, stop=True)
                  P.reduce(gm, pbm.rearrange("p (a b) -> p a b", a=2), ALU.max)
                  P.tt(negM_h, gm[:, 0:1], gm[:, 1:2], ALU.add)
                  P.ts(negM_h, negM_h, -0.5 * 0.125, ALU.mult)
                  if l == 0 and h == 0:
                      dump("xT", xT, [128, 8, S], BF16)
                      dump("negM", negM_h, [128, 1])
                      ck('qkt')

              def att_steps(h):
                  QTz, KT, Vaug, negM_h = QTzs[h % 2], KTs[h % 2], Vaugs[h % 2], negMs[h % 2]
                  for qh in range(2):
                      otok = otoks[qh]
                      steps = [(m, kt) for m in range(2) for kt in range(NT)]

                      def emit_scores(i):
                          m, kt = steps[i]
                          sb_i = (i % 2) * 2
                          for j_ in range(2):
                              P.matmul(ps[:, (sb_i + j_) * 512:(sb_i + j_ + 1) * 512],
                                       KT[:, kt * 128:(kt + 1) * 128],
                                       QTz[:, m, qh * 1024 + j_ * 512:qh * 1024 + (j_ + 1) * 512],
                                       start=True, stop=True)

                      emit_scores(0)
                      for i, (m, kt) in enumerate(steps):
                          if i + 1 < len(steps):
                              emit_scores(i + 1)
                          if i == 4 and fin_pending:
                              fin_pending.pop(0)()
                          sb_i = (i % 2) * 2
                          pss = ps[:, sb_i * 512:(sb_i + 2) * 512]
                          pt = PT[i % 3]
                          P.act(pt, pss, AF.Exp, bias=negM_h, scale=0.125)
                          for qi in range(8):
                              P.matmul(pso[qi], pt[:, qi * 128:(qi + 1) * 128], Vaug[:, kt, 0:129],
                                       start=(kt == 0 and qi % 3 == 0), stop=(kt == NT - 1), skip_group_check=True)
                          if kt != NT - 1:
                              continue
                          if m == 0:
                              for gi, (q0, n_) in enumerate(grp):
                                  P.recip(rz[:, q0:q0 + n_], pso_grp(gi, 128, 129).rearrange("p a b -> p (a b)"))
                                  P.tt(O1n[:, q0:q0 + n_, :], pso_grp(gi, 0, 128),
                                       rz[:, q0:q0 + n_].unsqueeze(2).broadcast_to([128, n_, 128]), ALU.mult)
                          else:
                              for gi, (q0, n_) in enumerate(grp):
                                  P.recip(rz[:, q0:q0 + n_], pso_grp(gi, 128, 129).rearrange("p a b -> p (a b)"))
                              P.ts(rz, rz, neglam, ALU.mult)
                              for gi, (q0, n_) in enumerate(grp):
                                  P.tt(O2t[:, q0:q0 + n_, :], pso_grp(gi, 0, 128),
                                       rz[:, q0:q0 + n_].unsqueeze(2).broadcast_to([128, n_, 128]), ALU.mult)
                              P.tt(O1n, O1n, O2t, ALU.add)

                              def chain_(h=h, qh=qh, otok=otok):
                                  P.tt(O2t, O1n, O1n, ALU.mult, eng="pool")
                                  P.reduce(ss8, O2t, ALU.add)
                                  rstd_from(rstd8, ss8, 128.0, 8)
                                  P.tt(O1n, O1n, rstd8.unsqueeze(2).broadcast_to([128, 8, 128]), ALU.mult)
                                  P.tt(otok, O1n, gdiff.unsqueeze(1).broadcast_to([128, 8, 128]), ALU.mult,
                                       eng="pool")
                                  for qi in range(8):
                                      dst_ = ocat[:, h, qh * 1024 + qi * 128:qh * 1024 + (qi + 1) * 128]
                                      P.add("sp", lambda e, dst_=dst_, src_=otok[:, qi, :]: e.dma_start(
                                          out=dst_, in_=src_, transpose=True),
                                          reads=[otok[:, qi, :]], writes=[dst_], dma=True)
                              fin_pending.append(chain_)
              att_inproj(0)
              att_inproj(1)
              for h in range(4):
                  att_steps(h)
                  if h + 2 < 4:
                      att_inproj(h + 2)
              while fin_pending:
                  fin_pending.pop(0)()
              A.release(mixer_mark)
              if l == 0:
                  dump("odiff", ocat[:, 0:4, :], [128, 4, S], BF16)
              ck('att')

              _skip = A.alloc([2 * 8 * 384], BF16)
              uT = A.alloc([2, S], BF16)
              ucs = A.alloc([NT, 512], BF16)
              P.dma("sp", lnbuf[:, 0, :], ln_d["ln1_g"][l].partition_broadcast(128))
              P.dma("sp", lnbuf[:, 1, :], ln_d["ln1_b"][l].partition_broadcast(128))
              dft_load(NDB - 1)
              for c_ in range(2):
                  pb = bank(4 + c_, 256)
                  P.matmul(pb[:, 0:128], ccbd, wbd[:, c_, :], start=True, stop=True)
                  P.matmul(pb[:, 128:256], scbd, wbd[:, c_, :], start=True, stop=True)
                  P.copy(Wcs[:, c_, :], pb, eng="act")
              for tb in range(4):
                  for c_ in range(2):
                      pb = bank((tb * 2 + c_) % 4)
                      for k in range(8):
                          P.matmul(pb, Wf[:, k, c_ * 128:(c_ + 1) * 128], xT[:, k, tb * 512:(tb + 1) * 512],
                                   start=(k == 0), stop=(k == 7))
                      P.copy(uT[:, c_, tb * 512:(tb + 1) * 512], pb, eng="act")
              for t in range(NT):
                  pb = bank(4 + t % 2)
                  for c_ in range(2):
                      P.matmul(pb[:, c_ * 256:(c_ + 1) * 256], uT[:, c_, t * 128:(t + 1) * 128], Wcs[:, c_, :],
                               start=True, stop=True)
                  P.copy(ucs[:, t, :], pb, eng="act")
                  if t == 7:
                      ucs_half = True
              for j_, (kb, which, tg) in enumerate(dft_chunks):
                  pbs = [bank(0 + (kb % 2) * 2), bank(1 + (kb % 2) * 2)]
                  db = dbuf[j_ % NDB]
                  for tt_ in range(8):
                      t = tg * 8 + tt_
                      first = (which == 0 and t == 0)
                      last = (which == 1 and t == NT - 1)
                      for c_ in range(2):
                          P.matmul(pbs[c_], ucs[:, t, c_ * 256 + which * 128:c_ * 256 + (which + 1) * 128],
                                   db[:, tt_, :], start=first, stop=last)
                  if j_ + NDB < len(dft_chunks):
                      dft_load(j_ + NDB)
                  if which == 1 and tg == 1:
                      for c_ in range(2):
                          P.copy(ocat[:, 4 + c_, kb * 512:(kb + 1) * 512], pbs[c_], eng=("act" if c_ else "dve"))
              A.release(mixer_mark)
              if l == 0:
                  dump("ofour", ocat[:, 4:6, :], [128, 2, S], BF16)
              ck('four')

              gqk = A.alloc([2, S], F32)
              gzT = A.alloc([S], BF16)
              gv = A.alloc([NT, 256], BF16)
              gate = A.alloc([NT, 256], BF16)
              w2pad = A.alloc([2, 128], BF16)
              qt = [A.alloc([S], BF16) for _ in range(2)]
              kt_ = [A.alloc([S], BF16) for _ in range(2)]
              Sbf = [A.alloc([NT, 256], BF16) for _ in range(2)]
              gla_mark = A.mark()
              P.memset(w2pad, 0.0)
              for d_ in range(2):
                  P.dma("pool", w2pad[d_ * 16:(d_ + 1) * 16, d_, :], w2_d[l, d_])
              for tb in range(4):
                  for c_ in range(2):
                      pb = bank((tb * 3 + c_) % 4)
                      for k in range(8):
                          P.matmul(pb, Wgf[:, k, c_ * 128:(c_ + 1) * 128], xT[:, k, tb * 512:(tb + 1) * 512],
                                   start=(k == 0), stop=(k == 7))
                      P.copy(gqk[:, c_, tb * 512:(tb + 1) * 512], pb, eng=("act" if c_ else "dve"))
                  pb = bank((tb * 3 + 2) % 4)
                  for k in range(8):
                      P.matmul(pb[0:32, :], Wgf[:, k, 256:288], xT[:, k, tb * 512:(tb + 1) * 512],
                               start=(k == 0), stop=(k == 7))
                  P.copy(gzT[0:32, tb * 512:(tb + 1) * 512], pb[0:32, :], eng="dve")
              for t in range(NT):
                  pb = bank(4 + t % 2)
                  for k in range(8):
                      P.matmul(pb, xT[:, k, t * 128:(t + 1) * 128], Wgt[:, k, :], start=(k == 0), stop=(k == 7))
                  P.copy(gv[:, t, :], pb[:, 0:256], eng="act")
                  P.act(gate[:, t, :], pb[:, 256:512], AF.Silu)
              A.release(gla_mark)
              Bc = A.alloc([S], F32)
              Ec = A.alloc([S], F32)
              kdec_tok = A.alloc([NT, 128], BF16)
              kdT = [A.alloc([128], BF16) for _ in range(2)]
              Srot = [A.alloc([256], F32) for _ in range(2)]
              wgt_flat = Wgt.rearrange("p a b -> p (a b)")
              kdec_toks = [kdec_tok, wgt_flat[:, 0:2048].rearrange("p (a b) -> p a b", a=NT)]
              Srots = [Srot, [wgt_flat[:, 2048:2560].bitcast(F32), wgt_flat[:, 2560:3072].bitcast(F32)]]
              elast_t = small[:, 572:604].rearrange("p (d n) -> p d n", d=2)
              P.tt(gate.rearrange("p t (h v) -> p (t h) v", h=4), gate.rearrange("p t (h v) -> p (t h) v", h=4),
                   ggla.unsqueeze(1).broadcast_to([128, NT * 4, 64]), ALU.mult, eng="pool")
              for d_ in range(2):
                  for tb in range(4):
                      pb = bank(tb % 4)
                      P.matmul(pb, w2pad[0:32, d_, :], gzT[0:32, tb * 512:(tb + 1) * 512], start=True, stop=True)
                      P.act(Ec[:, tb * 512:(tb + 1) * 512], pb, AF.Exp, bias=negb2[:, d_:d_ + 1], scale=-1.0)
                  P.act(Ec, Ec, AF.Ln, bias=1.0)
                  for n in range(NT):
                      o_ = Bc[:, n * 128:(n + 1) * 128]
                      i_ = Ec[:, n * 128:(n + 1) * 128]
                      if d_ == 1:
                          o_ = o_[:, ::-1]
                          i_ = i_[:, ::-1]
                      P.add("dve", lambda e, o_=o_, i_=i_: e.tensor_tensor_scan(
                          out=o_, data0=ones_f, data1=i_, initial=0.0, op0=ALU.mult, op1=ALU.add),
                          reads=[ones_f, Ec[:, n * 128:(n + 1) * 128]], writes=[Bc[:, n * 128:(n + 1) * 128]])
                  P.act(Ec, Bc, AF.Exp, scale=-1.0 / 16.0)
                  P.act(Bc, Bc, AF.Exp, scale=1.0 / 16.0)
                  ecol = 127 if d_ == 0 else 0
                  P.copy(elast_t[:, d_, :], Ec.rearrange("p (n c) -> p n c", c=128)[:, :, ecol], eng="dve")
                  P.stt(qt[d_], gqk[:, 0, :], 32.0 ** -0.5, Ec, ALU.mult, ALU.mult)
                  P.tt(kt_[d_], gqk[:, 1, :], Bc, ALU.mult)
                  pst = bankbf(7)
                  for n in range(NT):
                      kd = kdT[n % 2]
                      P.stt(kd, gqk[:, 1, n * 128:(n + 1) * 128], elast_t[:, d_, n:n + 1],
                            Bc[:, n * 128:(n + 1) * 128], ALU.mult, ALU.mult)
                      P.transpose(pst[:, (n % 8) * 128:(n % 8 + 1) * 128], kd, ident_b)
                      if n % 8 == 7:
                          P.copy(kdec_toks[d_][:, n - 7:n + 1, :], pst.rearrange("p (a b) -> p a b", a=8),
                                 eng="act")
              orders = [list(range(NT)), list(range(NT - 1, -1, -1))]
              prevs = []
              for d_ in range(2):
                  P.memset(Srots[d_][0], 0.0)
                  P.memset(Sbf[d_][:, orders[d_][0], :], 0.0)
                  prevs.append(Srots[d_][0])
              for i_ in range(NT - 1):
                  for d_ in range(2):
                      n = orders[d_][i_]
                      pb = bank(4 + 2 * d_ + i_ % 2, 256)
                      P.matmul(pb, kdec_toks[d_][:, n, :], gv[:, n, :], start=True, stop=True)
                      cur = Srots[d_][(i_ + 1) % 2]
                      P.stt(cur, prevs[d_], elast_t[:, d_, n:n + 1], pb, ALU.mult, ALU.add)
                      P.tt(Sbf[d_][:, orders[d_][i_ + 1], :], cur, bdmask, ALU.mult, eng="pool")
                      prevs[d_] = cur
              A.release(gla_mark)
              Qbd = [A.alloc([4, 128], BF16) for _ in range(4)]
              Asb = [A.alloc([4, 128], BF16) for _ in range(4)]
              ogf = [A.alloc([256], F32) for _ in range(2)]
              ogb = [A.alloc([256], BF16) for _ in range(3)]
              junk = A.alloc([64], F32)
              ss4s = [small[:, 528:532], small[:, 560:564]]
              rs4s = [small[:, 532:536], small[:, 564:568]]
              def gla_m(n):
                  po = bank(4 + n % 2, 256)
                  P.matmul(po, qt[0][:, n * 128:(n + 1) * 128], Sbf[0][:, n, :], start=True, stop=False)
                  P.matmul(po, qt[1][:, n * 128:(n + 1) * 128], Sbf[1][:, n, :], start=False, stop=False)
                  for d_ in range(2):
                      ci = 2 * n + d_
                      qb = Qbd[ci % 4]
                      P.tt(qb, qt[d_][:, n * 128:(n + 1) * 128].unsqueeze(1).broadcast_to([128, 4, 128]),
                           hmask4.unsqueeze(2).broadcast_to([128, 4, 128]), ALU.mult, eng="pool")
                      P.matmul(bank(ci % 4), kt_[d_][:, n * 128:(n + 1) * 128], qb.rearrange("p a b -> p (a b)"),
                               start=True, stop=True)
                  for d_ in range(2):
                      ci = 2 * n + d_
                      asb = Asb[ci % 4]
                      P.tt(asb, bank(ci % 4).rearrange("p (a b) -> p a b", a=4),
                           tri[d_].unsqueeze(1).broadcast_to([128, 4, 128]), ALU.mult)
                      for h in range(4):
                          P.matmul(po[:, h * 64:(h + 1) * 64], asb[:, h, :], gv[:, n, h * 64:(h + 1) * 64],
                                   start=False, stop=(d_ == 1 and h == 3))

              def gla_a(n):
                  po = bank(4 + n % 2, 256)
                  ss4 = ss4s[n % 2]
                  rs4 = rs4s[n % 2]
                  for h in range(4):
                      P.act(junk, po[:, h * 64:(h + 1) * 64], AF.Square, accum_out=ss4[:, h:h + 1])
                  P.act(rs4, ss4, AF.Ln, bias=epsc, scale=1.0 / 64.0)
                  P.act(rs4, rs4, AF.Exp, scale=-0.5)
                  of = ogf[n % 2]
                  of3 = of.rearrange("p (h v) -> p h v", h=4)
                  P.tt(of3, po.rearrange("p (h v) -> p h v", h=4), rs4.unsqueeze(2).broadcast_to([128, 4, 64]),
                       ALU.mult)
                  ob = ogb[n % 3]
                  P.tt(ob, of, gate[:, n, :], ALU.mult)
                  for c_ in range(2):
                      dst_ = ocat[:, 6 + c_, n * 128:(n + 1) * 128]
                      P.add("sp", lambda e, dst_=dst_, src_=ob[:, c_ * 128:(c_ + 1) * 128]: e.dma_start(
                          out=dst_, in_=src_, transpose=True),
                          reads=[ob[:, c_ * 128:(c_ + 1) * 128]], writes=[dst_], dma=True)

              for s_ in range(NT + 1):
                  if s_ < NT:
                      gla_m(s_)
                  if s_ >= 1:
                      gla_a(s_ - 1)
              A.release(mixer_mark)
              if l == 0:
                  dump("ogla", ocat[:, 6:8, :], [128, 2, S], BF16)
              ck('gla')

              A.n = ARENA - XTB
              assert A.top <= A.n
              post_mark = A.mark()
              ytmp = [A.alloc([D], F32) for _ in range(3)]
              xbs = [A.alloc([D], BF16) for _ in range(2)]
              for t in range(NT):
                  P.dma("sp", x_tok[:, t, :], xres_d[t * 128:(t + 1) * 128, :])
              def mm1_(t):
                  b0 = (t % 3) * 2
                  p2 = ps[:, b0 * 512:(b0 + 2) * 512]
                  for hf in range(2):
                      for e_ in range(8):
                          P.matmul(p2[:, hf * 512:(hf + 1) * 512], ocat[:, e_, t * 128:(t + 1) * 128],
                                   Wout[:, e_, hf * 512:(hf + 1) * 512], start=(e_ == 0), stop=(e_ == 7))

              def a1_(t, mid=None):
                  b0 = (t % 3) * 2
                  ln_a(ps[:, b0 * 512:(b0 + 2) * 512], x_tok[:, t, :], lnbuf, ytmp[t % 3], t % 2, mid=mid)

              def b1_(t):
                  ln_b(lnbuf, x_tok[:, t, :], ytmp[t % 3])

              def c1_(t):
                  make_xT(x_tok[:, t, :], t, xbs[t % 2])

              for s_ in range(NT + 3):
                  if s_ < NT:
                      mm1_(s_)
                  midf = (lambda t_=s_ - 3: b1_(t_)) if 0 <= s_ - 3 < NT else None
                  if 0 <= s_ - 1 < NT:
                      a1_(s_ - 1, mid=midf)
                  elif midf is not None:
                      midf()
                  if 0 <= s_ - 3 < NT:
                      c1_(s_ - 3)
              A.release(base_mark)
              if l == 0:
                  dump("x1", x_tok, [128, NT, D])
              ck('ln1')

              GF = NFC // 2
              Wdg = A.alloc([GF, D], BF16)
              hT = A.alloc([GF, S], BF16)
              NWB = 3
              wgu = [A.alloc([2, 8, 128], BF16) for _ in range(NWB)]
              sg = [A.alloc([512], BF16) for _ in range(2)]
              ytmp = [A.alloc([D], F32) for _ in range(1)]
              xbs = [A.alloc([D], BF16) for _ in range(2)]
              wdv = wd_d[l].rearrange("(f p) n -> p f n", p=128)
              wgv = wg_d[l].rearrange("(k p) n -> p k n", p=128)
              wuv = wu_d[l].rearrange("(k p) n -> p k n", p=128)
              P.dma("sp", lnbuf[:, 0, :], ln_d["ln2_g"][l].partition_broadcast(128))
              P.dma("sp", lnbuf[:, 1, :], ln_d["ln2_b"][l].partition_broadcast(128))
              wi = 0
              ui = 0
              for g_ in range(2):
                  for fi in range(GF):
                      f = g_ * GF + fi
                      wb = wgu[wi % NWB]
                      wi += 1
                      P.dma("pool", wb[:, 0, :, :], wgv[:, :, f * 128:(f + 1) * 128])
                      P.dma("pool", wb[:, 1, :, :], wuv[:, :, f * 128:(f + 1) * 128])
                      P.dma("pool", Wdg[:, fi, :], wdv[:, f, :])
                      for tb in range(4):
                          pg = bank((ui % 2) * 2)
                          pu = bank((ui % 2) * 2 + 1)
                          ui += 1
                          for k in range(8):
                              P.matmul(pg, wb[:, 0, k, :], xT[:, k, tb * 512:(tb + 1) * 512],
                                       start=(k == 0), stop=(k == 7))
                          for k in range(8):
                              P.matmul(pu, wb[:, 1, k, :], xT[:, k, tb * 512:(tb + 1) * 512],
                                       start=(k == 0), stop=(k == 7))
                          s_ = sg[ui % 2]
                          P.act(s_, pg, AF.Silu)
                          P.tt(hT[:, fi, tb * 512:(tb + 1) * 512], pu, s_, ALU.mult)
                  ytmps = [ytmp[0], wgu[0].rearrange("p a b c -> p (a b c)").bitcast(F32),
                           wgu[1].rearrange("p a b c -> p (a b c)").bitcast(F32)]

                  def mm2_(t):
                      b0 = (t % 3) * 2
                      p2 = ps[:, b0 * 512:(b0 + 2) * 512]
                      for hf in range(2):
                          for fi in range(GF):
                              P.matmul(p2[:, hf * 512:(hf + 1) * 512], hT[:, fi, t * 128:(t + 1) * 128],
                                       Wdg[:, fi, hf * 512:(hf + 1) * 512], start=(fi == 0), stop=(fi == GF - 1))

                  def a2_(t, mid=None):
                      b0 = (t % 3) * 2
                      p2 = ps[:, b0 * 512:(b0 + 2) * 512]
                      if g_ == 0:
                          P.stt(x_tok[:, t, :], x_tok[:, t, :], ALPHA, p2, ALU.mult, ALU.add)
                          if mid is not None:
                              mid()
                      else:
                          ln_a(p2, x_tok[:, t, :], lnbuf, ytmps[t % 3], t % 2, alpha=1.0, mid=mid)

                  def b2_(t):
                      if g_ == 0:
                          return
                      ln_b(lnbuf, x_tok[:, t, :], ytmps[t % 3])

                  def c2_(t):
                      if g_ == 0:
                          return
                      if l == n_layers - 1:
                          P.dma("sp", y_d[t * 128:(t + 1) * 128, :], x_tok[:, t, :])
                      else:
                          P.dma("sp", xs_d[t * 128:(t + 1) * 128, :], x_tok[:, t, :])
                          make_xT(x_tok[:, t, :], t, xbs[t % 2])

                  for s_ in range(NT + 3):
                      if s_ < NT:
                          mm2_(s_)
                      midf = (lambda t_=s_ - 3: b2_(t_)) if 0 <= s_ - 3 < NT else None
                      if 0 <= s_ - 1 < NT:
                          a2_(s_ - 1, mid=midf)
                      elif midf is not None:
                          midf()
                      if 0 <= s_ - 3 < NT:
                          c2_(s_ - 3)
              A.release(base_mark)
              A.n = ARENA

        except _Stop:
            pass
        P.emit()
    _CACHE['P'] = P
    return nc, dbg_outs


def kernel(**inputs):
    if "c" not in _CACHE:
        _CACHE["c"] = host_consts()
    cf, cb, dc, ds = _CACHE["c"]
    nc, _ = build()
    x = np.ascontiguousarray(inputs["x"], dtype=np.float32)
    common = {
        "w_in": np.ascontiguousarray(inputs["w_in"], dtype=np.float32),
        "diff_lambda": np.ascontiguousarray(inputs["diff_lambda"], dtype=np.float32).reshape(L, 256),
        "diff_norm_g": np.ascontiguousarray(inputs["diff_norm_g"], dtype=np.float32),
        "fourier_w": np.ascontiguousarray(inputs["fourier_w"], dtype=np.float32),
        "gla_gate_w2": np.ascontiguousarray(inputs["gla_gate_w2"], dtype=np.float32),
        "gla_gate_b2": np.ascontiguousarray(inputs["gla_gate_b2"], dtype=np.float32),
        "gla_norm_g": np.ascontiguousarray(inputs["gla_norm_g"], dtype=np.float32),
        "w_out": np.ascontiguousarray(inputs["w_out"], dtype=np.float32),
        "ln1_g": np.ascontiguousarray(inputs["ln1_g"], dtype=np.float32),
        "ln1_b": np.ascontiguousarray(inputs["ln1_b"], dtype=np.float32),
        "ln2_g": np.ascontiguousarray(inputs["ln2_g"], dtype=np.float32),
        "ln2_b": np.ascontiguousarray(inputs["ln2_b"], dtype=np.float32),
        "ffn_w_gate": np.ascontiguousarray(inputs["ffn_w_gate"], dtype=np.float32),
        "ffn_w_up": np.ascontiguousarray(inputs["ffn_w_up"], dtype=np.float32),
        "ffn_w_down": np.ascontiguousarray(inputs["ffn_w_down"], dtype=np.float32),
        "c_f32": cf, "c_bf": cb, "dft_c": dc, "dft_s": ds,
    }
    in_maps = [dict(common, x=x[b]) for b in range(8)]
    res = run_bass_kernel_spmd(nc, in_maps, core_ids=list(range(8)))
    return np.stack([np.asarray(r["y"], dtype=np.float32) for r in res.results], axis=0)
```

```python
import numpy as np
import concourse.bass as bass
import concourse.mybir as mybir

F32 = mybir.dt.float32
BF16 = mybir.dt.bfloat16
ALU = mybir.AluOpType
AF = mybir.ActivationFunctionType
AX = mybir.AxisListType

_DT_SIZE = {F32: 4, BF16: 2, mybir.dt.float32r: 4, mybir.dt.int32: 4,
            mybir.dt.uint32: 4, mybir.dt.float16: 2, mybir.dt.uint16: 2,
            mybir.dt.int16: 2, mybir.dt.uint8: 1, mybir.dt.int8: 1}

ENGS = ("pe", "act", "dve", "pool", "sp")
N_DMA_SEMS = 24
TINY_BYTES = 256
SAME_ENGINE_SYNC = False


def ap_box(ap):
    t = ap.tensor
    name = t.name
    esz = _DT_SIZE[ap.dtype]
    dims = list(ap.ap)
    off = ap.offset
    space = str(ap.space)
    if "DRAM" in space.upper() or "HBM" in space.upper():
        lo = off
        hi = off
        for (st, n) in dims:
            if st >= 0:
                hi += st * (n - 1)
            else:
                lo += st * (n - 1)
        return (name, 0, 1, lo * esz, (hi + 1) * esz)
    pstep, pcnt = dims[0]
    if pstep == 0:
        pstep = 1 << 40
    p0 = off // pstep if pstep < (1 << 40) else 0
    f = off - p0 * pstep if pstep < (1 << 40) else off
    lo = f
    hi = f
    for (st, n) in dims[1:]:
        if st >= 0:
            hi += st * (n - 1)
        else:
            lo += st * (n - 1)
    if "PSUM" in space.upper():
        b0 = (lo * esz) // 2048
        b1 = ((hi + 1) * esz - 1) // 2048
        return (name, 0, 128, b0 * 2048, (b1 + 1) * 2048, True)
    return (name, p0, p0 + pcnt, lo * esz, (hi + 1) * esz)


def _overlap(a, b):
    return a[1] < b[2] and b[1] < a[2] and a[3] < b[4] and b[3] < a[4]


def _contains(a, b):
    return a[1] <= b[1] and a[2] >= b[2] and a[3] <= b[3] and a[4] >= b[4]


class Prog:
    def __init__(self, nc):
        self.nc = nc
        self.ins = []
        self.recs = {}
        self.dma_rr = {e: 0 for e in ENGS}
        self.trace = {}

    def add(self, eng, fn, reads=(), writes=(), dma=False):
        idx = len(self.ins)
        rb = list(dict.fromkeys(ap_box(a) for a in reads))
        wb = list(dict.fromkeys(ap_box(a) for a in writes))
        deps = set()
        tiny_deps = set()
        for b in rb:
            psum = len(b) > 5
            tiny = (not psum) and (b[4] - b[3]) <= TINY_BYTES
            for rec in self.recs.get(b[0], ()):
                if (rec[1] or (psum and rec[3] != eng)) and _overlap(rec[0], b):
                    deps.add(rec[2])
                    if tiny and rec[1]:
                        tiny_deps.add(rec[2])
        for b in wb:
            for rec in self.recs.get(b[0], ()):
                if _overlap(rec[0], b):
                    deps.add(rec[2])
        for b in wb:
            lst = self.recs.setdefault(b[0], [])
            lst[:] = [r for r in lst if not _contains(b, r[0])]
            lst.append([b, True, idx, eng])
        for b in rb:
            lst = self.recs.setdefault(b[0], [])
            found = False
            for r in lst:
                if (not r[1]) and r[3] == eng and r[0] == b and not dma \
                        and not self.ins[r[2]]["dma"]:
                    r[2] = idx
                    found = True
                    break
            if not found:
                lst.append([b, False, idx, eng])
        real = set()
        for d in deps:
            p = self.ins[d]
            if p["eng"] == "pe" and eng == "pe" and not p["dma"] and not dma:
                continue
            if (not SAME_ENGINE_SYNC) and p["eng"] == eng and eng != "pool" and not p["dma"] and not dma \
                    and d not in tiny_deps:
                continue
            real.add(d)
        self.ins.append(dict(eng=eng, fn=fn, deps=real, dma=dma, needed=False,
                             sem=None, val=None))
        return idx

    def emit(self, final_wait_eng="sp"):
        nc = self.nc
        ins = self.ins
        for r in ins:
            for d in r["deps"]:
                ins[d]["needed"] = True
        last_dmas = [i for i, r in enumerate(ins) if r["dma"]]
        import contextlib
        with contextlib.ExitStack() as st:
            esem = {e: st.enter_context(nc.semaphore("s_" + e)) for e in ENGS}
            dsem = {e: [st.enter_context(nc.semaphore("d_%s_%d" % (e, i)))
                        for i in range(N_DMA_SEMS)] for e in ("sp", "act", "pool")}
            cnt = {e: 0 for e in ENGS}
            dcnt = {e: [0] * N_DMA_SEMS for e in dsem}
            drr = {e: 0 for e in dsem}
            prev_use = {}
            for i, r in enumerate(ins):
                e = r["eng"]
                if r["dma"]:
                    s = drr[e]
                    drr[e] = (s + 1) % N_DMA_SEMS
                    if dcnt[e][s] > 0:
                        r["prev"] = (dsem[e][s], dcnt[e][s], ("d", e, s))
                    else:
                        r["prev"] = None
                    dcnt[e][s] += 16
                    r["sem"] = dsem[e][s]
                    r["val"] = dcnt[e][s]
                    r["semkey"] = ("d", e, s)
                elif r["needed"]:
                    cnt[e] += 1
                    r["sem"] = esem[e]
                    r["val"] = cnt[e]
                    r["semkey"] = ("e", e)
            block = st.enter_context(nc.Block())
            per_eng = {e: [i for i, r in enumerate(ins) if r["eng"] == e] for e in ENGS}
            final = {}
            for i in last_dmas:
                r = ins[i]
                final[r["semkey"]] = (r["sem"], max(r["val"], final.get(r["semkey"], (None, 0))[1]))

            def body(ename):
                def run(engobj):
                    waited = {}
                    for i in per_eng[ename]:
                        r = ins[i]
                        need = {}
                        for d in r["deps"]:
                            p = ins[d]
                            k = p["semkey"]
                            if need.get(k, (None, 0))[1] < p["val"]:
                                need[k] = (p["sem"], p["val"])
                        if r["dma"] and r["prev"] is not None:
                            s, v, k = r["prev"]
                            if need.get(k, (None, 0))[1] < v:
                                need[k] = (s, v)
                        for k, (s, v) in need.items():
                            if waited.get(k, 0) < v:
                                engobj.wait_ge(s, v)
                                waited[k] = v
                                self.trace.setdefault(ename, []).append(("w", k, v, i))
                        h = r["fn"](engobj)
                        if r["dma"]:
                            h.then_inc(r["sem"], 16)
                            self.trace.setdefault(ename, []).append(("i", r["semkey"], 16, i))
                        elif r["needed"]:
                            h.then_inc(r["sem"], 1)
                            self.trace.setdefault(ename, []).append(("i", r["semkey"], 1, i))
                    if ename == final_wait_eng:
                        for k, (s, v) in final.items():
                            if waited.get(k, 0) < v:
                                engobj.wait_ge(s, v)
                return run

            block.tensor(body("pe"))
            block.scalar(body("act"))
            block.vector(body("dve"))
            block.gpsimd(body("pool"))
            block.sync(body("sp"))

    def dma(self, eng, out, in_, **kw):
        return self.add(eng, lambda e: e.dma_start(out=out, in_=in_, **kw),
                        reads=[in_], writes=[out], dma=True)

    def matmul(self, out, lhsT, rhs, start=True, stop=True, **kw):
        return self.add("pe", lambda e: e.matmul(out, lhsT, rhs, start=start, stop=stop, **kw),
                        reads=[lhsT, rhs], writes=[out])

    def transpose(self, out, in_, ident):
        return self.add("pe", lambda e: e.transpose(out, in_, ident),
                        reads=[in_, ident], writes=[out])

    def act(self, out, in_, func, bias=None, scale=1.0, accum_out=None, eng="act"):
        reads = [in_]
        writes = [out]
        kw = {}
        if bias is not None:
            kw["bias"] = bias
            if not isinstance(bias, (int, float)):
                reads.append(bias)
        if not isinstance(scale, (int, float)):
            reads.append(scale)
        if accum_out is not None:
            kw["accum_out"] = accum_out
            writes.append(accum_out)
        return self.add(eng, lambda e: e.activation(out=out, in_=in_, func=func, scale=scale, **kw),
                        reads=reads, writes=writes)

    def tt(self, out, in0, in1, op, eng="dve"):
        return self.add(eng, lambda e: e.tensor_tensor(out=out, in0=in0, in1=in1, op=op),
                        reads=[in0, in1], writes=[out])

    def ts(self, out, in0, s1, op0, s2=None, op1=None, eng="dve", accum_out=None):
        reads = [in0]
        if not isinstance(s1, (int, float)):
            reads.append(s1)
        if s2 is not None and not isinstance(s2, (int, float)):
            reads.append(s2)
        kw = {}
        writes = [out]
        if op1 is not None:
            kw["op1"] = op1
        if accum_out is not None:
            kw["accum_out"] = accum_out
            writes.append(accum_out)
        return self.add(eng, lambda e: e.tensor_scalar(out=out, in0=in0, scalar1=s1, scalar2=s2,
                                                       op0=op0, **kw),
                        reads=reads, writes=writes)

    def stt(self, out, in0, scalar, in1, op0, op1, eng="dve"):
        reads = [in0, in1]
        if not isinstance(scalar, (int, float)):
            reads.append(scalar)
        return self.add(eng, lambda e: e.scalar_tensor_tensor(out=out, in0=in0, scalar=scalar,
                                                              in1=in1, op0=op0, op1=op1),
                        reads=reads, writes=[out])

    def copy(self, out, in_, eng="dve"):
        if eng == "act":
            return self.add(eng, lambda e: e.activation(out=out, in_=in_, func=AF.Identity),
                            reads=[in_], writes=[out])
        return self.add(eng, lambda e: e.tensor_copy(out=out, in_=in_), reads=[in_], writes=[out])

    def memset(self, ap, val, eng="dve"):
        return self.add(eng, lambda e: e.memset(ap, val), reads=[], writes=[ap])

    def reduce(self, out, in_, op, axis=AX.X, eng="dve", **kw):
        return self.add(eng, lambda e: e.tensor_reduce(out=out, in_=in_, op=op, axis=axis, **kw),
                        reads=[in_], writes=[out])

    def recip(self, out, in_, eng="dve"):
        return self.add(eng, lambda e: e.reciprocal(out=out, in_=in_), reads=[in_], writes=[out])


def simulate_trace(trace):
    pos = {e: 0 for e in trace}
    sem = {}
    progress = True
    while progress:
        progress = False
        for e, ops in trace.items():
            while pos[e] < len(ops):
                kind, k, v, i = ops[pos[e]]
                if kind == "w":
                    if sem.get(k, 0) >= v:
                        pos[e] += 1
                        progress = True
                    else:
                        break
                else:
                    sem[k] = sem.get(k, 0) + v
                    pos[e] += 1
                    progress = True
    stuck = {e: ops[pos[e]] for e, ops in trace.items() if pos[e] < len(ops)}
    return stuck, sem

import contextlib
import os
_SK = set(os.environ.get('DBGSKIP', '').split(','))
import math
import ml_dtypes
from concourse.bass_utils import run_bass_kernel_spmd

S = 2048
D = 1024
L = 2
INW = 2592
FF = 2816
NT = 16
NFC = FF // 128
ALPHA = float((2 * L) ** 0.25)
EPS = 1e-5
C_DQ, C_DK, C_DV, C_FU, C_GQ, C_GK, C_GV, C_GR, C_GZ = 0, 512, 1024, 1536, 1792, 1920, 2048, 2304, 2560

CF_ID, CF_COS, CF_SIN, CF_HM, CF_CC, CF_SC, CF_ONES, CF_NEGH, CF_N = 0, 128, 640, 1152, 1156, 1284, 1412, 1540, 1548
CB_ID, CB_TRIF, CB_TRIB, CB_BD, CB_ONE, CB_N = 0, 128, 256, 384, 640, 768


def host_consts():
    p = np.arange(128)
    cf = np.zeros((128, CF_N), np.float32)
    cf[:, CF_ID:CF_ID + 128] = np.eye(128, dtype=np.float32)
    inv_freq = (10000.0 ** (-np.arange(0, 64, 2, dtype=np.float32) / 64)).astype(np.float32)
    pos = (np.arange(NT)[None, :] * 128 + p[:, None]).astype(np.float32)
    ang = pos[:, :, None] * inv_freq[None, None, :]
    cf[:, CF_COS:CF_COS + 512] = np.cos(ang).reshape(128, 512)
    cf[:, CF_SIN:CF_SIN + 512] = np.sin(ang).reshape(128, 512)
    cf[:, CF_HM:CF_HM + 4] = (p[:, None] // 32 == np.arange(4)[None, :]).astype(np.float32)
    c = np.arange(64)
    a = 2 * np.pi * np.outer(c, c) / 64.0
    cc = np.cos(a) / 8.0
    sc = np.sin(a) / 8.0
    z = np.zeros((64, 64))
    cf[:, CF_CC:CF_CC + 128] = np.block([[cc, z], [z, cc]])
    cf[:, CF_SC:CF_SC + 128] = np.block([[sc, z], [z, sc]])
    cf[:, CF_ONES:CF_ONES + 128] = 1.0
    cf[:, CF_NEGH:CF_NEGH + 8] = -0.5
    cb = np.zeros((128, CB_N), np.float32)
    cb[:, CB_ID:CB_ID + 128] = np.eye(128)
    cb[:, CB_TRIF:CB_TRIF + 128] = (p[:, None] <= p[None, :])
    cb[:, CB_TRIB:CB_TRIB + 128] = (p[:, None] >= p[None, :])
    cb[:, CB_BD:CB_BD + 256] = (p[:, None] // 32 == np.arange(256)[None, :] // 64)
    cb[:, CB_ONE:CB_ONE + 128] = 1.0
    s = np.arange(S, dtype=np.float64)
    sk = np.outer(s, s) % S
    ang = 2 * np.pi * sk / S
    dc = (np.cos(ang) / math.sqrt(S)).astype(np.float32).astype(ml_dtypes.bfloat16)
    ds = (-np.sin(ang) / math.sqrt(S)).astype(np.float32).astype(ml_dtypes.bfloat16)
    return cf, cb.astype(ml_dtypes.bfloat16), dc, ds


class Arena:
    def __init__(self, t, nbytes):
        self.t = t
        self.n = nbytes
        self.top = 0

    def mark(self):
        return self.top

    def release(self, m):
        self.top = m

    def alloc(self, shape, dt):
        n = 1
        for s_ in shape:
            n *= s_
        b = n * _DT_SIZE[dt]
        off = self.top
        self.top += (b + 63) // 64 * 64
        assert self.top <= self.n, ("arena overflow", self.top, self.n)
        ap = self.t[:, off // 2:(off + b) // 2]
        if dt != BF16:
            ap = ap.bitcast(dt)
        if len(shape) == 2:
            ap = ap.rearrange("p (a b) -> p a b", a=shape[0])
        elif len(shape) == 3:
            ap = ap.rearrange("p (a b c) -> p a b c", a=shape[0], b=shape[1])
        return ap


class Rot:
    def __init__(self, items):
        self.items = items
        self.i = 0

    def next(self):
        r = self.items[self.i % len(self.items)]
        self.i += 1
        return r


_CACHE = {}


class _Stop(Exception):
    pass


def build(n_layers=L, dbg=None, upto=None):
    nc = bass.Bass("TRN2", target_bir_lowering=False)
    dram = lambda name, shape, dt, kind="ExternalInput": nc.dram_tensor(name, shape, dt, kind=kind).ap()
    x_d = dram("x", [S, D], F32)
    w_in_d = dram("w_in", [L, D, INW], F32)
    lam_d = dram("diff_lambda", [L, 256], F32)
    dng_d = dram("diff_norm_g", [L, 128], F32)
    fw_d = dram("fourier_w", [L, 4, 64, 64], F32)
    w2_d = dram("gla_gate_w2", [L, 2, 16, 128], F32)
    b2_d = dram("gla_gate_b2", [L, 2, 128], F32)
    gng_d = dram("gla_norm_g", [L, 64], F32)
    wout_d = dram("w_out", [L, D, D], F32)
    ln_d = {k: dram(k, [L, D], F32) for k in ("ln1_g", "ln1_b", "ln2_g", "ln2_b")}
    wg_d = dram("ffn_w_gate", [L, D, FF], F32)
    wu_d = dram("ffn_w_up", [L, D, FF], F32)
    wd_d = dram("ffn_w_down", [L, FF, D], F32)
    cf_d = dram("c_f32", [128, CF_N], F32)
    cb_d = dram("c_bf", [128, CB_N], BF16)
    dftc_d = dram("dft_c", [S, S], BF16)
    dfts_d = dram("dft_s", [S, S], BF16)
    y_d = dram("y", [S, D], F32, kind="ExternalOutput")
    xs_d = dram("xs_scr", [S, D], F32, kind="Internal")
    dbg_outs = {}

    ARENA = 207 * 1024
    XTB = NT * D * 4
    with contextlib.ExitStack() as st:
        arena_t = st.enter_context(nc.sbuf_tensor("arena", [128, ARENA // 2], BF16))
        ps = st.enter_context(nc.psum_tensor("ps", [128, 4096], F32))
        P = Prog(nc)
        A = Arena(arena_t, ARENA)
        x_tok = arena_t[:, (ARENA - XTB) // 2:ARENA // 2].bitcast(F32).rearrange("p (t d) -> p t d", t=NT)

        def bank(b, n=512, off=0):
            return ps[:, b * 512 + off:b * 512 + off + n]

        def bankbf(b):
            return ps[:, b * 512:(b + 1) * 512].bitcast(BF16)

        def dump(name, ap, shape, dt=F32):
            if dbg is None or name not in dbg:
                return
            d_ = nc.dram_tensor("dbg_" + name, shape, dt, kind="ExternalOutput").ap()
            dbg_outs[name] = d_
            P.dma("sp", d_, ap)

        cf = A.alloc([CF_N], F32)
        cb = A.alloc([CB_N], BF16)
        xT = A.alloc([8, S], BF16)
        lnbuf = A.alloc([2, D], F32)
        small = A.alloc([640], F32)
        P.dma("sp", cf, cf_d)
        P.dma("sp", cb, cb_d)
        ident_f = cf[:, CF_ID:CF_ID + 128]
        ident_b = cb[:, CB_ID:CB_ID + 128]
        cos_t = cf[:, CF_COS:CF_COS + 512].rearrange("p (t i) -> p t i", t=NT)
        sin_t = cf[:, CF_SIN:CF_SIN + 512].rearrange("p (t i) -> p t i", t=NT)
        hmask4 = cf[:, CF_HM:CF_HM + 4]
        ccbd = cf[:, CF_CC:CF_CC + 128]
        scbd = cf[:, CF_SC:CF_SC + 128]
        ones_f = cf[:, CF_ONES:CF_ONES + 128]
        negh8 = cf[:, CF_NEGH:CF_NEGH + 8]
        tri = [cb[:, CB_TRIF:CB_TRIF + 128], cb[:, CB_TRIB:CB_TRIB + 128]]
        bdmask = cb[:, CB_BD:CB_BD + 256]
        lp_bc = small[:, 0:256]
        gdiff = small[:, 256:384]
        ggla = small[:, 384:448]
        negb2 = small[:, 448:450]
        neglam = small[:, 450:451]
        negM = small[:, 451:452]
        sc_tmp = small[:, 452:500]
        nrm = small[:, 500:504]
        epsc = small[:, 504:505]
        base_mark = A.mark()

        win = lambda l: w_in_d[l].rearrange("(k p) n -> p k n", p=128)

        def rstd_from(out, ss, n, width):
            tmp = sc_tmp[:, 40:40 + width]
            P.ts(tmp, ss, 1.0 / n, ALU.mult, EPS, ALU.add)
            P.tt(out, tmp, negh8[:, 0:width], ALU.pow, eng="pool")

        xb_rot = None

        def make_xT(x_tile_f32, t, xb, evac_eng="act", pbank=7):
            P.copy(xb, x_tile_f32, eng="act")
            pst = bankbf(pbank)
            for k in range(8):
                P.transpose(pst[:, k * 128:(k + 1) * 128], xb[:, k * 128:(k + 1) * 128], ident_b)
            P.copy(xT[:, :, t * 128:(t + 1) * 128], pst.rearrange("p (k n) -> p k n", k=8), eng=evac_eng)

        def ln_a(psum2, xres, gb, ytmp, par, alpha=ALPHA, mid=None):
            sct = sc_tmp[:, 0:16] if par == 0 else small[:, 540:556]
            P.stt(ytmp, xres, alpha, psum2, ALU.mult, ALU.add)
            stats = sct[:, 0:12]
            mv = sct[:, 12:14]
            for c_ in range(2):
                P.add("dve", lambda e, c_=c_: e.bn_stats(out=stats[:, c_ * 6:(c_ + 1) * 6],
                                                          in_=ytmp[:, c_ * 512:(c_ + 1) * 512]),
                      reads=[ytmp[:, c_ * 512:(c_ + 1) * 512]], writes=[stats[:, c_ * 6:(c_ + 1) * 6]])
            P.add("dve", lambda e: e.bn_aggr(out=mv, in_=stats), reads=[stats], writes=[mv])
            rs = sct[:, 14:15]
            nmr = sct[:, 15:16]
            P.act(rs, mv[:, 1:2], AF.Ln, bias=epsc)
            P.act(rs, rs, AF.Exp, scale=-0.5)
            if mid is not None:
                mid()
            P.stt(nmr, mv[:, 0:1], -1.0, rs, ALU.mult, ALU.mult)
            P.act(ytmp, ytmp, AF.Identity, bias=nmr, scale=rs)
            P.tt(ytmp, ytmp, gb[:, 0, :], ALU.mult, eng="pool")

        def ln_b(gb, out_tok, ytmp):
            P.tt(out_tok, ytmp, gb[:, 1, :], ALU.add)

        def ck(name):
            if upto == name:
                raise _Stop()

        m_ = A.mark()
        NXB = 8
        xin = [A.alloc([D], F32) for _ in range(NXB)]
        xbs = [A.alloc([D], BF16) for _ in range(2)]
        for t in range(NXB):
            P.dma("sp", xin[t], x_d[t * 128:(t + 1) * 128, :])
        for t in range(NT):
            make_xT(xin[t % NXB], t, xbs[t % 2], evac_eng="dve", pbank=(7 if t % 2 else 6))
            if t + NXB < NT:
                P.dma("sp", xin[t % NXB], x_d[(t + NXB) * 128:(t + NXB + 1) * 128, :])
        A.release(m_)
        try:
          for l in range(n_layers):
              lam_init = 0.8 - 0.6 * math.exp(-0.3 * l)
              A.release(base_mark)
              P.dma("sp", lp_bc, lam_d[l].partition_broadcast(128))
              P.dma("sp", gdiff, dng_d[l].partition_broadcast(128))
              P.dma("sp", ggla, gng_d[l].partition_broadcast(128))
              for d_ in range(2):
                  P.dma("sp", negb2[:, d_:d_ + 1], b2_d[l, d_].rearrange("(p o) -> p o", o=1))
              P.ts(negb2, negb2, -1.0, ALU.mult)
              P.ts(gdiff, gdiff, 1.0 - lam_init, ALU.mult)
              P.memset(epsc, EPS)
              pr = sc_tmp[:, 16:18]
              prod = A.alloc([128], F32)
              lp4 = lp_bc.rearrange("p (a d) -> p a d", a=4)
              for i_ in range(2):
                  P.tt(prod[:, 0:64], lp4[:, 2 * i_, :], lp4[:, 2 * i_ + 1, :], ALU.mult)
                  P.reduce(pr[:, i_:i_ + 1], prod[:, 0:64], ALU.add)
              P.act(pr, pr, AF.Exp)
              P.tt(neglam, pr[:, 1:2], pr[:, 0:1], ALU.subtract)
              P.ts(neglam, neglam, -lam_init, ALU.add)
              A.release(base_mark)
              ck('params')

              ck('xT')
              xres_d = x_d if l == 0 else xs_d

              ocat = A.alloc([8, S], BF16)
              Wout = A.alloc([8, D], BF16)
              Wgf = A.alloc([8, 288], BF16)
              Wgt = A.alloc([8, 512], BF16)
              wbd = A.alloc([2, 128], F32)
              Wcs = A.alloc([2, 256], BF16)
              mixer_mark = A.mark()
              NDB = 4
              DBB = 8 * 512 * 2
              dbuf = [arena_t[:, (ARENA - (i_ + 1) * DBB) // 2:(ARENA - i_ * DBB) // 2].rearrange(
                  "p (a b) -> p a b", a=8) for i_ in range(NDB)]
              dft_v = [dftc_d.rearrange("(t p) k -> p t k", p=128), dfts_d.rearrange("(t p) k -> p t k", p=128)]
              dft_chunks = [(kb, which, tg) for kb in range(4) for which in range(2) for tg in range(2)]

              def dft_load(j):
                  kb, which, tg = dft_chunks[j]
                  P.dma("sp", dbuf[j % NDB], dft_v[which][:, tg * 8:(tg + 1) * 8, kb * 512:(kb + 1) * 512])

              Wh = [A.alloc([8, 384], BF16) for _ in range(2)]
              QTzs = [A.alloc([2, S], BF16) for _ in range(2)]
              KTs = [A.alloc([S], BF16) for _ in range(2)]
              Vaugs = [A.alloc([NT, 132], BF16) for _ in range(2)]
              negMs = [negM, small[:, 570:571]]
              Wf = Wh[0][:, :, 0:256]
              PT = [A.alloc([1024], BF16) for _ in range(3)]
              O1n = A.alloc([8, 128], F32)
              O2t = A.alloc([8, 128], F32)
              otoks = [A.alloc([8, 128], BF16) for _ in range(2)]
              qkr = [A.alloc([256], BF16) for _ in range(3)]
              tmpAs = [A.alloc([128], F32) for _ in range(2)]
              tmpBs = [A.alloc([128], F32) for _ in range(2)]
              sq = A.alloc([256], F32)
              D2 = A.alloc([256], F32)
              sqs = [sq, A.alloc([256], F32)]
              red_all = A.alloc([NT, 4], F32)
              red4 = sc_tmp[:, 20:24]
              gm = sc_tmp[:, 24:26]
              nrm2 = sc_tmp[:, 26:28]
              rz = sc_tmp[:, 28:36]
              ss8 = small[:, 512:520]
              rstd8 = small[:, 520:528]
              for i_ in range(2):
                  P.memset(Vaugs[i_][:, :, 128:129], 1.0)
                  P.memset(QTzs[i_][64:128, 0, :], 0.0)
                  P.memset(QTzs[i_][0:64, 1, :], 0.0, eng="pool")
              assert A.top <= ARENA - 3 * DBB, A.top
              pso = [ps[:, (4 + qi // 3) * 512 + (qi % 3) * 160:(4 + qi // 3) * 512 + (qi % 3) * 160 + 129]
                     for qi in range(8)]
              grp = [(0, 3), (3, 3), (6, 2)]

              def pso_grp(gi, c0, c1):
                  q0, n_ = grp[gi]
                  base_ = (4 + gi) * 512
                  return ps[:, base_:base_ + n_ * 160].rearrange("p (a b) -> p a b", b=160)[:, :, c0:c1]

              def load_wh(h_):
                  for j_, c0 in enumerate((C_DQ, C_DK, C_DV)):
                      P.dma("pool", Wh[h_ % 2][:, :, j_ * 128:(j_ + 1) * 128],
                            win(l)[:, :, c0 + h_ * 128:c0 + (h_ + 1) * 128])

              load_wh(0)
              load_wh(1)
              fin_pending = []

              def att_inproj(h):
                  wh = Wh[h % 2]
                  QTz, KT, Vaug, negM_h = QTzs[h % 2], KTs[h % 2], Vaugs[h % 2], negMs[h % 2]
                  P.memset(nrm, 0.0)
                  tr_pending = []
                  for t in range(NT):
                      pb = bank(t % 4, 256)
                      pv_ = bank(4 + t % 2, 128)
                      for k in range(8):
                          P.matmul(pb, xT[:, k, t * 128:(t + 1) * 128], wh[:, k, 0:256], start=(k == 0), stop=(k == 7))
                      for k in range(8):
                          P.matmul(pv_, xT[:, k, t * 128:(t + 1) * 128], wh[:, k, 256:384], start=(k == 0), stop=(k == 7))
                      qk4 = pb.rearrange("p (g h d) -> p g h d", g=4, h=2)
                      t1 = qk4[:, :, 0, :]
                      t2 = qk4[:, :, 1, :]
                      cbt = cos_t[:, t:t + 1, :].broadcast_to([128, 4, 32])
                      sbt = sin_t[:, t:t + 1, :].broadcast_to([128, 4, 32])
                      q_ = qkr[t % 3]
                      q4 = q_.rearrange("p (g h d) -> p g h d", g=4, h=2)
                      ta = tmpAs[t % 2].rearrange("p (g d) -> p g d", g=4)
                      tb_ = tmpBs[t % 2].rearrange("p (g d) -> p g d", g=4)
                      P.tt(ta, t1, cbt, ALU.mult)
                      P.tt(tb_, t2, sbt, ALU.mult)
                      P.tt(q4[:, :, 0, :], ta, tb_, ALU.subtract)
                      P.tt(ta, t2, cbt, ALU.mult)
                      P.tt(tb_, t1, sbt, ALU.mult)
                      P.tt(q4[:, :, 1, :], ta, tb_, ALU.add)
                      P.copy(Vaug[:, t, 0:128], pv_, eng="act")
                      P.tt(sqs[t % 2], q_, q_, ALU.mult, eng="pool")
                      def tr_(t=t, q_=q_, sq_=sqs[t % 2]):
                          P.reduce(red_all[:, t, :], sq_.rearrange("p (g d) -> p g d", g=4), ALU.add)
                          pst = bankbf(7 if t % 2 else 6)
                          P.transpose(pst[:, 0:128], q_[:, 0:128], ident_b)
                          P.transpose(pst[:, 128:256], q_[:, 128:256], ident_b)
                          P.copy(QTz[0:64, 0, t * 128:(t + 1) * 128], pst[0:64, 0:128], eng="act")
                          P.copy(QTz[64:128, 1, t * 128:(t + 1) * 128], pst[64:128, 0:128], eng="act")
                          P.copy(KT[:, t * 128:(t + 1) * 128], pst[:, 128:256], eng="act")
                      tr_pending.append(tr_)
                      if len(tr_pending) > 1:
                          tr_pending.pop(0)()
                  while tr_pending:
                      tr_pending.pop(0)()
                  P.reduce(nrm, red_all.rearrange("p t g -> p g t"), ALU.max)
                  ck('h0proj')
                  if h + 2 < 4:
                      load_wh(h + 2)
                  if h == 3:
                      P.dma("pool", Wf, win(l)[:, :, C_FU:C_FU + 256])
                      P.dma("pool", Wgf[:, :, 0:256], win(l)[:, :, C_GQ:C_GQ + 256])
                      P.dma("pool", Wgf[:, :, 256:288], win(l)[:, :, C_GZ:C_GZ + 32])
                      P.dma("pool", Wgt, win(l)[:, :, C_GV:C_GV + 512])
                      for k in range(8):
                          P.dma("pool", Wout[:, k, :], wout_d[l, k * 128:(k + 1) * 128, :])
                      P.memset(wbd, 0.0, eng="pool")
                      for g_ in range(4):
                          c_, gl = g_ // 2, g_ % 2
                          P.dma("sp", wbd[gl * 64:(gl + 1) * 64, c_, gl * 64:(gl + 1) * 64], fw_d[l, g_])
                      for j_ in range(NDB - 1):
                          dft_load(j_)
                  P.reduce(nrm2, nrm.rearrange("p (a b) -> p a b", a=2), ALU.max)
                  P.ts(D2[:, 0:128], ident_f, nrm2[:, 0:1], ALU.mult)
                  P.ts(D2[:, 128:256], ident_f, nrm2[:, 1:2], ALU.mult)
                  pbm = bank(6, 256)
                  P.matmul(pbm, ones_f, D2, start=True, stop=True)
                  P.reduce(gm, pbm.rearrange("p (a b) -> p a b", a=2), ALU.max)
                  P.tt(negM_h, gm[:, 0:1], gm[:, 1:2], ALU.add)
                  P.ts(negM_h, negM_h, -0.5 * 0.125, ALU.mult)
                  if l == 0 and h == 0:
                      dump("xT", xT, [128, 8, S], BF16)
                      dump("negM", negM_h, [128, 1])
                      ck('qkt')

              def att_steps(h):
                  QTz, KT, Vaug, negM_h = QTzs[h % 2], KTs[h % 2], Vaugs[h % 2], negMs[h % 2]
                  for qh in range(2):
                      otok = otoks[qh]
                      steps = [(m, kt) for m in range(2) for kt in range(NT)]

                      def emit_scores(i):
                          m, kt = steps[i]
                          sb_i = (i % 2) * 2
                          for j_ in range(2):
                              P.matmul(ps[:, (sb_i + j_) * 512:(sb_i + j_ + 1) * 512],
                                       KT[:, kt * 128:(kt + 1) * 128],
                                       QTz[:, m, qh * 1024 + j_ * 512:qh * 1024 + (j_ + 1) * 512],
                                       start=True, stop=True)

                      emit_scores(0)
                      for i, (m, kt) in enumerate(steps):
                          if i + 1 < len(steps):
                              emit_scores(i + 1)
                          if i == 4 and fin_pending:
                              fin_pending.pop(0)()
                          sb_i = (i % 2) * 2
                          pss = ps[:, sb_i * 512:(sb_i + 2) * 512]
                          pt = PT[i % 3]
                          P.act(pt, pss, AF.Exp, bias=negM_h, scale=0.125)
                          for qi in range(8):
                              P.matmul(pso[qi], pt[:, qi * 128:(qi + 1) * 128], Vaug[:, kt, 0:129],
                                       start=(kt == 0 and qi % 3 == 0), stop=(kt == NT - 1), skip_group_check=True)
                          if kt != NT - 1:
                              continue
                          if m == 0:
                              for gi, (q0, n_) in enumerate(grp):
                                  P.recip(rz[:, q0:q0 + n_], pso_grp(gi, 128, 129).rearrange("p a b -> p (a b)"))
                                  P.tt(O1n[:, q0:q0 + n_, :], pso_grp(gi, 0, 128),
                                       rz[:, q0:q0 + n_].unsqueeze(2).broadcast_to([128, n_, 128]), ALU.mult)
                          else:
                              for gi, (q0, n_) in enumerate(grp):
                                  P.recip(rz[:, q0:q0 + n_], pso_grp(gi, 128, 129).rearrange("p a b -> p (a b)"))
                              P.ts(rz, rz, neglam, ALU.mult)
                              for gi, (q0, n_) in enumerate(grp):
                                  P.tt(O2t[:, q0:q0 + n_, :], pso_grp(gi, 0, 128),
                                       rz[:, q0:q0 + n_].unsqueeze(2).broadcast_to([128, n_, 128]), ALU.mult)
                              P.tt(O1n, O1n, O2t, ALU.add)

                              def chain_(h=h, qh=qh, otok=otok):
                                  P.tt(O2t, O1n, O1n, ALU.mult, eng="pool")
                                  P.reduce(ss8, O2t, ALU.add)
                                  rstd_from(rstd8, ss8, 128.0, 8)
                                  P.tt(O1n, O1n, rstd8.unsqueeze(2).broadcast_to([128, 8, 128]), ALU.mult)
                                  P.tt(otok, O1n, gdiff.unsqueeze(1).broadcast_to([128, 8, 128]), ALU.mult,
                                       eng="pool")
                                  for qi in range(8):
                                      dst_ = ocat[:, h, qh * 1024 + qi * 128:qh * 1024 + (qi + 1) * 128]
                                      P.add("sp", lambda e, dst_=dst_, src_=otok[:, qi, :]: e.dma_start(
                                          out=dst_, in_=src_, transpose=True),
                                          reads=[otok[:, qi, :]], writes=[dst_], dma=True)
                              fin_pending.append(chain_)
              att_inproj(0)
              att_inproj(1)
              for h in range(4):
                  att_steps(h)
                  if h + 2 < 4:
                      att_inproj(h + 2)
              while fin_pending:
                  fin_pending.pop(0)()
              A.release(mixer_mark)
              if l == 0:
                  dump("odiff", ocat[:, 0:4, :], [128, 4, S], BF16)
              ck('att')

              _skip = A.alloc([2 * 8 * 384], BF16)
              uT = A.alloc([2, S], BF16)
              ucs = A.alloc([NT, 512], BF16)
              P.dma("sp", lnbuf[:, 0, :], ln_d["ln1_g"][l].partition_broadcast(128))
              P.dma("sp", lnbuf[:, 1, :], ln_d["ln1_b"][l].partition_broadcast(128))
              dft_load(NDB - 1)
              for c_ in range(2):
                  pb = bank(4 + c_, 256)
                  P.matmul(pb[:, 0:128], ccbd, wbd[:, c_, :], start=True, stop=True)
                  P.matmul(pb[:, 128:256], scbd, wbd[:, c_, :], start=True, stop=True)
                  P.copy(Wcs[:, c_, :], pb, eng="act")
              for tb in range(4):
                  for c_ in range(2):
                      pb = bank((tb * 2 + c_) % 4)
                      for k in range(8):
                          P.matmul(pb, Wf[:, k, c_ * 128:(c_ + 1) * 128], xT[:, k, tb * 512:(tb + 1) * 512],
                                   start=(k == 0), stop=(k == 7))
                      P.copy(uT[:, c_, tb * 512:(tb + 1) * 512], pb, eng="act")
              for t in range(NT):
                  pb = bank(4 + t % 2)
                  for c_ in range(2):
                      P.matmul(pb[:, c_ * 256:(c_ + 1) * 256], uT[:, c_, t * 128:(t + 1) * 128], Wcs[:, c_, :],
                               start=True, stop=True)
                  P.copy(ucs[:, t, :], pb, eng="act")
                  if t == 7:
                      ucs_half = True
              for j_, (kb, which, tg) in enumerate(dft_chunks):
                  pbs = [bank(0 + (kb % 2) * 2), bank(1 + (kb % 2) * 2)]
                  db = dbuf[j_ % NDB]
                  for tt_ in range(8):
                      t = tg * 8 + tt_
                      first = (which == 0 and t == 0)
                      last = (which == 1 and t == NT - 1)
                      for c_ in range(2):
                          P.matmul(pbs[c_], ucs[:, t, c_ * 256 + which * 128:c_ * 256 + (which + 1) * 128],
                                   db[:, tt_, :], start=first, stop=last)
                  if j_ + NDB < len(dft_chunks):
                      dft_load(j_ + NDB)
                  if which == 1 and tg == 1:
                      for c_ in range(2):
                          P.copy(ocat[:, 4 + c_, kb * 512:(kb + 1) * 512], pbs[c_], eng=("act" if c_ else "dve"))
              A.release(mixer_mark)
              if l == 0:
                  dump("ofour", ocat[:, 4:6, :], [128, 2, S], BF16)
              ck('four')

              gqk = A.alloc([2, S], F32)
              gzT = A.alloc([S], BF16)
              gv = A.alloc([NT, 256], BF16)
              gate = A.alloc([NT, 256], BF16)
              w2pad = A.alloc([2, 128], BF16)
              qt = [A.alloc([S], BF16) for _ in range(2)]
              kt_ = [A.alloc([S], BF16) for _ in range(2)]
              Sbf = [A.alloc([NT, 256], BF16) for _ in range(2)]
              gla_mark = A.mark()
              P.memset(w2pad, 0.0)
              for d_ in range(2):
                  P.dma("pool", w2pad[d_ * 16:(d_ + 1) * 16, d_, :], w2_d[l, d_])
              for tb in range(4):
                  for c_ in range(2):
                      pb = bank((tb * 3 + c_) % 4)
                      for k in range(8):
                          P.matmul(pb, Wgf[:, k, c_ * 128:(c_ + 1) * 128], xT[:, k, tb * 512:(tb + 1) * 512],
                                   start=(k == 0), stop=(k == 7))
                      P.copy(gqk[:, c_, tb * 512:(tb + 1) * 512], pb, eng=("act" if c_ else "dve"))
                  pb = bank((tb * 3 + 2) % 4)
                  for k in range(8):
                      P.matmul(pb[0:32, :], Wgf[:, k, 256:288], xT[:, k, tb * 512:(tb + 1) * 512],
                               start=(k == 0), stop=(k == 7))
                  P.copy(gzT[0:32, tb * 512:(tb + 1) * 512], pb[0:32, :], eng="dve")
              for t in range(NT):
                  pb = bank(4 + t % 2)
                  for k in range(8):
                      P.matmul(pb, xT[:, k, t * 128:(t + 1) * 128], Wgt[:, k, :], start=(k == 0), stop=(k == 7))
                  P.copy(gv[:, t, :], pb[:, 0:256], eng="act")
                  P.act(gate[:, t, :], pb[:, 256:512], AF.Silu)
              A.release(gla_mark)
              Bc = A.alloc([S], F32)
              Ec = A.alloc([S], F32)
              kdec_tok = A.alloc([NT, 128], BF16)
              kdT = [A.alloc([128], BF16) for _ in range(2)]
              Srot = [A.alloc([256], F32) for _ in range(2)]
              wgt_flat = Wgt.rearrange("p a b -> p (a b)")
              kdec_toks = [kdec_tok, wgt_flat[:, 0:2048].rearrange("p (a b) -> p a b", a=NT)]
              Srots = [Srot, [wgt_flat[:, 2048:2560].bitcast(F32), wgt_flat[:, 2560:3072].bitcast(F32)]]
              elast_t = small[:, 572:604].rearrange("p (d n) -> p d n", d=2)
              P.tt(gate.rearrange("p t (h v) -> p (t h) v", h=4), gate.rearrange("p t (h v) -> p (t h) v", h=4),
                   ggla.unsqueeze(1).broadcast_to([128, NT * 4, 64]), ALU.mult, eng="pool")
              for d_ in range(2):
                  for tb in range(4):
                      pb = bank(tb % 4)
                      P.matmul(pb, w2pad[0:32, d_, :], gzT[0:32, tb * 512:(tb + 1) * 512], start=True, stop=True)
                      P.act(Ec[:, tb * 512:(tb + 1) * 512], pb, AF.Exp, bias=negb2[:, d_:d_ + 1], scale=-1.0)
                  P.act(Ec, Ec, AF.Ln, bias=1.0)
                  for n in range(NT):
                      o_ = Bc[:, n * 128:(n + 1) * 128]
                      i_ = Ec[:, n * 128:(n + 1) * 128]
                      if d_ == 1:
                          o_ = o_[:, ::-1]
                          i_ = i_[:, ::-1]
                      P.add("dve", lambda e, o_=o_, i_=i_: e.tensor_tensor_scan(
                          out=o_, data0=ones_f, data1=i_, initial=0.0, op0=ALU.mult, op1=ALU.add),
                          reads=[ones_f, Ec[:, n * 128:(n + 1) * 128]], writes=[Bc[:, n * 128:(n + 1) * 128]])
                  P.act(Ec, Bc, AF.Exp, scale=-1.0 / 16.0)
                  P.act(Bc, Bc, AF.Exp, scale=1.0 / 16.0)
                  ecol = 127 if d_ == 0 else 0
                  P.copy(elast_t[:, d_, :], Ec.rearrange("p (n c) -> p n c", c=128)[:, :, ecol], eng="dve")
                  P.stt(qt[d_], gqk[:, 0, :], 32.0 ** -0.5, Ec, ALU.mult, ALU.mult)
                  P.tt(kt_[d_], gqk[:, 1, :], Bc, ALU.mult)
                  pst = bankbf(7)
                  kd_all = Ec.bitcast(BF16)[:, 0:S]
                  P.tt(kd_all.rearrange("p (n c) -> p n c", c=128), kt_[d_].rearrange("p (n c) -> p n c", c=128),
                       elast_t[:, d_, :].unsqueeze(2).broadcast_to([128, NT, 128]), ALU.mult)
                  for n in range(NT):
                      kd = kd_all[:, n * 128:(n + 1) * 128]
                      P.transpose(pst[:, (n % 8) * 128:(n % 8 + 1) * 128], kd, ident_b)
                      if n % 8 == 7:
                          P.copy(kdec_toks[d_][:, n - 7:n + 1, :], pst.rearrange("p (a b) -> p a b", a=8),
                                 eng="act")
              orders = [list(range(NT)), list(range(NT - 1, -1, -1))]
              prevs = []
              for d_ in range(2):
                  P.memset(Srots[d_][0], 0.0)
                  P.memset(Sbf[d_][:, orders[d_][0], :], 0.0)
                  prevs.append(Srots[d_][0])
              for i_ in range(NT - 1):
                  for d_ in range(2):
                      n = orders[d_][i_]
                      pb = bank(4 + 2 * d_ + i_ % 2, 256)
                      P.matmul(pb, kdec_toks[d_][:, n, :], gv[:, n, :], start=True, stop=True)
                      cur = Srots[d_][(i_ + 1) % 2]
                      P.stt(cur, prevs[d_], elast_t[:, d_, n:n + 1], pb, ALU.mult, ALU.add)
                      P.tt(Sbf[d_][:, orders[d_][i_ + 1], :], cur, bdmask, ALU.mult, eng="pool")
                      prevs[d_] = cur
              A.release(gla_mark)
              Qbd = [A.alloc([4, 128], BF16) for _ in range(4)]
              Asb = [A.alloc([4, 128], BF16) for _ in range(4)]
              ogf = [A.alloc([256], F32) for _ in range(2)]
              ogb = [A.alloc([256], BF16) for _ in range(3)]
              junk = A.alloc([64], F32)
              ss4s = [small[:, 528:532], small[:, 560:564]]
              rs4s = [small[:, 532:536], small[:, 564:568]]
              def gla_m(n):
                  po = bank(4 + n % 2, 256)
                  P.matmul(po, qt[0][:, n * 128:(n + 1) * 128], Sbf[0][:, n, :], start=True, stop=False)
                  P.matmul(po, qt[1][:, n * 128:(n + 1) * 128], Sbf[1][:, n, :], start=False, stop=False)
                  for d_ in range(2):
                      ci = 2 * n + d_
                      qb = Qbd[ci % 4]
                      P.tt(qb, qt[d_][:, n * 128:(n + 1) * 128].unsqueeze(1).broadcast_to([128, 4, 128]),
                           hmask4.unsqueeze(2).broadcast_to([128, 4, 128]), ALU.mult, eng="pool")
                      P.matmul(bank(ci % 4), kt_[d_][:, n * 128:(n + 1) * 128], qb.rearrange("p a b -> p (a b)"),
                               start=True, stop=True)
                  for d_ in range(2):
                      ci = 2 * n + d_
                      asb = Asb[ci % 4]
                      P.tt(asb, bank(ci % 4).rearrange("p (a b) -> p a b", a=4),
                           tri[d_].unsqueeze(1).broadcast_to([128, 4, 128]), ALU.mult)
                      for h in range(4):
                          P.matmul(po[:, h * 64:(h + 1) * 64], asb[:, h, :], gv[:, n, h * 64:(h + 1) * 64],
                                   start=False, stop=(d_ == 1 and h == 3))

              def gla_a(n):
                  po = bank(4 + n % 2, 256)
                  ss4 = ss4s[n % 2]
                  rs4 = rs4s[n % 2]
                  for h in range(4):
                      P.act(junk, po[:, h * 64:(h + 1) * 64], AF.Square, accum_out=ss4[:, h:h + 1])
                  P.act(rs4, ss4, AF.Ln, bias=epsc, scale=1.0 / 64.0)
                  P.act(rs4, rs4, AF.Exp, scale=-0.5)
                  of = ogf[n % 2]
                  of3 = of.rearrange("p (h v) -> p h v", h=4)
                  P.tt(of3, po.rearrange("p (h v) -> p h v", h=4), rs4.unsqueeze(2).broadcast_to([128, 4, 64]),
                       ALU.mult)
                  ob = ogb[n % 3]
                  P.tt(ob, of, gate[:, n, :], ALU.mult)
                  for c_ in range(2):
                      dst_ = ocat[:, 6 + c_, n * 128:(n + 1) * 128]
                      P.add("sp", lambda e, dst_=dst_, src_=ob[:, c_ * 128:(c_ + 1) * 128]: e.dma_start(
                          out=dst_, in_=src_, transpose=True),
                          reads=[ob[:, c_ * 128:(c_ + 1) * 128]], writes=[dst_], dma=True)

              for s_ in range(NT + 1):
                  if s_ < NT:
                      gla_m(s_)
                  if s_ >= 1:
                      gla_a(s_ - 1)
              A.release(mixer_mark)
              if l == 0:
                  dump("ogla", ocat[:, 6:8, :], [128, 2, S], BF16)
              ck('gla')

              A.n = ARENA - XTB
              assert A.top <= A.n
              post_mark = A.mark()
              ytmp = [A.alloc([D], F32) for _ in range(3)]
              xbs = [A.alloc([D], BF16) for _ in range(2)]
              for t in range(NT):
                  P.dma("sp", x_tok[:, t, :], xres_d[t * 128:(t + 1) * 128, :])
              def mm1_(t):
                  b0 = (t % 3) * 2
                  p2 = ps[:, b0 * 512:(b0 + 2) * 512]
                  for hf in range(2):
                      for e_ in range(8):
                          P.matmul(p2[:, hf * 512:(hf + 1) * 512], ocat[:, e_, t * 128:(t + 1) * 128],
                                   Wout[:, e_, hf * 512:(hf + 1) * 512], start=(e_ == 0), stop=(e_ == 7))

              def a1_(t, mid=None):
                  b0 = (t % 3) * 2
                  ln_a(ps[:, b0 * 512:(b0 + 2) * 512], x_tok[:, t, :], lnbuf, ytmp[t % 3], t % 2, mid=mid)

              def b1_(t):
                  ln_b(lnbuf, x_tok[:, t, :], ytmp[t % 3])

              def c1_(t):
                  make_xT(x_tok[:, t, :], t, xbs[t % 2])

              for s_ in range(NT + 3):
                  if s_ < NT:
                      mm1_(s_)
                  midf = (lambda t_=s_ - 3: b1_(t_)) if 0 <= s_ - 3 < NT else None
                  if 0 <= s_ - 1 < NT:
                      a1_(s_ - 1, mid=midf)
                  elif midf is not None:
                      midf()
                  if 0 <= s_ - 3 < NT:
                      c1_(s_ - 3)
              A.release(base_mark)
              if l == 0:
                  dump("x1", x_tok, [128, NT, D])
              ck('ln1')

              GF = NFC // 2
              Wdg = A.alloc([GF, D], BF16)
              hT = A.alloc([GF, S], BF16)
              NWB = 3
              wgu = [A.alloc([2, 8, 128], BF16) for _ in range(NWB)]
              sg = [A.alloc([512], BF16) for _ in range(2)]
              ytmp = [A.alloc([D], F32) for _ in range(1)]
              xbs = [A.alloc([D], BF16) for _ in range(2)]
              wdv = wd_d[l].rearrange("(f p) n -> p f n", p=128)
              wgv = wg_d[l].rearrange("(k p) n -> p k n", p=128)
              wuv = wu_d[l].rearrange("(k p) n -> p k n", p=128)
              P.dma("sp", lnbuf[:, 0, :], ln_d["ln2_g"][l].partition_broadcast(128))
              P.dma("sp", lnbuf[:, 1, :], ln_d["ln2_b"][l].partition_broadcast(128))
              wi = 0
              ui = 0
              for g_ in range(2):
                  for fi in range(GF):
                      f = g_ * GF + fi
                      wb = wgu[wi % NWB]
                      wi += 1
                      P.dma("pool", wb[:, 0, :, :], wgv[:, :, f * 128:(f + 1) * 128])
                      P.dma("pool", wb[:, 1, :, :], wuv[:, :, f * 128:(f + 1) * 128])
                      P.dma("pool", Wdg[:, fi, :], wdv[:, f, :])
                      for tb in range(4):
                          pg = bank((ui % 2) * 2)
                          pu = bank((ui % 2) * 2 + 1)
                          ui += 1
                          for k in range(8):
                              P.matmul(pg, wb[:, 0, k, :], xT[:, k, tb * 512:(tb + 1) * 512],
                                       start=(k == 0), stop=(k == 7))
                          for k in range(8):
                              P.matmul(pu, wb[:, 1, k, :], xT[:, k, tb * 512:(tb + 1) * 512],
                                       start=(k == 0), stop=(k == 7))
                          s_ = sg[ui % 2]
                          P.act(s_, pg, AF.Silu)
                          P.tt(hT[:, fi, tb * 512:(tb + 1) * 512], pu, s_, ALU.mult)
                  ytmps = [ytmp[0], wgu[0].rearrange("p a b c -> p (a b c)").bitcast(F32),
                           wgu[1].rearrange("p a b c -> p (a b c)").bitcast(F32)]

                  def mm2_(t):
                      b0 = (t % 3) * 2
                      p2 = ps[:, b0 * 512:(b0 + 2) * 512]
                      for hf in range(2):
                          for fi in range(GF):
                              P.matmul(p2[:, hf * 512:(hf + 1) * 512], hT[:, fi, t * 128:(t + 1) * 128],
                                       Wdg[:, fi, hf * 512:(hf + 1) * 512], start=(fi == 0), stop=(fi == GF - 1))

                  def a2_(t, mid=None):
                      b0 = (t % 3) * 2
                      p2 = ps[:, b0 * 512:(b0 + 2) * 512]
                      if g_ == 0:
                          P.stt(x_tok[:, t, :], x_tok[:, t, :], ALPHA, p2, ALU.mult, ALU.add)
                          if mid is not None:
                              mid()
                      else:
                          ln_a(p2, x_tok[:, t, :], lnbuf, ytmps[t % 3], t % 2, alpha=1.0, mid=mid)

                  def b2_(t):
                      if g_ == 0:
                          return
                      ln_b(lnbuf, x_tok[:, t, :], ytmps[t % 3])

                  def c2_(t):
                      if g_ == 0:
                          return
                      if l == n_layers - 1:
                          P.dma("sp", y_d[t * 128:(t + 1) * 128, :], x_tok[:, t, :])
                      else:
                          P.dma("sp", xs_d[t * 128:(t + 1) * 128, :], x_tok[:, t, :])
                          make_xT(x_tok[:, t, :], t, xbs[t % 2])

                  for s_ in range(NT + 3):
                      if s_ < NT:
                          mm2_(s_)
                      midf = (lambda t_=s_ - 3: b2_(t_)) if 0 <= s_ - 3 < NT else None
                      if 0 <= s_ - 1 < NT:
                          a2_(s_ - 1, mid=midf)
                      elif midf is not None:
                          midf()
                      if 0 <= s_ - 3 < NT:
                          c2_(s_ - 3)
              A.release(base_mark)
              A.n = ARENA

        except _Stop:
            pass
        P.emit()
    _CACHE['P'] = P
    return nc, dbg_outs


def kernel(**inputs):
    if "c" not in _CACHE:
        _CACHE["c"] = host_consts()
    cf, cb, dc, ds = _CACHE["c"]
    nc, _ = build()
    x = np.ascontiguousarray(inputs["x"], dtype=np.float32)
    common = {
        "w_in": np.ascontiguousarray(inputs["w_in"], dtype=np.float32),
        "diff_lambda": np.ascontiguousarray(inputs["diff_lambda"], dtype=np.float32).reshape(L, 256),
        "diff_norm_g": np.ascontiguousarray(inputs["diff_norm_g"], dtype=np.float32),
        "fourier_w": np.ascontiguousarray(inputs["fourier_w"], dtype=np.float32),
        "gla_gate_w2": np.ascontiguousarray(inputs["gla_gate_w2"], dtype=np.float32),
        "gla_gate_b2": np.ascontiguousarray(inputs["gla_gate_b2"], dtype=np.float32),
        "gla_norm_g": np.ascontiguousarray(inputs["gla_norm_g"], dtype=np.float32),
        "w_out": np.ascontiguousarray(inputs["w_out"], dtype=np.float32),
        "ln1_g": np.ascontiguousarray(inputs["ln1_g"], dtype=np.float32),
        "ln1_b": np.ascontiguousarray(inputs["ln1_b"], dtype=np.float32),
        "ln2_g": np.ascontiguousarray(inputs["ln2_g"], dtype=np.float32),
        "ln2_b": np.ascontiguousarray(inputs["ln2_b"], dtype=np.float32),
        "ffn_w_gate": np.ascontiguousarray(inputs["ffn_w_gate"], dtype=np.float32),
        "ffn_w_up": np.ascontiguousarray(inputs["ffn_w_up"], dtype=np.float32),
        "ffn_w_down": np.ascontiguousarray(inputs["ffn_w_down"], dtype=np.float32),
        "c_f32": cf, "c_bf": cb, "dft_c": dc, "dft_s": ds,
    }
    in_maps = [dict(common, x=x[b]) for b in range(8)]
    res = run_bass_kernel_spmd(nc, in_maps, core_ids=list(range(8)))
    return np.stack([np.asarray(r["y"], dtype=np.float32) for r in res.results], axis=0)
```

```python
import numpy as np
import concourse.bass as bass
import concourse.mybir as mybir

F32 = mybir.dt.float32
BF16 = mybir.dt.bfloat16
ALU = mybir.AluOpType
AF = mybir.ActivationFunctionType
AX = mybir.AxisListType

_DT_SIZE = {F32: 4, BF16: 2, mybir.dt.float32r: 4, mybir.dt.int32: 4,
            mybir.dt.uint32: 4, mybir.dt.float16: 2, mybir.dt.uint16: 2,
            mybir.dt.int16: 2, mybir.dt.uint8: 1, mybir.dt.int8: 1}

ENGS = ("pe", "act", "dve", "pool", "sp")
N_DMA_SEMS = 24
TINY_BYTES = 256
SAME_ENGINE_SYNC = False


def ap_box(ap):
    t = ap.tensor
    name = t.name
    esz = _DT_SIZE[ap.dtype]
    dims = list(ap.ap)
    off = ap.offset
    space = str(ap.space)
    if "DRAM" in space.upper() or "HBM" in space.upper():
        lo = off
        hi = off
        for (st, n) in dims:
            if st >= 0:
                hi += st * (n - 1)
            else:
                lo += st * (n - 1)
        return (name, 0, 1, lo * esz, (hi + 1) * esz)
    pstep, pcnt = dims[0]
    if pstep == 0:
        pstep = 1 << 40
    p0 = off // pstep if pstep < (1 << 40) else 0
    f = off - p0 * pstep if pstep < (1 << 40) else off
    lo = f
    hi = f
    for (st, n) in dims[1:]:
        if st >= 0:
            hi += st * (n - 1)
        else:
            lo += st * (n - 1)
    if "PSUM" in space.upper():
        b0 = (lo * esz) // 2048
        b1 = ((hi + 1) * esz - 1) // 2048
        return (name, 0, 128, b0 * 2048, (b1 + 1) * 2048, True)
    return (name, p0, p0 + pcnt, lo * esz, (hi + 1) * esz)


def _overlap(a, b):
    return a[1] < b[2] and b[1] < a[2] and a[3] < b[4] and b[3] < a[4]


def _contains(a, b):
    return a[1] <= b[1] and a[2] >= b[2] and a[3] <= b[3] and a[4] >= b[4]


class Prog:
    def __init__(self, nc):
        self.nc = nc
        self.ins = []
        self.recs = {}
        self.dma_rr = {e: 0 for e in ENGS}
        self.trace = {}

    def add(self, eng, fn, reads=(), writes=(), dma=False):
        idx = len(self.ins)
        rb = list(dict.fromkeys(ap_box(a) for a in reads))
        wb = list(dict.fromkeys(ap_box(a) for a in writes))
        deps = set()
        tiny_deps = set()
        for b in rb:
            psum = len(b) > 5
            tiny = (not psum) and (b[4] - b[3]) <= TINY_BYTES
            for rec in self.recs.get(b[0], ()):
                if (rec[1] or (psum and rec[3] != eng)) and _overlap(rec[0], b):
                    deps.add(rec[2])
                    if tiny and rec[1]:
                        tiny_deps.add(rec[2])
        for b in wb:
            for rec in self.recs.get(b[0], ()):
                if _overlap(rec[0], b):
                    deps.add(rec[2])
        for b in wb:
            lst = self.recs.setdefault(b[0], [])
            lst[:] = [r for r in lst if not _contains(b, r[0])]
            lst.append([b, True, idx, eng])
        for b in rb:
            lst = self.recs.setdefault(b[0], [])
            found = False
            for r in lst:
                if (not r[1]) and r[3] == eng and r[0] == b and not dma \
                        and not self.ins[r[2]]["dma"]:
                    r[2] = idx
                    found = True
                    break
            if not found:
                lst.append([b, False, idx, eng])
        real = set()
        for d in deps:
            p = self.ins[d]
            if p["eng"] == "pe" and eng == "pe" and not p["dma"] and not dma:
                continue
            if (not SAME_ENGINE_SYNC) and p["eng"] == eng and eng != "pool" and not p["dma"] and not dma \
                    and d not in tiny_deps:
                continue
            real.add(d)
        self.ins.append(dict(eng=eng, fn=fn, deps=real, dma=dma, needed=False,
                             sem=None, val=None))
        return idx

    def emit(self, final_wait_eng="sp"):
        nc = self.nc
        ins = self.ins
        for r in ins:
            for d in r["deps"]:
                ins[d]["needed"] = True
        last_dmas = [i for i, r in enumerate(ins) if r["dma"]]
        import contextlib
        with contextlib.ExitStack() as st:
            esem = {e: st.enter_context(nc.semaphore("s_" + e)) for e in ENGS}
            dsem = {e: [st.enter_context(nc.semaphore("d_%s_%d" % (e, i)))
                        for i in range(N_DMA_SEMS)] for e in ("sp", "act", "pool")}
            cnt = {e: 0 for e in ENGS}
            dcnt = {e: [0] * N_DMA_SEMS for e in dsem}
            drr = {e: 0 for e in dsem}
            prev_use = {}
            for i, r in enumerate(ins):
                e = r["eng"]
                if r["dma"]:
                    s = drr[e]
                    drr[e] = (s + 1) % N_DMA_SEMS
                    if dcnt[e][s] > 0:
                        r["prev"] = (dsem[e][s], dcnt[e][s], ("d", e, s))
                    else:
                        r["prev"] = None
                    dcnt[e][s] += 16
                    r["sem"] = dsem[e][s]
                    r["val"] = dcnt[e][s]
                    r["semkey"] = ("d", e, s)
                elif r["needed"]:
                    cnt[e] += 1
                    r["sem"] = esem[e]
                    r["val"] = cnt[e]
                    r["semkey"] = ("e", e)
            block = st.enter_context(nc.Block())
            per_eng = {e: [i for i, r in enumerate(ins) if r["eng"] == e] for e in ENGS}
            final = {}
            for i in last_dmas:
                r = ins[i]
                final[r["semkey"]] = (r["sem"], max(r["val"], final.get(r["semkey"], (None, 0))[1]))

            def body(ename):
                def run(engobj):
                    waited = {}
                    for i in per_eng[ename]:
                        r = ins[i]
                        need = {}
                        for d in r["deps"]:
                            p = ins[d]
                            k = p["semkey"]
                            if need.get(k, (None, 0))[1] < p["val"]:
                                need[k] = (p["sem"], p["val"])
                        if r["dma"] and r["prev"] is not None:
                            s, v, k = r["prev"]
                            if need.get(k, (None, 0))[1] < v:
                                need[k] = (s, v)
                        for k, (s, v) in need.items():
                            if waited.get(k, 0) < v:
                                engobj.wait_ge(s, v)
                                waited[k] = v
                                self.trace.setdefault(ename, []).append(("w", k, v, i))
                        h = r["fn"](engobj)
                        if r["dma"]:
                            h.then_inc(r["sem"], 16)
                            self.trace.setdefault(ename, []).append(("i", r["semkey"], 16, i))
                        elif r["needed"]:
                            h.then_inc(r["sem"], 1)
                            self.trace.setdefault(ename, []).append(("i", r["semkey"], 1, i))
                    if ename == final_wait_eng:
                        for k, (s, v) in final.items():
                            if waited.get(k, 0) < v:
                                engobj.wait_ge(s, v)
                return run

            block.tensor(body("pe"))
            block.scalar(body("act"))
            block.vector(body("dve"))
            block.gpsimd(body("pool"))
            block.sync(body("sp"))

    def dma(self, eng, out, in_, **kw):
        return self.add(eng, lambda e: e.dma_start(out=out, in_=in_, **kw),
                        reads=[in_], writes=[out], dma=True)

    def matmul(self, out, lhsT, rhs, start=True, stop=True, **kw):
        return self.add("pe", lambda e: e.matmul(out, lhsT, rhs, start=start, stop=stop, **kw),
                        reads=[lhsT, rhs], writes=[out])

    def transpose(self, out, in_, ident):
        return self.add("pe", lambda e: e.transpose(out, in_, ident),
                        reads=[in_, ident], writes=[out])

    def act(self, out, in_, func, bias=None, scale=1.0, accum_out=None, eng="act"):
        reads = [in_]
        writes = [out]
        kw = {}
        if bias is not None:
            kw["bias"] = bias
            if not isinstance(bias, (int, float)):
                reads.append(bias)
        if not isinstance(scale, (int, float)):
            reads.append(scale)
        if accum_out is not None:
            kw["accum_out"] = accum_out
            writes.append(accum_out)
        return self.add(eng, lambda e: e.activation(out=out, in_=in_, func=func, scale=scale, **kw),
                        reads=reads, writes=writes)

    def tt(self, out, in0, in1, op, eng="dve"):
        return self.add(eng, lambda e: e.tensor_tensor(out=out, in0=in0, in1=in1, op=op),
                        reads=[in0, in1], writes=[out])

    def ts(self, out, in0, s1, op0, s2=None, op1=None, eng="dve", accum_out=None):
        reads = [in0]
        if not isinstance(s1, (int, float)):
            reads.append(s1)
        if s2 is not None and not isinstance(s2, (int, float)):
            reads.append(s2)
        kw = {}
        writes = [out]
        if op1 is not None:
            kw["op1"] = op1
        if accum_out is not None:
            kw["accum_out"] = accum_out
            writes.append(accum_out)
        return self.add(eng, lambda e: e.tensor_scalar(out=out, in0=in0, scalar1=s1, scalar2=s2,
                                                       op0=op0, **kw),
                        reads=reads, writes=writes)

    def stt(self, out, in0, scalar, in1, op0, op1, eng="dve"):
        reads = [in0, in1]
        if not isinstance(scalar, (int, float)):
            reads.append(scalar)
        return self.add(eng, lambda e: e.scalar_tensor_tensor(out=out, in0=in0, scalar=scalar,
                                                              in1=in1, op0=op0, op1=op1),
                        reads=reads, writes=[out])

    def copy(self, out, in_, eng="dve"):
        if eng == "act":
            return self.add(eng, lambda e: e.activation(out=out, in_=in_, func=AF.Identity),
                            reads=[in_], writes=[out])
        return self.add(eng, lambda e: e.tensor_copy(out=out, in_=in_), reads=[in_], writes=[out])

    def memset(self, ap, val, eng="dve"):
        return self.add(eng, lambda e: e.memset(ap, val), reads=[], writes=[ap])

    def reduce(self, out, in_, op, axis=AX.X, eng="dve", **kw):
        return self.add(eng, lambda e: e.tensor_reduce(out=out, in_=in_, op=op, axis=axis, **kw),
                        reads=[in_], writes=[out])

    def recip(self, out, in_, eng="dve"):
        return self.add(eng, lambda e: e.reciprocal(out=out, in_=in_), reads=[in_], writes=[out])


def simulate_trace(trace):
    pos = {e: 0 for e in trace}
    sem = {}
    progress = True
    while progress:
        progress = False
        for e, ops in trace.items():
            while pos[e] < len(ops):
                kind, k, v, i = ops[pos[e]]
                if kind == "w":
                    if sem.get(k, 0) >= v:
                        pos[e] += 1
                        progress = True
                    else:
                        break
                else:
                    sem[k] = sem.get(k, 0) + v
                    pos[e] += 1
                    progress = True
    stuck = {e: ops[pos[e]] for e, ops in trace.items() if pos[e] < len(ops)}
    return stuck, sem

import contextlib
import os
_SK = set(os.environ.get('DBGSKIP', '').split(','))
import math
import ml_dtypes
from concourse.bass_utils import run_bass_kernel_spmd

S = 2048
D = 1024
L = 2
INW = 2592
FF = 2816
NT = 16
NFC = FF // 128
ALPHA = float((2 * L) ** 0.25)
EPS = 1e-5
C_DQ, C_DK, C_DV, C_FU, C_GQ, C_GK, C_GV, C_GR, C_GZ = 0, 512, 1024, 1536, 1792, 1920, 2048, 2304, 2560

CF_ID, CF_COS, CF_SIN, CF_HM, CF_CC, CF_SC, CF_ONES, CF_NEGH, CF_N = 0, 128, 640, 1152, 1156, 1284, 1412, 1540, 1548
CB_ID, CB_TRIF, CB_TRIB, CB_BD, CB_ONE, CB_ALT, CB_N = 0, 128, 256, 384, 640, 768, 800


def host_consts():
    p = np.arange(128)
    cf = np.zeros((128, CF_N), np.float32)
    cf[:, CF_ID:CF_ID + 128] = np.eye(128, dtype=np.float32)
    inv_freq = (10000.0 ** (-np.arange(0, 64, 2, dtype=np.float32) / 64)).astype(np.float32)
    pos = (np.arange(NT)[None, :] * 128 + p[:, None]).astype(np.float32)
    ang = pos[:, :, None] * inv_freq[None, None, :]
    cf[:, CF_COS:CF_COS + 512] = np.cos(ang).reshape(128, 512)
    cf[:, CF_SIN:CF_SIN + 512] = np.sin(ang).reshape(128, 512)
    cf[:, CF_HM:CF_HM + 4] = (p[:, None] // 32 == np.arange(4)[None, :]).astype(np.float32)
    c = np.arange(64)
    a = 2 * np.pi * np.outer(c, c) / 64.0
    cc = np.cos(a) / 8.0
    sc = np.sin(a) / 8.0
    z = np.zeros((64, 64))
    cf[:, CF_CC:CF_CC + 128] = np.block([[cc, z], [z, cc]])
    cf[:, CF_SC:CF_SC + 128] = np.block([[sc, z], [z, sc]])
    cf[:, CF_ONES:CF_ONES + 128] = 1.0
    cf[:, CF_NEGH:CF_NEGH + 8] = -0.5
    cb = np.zeros((128, CB_N), np.float32)
    cb[:, CB_ID:CB_ID + 128] = np.eye(128)
    cb[:, CB_TRIF:CB_TRIF + 128] = (p[:, None] <= p[None, :])
    cb[:, CB_TRIB:CB_TRIB + 128] = (p[:, None] >= p[None, :])
    cb[:, CB_BD:CB_BD + 256] = (p[:, None] // 32 == np.arange(256)[None, :] // 64)
    cb[:, CB_ONE:CB_ONE + 128] = 1.0
    cb[:, CB_ALT] = ((-1.0) ** p) / math.sqrt(S)
    s = np.arange(S, dtype=np.float64)
    sk = np.outer(s, s) % S
    ang = 2 * np.pi * sk / S
    dc = (np.cos(ang) / math.sqrt(S)).astype(np.float32).astype(ml_dtypes.bfloat16)
    ds = (-np.sin(ang) / math.sqrt(S)).astype(np.float32).astype(ml_dtypes.bfloat16)
    return cf, cb.astype(ml_dtypes.bfloat16), dc, ds


class Arena:
    def __init__(self, t, nbytes):
        self.t = t
        self.n = nbytes
        self.top = 0

    def mark(self):
        return self.top

    def release(self, m):
        self.top = m

    def alloc(self, shape, dt):
        n = 1
        for s_ in shape:
            n *= s_
        b = n * _DT_SIZE[dt]
        off = self.top
        self.top += (b + 63) // 64 * 64
        assert self.top <= self.n, ("arena overflow", self.top, self.n)
        ap = self.t[:, off // 2:(off + b) // 2]
        if dt != BF16:
            ap = ap.bitcast(dt)
        if len(shape) == 2:
            ap = ap.rearrange("p (a b) -> p a b", a=shape[0])
        elif len(shape) == 3:
            ap = ap.rearrange("p (a b c) -> p a b c", a=shape[0], b=shape[1])
        return ap


class Rot:
    def __init__(self, items):
        self.items = items
        self.i = 0

    def next(self):
        r = self.items[self.i % len(self.items)]
        self.i += 1
        return r


_CACHE = {}


class _Stop(Exception):
    pass


def build(n_layers=L, dbg=None, upto=None):
    nc = bass.Bass("TRN2", target_bir_lowering=False)
    dram = lambda name, shape, dt, kind="ExternalInput": nc.dram_tensor(name, shape, dt, kind=kind).ap()
    x_d = dram("x", [S, D], F32)
    w_in_d = dram("w_in", [L, D, INW], F32)
    lam_d = dram("diff_lambda", [L, 256], F32)
    dng_d = dram("diff_norm_g", [L, 128], F32)
    fw_d = dram("fourier_w", [L, 4, 64, 64], F32)
    w2_d = dram("gla_gate_w2", [L, 2, 16, 128], F32)
    b2_d = dram("gla_gate_b2", [L, 2, 128], F32)
    gng_d = dram("gla_norm_g", [L, 64], F32)
    wout_d = dram("w_out", [L, D, D], F32)
    ln_d = {k: dram(k, [L, D], F32) for k in ("ln1_g", "ln1_b", "ln2_g", "ln2_b")}
    wg_d = dram("ffn_w_gate", [L, D, FF], F32)
    wu_d = dram("ffn_w_up", [L, D, FF], F32)
    wd_d = dram("ffn_w_down", [L, FF, D], F32)
    cf_d = dram("c_f32", [128, CF_N], F32)
    cb_d = dram("c_bf", [128, CB_N], BF16)
    dftc_d = dram("dft_c", [S, S], BF16)
    dfts_d = dram("dft_s", [S, S], BF16)
    y_d = dram("y", [S, D], F32, kind="ExternalOutput")
    xs_d = dram("xs_scr", [S, D], F32, kind="Internal")
    dbg_outs = {}

    ARENA = 207 * 1024
    XTB = NT * D * 4
    with contextlib.ExitStack() as st:
        arena_t = st.enter_context(nc.sbuf_tensor("arena", [128, ARENA // 2], BF16))
        ps = st.enter_context(nc.psum_tensor("ps", [128, 4096], F32))
        P = Prog(nc)
        A = Arena(arena_t, ARENA)
        x_tok = arena_t[:, (ARENA - XTB) // 2:ARENA // 2].bitcast(F32).rearrange("p (t d) -> p t d", t=NT)

        def bank(b, n=512, off=0):
            return ps[:, b * 512 + off:b * 512 + off + n]

        def bankbf(b):
            return ps[:, b * 512:(b + 1) * 512].bitcast(BF16)

        def dump(name, ap, shape, dt=F32):
            if dbg is None or name not in dbg:
                return
            d_ = nc.dram_tensor("dbg_" + name, shape, dt, kind="ExternalOutput").ap()
            dbg_outs[name] = d_
            P.dma("sp", d_, ap)

        cf = A.alloc([CF_N], F32)
        cb = A.alloc([CB_N], BF16)
        xT = A.alloc([8, S], BF16)
        lnbuf = A.alloc([2, D], F32)
        small = A.alloc([640], F32)
        P.dma("sp", cf, cf_d)
        P.dma("sp", cb, cb_d)
        ident_f = cf[:, CF_ID:CF_ID + 128]
        ident_b = cb[:, CB_ID:CB_ID + 128]
        cos_t = cf[:, CF_COS:CF_COS + 512].rearrange("p (t i) -> p t i", t=NT)
        sin_t = cf[:, CF_SIN:CF_SIN + 512].rearrange("p (t i) -> p t i", t=NT)
        hmask4 = cf[:, CF_HM:CF_HM + 4]
        ccbd = cf[:, CF_CC:CF_CC + 128]
        scbd = cf[:, CF_SC:CF_SC + 128]
        ones_f = cf[:, CF_ONES:CF_ONES + 128]
        negh8 = cf[:, CF_NEGH:CF_NEGH + 8]
        tri = [cb[:, CB_TRIF:CB_TRIF + 128], cb[:, CB_TRIB:CB_TRIB + 128]]
        bdmask = cb[:, CB_BD:CB_BD + 256]
        lp_bc = small[:, 0:256]
        gdiff = small[:, 256:384]
        ggla = small[:, 384:448]
        negb2 = small[:, 448:450]
        neglam = small[:, 450:451]
        negM = small[:, 451:452]
        sc_tmp = small[:, 452:500]
        nrm = small[:, 500:504]
        epsc = small[:, 504:505]
        base_mark = A.mark()

        win = lambda l: w_in_d[l].rearrange("(k p) n -> p k n", p=128)

        def rstd_from(out, ss, n, width):
            tmp = sc_tmp[:, 40:40 + width]
            P.ts(tmp, ss, 1.0 / n, ALU.mult, EPS, ALU.add)
            P.tt(out, tmp, negh8[:, 0:width], ALU.pow, eng="pool")

        xb_rot = None

        def make_xT(x_tile_f32, t, xb, evac_eng="act", pbank=7):
            P.copy(xb, x_tile_f32, eng="act")
            pst = bankbf(pbank)
            for k in range(8):
                P.transpose(pst[:, k * 128:(k + 1) * 128], xb[:, k * 128:(k + 1) * 128], ident_b)
            P.copy(xT[:, :, t * 128:(t + 1) * 128], pst.rearrange("p (k n) -> p k n", k=8), eng=evac_eng)

        def ln_a(psum2, xres, gb, ytmp, par, alpha=ALPHA, mid=None):
            sct = sc_tmp[:, 0:16] if par == 0 else small[:, 540:556]
            P.stt(ytmp, xres, alpha, psum2, ALU.mult, ALU.add)
            stats = sct[:, 0:12]
            mv = sct[:, 12:14]
            for c_ in range(2):
                P.add("dve", lambda e, c_=c_: e.bn_stats(out=stats[:, c_ * 6:(c_ + 1) * 6],
                                                          in_=ytmp[:, c_ * 512:(c_ + 1) * 512]),
                      reads=[ytmp[:, c_ * 512:(c_ + 1) * 512]], writes=[stats[:, c_ * 6:(c_ + 1) * 6]])
            P.add("dve", lambda e: e.bn_aggr(out=mv, in_=stats), reads=[stats], writes=[mv])
            rs = sct[:, 14:15]
            nmr = sct[:, 15:16]
            P.act(rs, mv[:, 1:2], AF.Ln, bias=epsc)
            P.act(rs, rs, AF.Exp, scale=-0.5)
            if mid is not None:
                mid()
            P.stt(nmr, mv[:, 0:1], -1.0, rs, ALU.mult, ALU.mult)
            P.act(ytmp, ytmp, AF.Identity, bias=nmr, scale=rs)
            P.tt(ytmp, ytmp, gb[:, 0, :], ALU.mult, eng="pool")

        def ln_b(gb, out_tok, ytmp):
            P.tt(out_tok, ytmp, gb[:, 1, :], ALU.add)

        def ck(name):
            if upto == name:
                raise _Stop()

        m_ = A.mark()
        NXB = 8
        xin = [A.alloc([D], F32) for _ in range(NXB)]
        xbs = [A.alloc([D], BF16) for _ in range(2)]
        for t in range(NXB):
            P.dma("sp", xin[t], x_d[t * 128:(t + 1) * 128, :])
        for t in range(NT):
            make_xT(xin[t % NXB], t, xbs[t % 2], evac_eng="dve", pbank=(7 if t % 2 else 6))
            if t + NXB < NT:
                P.dma("sp", xin[t % NXB], x_d[(t + NXB) * 128:(t + NXB + 1) * 128, :])
        A.release(m_)
        try:
          for l in range(n_layers):
              lam_init = 0.8 - 0.6 * math.exp(-0.3 * l)
              A.release(base_mark)
              P.dma("sp", lp_bc, lam_d[l].partition_broadcast(128))
              P.dma("sp", gdiff, dng_d[l].partition_broadcast(128))
              P.dma("sp", ggla, gng_d[l].partition_broadcast(128))
              for d_ in range(2):
                  P.dma("sp", negb2[:, d_:d_ + 1], b2_d[l, d_].rearrange("(p o) -> p o", o=1))
              P.ts(negb2, negb2, -1.0, ALU.mult)
              P.ts(gdiff, gdiff, 1.0 - lam_init, ALU.mult)
              P.memset(epsc, EPS)
              pr = sc_tmp[:, 16:18]
              prod = A.alloc([128], F32)
              lp4 = lp_bc.rearrange("p (a d) -> p a d", a=4)
              for i_ in range(2):
                  P.tt(prod[:, 0:64], lp4[:, 2 * i_, :], lp4[:, 2 * i_ + 1, :], ALU.mult)
                  P.reduce(pr[:, i_:i_ + 1], prod[:, 0:64], ALU.add)
              P.act(pr, pr, AF.Exp)
              P.tt(neglam, pr[:, 1:2], pr[:, 0:1], ALU.subtract)
              P.ts(neglam, neglam, -lam_init, ALU.add)
              A.release(base_mark)
              ck('params')

              ck('xT')
              xres_d = x_d if l == 0 else xs_d

              ocat = A.alloc([8, S], BF16)
              Wout = A.alloc([8, D], BF16)
              Wgf = A.alloc([8, 288], BF16)
              Wgt = A.alloc([8, 512], BF16)
              wbd = A.alloc([2, 128], F32)
              Wcs = A.alloc([2, 256], BF16)
              mixer_mark = A.mark()
              NDB = 4
              DBB = 8 * 512 * 2
              dbuf = [arena_t[:, (ARENA - (i_ + 1) * DBB) // 2:(ARENA - i_ * DBB) // 2].rearrange(
                  "p (a b) -> p a b", a=8) for i_ in range(NDB)]
              dft_v = [dftc_d.rearrange("(t p) k -> p t k", p=128), dfts_d.rearrange("(t p) k -> p t k", p=128)]
              dft_chunks = [(kb, which, tg) for kb in range(2) for which in range(2) for tg in range(2)]

              def dft_load(j):
                  kb, which, tg = dft_chunks[j]
                  P.dma("sp", dbuf[j % NDB], dft_v[which][:, tg * 8:(tg + 1) * 8, kb * 512:(kb + 1) * 512])

              Wh = [A.alloc([8, 384], BF16) for _ in range(2)]
              QTzs = [A.alloc([2, S], BF16) for _ in range(2)]
              KTs = [A.alloc([S], BF16) for _ in range(2)]
              Vaugs = [A.alloc([NT, 132], BF16) for _ in range(2)]
              negMs = [negM, small[:, 570:571]]
              Wf = Wh[0][:, :, 0:256]
              PT = [A.alloc([1024], BF16) for _ in range(3)]
              O1n = A.alloc([8, 128], F32)
              O2t = A.alloc([8, 128], F32)
              otoks = [A.alloc([8, 128], BF16) for _ in range(2)]
              qkr = [A.alloc([256], BF16) for _ in range(3)]
              tmpAs = [A.alloc([128], F32) for _ in range(2)]
              tmpBs = [A.alloc([128], F32) for _ in range(2)]
              sq = A.alloc([256], F32)
              D2 = A.alloc([256], F32)
              sqs = [sq, A.alloc([256], F32)]
              red_all = A.alloc([NT, 4], F32)
              red4 = sc_tmp[:, 20:24]
              gm = sc_tmp[:, 24:26]
              nrm2 = sc_tmp[:, 26:28]
              rz = sc_tmp[:, 28:36]
              ss8 = small[:, 512:520]
              rstd8 = small[:, 520:528]
              for i_ in range(2):
                  P.memset(Vaugs[i_][:, :, 128:129], 1.0)
                  P.memset(QTzs[i_][64:128, 0, :], 0.0)
                  P.memset(QTzs[i_][0:64, 1, :], 0.0, eng="pool")
              assert A.top <= ARENA - 3 * DBB, A.top
              pso = [ps[:, (4 + qi // 3) * 512 + (qi % 3) * 160:(4 + qi // 3) * 512 + (qi % 3) * 160 + 129]
                     for qi in range(8)]
              grp = [(0, 3), (3, 3), (6, 2)]

              def pso_grp(gi, c0, c1):
                  q0, n_ = grp[gi]
                  base_ = (4 + gi) * 512
                  return ps[:, base_:base_ + n_ * 160].rearrange("p (a b) -> p a b", b=160)[:, :, c0:c1]

              def load_wh(h_):
                  for j_, c0 in enumerate((C_DQ, C_DK, C_DV)):
                      P.dma("pool", Wh[h_ % 2][:, :, j_ * 128:(j_ + 1) * 128],
                            win(l)[:, :, c0 + h_ * 128:c0 + (h_ + 1) * 128])

              load_wh(0)
              load_wh(1)
              fin_pending = []

              def att_inproj(h):
                  wh = Wh[h % 2]
                  QTz, KT, Vaug, negM_h = QTzs[h % 2], KTs[h % 2], Vaugs[h % 2], negMs[h % 2]
                  P.memset(nrm, 0.0)
                  tr_pending = []
                  for t in range(NT):
                      pb = bank(t % 4, 256)
                      pv_ = bank(4 + t % 2, 128)
                      for k in range(8):
                          P.matmul(pb, xT[:, k, t * 128:(t + 1) * 128], wh[:, k, 0:256], start=(k == 0), stop=(k == 7))
                      for k in range(8):
                          P.matmul(pv_, xT[:, k, t * 128:(t + 1) * 128], wh[:, k, 256:384], start=(k == 0), stop=(k == 7))
                      qk4 = pb.rearrange("p (g h d) -> p g h d", g=4, h=2)
                      t1 = qk4[:, :, 0, :]
                      t2 = qk4[:, :, 1, :]
                      cbt = cos_t[:, t:t + 1, :].broadcast_to([128, 4, 32])
                      sbt = sin_t[:, t:t + 1, :].broadcast_to([128, 4, 32])
                      q_ = qkr[t % 3]
                      q4 = q_.rearrange("p (g h d) -> p g h d", g=4, h=2)
                      ta = tmpAs[t % 2].rearrange("p (g d) -> p g d", g=4)
                      tb_ = tmpBs[t % 2].rearrange("p (g d) -> p g d", g=4)
                      P.tt(ta, t1, cbt, ALU.mult)
                      P.tt(tb_, t2, sbt, ALU.mult)
                      P.tt(q4[:, :, 0, :], ta, tb_, ALU.subtract)
                      P.tt(ta, t2, cbt, ALU.mult)
                      P.tt(tb_, t1, sbt, ALU.mult)
                      P.tt(q4[:, :, 1, :], ta, tb_, ALU.add)
                      P.copy(Vaug[:, t, 0:128], pv_, eng="act")
                      P.tt(sqs[t % 2], q_, q_, ALU.mult, eng="pool")
                      def tr_(t=t, q_=q_, sq_=sqs[t % 2]):
                          P.reduce(red_all[:, t, :], sq_.rearrange("p (g d) -> p g d", g=4), ALU.add)
                          pst = bankbf(7 if t % 2 else 6)
                          P.transpose(pst[:, 0:128], q_[:, 0:128], ident_b)
                          P.transpose(pst[:, 128:256], q_[:, 128:256], ident_b)
                          P.copy(QTz[0:64, 0, t * 128:(t + 1) * 128], pst[0:64, 0:128], eng="act")
                          P.copy(QTz[64:128, 1, t * 128:(t + 1) * 128], pst[64:128, 0:128], eng="act")
                          P.copy(KT[:, t * 128:(t + 1) * 128], pst[:, 128:256], eng="act")
                      tr_pending.append(tr_)
                      if len(tr_pending) > 1:
                          tr_pending.pop(0)()
                  while tr_pending:
                      tr_pending.pop(0)()
                  P.reduce(nrm, red_all.rearrange("p t g -> p g t"), ALU.max)
                  ck('h0proj')
                  if h + 2 < 4:
                      load_wh(h + 2)
                  if h == 3:
                      P.dma("pool", Wf, win(l)[:, :, C_FU:C_FU + 256])
                      P.dma("pool", Wgf[:, :, 0:256], win(l)[:, :, C_GQ:C_GQ + 256])
                      P.dma("pool", Wgf[:, :, 256:288], win(l)[:, :, C_GZ:C_GZ + 32])
                      P.dma("pool", Wgt, win(l)[:, :, C_GV:C_GV + 512])
                      for k in range(8):
                          P.dma("pool", Wout[:, k, :], wout_d[l, k * 128:(k + 1) * 128, :])
                      P.memset(wbd, 0.0, eng="pool")
                      for g_ in range(4):
                          c_, gl = g_ // 2, g_ % 2
                          P.dma("sp", wbd[gl * 64:(gl + 1) * 64, c_, gl * 64:(gl + 1) * 64], fw_d[l, g_])
                      for j_ in range(NDB - 1):
                          dft_load(j_)
                  P.reduce(nrm2, nrm.rearrange("p (a b) -> p a b", a=2), ALU.max)
                  P.ts(D2[:, 0:128], ident_f, nrm2[:, 0:1], ALU.mult)
                  P.ts(D2[:, 128:256], ident_f, nrm2[:, 1:2], ALU.mult)
                  pbm = bank(6, 256)
                  P.matmul(pbm, ones_f, D2, start=True, stop=True)
                  P.reduce(gm, pbm.rearrange("p (a b) -> p a b", a=2), ALU.max)
                  P.tt(negM_h, gm[:, 0:1], gm[:, 1:2], ALU.add)
                  P.ts(negM_h, negM_h, -0.5 * 0.125, ALU.mult)
                  if l == 0 and h == 0:
                      dump("xT", xT, [128, 8, S], BF16)
                      dump("negM", negM_h, [128, 1])
                      ck('qkt')

              def att_steps(h):
                  QTz, KT, Vaug, negM_h = QTzs[h % 2], KTs[h % 2], Vaugs[h % 2], negMs[h % 2]
                  for qh in range(2):
                      otok = otoks[qh]
                      steps = [(m, kt) for m in range(2) for kt in range(NT)]

                      def emit_scores(i):
                          m, kt = steps[i]
                          sb_i = (i % 2) * 2
                          for j_ in range(2):
                              P.matmul(ps[:, (sb_i + j_) * 512:(sb_i + j_ + 1) * 512],
                                       KT[:, kt * 128:(kt + 1) * 128],
                                       QTz[:, m, qh * 1024 + j_ * 512:qh * 1024 + (j_ + 1) * 512],
                                       start=True, stop=True)

                      emit_scores(0)
                      for i, (m, kt) in enumerate(steps):
                          if i + 1 < len(steps):
                              emit_scores(i + 1)
                          if i == 4 and fin_pending:
                              fin_pending.pop(0)()
                          sb_i = (i % 2) * 2
                          pss = ps[:, sb_i * 512:(sb_i + 2) * 512]
                          pt = PT[i % 3]
                          P.act(pt, pss, AF.Exp, bias=negM_h, scale=0.125)
                          for qi in range(8):
                              P.matmul(pso[qi], pt[:, qi * 128:(qi + 1) * 128], Vaug[:, kt, 0:129],
                                       start=(kt == 0 and qi % 3 == 0), stop=(kt == NT - 1), skip_group_check=True)
                          if kt != NT - 1:
                              continue
                          if m == 0:
                              for gi, (q0, n_) in enumerate(grp):
                                  P.recip(rz[:, q0:q0 + n_], pso_grp(gi, 128, 129).rearrange("p a b -> p (a b)"))
                                  P.tt(O1n[:, q0:q0 + n_, :], pso_grp(gi, 0, 128),
                                       rz[:, q0:q0 + n_].unsqueeze(2).broadcast_to([128, n_, 128]), ALU.mult)
                          else:
                              for gi, (q0, n_) in enumerate(grp):
                                  P.recip(rz[:, q0:q0 + n_], pso_grp(gi, 128, 129).rearrange("p a b -> p (a b)"))
                              P.ts(rz, rz, neglam, ALU.mult)
                              for gi, (q0, n_) in enumerate(grp):
                                  P.tt(O2t[:, q0:q0 + n_, :], pso_grp(gi, 0, 128),
                                       rz[:, q0:q0 + n_].unsqueeze(2).broadcast_to([128, n_, 128]), ALU.mult)
                              P.tt(O1n, O1n, O2t, ALU.add)

                              def chain_(h=h, qh=qh, otok=otok):
                                  P.tt(O2t, O1n, O1n, ALU.mult, eng="pool")
                                  P.reduce(ss8, O2t, ALU.add)
                                  rstd_from(rstd8, ss8, 128.0, 8)
                                  P.tt(O1n, O1n, rstd8.unsqueeze(2).broadcast_to([128, 8, 128]), ALU.mult)
                                  P.tt(otok, O1n, gdiff.unsqueeze(1).broadcast_to([128, 8, 128]), ALU.mult,
                                       eng="pool")
                                  for qi in range(8):
                                      dst_ = ocat[:, h, qh * 1024 + qi * 128:qh * 1024 + (qi + 1) * 128]
                                      P.add("sp", lambda e, dst_=dst_, src_=otok[:, qi, :]: e.dma_start(
                                          out=dst_, in_=src_, transpose=True),
                                          reads=[otok[:, qi, :]], writes=[dst_], dma=True)
                              fin_pending.append(chain_)
              att_inproj(0)
              att_inproj(1)
              for h in range(4):
                  att_steps(h)
                  if h + 2 < 4:
                      att_inproj(h + 2)
              while fin_pending:
                  fin_pending.pop(0)()
              A.release(mixer_mark)
              if l == 0:
                  dump("odiff", ocat[:, 0:4, :], [128, 4, S], BF16)
              ck('att')

              _skip = A.alloc([2 * 8 * 384], BF16)
              uT = A.alloc([2, S], BF16)
              ucs = A.alloc([NT, 512], BF16)
              P.dma("sp", lnbuf[:, 0, :], ln_d["ln1_g"][l].partition_broadcast(128))
              P.dma("sp", lnbuf[:, 1, :], ln_d["ln1_b"][l].partition_broadcast(128))
              dft_load(NDB - 1)
              for c_ in range(2):
                  pb = bank(4 + c_, 256)
                  P.matmul(pb[:, 0:128], ccbd, wbd[:, c_, :], start=True, stop=True)
                  P.matmul(pb[:, 128:256], scbd, wbd[:, c_, :], start=True, stop=True)
                  P.copy(Wcs[:, c_, :], pb, eng="act")
              for tb in range(4):
                  for c_ in range(2):
                      pb = bank((tb * 2 + c_) % 4)
                      for k in range(8):
                          P.matmul(pb, Wf[:, k, c_ * 128:(c_ + 1) * 128], xT[:, k, tb * 512:(tb + 1) * 512],
                                   start=(k == 0), stop=(k == 7))
                      P.copy(uT[:, c_, tb * 512:(tb + 1) * 512], pb, eng="act")
              for t in range(NT):
                  pb = bank(4 + t % 2)
                  for c_ in range(2):
                      P.matmul(pb[:, c_ * 256:(c_ + 1) * 256], uT[:, c_, t * 128:(t + 1) * 128], Wcs[:, c_, :],
                               start=True, stop=True)
                  P.copy(ucs[:, t, :], pb, eng="act")
                  if t == 7:
                      ucs_half = True
              ftmp = [A.alloc([512], F32) for _ in range(2)]
              altcol = cb[:, CB_ALT:CB_ALT + 1]
              for j_, (kb, which, tg) in enumerate(dft_chunks):
                  pbs = [bank(kb * 4 + which * 2 + c_) for c_ in range(2)]
                  db = dbuf[j_ % NDB]
                  for tt_ in range(8):
                      t = tg * 8 + tt_
                      for c_ in range(2):
                          P.matmul(pbs[c_], ucs[:, t, c_ * 256 + which * 128:c_ * 256 + (which + 1) * 128],
                                   db[:, tt_, :], start=(t == 0), stop=(t == NT - 1))
                  if j_ + NDB < len(dft_chunks):
                      dft_load(j_ + NDB)
                  if which == 1 and tg == 1:
                      for c_ in range(2):
                          yc = bank(kb * 4 + c_)
                          ys = bank(kb * 4 + 2 + c_)
                          tmp = ftmp[c_]
                          P.copy(tmp, ys, eng="act")
                          P.tt(ocat[:, 4 + c_, kb * 512:(kb + 1) * 512], yc, tmp, ALU.add)
                          if kb == 0:
                              P.tt(ocat[:, 4 + c_, S - 511:S][:, ::-1], yc[:, 1:512], tmp[:, 1:512], ALU.subtract)
                          else:
                              P.tt(ocat[:, 4 + c_, S // 2 + 1:S - 511][:, ::-1], yc, tmp, ALU.subtract)
              for c_ in range(2):
                  pn = bank(c_, 2)
                  for t in range(NT):
                      P.matmul(pn[:, 0:1], ucs[:, t, c_ * 256:c_ * 256 + 128], altcol, start=(t == 0), stop=(t == NT - 1))
                  P.copy(ocat[:, 4 + c_, S // 2:S // 2 + 1], pn[:, 0:1], eng="act")
              A.release(mixer_mark)
              if l == 0:
                  dump("ofour", ocat[:, 4:6, :], [128, 2, S], BF16)
              ck('four')

              gqk = A.alloc([2, S], F32)
              gzT = A.alloc([S], BF16)
              gv = A.alloc([NT, 256], BF16)
              gate = A.alloc([NT, 256], BF16)
              w2pad = A.alloc([2, 128], BF16)
              qt = [A.alloc([S], BF16) for _ in range(2)]
              kt_ = [A.alloc([S], BF16) for _ in range(2)]
              Sbf = [A.alloc([NT, 256], BF16) for _ in range(2)]
              gla_mark = A.mark()
              P.memset(w2pad, 0.0)
              for d_ in range(2):
                  P.dma("pool", w2pad[d_ * 16:(d_ + 1) * 16, d_, :], w2_d[l, d_])
              for tb in range(4):
                  for c_ in range(2):
                      pb = bank((tb * 3 + c_) % 4)
                      for k in range(8):
                          P.matmul(pb, Wgf[:, k, c_ * 128:(c_ + 1) * 128], xT[:, k, tb * 512:(tb + 1) * 512],
                                   start=(k == 0), stop=(k == 7))
                      P.copy(gqk[:, c_, tb * 512:(tb + 1) * 512], pb, eng=("act" if c_ else "dve"))
                  pb = bank((tb * 3 + 2) % 4)
                  for k in range(8):
                      P.matmul(pb[0:32, :], Wgf[:, k, 256:288], xT[:, k, tb * 512:(tb + 1) * 512],
                               start=(k == 0), stop=(k == 7))
                  P.copy(gzT[0:32, tb * 512:(tb + 1) * 512], pb[0:32, :], eng="dve")
              for t in range(NT):
                  pb = bank(4 + t % 2)
                  for k in range(8):
                      P.matmul(pb, xT[:, k, t * 128:(t + 1) * 128], Wgt[:, k, :], start=(k == 0), stop=(k == 7))
                  P.copy(gv[:, t, :], pb[:, 0:256], eng="act")
                  P.act(gate[:, t, :], pb[:, 256:512], AF.Silu)
              A.release(gla_mark)
              Bc = A.alloc([S], F32)
              Ec = A.alloc([S], F32)
              kdec_tok = A.alloc([NT, 128], BF16)
              kdT = [A.alloc([128], BF16) for _ in range(2)]
              Srot = [A.alloc([256], F32) for _ in range(2)]
              wgt_flat = Wgt.rearrange("p a b -> p (a b)")
              kdec_toks = [kdec_tok, wgt_flat[:, 0:2048].rearrange("p (a b) -> p a b", a=NT)]
              Srots = [Srot, [wgt_flat[:, 2048:2560].bitcast(F32), wgt_flat[:, 2560:3072].bitcast(F32)]]
              elast_t = small[:, 572:604].rearrange("p (d n) -> p d n", d=2)
              P.tt(gate.rearrange("p t (h v) -> p (t h) v", h=4), gate.rearrange("p t (h v) -> p (t h) v", h=4),
                   ggla.unsqueeze(1).broadcast_to([128, NT * 4, 64]), ALU.mult, eng="pool")
              for d_ in range(2):
                  for tb in range(4):
                      pb = bank(tb % 4)
                      P.matmul(pb, w2pad[0:32, d_, :], gzT[0:32, tb * 512:(tb + 1) * 512], start=True, stop=True)
                      P.act(Ec[:, tb * 512:(tb + 1) * 512], pb, AF.Exp, bias=negb2[:, d_:d_ + 1], scale=-1.0)
                  P.act(Ec, Ec, AF.Ln, bias=1.0)
                  for n in range(NT):
                      o_ = Bc[:, n * 128:(n + 1) * 128]
                      i_ = Ec[:, n * 128:(n + 1) * 128]
                      if d_ == 1:
                          o_ = o_[:, ::-1]
                          i_ = i_[:, ::-1]
                      P.add("dve", lambda e, o_=o_, i_=i_: e.tensor_tensor_scan(
                          out=o_, data0=ones_f, data1=i_, initial=0.0, op0=ALU.mult, op1=ALU.add),
                          reads=[ones_f, Ec[:, n * 128:(n + 1) * 128]], writes=[Bc[:, n * 128:(n + 1) * 128]])
                  P.act(Ec, Bc, AF.Exp, scale=-1.0 / 16.0)
                  P.act(Bc, Bc, AF.Exp, scale=1.0 / 16.0)
                  ecol = 127 if d_ == 0 else 0
                  P.copy(elast_t[:, d_, :], Ec.rearrange("p (n c) -> p n c", c=128)[:, :, ecol], eng="dve")
                  P.stt(qt[d_], gqk[:, 0, :], 32.0 ** -0.5, Ec, ALU.mult, ALU.mult)
                  P.tt(kt_[d_], gqk[:, 1, :], Bc, ALU.mult)
                  pst = bankbf(7)
                  kd_all = Ec.bitcast(BF16)[:, 0:S]
                  P.tt(kd_all.rearrange("p (n c) -> p n c", c=128), kt_[d_].rearrange("p (n c) -> p n c", c=128),
                       elast_t[:, d_, :].unsqueeze(2).broadcast_to([128, NT, 128]), ALU.mult)
                  for n in range(NT):
                      kd = kd_all[:, n * 128:(n + 1) * 128]
                      P.transpose(pst[:, (n % 8) * 128:(n % 8 + 1) * 128], kd, ident_b)
                      if n % 8 == 7:
                          P.copy(kdec_toks[d_][:, n - 7:n + 1, :], pst.rearrange("p (a b) -> p a b", a=8),
                                 eng="act")
              orders = [list(range(NT)), list(range(NT - 1, -1, -1))]
              prevs = []
              for d_ in range(2):
                  P.memset(Srots[d_][0], 0.0)
                  P.memset(Sbf[d_][:, orders[d_][0], :], 0.0)
                  prevs.append(Srots[d_][0])
              for i_ in range(NT - 1):
                  for d_ in range(2):
                      n = orders[d_][i_]
                      pb = bank(4 + 2 * d_ + i_ % 2, 256)
                      P.matmul(pb, kdec_toks[d_][:, n, :], gv[:, n, :], start=True, stop=True)
                      cur = Srots[d_][(i_ + 1) % 2]
                      P.stt(cur, prevs[d_], elast_t[:, d_, n:n + 1], pb, ALU.mult, ALU.add)
                      P.tt(Sbf[d_][:, orders[d_][i_ + 1], :], cur, bdmask, ALU.mult, eng="pool")
                      prevs[d_] = cur
              A.release(gla_mark)
              Qbd = [A.alloc([4, 128], BF16) for _ in range(4)]
              Asb = [A.alloc([4, 128], BF16) for _ in range(4)]
              ogf = [A.alloc([256], F32) for _ in range(2)]
              ogb = [A.alloc([256], BF16) for _ in range(3)]
              junk = A.alloc([64], F32)
              ss4s = [small[:, 528:532], small[:, 560:564]]
              rs4s = [small[:, 532:536], small[:, 564:568]]
              def gla_m(n):
                  po = bank(4 + n % 2, 256)
                  P.matmul(po, qt[0][:, n * 128:(n + 1) * 128], Sbf[0][:, n, :], start=True, stop=False)
                  P.matmul(po, qt[1][:, n * 128:(n + 1) * 128], Sbf[1][:, n, :], start=False, stop=False)
                  for d_ in range(2):
                      ci = 2 * n + d_
                      qb = Qbd[ci % 4]
                      P.tt(qb, qt[d_][:, n * 128:(n + 1) * 128].unsqueeze(1).broadcast_to([128, 4, 128]),
                           hmask4.unsqueeze(2).broadcast_to([128, 4, 128]), ALU.mult, eng="pool")
                      P.matmul(bank(ci % 4), kt_[d_][:, n * 128:(n + 1) * 128], qb.rearrange("p a b -> p (a b)"),
                               start=True, stop=True)
                  for d_ in range(2):
                      ci = 2 * n + d_
                      asb = Asb[ci % 4]
                      P.tt(asb, bank(ci % 4).rearrange("p (a b) -> p a b", a=4),
                           tri[d_].unsqueeze(1).broadcast_to([128, 4, 128]), ALU.mult)
                      for h in range(4):
                          P.matmul(po[:, h * 64:(h + 1) * 64], asb[:, h, :], gv[:, n, h * 64:(h + 1) * 64],
                                   start=False, stop=(d_ == 1 and h == 3))

              def gla_a(n):
                  po = bank(4 + n % 2, 256)
                  ss4 = ss4s[n % 2]
                  rs4 = rs4s[n % 2]
                  for h in range(4):
                      P.act(junk, po[:, h * 64:(h + 1) * 64], AF.Square, accum_out=ss4[:, h:h + 1])
                  P.act(rs4, ss4, AF.Ln, bias=epsc, scale=1.0 / 64.0)
                  P.act(rs4, rs4, AF.Exp, scale=-0.5)
                  of = ogf[n % 2]
                  of3 = of.rearrange("p (h v) -> p h v", h=4)
                  P.tt(of3, po.rearrange("p (h v) -> p h v", h=4), rs4.unsqueeze(2).broadcast_to([128, 4, 64]),
                       ALU.mult)
                  ob = ogb[n % 3]
                  P.tt(ob, of, gate[:, n, :], ALU.mult)
                  for c_ in range(2):
                      dst_ = ocat[:, 6 + c_, n * 128:(n + 1) * 128]
                      P.add("sp", lambda e, dst_=dst_, src_=ob[:, c_ * 128:(c_ + 1) * 128]: e.dma_start(
                          out=dst_, in_=src_, transpose=True),
                          reads=[ob[:, c_ * 128:(c_ + 1) * 128]], writes=[dst_], dma=True)

              for s_ in range(NT + 1):
                  if s_ < NT:
                      gla_m(s_)
                  if s_ >= 1:
                      gla_a(s_ - 1)
              A.release(mixer_mark)
              if l == 0:
                  dump("ogla", ocat[:, 6:8, :], [128, 2, S], BF16)
              ck('gla')

              A.n = ARENA - XTB
              assert A.top <= A.n
              post_mark = A.mark()
              ytmp = [A.alloc([D], F32) for _ in range(3)]
              xbs = [A.alloc([D], BF16) for _ in range(2)]
              for t in range(NT):
                  P.dma("sp", x_tok[:, t, :], xres_d[t * 128:(t + 1) * 128, :])
              def mm1_(t):
                  b0 = (t % 3) * 2
                  p2 = ps[:, b0 * 512:(b0 + 2) * 512]
                  for hf in range(2):
                      for e_ in range(8):
                          P.matmul(p2[:, hf * 512:(hf + 1) * 512], ocat[:, e_, t * 128:(t + 1) * 128],
                                   Wout[:, e_, hf * 512:(hf + 1) * 512], start=(e_ == 0), stop=(e_ == 7))

              def a1_(t, mid=None):
                  b0 = (t % 3) * 2
                  ln_a(ps[:, b0 * 512:(b0 + 2) * 512], x_tok[:, t, :], lnbuf, ytmp[t % 3], t % 2, mid=mid)

              def b1_(t):
                  ln_b(lnbuf, x_tok[:, t, :], ytmp[t % 3])

              def c1_(t):
                  make_xT(x_tok[:, t, :], t, xbs[t % 2])

              for s_ in range(NT + 3):
                  if s_ < NT:
                      mm1_(s_)
                  midf = (lambda t_=s_ - 3: b1_(t_)) if 0 <= s_ - 3 < NT else None
                  if 0 <= s_ - 1 < NT:
                      a1_(s_ - 1, mid=midf)
                  elif midf is not None:
                      midf()
                  if 0 <= s_ - 3 < NT:
                      c1_(s_ - 3)
              A.release(base_mark)
              if l == 0:
                  dump("x1", x_tok, [128, NT, D])
              ck('ln1')

              GF = NFC // 2
              Wdg = A.alloc([GF, D], BF16)
              hT = A.alloc([GF, S], BF16)
              NWB = 3
              wgu = [A.alloc([2, 8, 128], BF16) for _ in range(NWB)]
              sg = [A.alloc([512], BF16) for _ in range(2)]
              ytmp = [A.alloc([D], F32) for _ in range(1)]
              xbs = [A.alloc([D], BF16) for _ in range(2)]
              wdv = wd_d[l].rearrange("(f p) n -> p f n", p=128)
              wgv = wg_d[l].rearrange("(k p) n -> p k n", p=128)
              wuv = wu_d[l].rearrange("(k p) n -> p k n", p=128)
              P.dma("sp", lnbuf[:, 0, :], ln_d["ln2_g"][l].partition_broadcast(128))
              P.dma("sp", lnbuf[:, 1, :], ln_d["ln2_b"][l].partition_broadcast(128))
              wi = 0
              ui = 0
              for g_ in range(2):
                  for fi in range(GF):
                      f = g_ * GF + fi
                      wb = wgu[wi % NWB]
                      wi += 1
                      P.dma("pool", wb[:, 0, :, :], wgv[:, :, f * 128:(f + 1) * 128])
                      P.dma("pool", wb[:, 1, :, :], wuv[:, :, f * 128:(f + 1) * 128])
                      P.dma("pool", Wdg[:, fi, :], wdv[:, f, :])
                      for tb in range(4):
                          pg = bank((ui % 2) * 2)
                          pu = bank((ui % 2) * 2 + 1)
                          ui += 1
                          for k in range(8):
                              P.matmul(pg, wb[:, 0, k, :], xT[:, k, tb * 512:(tb + 1) * 512],
                                       start=(k == 0), stop=(k == 7))
                          for k in range(8):
                              P.matmul(pu, wb[:, 1, k, :], xT[:, k, tb * 512:(tb + 1) * 512],
                                       start=(k == 0), stop=(k == 7))
                          s_ = sg[ui % 2]
                          P.act(s_, pg, AF.Silu)
                          P.tt(hT[:, fi, tb * 512:(tb + 1) * 512], pu, s_, ALU.mult)
                  ytmps = [ytmp[0], wgu[0].rearrange("p a b c -> p (a b c)").bitcast(F32),
                           wgu[1].rearrange("p a b c -> p (a b c)").bitcast(F32)]

                  def mm2_(t):
                      b0 = (t % 3) * 2
                      p2 = ps[:, b0 * 512:(b0 + 2) * 512]
                      for hf in range(2):
                          for fi in range(GF):
                              P.matmul(p2[:, hf * 512:(hf + 1) * 512], hT[:, fi, t * 128:(t + 1) * 128],
                                       Wdg[:, fi, hf * 512:(hf + 1) * 512], start=(fi == 0), stop=(fi == GF - 1))

                  def a2_(t, mid=None):
                      b0 = (t % 3) * 2
                      p2 = ps[:, b0 * 512:(b0 + 2) * 512]
                      if g_ == 0:
                          P.stt(x_tok[:, t, :], x_tok[:, t, :], ALPHA, p2, ALU.mult, ALU.add)
                          if mid is not None:
                              mid()
                      else:
                          ln_a(p2, x_tok[:, t, :], lnbuf, ytmps[t % 3], t % 2, alpha=1.0, mid=mid)

                  def b2_(t):
                      if g_ == 0:
                          return
                      ln_b(lnbuf, x_tok[:, t, :], ytmps[t % 3])

                  def c2_(t):
                      if g_ == 0:
                          return
                      if l == n_layers - 1:
                          P.dma("sp", y_d[t * 128:(t + 1) * 128, :], x_tok[:, t, :])
                      else:
                          P.dma("sp", xs_d[t * 128:(t + 1) * 128, :], x_tok[:, t, :])
                          make_xT(x_tok[:, t, :], t, xbs[t % 2])

                  for s_ in range(NT + 3):
                      if s_ < NT:
                          mm2_(s_)
                      midf = (lambda t_=s_ - 3: b2_(t_)) if 0 <= s_ - 3 < NT else None
                      if 0 <= s_ - 1 < NT:
                          a2_(s_ - 1, mid=midf)
                      elif midf is not None:
                          midf()
                      if 0 <= s_ - 3 < NT:
                          c2_(s_ - 3)
              A.release(base_mark)
              A.n = ARENA

        except _Stop:
            pass
        P.emit()
    _CACHE['P'] = P
    return nc, dbg_outs


def kernel(**inputs):
    if "c" not in _CACHE:
        _CACHE["c"] = host_consts()
    cf, cb, dc, ds = _CACHE["c"]
    nc, _ = build()
    x = np.ascontiguousarray(inputs["x"], dtype=np.float32)
    common = {
        "w_in": np.ascontiguousarray(inputs["w_in"], dtype=np.float32),
        "diff_lambda": np.ascontiguousarray(inputs["diff_lambda"], dtype=np.float32).reshape(L, 256),
        "diff_norm_g": np.ascontiguousarray(inputs["diff_norm_g"], dtype=np.float32),
        "fourier_w": np.ascontiguousarray(inputs["fourier_w"], dtype=np.float32),
        "gla_gate_w2": np.ascontiguousarray(inputs["gla_gate_w2"], dtype=np.float32),
        "gla_gate_b2": np.ascontiguousarray(inputs["gla_gate_b2"], dtype=np.float32),
        "gla_norm_g": np.ascontiguousarray(inputs["gla_norm_g"], dtype=np.float32),
        "w_out": np.ascontiguousarray(inputs["w_out"], dtype=np.float32),
        "ln1_g": np.ascontiguousarray(inputs["ln1_g"], dtype=np.float32),
        "ln1_b": np.ascontiguousarray(inputs["ln1_b"], dtype=np.float32),
        "ln2_g": np.ascontiguousarray(inputs["ln2_g"], dtype=np.float32),
        "ln2_b": np.ascontiguousarray(inputs["ln2_b"], dtype=np.float32),
        "ffn_w_gate": np.ascontiguousarray(inputs["ffn_w_gate"], dtype=np.float32),
        "ffn_w_up": np.ascontiguousarray(inputs["ffn_w_up"], dtype=np.float32),
        "ffn_w_down": np.ascontiguousarray(inputs["ffn_w_down"], dtype=np.float32),
        "c_f32": cf, "c_bf": cb, "dft_c": dc, "dft_s": ds,
    }
    in_maps = [dict(common, x=x[b]) for b in range(8)]
    res = run_bass_kernel_spmd(nc, in_maps, core_ids=list(range(8)))
    return np.stack([np.asarray(r["y"], dtype=np.float32) for r in res.results], axis=0)
```
